# Optimizing a Trainium2 kernel written in Bass

```python
import math
import jax, jax.numpy as jnp
from jax import lax
import numpy as np

D_MODEL = 1024
BATCH = 2
SEQ = 16384
DEPTH = 4

N_MIXERS = 4
HEAD_DIM = 64
Q_BLOCK = 128
D_FF = 2816
EPS = 1e-6
NEG = -1e30
FORCE = 1e9

NSA_HQ = 16
NSA_HKV = 4
CMP_BLOCK = 32
CMP_STRIDE = 16
CMP_HIDDEN = 256
SEL_BLOCK = 64
SEL_TOPK = 16
NSA_WINDOW = 512
NSA_IN = NSA_HQ * HEAD_DIM + 6 * NSA_HKV * HEAD_DIM + 3 * NSA_HQ

DIFF_H = 8
DIFF_IN = 3 * DIFF_H * 2 * HEAD_DIM

GDN_H = 8
GDN_DK = 128
GDN_DV = 128
GDN_CONV = 4
GDN_CHUNK = 64
GDN_QKV = GDN_H * (2 * GDN_DK + GDN_DV)
GDN_IN = GDN_QKV + 2 * GDN_H + GDN_H * GDN_DV

SWA_HQ = 16
SWA_HKV = 4
SWA_WINDOW = 128
SWA_IN = (SWA_HQ + 2 * SWA_HKV) * HEAD_DIM

kernel_name = "hybrid_nsa_diff_gdn_swa_macaron"

f32 = jnp.float32


def rmsnorm(x, g):
    xf = x.astype(f32)
    y = xf * lax.rsqrt(jnp.mean(xf * xf, axis=-1, keepdims=True) + EPS)
    return (y * g.astype(f32)).astype(x.dtype)


def l2norm(x):
    xf = x.astype(f32)
    return xf * lax.rsqrt(jnp.sum(xf * xf, axis=-1, keepdims=True) + EPS)


def swiglu(x, w_in, w_out):
    a, b = jnp.split(x @ w_in, 2, axis=-1)
    return (jax.nn.silu(a) * b) @ w_out


def alibi_slopes(n):
    return 2.0 ** (-8.0 * jnp.arange(1, n + 1, dtype=f32) / n)


def masked_softmax(s, mask):
    return jax.nn.softmax(jnp.where(mask, s, NEG), axis=-1) * mask


def banded_attention(q, k, v, slopes, window, sinks=None):
    B, Hkv, G, S, dh = q.shape
    nb = S // Q_BLOCK
    span = window + Q_BLOCK
    pad = ((0, 0), (0, 0), (window, 0), (0, 0))
    kp, vp = jnp.pad(k, pad), jnp.pad(v, pad)
    qb = q.reshape(B, Hkv, G, nb, Q_BLOCK, dh).transpose(3, 0, 1, 2, 4, 5)
    scale = dh ** -0.5

    def block(args):
        qi, i = args
        start = i * Q_BLOCK
        ki = lax.dynamic_slice_in_dim(kp, start, span, axis=2)
        vi = lax.dynamic_slice_in_dim(vp, start, span, axis=2)
        t = start + jnp.arange(Q_BLOCK)
        s_pos = start - window + jnp.arange(span)
        dist = t[:, None] - s_pos[None, :]
        mask = (dist >= 0) & (dist < window) & (s_pos[None, :] >= 0)
        sc = jnp.einsum('bhgtd,bhsd->bhgts', qi, ki).astype(f32) * scale \
            - slopes[:, :, None, None] * dist
        if sinks is None:
            p = masked_softmax(sc, mask)
        else:
            sink = jnp.broadcast_to(sinks.astype(f32)[:, :, None, None], sc.shape[:-1] + (1,))
            p = jax.nn.softmax(jnp.concatenate([jnp.where(mask, sc, NEG), sink], axis=-1), axis=-1)[..., :-1]
        return jnp.einsum('bhgts,bhsd->bhgtd', p.astype(v.dtype), vi)

    o = lax.map(block, (qb, jnp.arange(nb)))
    return o.transpose(1, 2, 3, 0, 4, 5).reshape(B, Hkv, G, S, dh)


def compress_blocks(t, pos, w1, w2):
    B, H, S, dh = t.shape
    ratio = CMP_BLOCK // CMP_STRIDE
    n_sub = S // CMP_STRIDE
    n_cmp = n_sub - ratio + 1
    sub = t.reshape(B, H, n_sub, CMP_STRIDE, dh)
    blocks = jnp.concatenate([sub[:, :, r:r + n_cmp] for r in range(ratio)], axis=3)
    blocks = (blocks + pos).reshape(B, H, n_cmp, CMP_BLOCK * dh)
    return jax.nn.gelu(blocks @ w1) @ w2


def nsa_compressed_selected(q, kc, vc, ks, vs, slopes):
    B, Hkv, G, S, dh = q.shape
    nb = S // Q_BLOCK
    n_cmp = kc.shape[2]
    n_sel = S // SEL_BLOCK
    n_top = min(SEL_TOPK, n_sel)
    scale = dh ** -0.5
    cmp_start = jnp.arange(n_cmp) * CMP_STRIDE
    cmp_end = cmp_start + CMP_BLOCK - 1
    sel_start = jnp.arange(n_sel) * SEL_BLOCK
    overlap = ((cmp_start[:, None] < sel_start[None, :] + SEL_BLOCK)
               & (cmp_start[:, None] + CMP_BLOCK > sel_start[None, :])).astype(f32)
    ksb = ks.reshape(B, Hkv, n_sel, SEL_BLOCK, dh)
    vsb = vs.reshape(B, Hkv, n_sel, SEL_BLOCK, dh)
    gather = jax.vmap(jax.vmap(lambda tb, idx: tb[idx]))
    qb = q.reshape(B, Hkv, G, nb, Q_BLOCK, dh).transpose(3, 0, 1, 2, 4, 5)
    jsel = jnp.arange(n_sel)

    def block(args):
        qi, i = args
        t = i * Q_BLOCK + jnp.arange(Q_BLOCK)
        dist_c = t[:, None] - cmp_end[None, :]
        sc = jnp.einsum('bhgtd,bhnd->bhgtn', qi, kc).astype(f32) * scale \
            - slopes[:, :, None, None] * dist_c
        p_c = masked_softmax(sc, dist_c >= 0)
        o_cmp = jnp.einsum('bhgtn,bhnd->bhgtd', p_c.astype(vc.dtype), vc)
        imp = jnp.einsum('bhgtn,ns->bhts', p_c, overlap)
        cur = t // SEL_BLOCK
        forced = (jsel[None, :] == 0) | (jsel[None, :] == cur[:, None]) | (jsel[None, :] == cur[:, None] - 1)
        causal = jsel[None, :] <= cur[:, None]
        imp = jnp.where(forced, FORCE, jnp.where(causal, imp, -jnp.inf))
        _, idx = lax.top_k(imp, n_top)
        kg = gather(ksb, idx).reshape(B, Hkv, Q_BLOCK, n_top * SEL_BLOCK, dh)
        vg = gather(vsb, idx).reshape(B, Hkv, Q_BLOCK, n_top * SEL_BLOCK, dh)
        kpos = (idx[..., None] * SEL_BLOCK + jnp.arange(SEL_BLOCK)).reshape(B, Hkv, Q_BLOCK, n_top * SEL_BLOCK)
        dist_s = (t[None, None, :, None] - kpos)[:, :, None]
        ss = jnp.einsum('bhgtd,bhtkd->bhgtk', qi, kg).astype(f32) * scale \
            - slopes[:, :, None, None] * dist_s
        p_s = masked_softmax(ss, dist_s >= 0)
        o_sel = jnp.einsum('bhgtk,bhtkd->bhgtd', p_s.astype(vs.dtype), vg)
        return o_cmp, o_sel

    o_cmp, o_sel = lax.map(block, (qb, jnp.arange(nb)))
    back = lambda o: o.transpose(1, 2, 3, 0, 4, 5).reshape(B, Hkv, G, S, dh)
    return back(o_cmp), back(o_sel)


def nsa_mixer(h, w_in, w_out, q_norm, k_norm, cmp_pos, cmp_w1, cmp_w2):
    B, S, _ = h.shape
    G = NSA_HQ // NSA_HKV
    dh = HEAD_DIM
    splits = np.cumsum([NSA_HQ * dh] + [NSA_HKV * dh] * 6).tolist()
    q, kc, vc, ks, vs, kw, vw, gates = jnp.split(h @ w_in, splits, axis=-1)
    q = rmsnorm(q.reshape(B, S, NSA_HKV, G, dh), q_norm).transpose(0, 2, 3, 1, 4)
    heads = lambda t: t.reshape(B, S, NSA_HKV, dh).transpose(0, 2, 1, 3)
    kc = rmsnorm(compress_blocks(heads(kc), cmp_pos[0], cmp_w1[0], cmp_w2[0]), k_norm[0])
    vc = compress_blocks(heads(vc), cmp_pos[1], cmp_w1[1], cmp_w2[1])
    ks, vs = rmsnorm(heads(ks), k_norm[1]), heads(vs)
    kw, vw = rmsnorm(heads(kw), k_norm[2]), heads(vw)
    slopes = alibi_slopes(NSA_HQ).reshape(NSA_HKV, G)
    o_cmp, o_sel = nsa_compressed_selected(q, kc, vc, ks, vs, slopes)
    o_win = banded_attention(q, kw, vw, slopes, NSA_WINDOW)
    g = jax.nn.sigmoid(gates.reshape(B, S, NSA_HKV, G, 3).transpose(0, 2, 3, 1, 4))
    o = g[..., 0:1] * o_cmp + g[..., 1:2] * o_sel + g[..., 2:3] * o_win
    return o.transpose(0, 3, 1, 2, 4).reshape(B, S, NSA_HQ * dh) @ w_out


def dense_diff_attention(q, k, v, slopes, lmbda):
    B, H, _, S, dh = q.shape
    nb = S // Q_BLOCK
    scale = dh ** -0.5
    qb = q.reshape(B, H, 2, nb, Q_BLOCK, dh).transpose(3, 0, 1, 2, 4, 5)
    s_pos = jnp.arange(S)

    def block(args):
        qi, i = args
        t = i * Q_BLOCK + jnp.arange(Q_BLOCK)
        dist = t[:, None] - s_pos[None, :]
        sc = jnp.einsum('bhctd,bhcsd->bhcts', qi, k).astype(f32) * scale \
            - slopes[:, None, None, None] * dist
        p = masked_softmax(sc, dist >= 0)
        a = p[:, :, 0] - lmbda * p[:, :, 1]
        return jnp.einsum('bhts,bhsd->bhtd', a.astype(v.dtype), v)

    o = lax.map(block, (qb, jnp.arange(nb)))
    return o.transpose(1, 2, 0, 3, 4).reshape(B, H, S, 2 * dh)


def diff_mixer(h, w_in, w_out, q_norm, k_norm, lam, subln, lam_init):
    B, S, _ = h.shape
    H, dh = DIFF_H, HEAD_DIM
    q, k, v = jnp.split(h @ w_in, 3, axis=-1)
    q = rmsnorm(q.reshape(B, S, H, 2, dh), q_norm).transpose(0, 2, 3, 1, 4)
    k = rmsnorm(k.reshape(B, S, H, 2, dh), k_norm).transpose(0, 2, 3, 1, 4)
    v = v.reshape(B, S, H, 2 * dh).transpose(0, 2, 1, 3)
    lf = lam.astype(f32)
    lmbda = jnp.exp(jnp.sum(lf[0] * lf[1])) - jnp.exp(jnp.sum(lf[2] * lf[3])) + lam_init
    o = dense_diff_attention(q, k, v, alibi_slopes(H), lmbda)
    o = rmsnorm(o, subln) * (1.0 - lam_init)
    return o.transpose(0, 2, 1, 3).reshape(B, S, H * 2 * dh) @ w_out


def causal_conv(x, w):
    K, C = w.shape
    return lax.conv_general_dilated(x, w[:, None, :], window_strides=(1,), padding=[(K - 1, 0)],
                                    dimension_numbers=('NWC', 'WIO', 'NWC'), feature_group_count=C)


def chunked_gated_delta(q, k, v, beta, g):
    B, S, H, dk = q.shape
    dv = v.shape[-1]
    C = GDN_CHUNK
    n = S // C
    ch = lambda t: jnp.moveaxis(t.astype(f32).reshape((B, n, C, H) + t.shape[3:]), 3, 1)
    q, k, v, beta, g = ch(q), ch(k), ch(v), ch(beta), ch(g)
    gc = jnp.cumsum(g, axis=-1)
    tril = jnp.tril(jnp.ones((C, C), bool))
    strict = jnp.tril(jnp.ones((C, C), bool), -1)
    decay = jnp.exp(jnp.where(tril, gc[..., :, None] - gc[..., None, :], -jnp.inf))
    kb = k * beta[..., None]
    A = jnp.where(strict, jnp.einsum('bhnid,bhnjd->bhnij', kb, k) * decay, 0.0)
    eye = jnp.eye(C, dtype=f32)
    T = lax.linalg.triangular_solve(eye + A, jnp.broadcast_to(eye, A.shape), left_side=True, lower=True)
    u = T @ (v * beta[..., None])
    w = T @ (kb * jnp.exp(gc)[..., None])
    attn = jnp.einsum('bhnid,bhnjd->bhnij', q, k) * decay
    qg = q * jnp.exp(gc)[..., None]
    kd = k * jnp.exp(gc[..., -1:] - gc)[..., None]
    glast = jnp.exp(gc[..., -1])

    def step(state, xs):
        qg_i, w_i, u_i, attn_i, kd_i, gl_i = xs
        v_new = u_i - w_i @ state
        o = qg_i @ state + attn_i @ v_new
        state = state * gl_i[..., None, None] + jnp.swapaxes(kd_i, -1, -2) @ v_new
        return state, o

    xs = tuple(jnp.moveaxis(t, 2, 0) for t in (qg, w, u, attn, kd, glast))
    _, o = lax.scan(step, jnp.zeros((B, H, dk, dv), f32), xs)
    return o.transpose(1, 0, 3, 2, 4).reshape(B, S, H, dv)


def gdn_mixer(h, w_in, w_out, conv_w, a_log, dt_bias, o_norm):
    B, S, _ = h.shape
    H, dk, dv = GDN_H, GDN_DK, GDN_DV
    qkv, b, a, gate = jnp.split(h @ w_in, [GDN_QKV, GDN_QKV + H, GDN_QKV + 2 * H], axis=-1)
    qkv = jax.nn.silu(causal_conv(qkv, conv_w))
    q, k, v = jnp.split(qkv, [H * dk, 2 * H * dk], axis=-1)
    q = l2norm(q.reshape(B, S, H, dk)) * dk ** -0.5
    k = l2norm(k.reshape(B, S, H, dk))
    v = v.reshape(B, S, H, dv)
    beta = jax.nn.sigmoid(b.astype(f32))
    log_alpha = -jnp.exp(a_log.astype(f32)) * jax.nn.softplus(a.astype(f32) + dt_bias.astype(f32))
    o = chunked_gated_delta(q, k, v, beta, log_alpha).astype(h.dtype)
    o = rmsnorm(o, o_norm) * jax.nn.silu(gate.reshape(B, S, H, dv))
    return o.reshape(B, S, H * dv) @ w_out


def swa_mixer(h, w_in, w_out, q_norm, k_norm, sinks):
    B, S, _ = h.shape
    G = SWA_HQ // SWA_HKV
    dh = HEAD_DIM
    q, k, v = jnp.split(h @ w_in, [SWA_HQ * dh, (SWA_HQ + SWA_HKV) * dh], axis=-1)
    q = rmsnorm(q.reshape(B, S, SWA_HKV, G, dh), q_norm).transpose(0, 2, 3, 1, 4)
    k = rmsnorm(k.reshape(B, S, SWA_HKV, dh), k_norm).transpose(0, 2, 1, 3)
    v = v.reshape(B, S, SWA_HKV, dh).transpose(0, 2, 1, 3)
    slopes = alibi_slopes(SWA_HQ).reshape(SWA_HKV, G)
    o = banded_attention(q, k, v, slopes, SWA_WINDOW, sinks.reshape(SWA_HKV, G))
    return o.transpose(0, 3, 1, 2, 4).reshape(B, S, SWA_HQ * dh) @ w_out


def setup_inputs(seed: int = 0) -> dict:
    key = jax.random.key(seed)
    ks = iter(jax.random.split(key, 40))
    nrm = lambda shape, scale: jax.random.normal(next(ks), shape, f32) * scale
    gain = lambda shape: 1.0 + 0.02 * jax.random.normal(next(ks), shape, f32)
    n_of = lambda m: len(range(m, DEPTH, N_MIXERS))
    nA, nB, nC, nD = n_of(0), n_of(1), n_of(2), n_of(3)
    D, dh = D_MODEL, HEAD_DIM
    dt = jnp.exp(jax.random.uniform(next(ks), (nC, GDN_H), f32, math.log(1e-3), math.log(1e-1)))
    return {
        "x": jax.random.normal(next(ks), (BATCH, SEQ, D), f32),
        "ffn1_norm": gain((DEPTH, D)),
        "ffn1_w_in": nrm((DEPTH, D, 2 * D_FF), D ** -0.5),
        "ffn1_w_out": nrm((DEPTH, D_FF, D), D_FF ** -0.5),
        "mix_norm": gain((DEPTH, D)),
        "ffn2_norm": gain((DEPTH, D)),
        "ffn2_w_in": nrm((DEPTH, D, 2 * D_FF), D ** -0.5),
        "ffn2_w_out": nrm((DEPTH, D_FF, D), D_FF ** -0.5),
        "nsa_w_in": nrm((nA, D, NSA_IN), D ** -0.5),
        "nsa_w_out": nrm((nA, NSA_HQ * dh, D), (NSA_HQ * dh) ** -0.5),
        "nsa_q_norm": gain((nA, dh)),
        "nsa_k_norm": gain((nA, 3, dh)),
        "nsa_cmp_pos": nrm((nA, 2, CMP_BLOCK, dh), 0.02),
        "nsa_cmp_w1": nrm((nA, 2, CMP_BLOCK * dh, CMP_HIDDEN), (CMP_BLOCK * dh) ** -0.5),
        "nsa_cmp_w2": nrm((nA, 2, CMP_HIDDEN, dh), CMP_HIDDEN ** -0.5),
        "diff_w_in": nrm((nB, D, DIFF_IN), D ** -0.5),
        "diff_w_out": nrm((nB, DIFF_H * 2 * dh, D), (DIFF_H * 2 * dh) ** -0.5),
        "diff_q_norm": gain((nB, dh)),
        "diff_k_norm": gain((nB, dh)),
        "diff_lambda": nrm((nB, 4, dh), 0.1),
        "diff_subln": gain((nB, 2 * dh)),
        "gdn_w_in": nrm((nC, D, GDN_IN), D ** -0.5),
        "gdn_w_out": nrm((nC, GDN_H * GDN_DV, D), (GDN_H * GDN_DV) ** -0.5),
        "gdn_conv_w": nrm((nC, GDN_CONV, GDN_QKV), GDN_CONV ** -0.5),
        "gdn_a_log": jnp.log(jax.random.uniform(next(ks), (nC, GDN_H), f32, 1.0, 16.0)),
        "gdn_dt_bias": dt + jnp.log(-jnp.expm1(-dt)),
        "gdn_o_norm": gain((nC, GDN_DV)),
        "swa_w_in": nrm((nD, D, SWA_IN), D ** -0.5),
        "swa_w_out": nrm((nD, SWA_HQ * dh, D), (SWA_HQ * dh) ** -0.5),
        "swa_q_norm": gain((nD, dh)),
        "swa_k_norm": gain((nD, dh)),
        "swa_sinks": nrm((nD, SWA_HQ), 1.0),
    }


def reference(x, ffn1_norm, ffn1_w_in, ffn1_w_out, mix_norm, ffn2_norm, ffn2_w_in, ffn2_w_out,
              nsa_w_in, nsa_w_out, nsa_q_norm, nsa_k_norm, nsa_cmp_pos, nsa_cmp_w1, nsa_cmp_w2,
              diff_w_in, diff_w_out, diff_q_norm, diff_k_norm, diff_lambda, diff_subln,
              gdn_w_in, gdn_w_out, gdn_conv_w, gdn_a_log, gdn_dt_bias, gdn_o_norm,
              swa_w_in, swa_w_out, swa_q_norm, swa_k_norm, swa_sinks):
    for layer in range(DEPTH):
        kind, j = layer % N_MIXERS, layer // N_MIXERS
        x = x + 0.5 * swiglu(rmsnorm(x, ffn1_norm[layer]), ffn1_w_in[layer], ffn1_w_out[layer])
        hn = rmsnorm(x, mix_norm[layer])
        if kind == 0:
            y = nsa_mixer(hn, nsa_w_in[j], nsa_w_out[j], nsa_q_norm[j], nsa_k_norm[j],
                          nsa_cmp_pos[j], nsa_cmp_w1[j], nsa_cmp_w2[j])
        elif kind == 1:
            lam_init = 0.8 - 0.6 * math.exp(-0.3 * layer)
            y = diff_mixer(hn, diff_w_in[j], diff_w_out[j], diff_q_norm[j], diff_k_norm[j],
                           diff_lambda[j], diff_subln[j], lam_init)
        elif kind == 2:
            y = gdn_mixer(hn, gdn_w_in[j], gdn_w_out[j], gdn_conv_w[j], gdn_a_log[j],
                          gdn_dt_bias[j], gdn_o_norm[j])
        else:
            y = swa_mixer(hn, swa_w_in[j], swa_w_out[j], swa_q_norm[j], swa_k_norm[j], swa_sinks[j])
        x = x + y
        x = x + 0.5 * swiglu(rmsnorm(x, ffn2_norm[layer]), ffn2_w_in[layer], ffn2_w_out[layer])
    return x
```

```python
import contextlib
import numpy as np
import concourse.bass as bass
import concourse.mybir as mybir
from concourse.bass_utils import run_bass_kernel_spmd

F32 = mybir.dt.float32
BF16 = mybir.dt.bfloat16
ALU = mybir.AluOpType
AF = mybir.ActivationFunctionType

N_CORES = 8
D = 1024
DFF = 2816
EPS = 1e-6

COMPUTE = ("tensor", "vector", "scalar", "gpsimd")


PSUM_NAMES = {"ps_s", "ps_n", "psp", "psa", "psb", "psy", "st", "num", "den", "imp"}


def is_psum_key(k):
    n = k[0] if isinstance(k, tuple) else k
    return isinstance(n, str) and (n in PSUM_NAMES or (len(n) == 2 and n[0] == "b" and n[1].isdigit()))


class Sched:
    def __init__(self, nc):
        self.nc = nc
        self.ops = []

    LIMIT = None
    seen = 0

    def add(self, eng, fn, reads=(), writes=(), dma=None):
        Sched.seen += 1
        if Sched.LIMIT is not None and Sched.seen > Sched.LIMIT and not (dma is not None and not writes):
            return -1
        self.ops.append(dict(eng=eng, fn=fn, reads=tuple(reads), writes=tuple(writes), dma=dma))
        return len(self.ops) - 1

    def emit(self, final_wait_engine="sync"):
        nc = self.nc
        ops = self.ops
        last_writer = {}
        readers = {}
        deps = []
        for i, op in enumerate(ops):
            d = set()
            for r in op["reads"]:
                if r in last_writer:
                    d.add(last_writer[r])
                if is_psum_key(r):
                    for j in readers.get(r, ()):
                        if ops[j]["eng"] != op["eng"]:
                            d.add(j)
            for w in op["writes"]:
                if w in last_writer:
                    d.add(last_writer[w])
                for j in readers.get(w, ()):
                    d.add(j)
            d.discard(i)
            deps.append(d)
            for r in op["reads"]:
                readers.setdefault(r, []).append(i)
            for w in op["writes"]:
                last_writer[w] = i
                readers[w] = []

        def pe_pe(i, j):
            return ops[j]["eng"] == "tensor" and ops[i]["eng"] == "tensor" and ops[j]["dma"] is None \
                and ops[i]["dma"] is None

        needed = set()
        for i, d in enumerate(deps):
            for j in d:
                if not pe_pe(i, j):
                    needed.add(j)
        dma_keys = []
        for op in ops:
            if op["dma"] is not None and op["dma"] not in dma_keys:
                dma_keys.append(op["dma"])
        engines_used = []
        for op in ops:
            if op["eng"] not in engines_used:
                engines_used.append(op["eng"])
        if final_wait_engine not in engines_used:
            engines_used.append(final_wait_engine)

        with contextlib.ExitStack() as st:
            esem = {e: st.enter_context(nc.semaphore("s_" + e)) for e in COMPUTE}
            dsem = {k: st.enter_context(nc.semaphore("d_%d" % n)) for n, k in enumerate(dma_keys)}
            cnt = {e: 0 for e in COMPUTE}
            dcnt = {k: 0 for k in dma_keys}
            sig = {}
            for i, op in enumerate(ops):
                if op["dma"] is not None:
                    dcnt[op["dma"]] += 16
                    sig[i] = (dsem[op["dma"]], dcnt[op["dma"]], ("d", op["dma"]))
                elif i in needed:
                    cnt[op["eng"]] += 1
                    sig[i] = (esem[op["eng"]], cnt[op["eng"]], ("e", op["eng"]))
            per_eng = {e: [] for e in engines_used}
            for i, op in enumerate(ops):
                per_eng[op["eng"]].append(i)
            final = [(dsem[k], dcnt[k]) for k in dma_keys]
            block = st.enter_context(nc.Block())

            def make(ename):
                def body(e):
                    seen = {}
                    for i in per_eng[ename]:
                        op = ops[i]
                        waits = {}
                        for j in deps[i]:
                            if j not in sig or pe_pe(i, j):
                                continue
                            s, v, key = sig[j]
                            if seen.get(key, 0) >= v:
                                continue
                            if key not in waits or waits[key][1] < v:
                                waits[key] = (s, v)
                        for key, (s, v) in waits.items():
                            e.wait_ge(s, v)
                            seen[key] = v
                        ins = op["fn"](e)
                        if i in sig:
                            ins.then_inc(sig[i][0], 16 if op["dma"] is not None else 1)
                    if ename == final_wait_engine:
                        for s, v in final:
                            if v > 0:
                                e.wait_ge(s, v)
                return body

            for ename in engines_used:
                getattr(block, ename)(make(ename))
        return len(ops)


def build_ffn(T, TT=512):
    nc = bass.Bass("TRN2", target_bir_lowering=False)
    xT = nc.dram_tensor("xT", [D, T], F32, kind="ExternalInput").ap()
    g = nc.dram_tensor("g", [128, 8], F32, kind="ExternalInput").ap()
    w_in = nc.dram_tensor("w_in", [D, 2 * DFF], F32, kind="ExternalInput").ap()
    w_out = nc.dram_tensor("w_out", [DFF, D], F32, kind="ExternalInput").ap()
    oT = nc.dram_tensor("oT", [D, T], F32, kind="ExternalOutput").ap()
    NJ = DFF // 128
    GW = 704
    ntiles = T // TT
    xv = xT.rearrange("(c p) t -> p c t", p=128)
    ov = oT.rearrange("(c p) t -> p c t", p=128)
    wiv = w_in.rearrange("(c p) f -> p c f", p=128)
    wov = w_out.rearrange("(j p) d -> p j d", p=128)
    with contextlib.ExitStack() as st:
        sb = lambda name, shape, dt: st.enter_context(nc.sbuf_tensor(name, shape, dt))
        ps = lambda name: st.enter_context(nc.psum_tensor(name, [128, TT], F32))
        wi = sb("wi", [128, 8, 2 * DFF], BF16)
        wo = sb("wo", [128, NJ, D], BF16)
        stage = [sb("stage%d" % i, [128, GW], F32) for i in range(2)]
        gs = sb("gs", [128, 8], F32)
        ones = sb("ones", [128, 128], BF16)
        epsb = sb("epsb", [128, 1], F32)
        xts = [sb("xt%d" % i, [128, 8, TT], F32) for i in range(2)]
        h = sb("h", [128, 8, TT], BF16)
        rstd = sb("rstd", [128, TT], F32)
        sa = [sb("sa%d" % i, [128, TT], F32) for i in range(2)]
        act = sb("act", [128, NJ, TT], BF16)
        sq = act
        ps_s = ps("ps_s")
        ps_a = [ps("ps_a%d" % i) for i in range(2)]
        ps_b = [ps("ps_b%d" % i) for i in range(2)]
        ps_y = [ps("ps_y%d" % i) for i in range(2)]

        S = Sched(nc)
        S.add("vector", lambda e: e.memset(ones[:], 1.0 / D), writes=["ones"])
        S.add("vector", lambda e: e.memset(epsb[:], EPS), writes=["epsb"])
        S.add("sync", lambda e: e.dma_start(out=gs[:], in_=g), writes=["gs"], dma="gs")
        XTK = lambda b: [("xt", b, c) for c in range(8)]

        def load_x(t):
            b = t % 2
            S.add("sync", lambda e, t=t, b=b: e.dma_start(out=xts[b][:], in_=xv[:, :, t * TT:(t + 1) * TT]),
                  writes=XTK(b), dma=("xt_in", b))

        load_x(0)
        k = 0
        cast_engs = ["vector", "gpsimd"]
        ngrp = 2 * DFF // GW
        order = []
        for i in range(ngrp // 2):
            order += [i, ngrp // 2 + i]
        for grp in order:
            for c in range(8):
                sg = stage[k % 2]
                S.add("sync", lambda e, sg=sg, c=c, grp=grp: e.dma_start(out=sg[:], in_=wiv[:, c, grp * GW:(grp + 1) * GW]),
                      writes=[("stage", k % 2)], dma=("stage", k % 2))
                S.add(cast_engs[k % 2], lambda e, sg=sg, c=c, grp=grp: e.tensor_scalar(
                    out=wi[:, c, grp * GW:(grp + 1) * GW], in0=sg[:], scalar1=gs[:, c:c + 1], scalar2=None, op0=ALU.mult),
                    reads=[("stage", k % 2), "gs"], writes=[("wi", grp)])
                k += 1
        for j in range(NJ):
            for hf in range(2):
                sg = stage[k % 2]
                S.add("sync", lambda e, sg=sg, j=j, hf=hf: e.dma_start(out=sg[:, 0:512], in_=wov[:, j, hf * 512:(hf + 1) * 512]),
                      writes=[("stage", k % 2)], dma=("stage", k % 2))
                S.add(cast_engs[k % 2], lambda e, sg=sg, j=j, hf=hf: e.tensor_copy(out=wo[:, j, hf * 512:(hf + 1) * 512], in_=sg[:, 0:512]),
                      reads=[("stage", k % 2)], writes=[("wo", j)])
                k += 1
        wgrp = lambda c0: [("wi", gidx) for gidx in range(c0 // GW, (c0 + 127) // GW + 1)]
        for t in range(ntiles):
            b = t % 2
            xt = xts[b]
            ts = slice(t * TT, (t + 1) * TT)
            if t + 1 < ntiles:
                load_x(t + 1)
            S.add("scalar", lambda e, xt=xt: e.activation(out=sq[:, 0:8, :], in_=xt[:], func=AF.Square),
                  reads=XTK(b), writes=[("act", c) for c in range(8)])
            for c in range(8):
                S.add("tensor", lambda e, c=c: e.matmul(ps_s[:], lhsT=ones[:], rhs=sq[:, c, :],
                                                        start=(c == 0), stop=(c == 7)),
                      reads=["ones", ("act", c)], writes=["ps_s"])
            S.add("scalar", lambda e: e.activation(out=rstd[:], in_=ps_s[:], func=AF.Sqrt, bias=epsb[:, 0:1]),
                  reads=["ps_s", "epsb"], writes=["rstd"])
            S.add("vector", lambda e: e.reciprocal(out=rstd[:], in_=rstd[:]), reads=["rstd"], writes=["rstd"])
            for c in range(8):
                S.add("vector" if c % 2 == 0 else "gpsimd",
                      lambda e, c=c, xt=xt: e.tensor_tensor(out=h[:, c, :], in0=xt[:, c, :], in1=rstd[:], op=ALU.mult),
                      reads=[("xt", b, c), "rstd"], writes=[("h", c)])
            for j in range(NJ):
                pa, pb, sj = ps_a[j % 2], ps_b[j % 2], sa[j % 2]
                for c in range(8):
                    S.add("tensor", lambda e, c=c, j=j, pa=pa: e.matmul(
                        pa[:], lhsT=wi[:, c, j * 128:(j + 1) * 128], rhs=h[:, c, :],
                        start=(c == 0), stop=(c == 7)),
                        reads=wgrp(j * 128) + [("h", c)], writes=[("psa", j % 2)])
                for c in range(8):
                    S.add("tensor", lambda e, c=c, j=j, pb=pb: e.matmul(
                        pb[:], lhsT=wi[:, c, DFF + j * 128:DFF + (j + 1) * 128], rhs=h[:, c, :],
                        start=(c == 0), stop=(c == 7)),
                        reads=wgrp(DFF + j * 128) + [("h", c)], writes=[("psb", j % 2)])
                S.add("scalar", lambda e, pa=pa, sj=sj: e.activation(out=sj[:], in_=pa[:], func=AF.Silu),
                      reads=[("psa", j % 2)], writes=[("sa", j % 2)])
                S.add("vector", lambda e, pb=pb, sj=sj, j=j: e.tensor_tensor(
                    out=act[:, j, :], in0=pb[:], in1=sj[:], op=ALU.mult),
                    reads=[("psb", j % 2), ("sa", j % 2)], writes=[("act", j)])
            for m in range(8):
                py = ps_y[m % 2]
                for j in range(NJ):
                    S.add("tensor", lambda e, m=m, j=j, py=py: e.matmul(
                        py[:], lhsT=wo[:, j, m * 128:(m + 1) * 128], rhs=act[:, j, :],
                        start=(j == 0), stop=(j == NJ - 1)),
                        reads=[("wo", j), ("act", j)], writes=[("psy", m % 2)])
                S.add("vector", lambda e, m=m, py=py, xt=xt: e.scalar_tensor_tensor(
                    out=xt[:, m, :], in0=py[:], scalar=0.5, in1=xt[:, m, :], op0=ALU.mult, op1=ALU.add),
                    reads=[("psy", m % 2), ("xt", b, m)], writes=[("xt", b, m)])
            S.add("sync", lambda e, ts=ts, xt=xt: e.dma_start(out=ov[:, :, ts], in_=xt[:]),
                  reads=XTK(b), dma=("xt_out", b))
        S.emit()
    return nc


def gain_layout(gv):
    return np.ascontiguousarray(np.asarray(gv, np.float32).reshape(8, 128).T)


_NC_CACHE = {}


def run_ffn(xT_full, gv, w_in, w_out):
    ntok = xT_full.shape[1]
    T = ntok // N_CORES
    key = ("ffn", T)
    if key not in _NC_CACHE:
        _NC_CACHE[key] = build_ffn(T)
    nc = _NC_CACHE[key]
    gl = gain_layout(gv)
    w_in = np.ascontiguousarray(w_in, dtype=np.float32)
    w_out = np.ascontiguousarray(w_out, dtype=np.float32)
    in_maps = [{"xT": np.ascontiguousarray(xT_full[:, c * T:(c + 1) * T]), "g": gl,
                "w_in": w_in, "w_out": w_out} for c in range(N_CORES)]
    res = run_bass_kernel_spmd(nc, in_maps, core_ids=list(range(N_CORES)))
    return np.concatenate([r["oT"] for r in res.results], axis=1)


import ml_dtypes
NPBF = ml_dtypes.bfloat16
NEGM = -30000.0


def split3(v):
    v = np.asarray(v, np.float64)
    hi = v.astype(np.float32).astype(NPBF)
    r = v - hi.astype(np.float64)
    mid = r.astype(np.float32).astype(NPBF)
    r = r - mid.astype(np.float64)
    lo = r.astype(np.float32).astype(NPBF)
    return hi, mid, lo


def alibi_tabs(slopes, qpos, kpos):
    kpos = np.asarray(kpos, np.int64)
    jb, sl = kpos // 128, kpos % 128
    one = np.ones_like(jb)
    ktab = np.stack([jb, jb, jb, sl, sl, sl, one, one, one]).astype(np.float32).astype(NPBF)
    qt = []
    for s in slopes:
        a = split3(np.full(len(qpos), 128.0 * s))
        b = split3(np.full(len(qpos), float(s)))
        c = split3(-float(s) * np.asarray(qpos, np.float64))
        qt.append(np.stack(list(a) + list(b) + list(c)))
    return ktab, np.stack(qt).astype(NPBF)


class Ctx:
    def __init__(self, nc, st):
        self.nc, self.st = nc, st

    def sb(self, name, shape, dt):
        return self.st.enter_context(self.nc.sbuf_tensor(name, shape, dt))

    def ps(self, name, shape=(128, 512), dt=F32):
        return self.st.enter_context(self.nc.psum_tensor(name, list(shape), dt))

    def din(self, name, shape, dt=F32):
        return self.nc.dram_tensor(name, list(shape), dt, kind="ExternalInput").ap()

    def dout(self, name, shape, dt=F32):
        return self.nc.dram_tensor(name, list(shape), dt, kind="ExternalOutput").ap()


def emit_weight_load(S, wdram_v, wsb, gs, stage, ncols, kbase=0, chunk=1408, tag="w"):
    k = kbase
    engs = ["vector", "gpsimd"]
    for c in range(8):
        for c0 in range(0, ncols, chunk):
            w = min(chunk, ncols - c0)
            sg = stage[k % 2]
            S.add("sync", lambda e, sg=sg, c=c, c0=c0, w=w: e.dma_start(out=sg[:, 0:w], in_=wdram_v[:, c, c0:c0 + w]),
                  writes=[("stage", k % 2)], dma=("stage", k % 2))
            S.add(engs[k % 2], lambda e, sg=sg, c=c, c0=c0, w=w: e.tensor_scalar(
                out=wsb[:, c, c0:c0 + w], in0=sg[:, 0:w], scalar1=gs[:, c:c + 1], scalar2=None, op0=ALU.mult),
                reads=[("stage", k % 2), "gs"], writes=[(tag, c)])
            k += 1
    return k


def emit_rmsnorm(S, xt, h, sqbuf, ps_s, rstd, ones, epsb, n, tagx="xt", tagh="h", sqtag="sq"):
    S.add("scalar", lambda e: e.activation(out=sqbuf[:, 0:8, 0:n], in_=xt[:, :, 0:n], func=AF.Square),
          reads=[(tagx, c) for c in range(8)], writes=[(sqtag, c) for c in range(8)])
    for c in range(8):
        S.add("tensor", lambda e, c=c: e.matmul(ps_s[:, 0:n], lhsT=ones[:], rhs=sqbuf[:, c, 0:n],
                                                start=(c == 0), stop=(c == 7)),
              reads=["ones", (sqtag, c)], writes=["ps_s"])
    S.add("scalar", lambda e: e.activation(out=rstd[:, 0:n], in_=ps_s[:, 0:n], func=AF.Sqrt, bias=epsb[:, 0:1]),
          reads=["ps_s", "epsb"], writes=["rstd"])
    S.add("vector", lambda e: e.reciprocal(out=rstd[:, 0:n], in_=rstd[:, 0:n]), reads=["rstd"], writes=["rstd"])
    for c in range(8):
        S.add("vector" if c % 2 == 0 else "gpsimd",
              lambda e, c=c: e.tensor_tensor(out=h[:, c, 0:n], in0=xt[:, c, 0:n], in1=rstd[:, 0:n], op=ALU.mult),
              reads=[(tagx, c), "rstd"], writes=[(tagh, c)])


def emit_headnorm(S, src_ps, dst, gcol, sqh, ps_n, rs, ones64, epsb, n, rd, wr, P=64):
    S.add("scalar", lambda e: e.activation(out=sqh[0:P, 0:n], in_=src_ps, func=AF.Square),
          reads=rd, writes=["sqh"])
    S.add("tensor", lambda e: e.matmul(ps_n[0:P, 0:n], lhsT=ones64[0:P, 0:P], rhs=sqh[0:P, 0:n], start=True, stop=True),
          reads=["sqh", "ones64"], writes=["ps_n"])
    S.add("scalar", lambda e: e.activation(out=rs[0:P, 0:n], in_=ps_n[0:P, 0:n], func=AF.Sqrt, bias=epsb[0:P, 0:1]),
          reads=["ps_n", "epsb"], writes=["rs"])
    S.add("vector", lambda e: e.reciprocal(out=rs[0:P, 0:n], in_=rs[0:P, 0:n]), reads=["rs"], writes=["rs"])
    S.add("vector", lambda e: e.scalar_tensor_tensor(out=dst, in0=src_ps, scalar=gcol, in1=rs[0:P, 0:n],
                                                     op0=ALU.mult, op1=ALU.mult),
          reads=list(rd) + ["rs", "gcols"], writes=wr)


def build_swa(T):
    TH = T + 128
    nc = bass.Bass("TRN2", target_bir_lowering=False)
    with contextlib.ExitStack() as st:
        C = Ctx(nc, st)
        xT = C.din("xT", [D, TH]); g = C.din("g", [128, 8])
        w_in = C.din("w_in", [D, 1536]); w_out = C.din("w_out", [D, D])
        gq = C.din("gq", [64, 1]); gk = C.din("gk", [64, 1]); sinks = C.din("sinks", [64, 16])
        ktab = C.din("ktab", [9, TH], BF16); qtab = C.din("qtab", [9, 16, T], BF16)
        masks = C.din("masks", [128, 3, 512])
        oT = C.dout("oT", [D, T])
        xv = xT.rearrange("(c p) t -> p c t", p=128)
        ov = oT.rearrange("(c p) t -> p c t", p=128)
        wiv = w_in.rearrange("(c p) f -> p c f", p=128)
        wov = w_out.rearrange("(h d) n -> d h n", d=64)

        w = C.sb("w", [128, 8, 1536], BF16)
        wo = C.sb("wo", [64, 16, D], BF16)
        stage = [C.sb("stage%d" % i, [128, 1024], F32) for i in range(2)]
        gs = C.sb("gs", [128, 8], F32); epsb = C.sb("epsb", [128, 1], F32)
        ones = C.sb("ones", [128, 128], BF16); ones64 = C.sb("ones64", [64, 64], BF16)
        onesd = C.sb("onesd", [128, 64], BF16)
        gqs = C.sb("gqs", [64, 1], F32); gks = C.sb("gks", [64, 1], F32); es = C.sb("es", [64, 16], F32)
        msk = C.sb("msk", [128, 3, 512], F32)
        xt = C.sb("xt", [128, 8, 512], F32); h = C.sb("h", [128, 8, 512], BF16)
        rstd = C.sb("rstd", [128, 512], F32)
        sqh = C.sb("sqh", [64, 512], BF16); rs = C.sb("rs", [64, 512], F32)
        kaug = C.sb("kaug", [73, 4, TH], BF16)
        vtok = C.sb("vtok", [128, TH // 128, 256], BF16)
        qaug = C.sb("qaug", [73, 16, 512], BF16)
        PT = [C.sb("PT%d" % i, [128, 2, 512], BF16) for i in range(2)]
        tmp = [C.sb("tmp%d" % i, [128, 512], F32) for i in range(2)]
        oTt = C.sb("oTt", [64, 16, 512], BF16); rec = C.sb("rec", [64, 512], F32)
        ps_s = C.ps("ps_s"); ps_p = [C.ps("ps_p%d" % i) for i in range(2)]; ps_n = C.ps("ps_n")
        st_ = [C.ps("st%d" % i) for i in range(2)]; num = C.ps("num"); den = C.ps("den")

        S = Sched(nc)
        S.add("vector", lambda e: e.memset(ones[:], 1.0 / D), writes=["ones"])
        S.add("vector", lambda e: e.memset(ones64[:], 1.0 / 64), writes=["ones64"])
        S.add("vector", lambda e: e.memset(onesd[:], 1.0), writes=["onesd"])
        S.add("vector", lambda e: e.memset(epsb[:], EPS), writes=["epsb"])
        S.add("sync", lambda e: e.dma_start(out=gs[:], in_=g), writes=["gs"], dma="c0")
        S.add("sync", lambda e: e.dma_start(out=gqs[:], in_=gq), writes=["gcols"], dma="c1")
        S.add("sync", lambda e: e.dma_start(out=gks[:], in_=gk), writes=["gcols"], dma="c1")
        S.add("sync", lambda e: e.dma_start(out=es[:], in_=sinks), writes=["es"], dma="c2")
        S.add("sync", lambda e: e.dma_start(out=msk[:], in_=masks), writes=["msk"], dma="c3")
        for gg in range(4):
            S.add("sync", lambda e, gg=gg: e.dma_start(out=kaug[64:73, gg, :], in_=ktab), writes=[("kaugc", gg)], dma="c4")
        S.add("vector", lambda e: e.tensor_scalar(out=gqs[:], in0=gqs[:], scalar1=0.125, scalar2=None, op0=ALU.mult),
              reads=["gcols"], writes=["gcols"])
        S.add("scalar", lambda e: e.activation(out=es[:], in_=es[:], func=AF.Exp), reads=["es"], writes=["es"])
        k = emit_weight_load(S, wiv, w, gs, stage, 1536, chunk=768)
        for hh in range(16):
            sg = stage[k % 2]
            S.add("sync", lambda e, sg=sg, hh=hh: e.dma_start(out=sg[0:64, 0:D], in_=wov[:, hh, :]),
                  writes=[("stage", k % 2)], dma=("stage", k % 2))
            S.add(["vector", "gpsimd"][k % 2], lambda e, sg=sg, hh=hh: e.tensor_copy(out=wo[:, hh, :], in_=sg[0:64, 0:D]),
                  reads=[("stage", k % 2)], writes=[("wo", hh)])
            k += 1
        XT = [("xt", c) for c in range(8)]
        H = [("h", c) for c in range(8)]
        W = [("w", c) for c in range(8)]
        pcount = [0]

        def proj_fm(col0, M, n):
            p = pcount[0] % 2
            pcount[0] += 1
            for c in range(8):
                S.add("tensor", lambda e, c=c, p=p: e.matmul(ps_p[p][0:M, 0:n], lhsT=w[:, c, col0:col0 + M], rhs=h[:, c, 0:n],
                                                             start=(c == 0), stop=(c == 7)),
                      reads=[("w", c), ("h", c)], writes=[("psp", p)])
            return p

        def proj_tile(tok0, n, with_q):
            emit_rmsnorm(S, xt, h, h, ps_s, rstd, ones, epsb, n, sqtag="h")
            for gg in range(4):
                p = proj_fm(1024 + 64 * gg, 64, n)
                emit_headnorm(S, ps_p[p][0:64, 0:n], kaug[0:64, gg, tok0:tok0 + n], gks[:, 0:1], sqh, ps_n, rs, ones64, epsb,
                              n, rd=[("psp", p)], wr=[("kaug", gg, tok0 // 128 + b) for b in range(n // 128)])
            for b in range(n // 128):
                p = pcount[0] % 2
                pcount[0] += 1
                for c in range(8):
                    S.add("tensor", lambda e, c=c, p=p, b=b: e.matmul(ps_p[p][:, 0:256], lhsT=h[:, c, b * 128:(b + 1) * 128],
                                                                      rhs=w[:, c, 1280:1536], start=(c == 0), stop=(c == 7)),
                          reads=[("w", c), ("h", c)], writes=[("psp", p)])
                S.add("scalar", lambda e, p=p, b=b: e.copy(out=vtok[:, tok0 // 128 + b, :], in_=ps_p[p][:, 0:256]),
                      reads=[("psp", p)], writes=[("vtok", tok0 // 128 + b)])
            if with_q:
                for hq in range(16):
                    p = proj_fm(64 * hq, 64, n)
                    emit_headnorm(S, ps_p[p][0:64, 0:n], qaug[0:64, hq, 0:n], gqs[:, 0:1], sqh, ps_n, rs, ones64, epsb,
                                  n, rd=[("psp", p)], wr=[("qaug", hq)])

        S.add("sync", lambda e: e.dma_start(out=xt[:, :, 0:128], in_=xv[:, :, 0:128]), writes=XT, dma="xt_in")
        proj_tile(0, 128, False)
        acount = 0
        for tt in range(T // 512):
            tok0 = 128 + 512 * tt
            S.add("sync", lambda e, tok0=tok0: e.dma_start(out=xt[:], in_=xv[:, :, tok0:tok0 + 512]), writes=XT, dma="xt_in")
            S.add("gpsimd", lambda e, tt=tt: e.dma_start(out=qaug[64:73, :, :], in_=qtab[:, :, tt * 512:(tt + 1) * 512]),
                  writes=["qaugc"], dma="qc")
            proj_tile(tok0, 512, True)
            for qb in range(4):
                cur = tok0 // 128 + qb
                prev = cur - 1
                for gg in range(4):
                    pt = PT[acount % 2]
                    ptk = ("PT", acount % 2)
                    rhs_q = qaug[:, 4 * gg:4 * gg + 4, qb * 128:(qb + 1) * 128]
                    qreads = [("qaug", 4 * gg + i) for i in range(4)] + ["qaugc"]
                    for i, kb in enumerate((prev, cur)):
                        S.add("tensor", lambda e, i=i, kb=kb, gg=gg, rhs_q=rhs_q: e.matmul(
                            st_[i][:].rearrange("p (a b) -> p a b", a=4), lhsT=kaug[:, gg, kb * 128:(kb + 1) * 128], rhs=rhs_q,
                            start=True, stop=True),
                            reads=qreads + [("kaug", gg, kb), ("kaugc", gg)], writes=[("st", i)])
                        mi = (0 if (tt == 0 and qb == 0) else 1) if i == 0 else 2
                        S.add("vector", lambda e, i=i, mi=mi: e.tensor_tensor(out=tmp[i][:], in0=st_[i][:], in1=msk[:, mi, :],
                                                                              op=ALU.add),
                              reads=[("st", i), "msk"], writes=[("tmp", i)])
                        S.add("scalar", lambda e, i=i, pt=pt: e.activation(out=pt[:, i, :], in_=tmp[i][:], func=AF.Exp),
                              reads=[("tmp", i)], writes=[ptk + (i,)])
                    for i, kb in enumerate((prev, cur)):
                        S.add("tensor", lambda e, i=i, kb=kb, gg=gg, pt=pt: e.matmul(
                            num[0:64, :], lhsT=vtok[:, kb, 64 * gg:64 * gg + 64], rhs=pt[:, i, :], start=(i == 0), stop=(i == 1)),
                            reads=[("vtok", kb), ptk + (i,)], writes=["num"])
                    for i in range(2):
                        S.add("tensor", lambda e, i=i, pt=pt: e.matmul(
                            den[0:64, :], lhsT=onesd[:], rhs=pt[:, i, :], start=(i == 0), stop=(i == 1)),
                            reads=["onesd", ptk + (i,)], writes=["den"])
                    S.add("vector", lambda e, gg=gg: e.tensor_tensor(
                        out=rec[:].rearrange("p (a b) -> p a b", a=4), in0=den[0:64, :].rearrange("p (a b) -> p a b", a=4),
                        in1=es[:, 4 * gg:4 * gg + 4].unsqueeze(2).to_broadcast([64, 4, 128]), op=ALU.add),
                        reads=["den", "es"], writes=["rec"])
                    S.add("vector", lambda e: e.reciprocal(out=rec[:], in_=rec[:]), reads=["rec"], writes=["rec"])
                    S.add("vector", lambda e, gg=gg, qb=qb: e.tensor_tensor(
                        out=oTt[:, 4 * gg:4 * gg + 4, qb * 128:(qb + 1) * 128], in0=num[0:64, :].rearrange("p (a b) -> p a b", a=4),
                        in1=rec[:].rearrange("p (a b) -> p a b", a=4), op=ALU.mult),
                        reads=["num", "rec"], writes=[("oTt", 4 * gg + i) for i in range(4)])
                    acount += 1
            for m in range(8):
                p = pcount[0] % 2
                pcount[0] += 1
                for hh in range(16):
                    S.add("tensor", lambda e, m=m, hh=hh, p=p: e.matmul(ps_p[p][:], lhsT=wo[:, hh, m * 128:(m + 1) * 128],
                                                                        rhs=oTt[:, hh, :], start=(hh == 0), stop=(hh == 15)),
                          reads=[("wo", hh), ("oTt", hh)], writes=[("psp", p)])
                S.add("vector", lambda e, m=m, p=p: e.tensor_tensor(out=xt[:, m, :], in0=ps_p[p][:], in1=xt[:, m, :], op=ALU.add),
                      reads=[("psp", p), ("xt", m)], writes=[("xt", m)])
            S.add("sync", lambda e, tt=tt: e.dma_start(out=ov[:, :, tt * 512:(tt + 1) * 512], in_=xt[:]), reads=XT, dma="xt_out")
        S.emit()
    return nc


def alibi_slopes(n):
    return 2.0 ** (-8.0 * np.arange(1, n + 1, dtype=np.float64) / n)


def swa_consts(T):
    TH = T + 128
    ktab, qtab = alibi_tabs(alibi_slopes(16), np.arange(T) + 128, np.arange(TH))
    qtab = np.ascontiguousarray(qtab.transpose(1, 0, 2))
    sl = np.arange(128)[:, None]
    tl = np.arange(128)[None, :]
    mprev = np.where(sl > tl, 0.0, NEGM).astype(np.float32)
    mcur = np.where(sl <= tl, 0.0, NEGM).astype(np.float32)
    t4 = lambda m: np.tile(m, (1, 4))
    m_mid = np.stack([t4(mprev), t4(mprev), t4(mcur)], axis=1)
    m_first = np.stack([np.full((128, 512), NEGM, np.float32), t4(mprev), t4(mcur)], axis=1)
    return ktab, qtab, np.ascontiguousarray(m_first), np.ascontiguousarray(m_mid)


def run_swa(xT_full, S_seq, gv, w_in, w_out, q_norm, k_norm, sinks):
    ntok = xT_full.shape[1]
    T = ntok // N_CORES
    key = ("swa", T)
    if key not in _NC_CACHE:
        _NC_CACHE[key] = build_swa(T)
    nc = _NC_CACHE[key]
    ktab, qtab, m_first, m_mid = swa_consts(T)
    common = {"g": gain_layout(gv), "w_in": np.ascontiguousarray(w_in, np.float32),
              "w_out": np.ascontiguousarray(w_out, np.float32),
              "gq": np.ascontiguousarray(np.asarray(q_norm, np.float32).reshape(64, 1)),
              "gk": np.ascontiguousarray(np.asarray(k_norm, np.float32).reshape(64, 1)),
              "sinks": np.ascontiguousarray(np.broadcast_to(np.asarray(sinks, np.float32).reshape(1, 16), (64, 16))),
              "ktab": ktab, "qtab": qtab}
    in_maps = []
    for c in range(N_CORES):
        t0 = c * T
        first = (t0 % S_seq) == 0
        xh = np.zeros((D, T + 128), np.float32)
        xh[:, 128:] = xT_full[:, t0:t0 + T]
        if not first:
            xh[:, :128] = xT_full[:, t0 - 128:t0]
        in_maps.append(dict(common, xT=xh, masks=m_first if first else m_mid))
    res = run_bass_kernel_spmd(nc, in_maps, core_ids=list(range(N_CORES)))
    return np.concatenate([r["oT"] for r in res.results], axis=1)


def build_diff(S_seq, NH, lam_init):
    nc = bass.Bass("TRN2", target_bir_lowering=False)
    NT = S_seq // 512
    with contextlib.ExitStack() as st:
        C = Ctx(nc, st)
        xT = C.din("xT", [D, S_seq]); g = C.din("g", [128, 8])
        w_in = C.din("w_in", [D, NH, 384])
        gq = C.din("gq", [64, 1]); gk = C.din("gk", [64, 1]); gsub = C.din("gsub", [128, 1])
        lam = C.din("lam", [128, 4, 64])
        ktab = C.din("ktab", [9, S_seq], BF16); qtab = C.din("qtab", [NH, 9, S_seq], BF16)
        masks = C.din("masks", [128, 4, 512])
        oT = C.dout("oT", [NH, 128, S_seq])
        xv = xT.rearrange("(c p) t -> p c t", p=128)
        wiv = w_in.rearrange("(c p) h f -> p c h f", p=128)

        w = C.sb("w", [128, 8, 384], BF16)
        stage = [C.sb("stage%d" % i, [128, 384], F32) for i in range(2)]
        gs = C.sb("gs", [128, 8], F32); epsb = C.sb("epsb", [128, 1], F32)
        ones = C.sb("ones", [128, 128], BF16); ones64 = C.sb("ones64", [64, 64], BF16)
        ones128 = C.sb("ones128", [128, 128], BF16); onesd = C.sb("onesd", [128, 128], BF16)
        gqs = C.sb("gqs", [64, 1], F32); gks = C.sb("gks", [64, 1], F32); gsubs = C.sb("gsubs", [128, 1], F32)
        lams = C.sb("lams", [128, 4, 64], F32); lp = C.sb("lp", [128, 2, 64], F32); l2 = C.sb("l2", [128, 2], F32)
        nlam = C.sb("nlam", [128, 1], F32)
        msk = C.sb("msk", [128, 4, 512], F32)
        xt = C.sb("xt", [128, 8, 512], F32); h = C.sb("h", [128, 8, 512], BF16)
        rstd = C.sb("rstd", [128, 512], F32)
        sqh = C.sb("sqh", [128, 512], BF16); rs = C.sb("rs", [128, 512], F32)
        kaug = C.sb("kaug", [73, 2, S_seq], BF16)
        vtok = C.sb("vtok", [128, S_seq // 128, 128], BF16)
        qaug = C.sb("qaug", [73, 2, 512], BF16)
        NPT = 4
        PT = [C.sb("PT%d" % i, [128, 512], BF16) for i in range(NPT)]
        tmp = [C.sb("tmp%d" % i, [128, 512], F32) for i in range(2)]
        oc = [C.sb("oc%d" % i, [128, 512], F32) for i in range(2)]
        rec = C.sb("rec", [128, 512], F32); ot = C.sb("ot", [128, 512], F32)
        ps_s = C.ps("ps_s"); ps_p = [C.ps("ps_p%d" % i) for i in range(2)]
        st_ = [C.ps("st%d" % i) for i in range(3)]; num = C.ps("num"); den = C.ps("den")
        ps_n = ps_s

        S = Sched(nc)
        S.add("vector", lambda e: e.memset(ones[:], 1.0 / D), writes=["ones"])
        S.add("vector", lambda e: e.memset(ones64[:], 1.0 / 64), writes=["ones64"])
        S.add("vector", lambda e: e.memset(ones128[:], 1.0 / 128), writes=["ones128"])
        S.add("vector", lambda e: e.memset(onesd[:], 1.0), writes=["onesd"])
        S.add("vector", lambda e: e.memset(epsb[:], EPS), writes=["epsb"])
        S.add("sync", lambda e: e.dma_start(out=gs[:], in_=g), writes=["gs"], dma="c0")
        S.add("sync", lambda e: e.dma_start(out=gqs[:], in_=gq), writes=["gcols"], dma="c1")
        S.add("sync", lambda e: e.dma_start(out=gks[:], in_=gk), writes=["gcols"], dma="c1")
        S.add("sync", lambda e: e.dma_start(out=gsubs[:], in_=gsub), writes=["gcols"], dma="c1")
        S.add("sync", lambda e: e.dma_start(out=lams[:], in_=lam), writes=["lams"], dma="c2")
        S.add("sync", lambda e: e.dma_start(out=msk[:], in_=masks), writes=["msk"], dma="c3")
        for cc in range(2):
            S.add("sync", lambda e, cc=cc: e.dma_start(out=kaug[64:73, cc, :], in_=ktab), writes=[("kaugc", cc)], dma="c4")
        S.add("vector", lambda e: e.tensor_scalar(out=gqs[:], in0=gqs[:], scalar1=0.125, scalar2=None, op0=ALU.mult),
              reads=["gcols"], writes=["gcols"])
        S.add("vector", lambda e: e.tensor_scalar(out=gsubs[:], in0=gsubs[:], scalar1=1.0 - lam_init, scalar2=None, op0=ALU.mult),
              reads=["gcols"], writes=["gcols"])
        S.add("vector", lambda e: e.tensor_tensor(out=lp[:], in0=lams[:, 0:4:2, :], in1=lams[:, 1:4:2, :], op=ALU.mult),
              reads=["lams"], writes=["lp"])
        S.add("vector", lambda e: e.reduce_sum(out=l2[:], in_=lp[:], axis=mybir.AxisListType.X), reads=["lp"], writes=["l2"])
        S.add("scalar", lambda e: e.activation(out=l2[:], in_=l2[:], func=AF.Exp), reads=["l2"], writes=["l2"])
        S.add("vector", lambda e: e.tensor_tensor(out=nlam[:], in0=l2[:, 1:2], in1=l2[:, 0:1], op=ALU.subtract),
              reads=["l2"], writes=["nlam"])
        S.add("vector", lambda e: e.tensor_scalar(out=nlam[:], in0=nlam[:], scalar1=-lam_init, scalar2=None, op0=ALU.add),
              reads=["nlam"], writes=["nlam"])
        XT = [("xt", c) for c in range(8)]
        pcount = [0]
        kst = 0
        stc = 0
        ptc = 0
        for hl in range(NH):
            for c in range(8):
                sg = stage[kst % 2]
                S.add("sync", lambda e, sg=sg, c=c, hl=hl: e.dma_start(out=sg[:], in_=wiv[:, c, hl, :]),
                      writes=[("stage", kst % 2)], dma=("stage", kst % 2))
                S.add(["vector", "gpsimd"][kst % 2], lambda e, sg=sg, c=c: e.tensor_scalar(
                    out=w[:, c, :], in0=sg[:], scalar1=gs[:, c:c + 1], scalar2=None, op0=ALU.mult),
                    reads=[("stage", kst % 2), "gs"], writes=[("w", c)])
                kst += 1

            def proj_fm(col0, M, n=512):
                p = pcount[0] % 2
                pcount[0] += 1
                for c in range(8):
                    S.add("tensor", lambda e, c=c, p=p: e.matmul(ps_p[p][0:M, 0:n], lhsT=w[:, c, col0:col0 + M], rhs=h[:, c, 0:n],
                                                                 start=(c == 0), stop=(c == 7)),
                          reads=[("w", c), ("h", c)], writes=[("psp", p)])
                return p

            for tt in range(NT):
                tok0 = 512 * tt
                S.add("sync", lambda e, tok0=tok0: e.dma_start(out=xt[:], in_=xv[:, :, tok0:tok0 + 512]), writes=XT, dma="xt_in")
                S.add("gpsimd", lambda e, tok0=tok0, hl=hl: e.dma_start(
                    out=qaug[64:73, :, :], in_=qtab[hl, :, tok0:tok0 + 512].unsqueeze(1).to_broadcast([9, 2, 512])),
                    writes=["qaugc"], dma="qc")
                emit_rmsnorm(S, xt, h, h, ps_s, rstd, ones, epsb, 512, sqtag="h")
                for cc in range(2):
                    p = proj_fm(128 + 64 * cc, 64)
                    emit_headnorm(S, ps_p[p][0:64, :], kaug[0:64, cc, tok0:tok0 + 512], gks[:, 0:1], sqh, ps_n, rs, ones64, epsb,
                                  512, rd=[("psp", p)], wr=[("kaug", cc, 4 * tt + b) for b in range(4)])
                for b in range(4):
                    p = pcount[0] % 2
                    pcount[0] += 1
                    for c in range(8):
                        S.add("tensor", lambda e, c=c, p=p, b=b: e.matmul(ps_p[p][:, 0:128], lhsT=h[:, c, b * 128:(b + 1) * 128],
                                                                          rhs=w[:, c, 256:384], start=(c == 0), stop=(c == 7)),
                              reads=[("w", c), ("h", c)], writes=[("psp", p)])
                    S.add("scalar", lambda e, p=p, b=b, tt=tt: e.copy(out=vtok[:, 4 * tt + b, :], in_=ps_p[p][:, 0:128]),
                          reads=[("psp", p)], writes=[("vtok", 4 * tt + b)])
                for cc in range(2):
                    p = proj_fm(64 * cc, 64)
                    emit_headnorm(S, ps_p[p][0:64, :], qaug[0:64, cc, :], gqs[:, 0:1], sqh, ps_n, rs, ones64, epsb,
                                  512, rd=[("psp", p)], wr=[("qaug", cc)])
                for cc in range(2):
                    nj = 4 * tt + 4
                    LA = 2
                    slots = {}

                    def emit_front(j, cc=cc, tt=tt, slots=slots):
                        nonlocal stc, ptc
                        sti = stc % 3
                        stc += 1
                        pti = ptc % NPT
                        ptc += 1
                        slots[j] = pti
                        S.add("tensor", lambda e, j=j, cc=cc, sti=sti: e.matmul(
                            st_[sti][:], lhsT=kaug[:, cc, j * 128:(j + 1) * 128], rhs=qaug[:, cc, :], start=True, stop=True),
                            reads=[("qaug", cc), "qaugc", ("kaug", cc, j), ("kaugc", cc)], writes=[("st", sti)])
                        if j >= 4 * tt:
                            bvar = j - 4 * tt
                            ti = j % 2
                            S.add("vector", lambda e, sti=sti, bvar=bvar, ti=ti: e.tensor_tensor(
                                out=tmp[ti][:], in0=st_[sti][:], in1=msk[:, bvar, :], op=ALU.add),
                                reads=[("st", sti), "msk"], writes=[("tmp", ti)])
                            S.add("scalar", lambda e, ti=ti, pti=pti: e.activation(out=PT[pti][:], in_=tmp[ti][:], func=AF.Exp),
                                  reads=[("tmp", ti)], writes=[("PT", pti)])
                        else:
                            S.add("scalar", lambda e, sti=sti, pti=pti: e.activation(out=PT[pti][:], in_=st_[sti][:], func=AF.Exp),
                                  reads=[("st", sti)], writes=[("PT", pti)])

                    def emit_back(j, nj=nj, slots=slots):
                        pti = slots[j]
                        S.add("tensor", lambda e, j=j, pti=pti, nj=nj: e.matmul(
                            num[:], lhsT=vtok[:, j, :], rhs=PT[pti][:], start=(j == 0), stop=(j == nj - 1)),
                            reads=[("vtok", j), ("PT", pti)], writes=["num"])
                        S.add("tensor", lambda e, j=j, pti=pti, nj=nj: e.matmul(
                            den[:], lhsT=onesd[:], rhs=PT[pti][:], start=(j == 0), stop=(j == nj - 1)),
                            reads=["onesd", ("PT", pti)], writes=["den"])

                    for j in range(nj):
                        emit_front(j)
                        if j >= LA:
                            emit_back(j - LA)
                    for j in range(max(0, nj - LA), nj):
                        emit_back(j)
                    S.add("vector", lambda e: e.reciprocal(out=rec[:], in_=den[:]), reads=["den"], writes=["rec"])
                    S.add("vector", lambda e, cc=cc: e.tensor_tensor(out=oc[cc][:], in0=num[:], in1=rec[:], op=ALU.mult),
                          reads=["num", "rec"], writes=[("oc", cc)])
                S.add("vector", lambda e: e.scalar_tensor_tensor(out=ot[:], in0=oc[1][:], scalar=nlam[:, 0:1], in1=oc[0][:],
                                                                 op0=ALU.mult, op1=ALU.add),
                      reads=[("oc", 0), ("oc", 1), "nlam"], writes=["ot"])
                emit_headnorm(S, ot[:], ot[:], gsubs[:, 0:1], sqh, ps_n, rs, ones128, epsb, 512, rd=["ot"], wr=["ot"], P=128)
                S.add("sync", lambda e, tok0=tok0, hl=hl: e.dma_start(out=oT[hl, :, tok0:tok0 + 512], in_=ot[:]),
                      reads=["ot"], dma="o_out")
        S.emit()
    return nc


def diff_masks():
    sl = np.arange(128)[:, None]
    tl = np.arange(128)[None, :]
    caus = np.where(sl <= tl, 0.0, NEGM).astype(np.float32)
    m = np.zeros((128, 4, 4, 128), np.float32)
    for b in range(4):
        for a in range(4):
            m[:, b, a, :] = 0.0 if a > b else (caus if a == b else NEGM)
    return np.ascontiguousarray(m.reshape(128, 4, 512))


def run_diff(xT_full, B, S_seq, gv, w_in, q_norm, k_norm, lam, subln, lam_init):
    NH = 8 * B // N_CORES
    key = ("diff", S_seq, NH, lam_init)
    if key not in _NC_CACHE:
        _NC_CACHE[key] = build_diff(S_seq, NH, lam_init)
    nc = _NC_CACHE[key]
    slopes = alibi_slopes(8)
    pos = np.arange(S_seq)
    ktab, qtab_all = alibi_tabs(slopes, pos, pos)
    w_in = np.asarray(w_in, np.float32)
    common = {"g": gain_layout(gv),
              "gq": np.ascontiguousarray(np.asarray(q_norm, np.float32).reshape(64, 1)),
              "gk": np.ascontiguousarray(np.asarray(k_norm, np.float32).reshape(64, 1)),
              "gsub": np.ascontiguousarray(np.asarray(subln, np.float32).reshape(128, 1)),
              "lam": np.ascontiguousarray(np.broadcast_to(np.asarray(lam, np.float32).reshape(1, 4, 64), (128, 4, 64))),
              "ktab": ktab, "masks": diff_masks()}
    in_maps = []
    per_b = N_CORES // B
    for c in range(N_CORES):
        b = c // per_b
        heads = [(c % per_b) * NH + i for i in range(NH)]
        wsel = np.stack([np.concatenate([w_in[:, hh * 128:(hh + 1) * 128], w_in[:, 1024 + hh * 128:1024 + (hh + 1) * 128],
                                         w_in[:, 2048 + hh * 128:2048 + (hh + 1) * 128]], axis=1) for hh in heads], axis=1)
        in_maps.append(dict(common, xT=np.ascontiguousarray(xT_full[:, b * S_seq:(b + 1) * S_seq]),
                            w_in=np.ascontiguousarray(wsel), qtab=np.ascontiguousarray(qtab_all[heads])))
    res = run_bass_kernel_spmd(nc, in_maps, core_ids=list(range(N_CORES)))
    out = np.zeros((B, 1024, S_seq), np.float32)
    for c in range(N_CORES):
        b = c // per_b
        for i in range(NH):
            hh = (c % per_b) * NH + i
            out[b, hh * 128:(hh + 1) * 128, :] = res.results[c]["oT"][i]
    return out


def build_nsa(S_seq, debug=None):
    nc = bass.Bass("TRN2", target_bir_lowering=False)
    NT = S_seq // 512
    NBLK = S_seq // 128
    NCMP = S_seq // 16 - 1
    NCT = (NCMP + 127) // 128
    NSEL = S_seq // 64
    with contextlib.ExitStack() as st:
        C = Ctx(nc, st)
        xT = C.din("xT", [D, S_seq]); g = C.din("g", [128, 8])
        w_in = C.din("w_in", [D, 652])
        gq = C.din("gq", [64, 1]); gk3 = C.din("gk3", [64, 3])
        cpos = C.din("cpos", [128, 32])
        cw1 = C.din("cw1", [2, 64, 32, 256])
        cw2 = C.din("cw2", [128, 2, 2, 64])
        ktab = C.din("ktab", [9, S_seq], BF16); qtab = C.din("qtab", [9, 4, S_seq], BF16)
        kctab = C.din("kctab", [9, NCT * 128], BF16)
        cmask = C.din("cmask", [128, 17, 128]); wmask = C.din("wmask", [128, 2, 128])
        tmpl = C.din("tmpl", [128, 2, 2 * NSEL])
        ovl = C.din("ovl", [128, NCT, NSEL], BF16)
        i4 = C.din("i4", [128, 512], BF16)
        selrow = C.din("selrow", [12, 12, 64])
        oT = C.dout("oT", [256, S_seq])
        xv = xT.rearrange("(c p) t -> p c t", p=128)
        wiv = w_in.rearrange("(c p) f -> p c f", p=128)

        w = C.sb("w", [128, 8, 652], BF16)
        stage = [C.sb("stage%d" % i, [128, 1024], F32) for i in range(2)]
        gs = C.sb("gs", [128, 8], F32); epsb = C.sb("epsb", [128, 1], F32)
        ones = C.sb("ones", [128, 128], BF16); ones64 = C.sb("ones64", [64, 64], BF16)
        onesd = C.sb("onesd", [128, 128], BF16)
        gqs = C.sb("gqs", [64, 1], F32); gk3s = C.sb("gk3s", [64, 3], F32)
        w1sb = C.sb("w1sb", [128, 32, 256], BF16)
        w2sb = C.sb("w2sb", [128, 2, 2, 64], BF16)
        posT = C.sb("posT", [128, 32], BF16)
        cposb = C.sb("cposb", [128, 2, 2], F32)
        big = C.sb("big", [128, S_seq], BF16)
        kcmp = C.sb("kcmp", [73, NCT * 128], BF16)
        vcmp = C.sb("vcmp", [128, NCT, 64], BF16)
        ov = C.sb("ov", [128, NCT, NSEL], BF16)
        vs_tok = C.sb("vs_tok", [128, NBLK, 64], BF16)
        kwr = C.sb("kwr", [73, 8 * 128], BF16)
        vwr = C.sb("vwr", [128, 8, 64], BF16)
        qaug = C.sb("qaug", [73, 4, 512], BF16)
        xt = C.sb("xt", [128, 8, 512], F32); h = C.sb("h", [128, 8, 512], BF16)
        rstd = C.sb("rstd", [128, 512], F32)
        sqh = C.sb("sqh", [128, 512], BF16); rs = C.sb("rs", [128, 512], F32)
        gel = [C.sb("gel%d" % i, [128, 512], F32) for i in range(3)]
        gelu = C.sb("gelu", [128, 2, 512], BF16)
        PTc = C.sb("PTc", [128, NCT, 512], BF16)
        NPT = 4
        PT = [C.sb("PT%d" % i, [128, 512], BF16) for i in range(NPT)]
        tmp = [C.sb("tmp%d" % i, [128, 512], F32) for i in range(2)]
        cm = C.sb("cm", [128, 17, 128], F32); wm = C.sb("wm", [128, 2, 128], F32)
        tm = C.sb("tm", [128, 2, 2 * NSEL], F32)
        i4s = C.sb("i4s", [128, 512], BF16)
        srow = C.sb("srow", [12, 12, 64], F32)
        rden = C.sb("rden", [128, 512], F32)
        imp2 = C.sb("imp2", [128, NSEL], F32); imp3 = C.sb("imp3", [128, NSEL], F32)
        m8 = C.sb("m8", [128, 8], F32)
        negsel = C.sb("negsel", [128, NSEL], BF16)
        negx = [C.sb("negx%d" % i, [128, 16, 64], BF16) for i in range(2)]
        gT = C.sb("gT", [12, 512], F32)
        obr = [C.sb("obr%d" % i, [64, 512], F32) for i in range(3)]
        oTt = C.sb("oTt", [64, 4, 512], F32)
        ps_s = C.ps("ps_s"); ps_p = [C.ps("ps_p%d" % i) for i in range(2)]
        st_ = [C.ps("st%d" % i) for i in range(3)]; num = C.ps("num"); den = C.ps("den")
        imp = ps_s
        ps_n = ps_s

        S = Sched(nc)
        S.add("vector", lambda e: e.memset(ones[:], 1.0 / D), writes=["ones"])
        S.add("vector", lambda e: e.memset(ones64[:], 1.0 / 64), writes=["ones64"])
        S.add("vector", lambda e: e.memset(onesd[:], 1.0), writes=["onesd"])
        S.add("vector", lambda e: e.memset(epsb[:], EPS), writes=["epsb"])
        S.add("vector", lambda e: e.memset(kcmp[0:64, :], 0.0), writes=["kcmp"])
        S.add("vector", lambda e: e.memset(vcmp[:], 0.0), writes=["vcmp"])
        ld = [(gs, g, "gs"), (gqs, gq, "gcols"), (gk3s, gk3, "gcols"), (cm, cmask, "cm"), (wm, wmask, "wm"), (tm, tmpl, "tm"),
              (i4s, i4, "i4s"), (srow, selrow, "srow"), (ov, ovl, "ov")]
        for n_, (dst, src, key) in enumerate(ld):
            S.add("sync", lambda e, dst=dst, src=src: e.dma_start(out=dst[:], in_=src), writes=[key], dma="c%d" % n_)
        S.add("sync", lambda e: e.dma_start(out=kcmp[64:73, :], in_=kctab), writes=["kcmpc"], dma="ck")
        S.add("vector", lambda e: e.tensor_scalar(out=gqs[:], in0=gqs[:], scalar1=0.125, scalar2=None, op0=ALU.mult),
              reads=["gcols"], writes=["gcols"])
        kst = emit_weight_load(S, wiv, w, gs, stage, 652, chunk=652)
        for kv in range(2):
            for pg in range(8):
                sg = stage[kst % 2]
                S.add("sync", lambda e, sg=sg, kv=kv, pg=pg: e.dma_start(
                    out=sg[64 * kv:64 * kv + 64, :].rearrange("d (p f) -> d p f", p=4), in_=cw1[kv, :, 4 * pg:4 * pg + 4, :]),
                    writes=[("stage", kst % 2)], dma=("stage", kst % 2))
                S.add(["vector", "gpsimd"][kst % 2], lambda e, sg=sg, kv=kv, pg=pg: e.tensor_copy(
                    out=w1sb[64 * kv:64 * kv + 64, 4 * pg:4 * pg + 4, :],
                    in_=sg[64 * kv:64 * kv + 64, :].rearrange("d (p f) -> d p f", p=4)),
                    reads=[("stage", kst % 2)], writes=["w1sb"])
                kst += 1
        sg = stage[kst % 2]
        S.add("sync", lambda e, sg=sg: e.dma_start(out=sg[:, 0:256].rearrange("p (a b d) -> p a b d", a=2, b=2), in_=cw2),
              writes=[("stage", kst % 2)], dma=("stage", kst % 2))
        S.add("vector", lambda e, sg=sg: e.tensor_copy(out=w2sb[:], in_=sg[:, 0:256].rearrange("p (a b d) -> p a b d", a=2, b=2)),
              reads=[("stage", kst % 2)], writes=["w2sb"])
        kst += 1
        sg = stage[kst % 2]
        S.add("sync", lambda e, sg=sg: e.dma_start(out=sg[:, 0:32], in_=cpos), writes=[("stage", kst % 2)], dma=("stage", kst % 2))
        S.add("vector", lambda e, sg=sg: e.tensor_copy(out=posT[:], in_=sg[:, 0:32]), reads=[("stage", kst % 2)], writes=["posT"])
        kst += 1
        for kv in range(2):
            for hf in range(2):
                for p in range(32):
                    S.add("tensor", lambda e, kv=kv, hf=hf, p=p: e.matmul(
                        ps_p[0][:, 0:1], lhsT=w1sb[64 * kv:64 * kv + 64, p, 128 * hf:128 * hf + 128],
                        rhs=posT[64 * kv:64 * kv + 64, p:p + 1], start=(p == 0), stop=(p == 31)),
                        reads=["w1sb", "posT"], writes=[("psp", 0)])
                S.add("vector", lambda e, kv=kv, hf=hf: e.tensor_copy(out=cposb[:, kv, hf:hf + 1], in_=ps_p[0][:, 0:1]),
                      reads=[("psp", 0)], writes=["cposb"])
        XT = [("xt", c) for c in range(8)]
        pcount = [0]

        def proj_fm(col0, M, n=512):
            p = pcount[0] % 2
            pcount[0] += 1
            for c in range(8):
                S.add("tensor", lambda e, c=c, p=p: e.matmul(ps_p[p][0:M, 0:n], lhsT=w[:, c, col0:col0 + M], rhs=h[:, c, 0:n],
                                                             start=(c == 0), stop=(c == 7)),
                      reads=[("w", c), ("h", c)], writes=[("psp", p)])
            return p

        def proj_tok(col0, ncol, b):
            p = pcount[0] % 2
            pcount[0] += 1
            for c in range(8):
                S.add("tensor", lambda e, c=c, p=p: e.matmul(ps_p[p][:, 0:ncol], lhsT=h[:, c, b * 128:(b + 1) * 128],
                                                             rhs=w[:, c, col0:col0 + ncol], start=(c == 0), stop=(c == 7)),
                      reads=[("w", c), ("h", c)], writes=[("psp", p)])
            return p

        for tt in range(NT):
            tok0 = 512 * tt
            S.add("sync", lambda e, tok0=tok0: e.dma_start(out=xt[:], in_=xv[:, :, tok0:tok0 + 512]), writes=XT, dma="xt_in")
            emit_rmsnorm(S, xt, h, h, ps_s, rstd, ones, epsb, 512, sqtag="h")
            p = proj_fm(256, 128)
            S.add("scalar", lambda e, p=p, tok0=tok0: e.copy(out=big[:, tok0:tok0 + 512], in_=ps_p[p][:]),
                  reads=[("psp", p)], writes=[("big", 4 * tt + b) for b in range(4)])
        BIGALL = [("big", b) for b in range(NBLK)]
        for n0 in range(0, NCMP, 512):
            N = min(512, NCMP - n0)
            for kv in range(2):
                for hf in range(2):
                    pp = pcount[0] % 2
                    pcount[0] += 1
                    for p in range(32):
                        a0 = 16 * n0 + p
                        S.add("tensor", lambda e, kv=kv, hf=hf, p=p, pp=pp, a0=a0, N=N: e.matmul(
                            ps_p[pp][:, 0:N], lhsT=w1sb[64 * kv:64 * kv + 64, p, 128 * hf:128 * hf + 128],
                            rhs=big[64 * kv:64 * kv + 64, a0:a0 + 16 * (N - 1) + 1:16], start=(p == 0), stop=(p == 31)),
                            reads=["w1sb"] + BIGALL, writes=[("psp", pp)])
                    S.add("scalar", lambda e, kv=kv, hf=hf, pp=pp, N=N: e.activation(
                        out=gel[0][:, 0:N], in_=ps_p[pp][:, 0:N], func=AF.Identity, bias=cposb[:, kv, hf:hf + 1]),
                        reads=[("psp", pp), "cposb"], writes=["gel0"])
                    S.add("vector", lambda e, N=N: e.tensor_tensor(out=gel[1][:, 0:N], in0=gel[0][:, 0:N], in1=gel[0][:, 0:N], op=ALU.mult),
                          reads=["gel0"], writes=["gel1"])
                    S.add("vector", lambda e, N=N: e.tensor_scalar(out=gel[1][:, 0:N], in0=gel[1][:, 0:N], scalar1=0.044715, scalar2=1.0,
                                                                   op0=ALU.mult, op1=ALU.add), reads=["gel1"], writes=["gel1"])
                    S.add("vector", lambda e, N=N: e.tensor_tensor(out=gel[1][:, 0:N], in0=gel[1][:, 0:N], in1=gel[0][:, 0:N], op=ALU.mult),
                          reads=["gel1", "gel0"], writes=["gel1"])
                    S.add("scalar", lambda e, N=N: e.activation(out=gel[2][:, 0:N], in_=gel[1][:, 0:N], func=AF.Sigmoid,
                                                                scale=1.5957691216057308), reads=["gel1"], writes=["gel2"])
                    S.add("vector", lambda e, hf=hf, N=N: e.tensor_tensor(out=gelu[:, hf, 0:N], in0=gel[0][:, 0:N], in1=gel[2][:, 0:N],
                                                                          op=ALU.mult), reads=["gel0", "gel2"], writes=[("gelu", hf)])
                if kv == 0:
                    pp = pcount[0] % 2
                    pcount[0] += 1
                    for hf in range(2):
                        S.add("tensor", lambda e, hf=hf, pp=pp, N=N: e.matmul(ps_p[pp][0:64, 0:N], lhsT=w2sb[:, 0, hf, :], rhs=gelu[:, hf, 0:N],
                                                                              start=(hf == 0), stop=(hf == 1)),
                              reads=["w2sb", ("gelu", hf)], writes=[("psp", pp)])
                    emit_headnorm(S, ps_p[pp][0:64, 0:N], kcmp[0:64, n0:n0 + N], gk3s[:, 0:1], sqh, ps_n, rs, ones64, epsb, N,
                                  rd=[("psp", pp)], wr=["kcmp"])
                else:
                    for nt in range(0, N, 128):
                        M = min(128, N - nt)
                        pp = pcount[0] % 2
                        pcount[0] += 1
                        for hf in range(2):
                            S.add("tensor", lambda e, hf=hf, pp=pp, nt=nt, M=M: e.matmul(
                                ps_p[pp][0:M, 0:64], lhsT=gelu[:, hf, nt:nt + M], rhs=w2sb[:, 1, hf, :], start=(hf == 0), stop=(hf == 1)),
                                reads=["w2sb", ("gelu", hf)], writes=[("psp", pp)])
                        S.add("scalar", lambda e, pp=pp, nt=nt, M=M, n0=n0: e.copy(out=vcmp[0:M, (n0 + nt) // 128, :], in_=ps_p[pp][0:M, 0:64]),
                              reads=[("psp", pp)], writes=["vcmp"])
        S.add("sync", lambda e: e.dma_start(out=big[64:73, :], in_=ktab), reads=BIGALL, writes=BIGALL + ["bigc"], dma="ck2")
        stc = [0]
        ptc = [0]

        def attn_steps(steps, vM, final_name):
            n = len(steps)
            LA = 2
            slots = {}

            def emit_front(i):
                sp = steps[i]
                sti = stc[0] % 3
                stc[0] += 1
                if sp.get("pt") is None:
                    pti = ptc[0] % NPT
                    ptc[0] += 1
                    pt, ptkey = PT[pti][:], ("PT", pti)
                else:
                    pt, ptkey = sp["pt"]
                slots[i] = (pt, ptkey)
                if sp.get("pre") is not None:
                    S.add(sp["pre"][0], sp["pre"][1], reads=sp["pre"][2], writes=sp["pre"][3])
                S.add("tensor", lambda e, sp=sp, sti=sti: e.matmul(
                    st_[sti][:].rearrange("p (a b) -> p a b", a=4), lhsT=sp["lhsT_k"], rhs=sp["rhs_q"], start=True,
                    stop=(sp.get("extra") is None)),
                    reads=sp["qreads"] + sp["kreads"], writes=[("st", sti)])
                if sp.get("extra") is not None:
                    xl, xr, xreads = sp["extra"]
                    S.add("tensor", lambda e, xl=xl, xr=xr, sti=sti: e.matmul(st_[sti][:], lhsT=xl, rhs=xr, start=False, stop=True),
                          reads=xreads, writes=[("st", sti)])
                if sp.get("mask") is not None:
                    ti = i % 2
                    S.add("vector", lambda e, sp=sp, sti=sti, ti=ti: e.tensor_tensor(
                        out=tmp[ti][:].rearrange("p (a b) -> p a b", a=4), in0=st_[sti][:].rearrange("p (a b) -> p a b", a=4),
                        in1=sp["mask"].unsqueeze(1).to_broadcast([128, 4, 128]), op=ALU.add),
                        reads=[("st", sti)] + sp["mreads"], writes=[("tmp", ti)])
                    S.add("scalar", lambda e, ti=ti, pt=pt: e.activation(out=pt, in_=tmp[ti][:], func=AF.Exp),
                          reads=[("tmp", ti)], writes=[ptkey])
                else:
                    S.add("scalar", lambda e, sti=sti, pt=pt: e.activation(out=pt, in_=st_[sti][:], func=AF.Exp),
                          reads=[("st", sti)], writes=[ptkey])

            def emit_back(i):
                sp = steps[i]
                pt, ptkey = slots[i]
                S.add("tensor", lambda e, sp=sp, pt=pt, i=i: e.matmul(num[0:vM, :], lhsT=sp["v_lhsT"], rhs=pt, start=(i == 0), stop=(i == n - 1)),
                      reads=sp["vreads"] + [ptkey], writes=["num"])
                S.add("tensor", lambda e, pt=pt, i=i: e.matmul(den[:], lhsT=onesd[:], rhs=pt, start=(i == 0), stop=(i == n - 1)),
                      reads=["onesd", ptkey], writes=["den"])

            for i in range(n):
                emit_front(i)
                if i >= LA:
                    emit_back(i - LA)
            for i in range(max(0, n - LA), n):
                emit_back(i)

        for tt in range(NT):
            tok0 = 512 * tt
            S.add("sync", lambda e, tok0=tok0: e.dma_start(out=xt[:], in_=xv[:, :, tok0:tok0 + 512]), writes=XT, dma="xt_in")
            S.add("gpsimd", lambda e, tok0=tok0: e.dma_start(out=qaug[64:73, :, :], in_=qtab[:, :, tok0:tok0 + 512]),
                  writes=["qaugc"], dma="qc")
            for b in range(4):
                slot = (4 * tt + b) % 8
                S.add("gpsimd", lambda e, slot=slot, b=b, tok0=tok0: e.dma_start(
                    out=kwr[64:73, slot * 128:(slot + 1) * 128], in_=ktab[:, tok0 + b * 128:tok0 + (b + 1) * 128]),
                    writes=[("kwrc", slot)], dma=("kwc", slot))
            emit_rmsnorm(S, xt, h, h, ps_s, rstd, ones, epsb, 512, sqtag="h")
            p = proj_fm(384, 64)
            emit_headnorm(S, ps_p[p][0:64, :], big[0:64, tok0:tok0 + 512], gk3s[:, 1:2], sqh, ps_n, rs, ones64, epsb, 512,
                          rd=[("psp", p)], wr=[("big", 4 * tt + b) for b in range(4)])
            p = proj_fm(512, 64)
            s0 = (4 * tt) % 8
            emit_headnorm(S, ps_p[p][0:64, :], kwr[0:64, s0 * 128:(s0 + 4) * 128], gk3s[:, 2:3], sqh, ps_n, rs, ones64, epsb, 512,
                          rd=[("psp", p)], wr=[("kwr", s0 + b) for b in range(4)])
            for b in range(4):
                p = proj_tok(448, 64, b)
                S.add("scalar", lambda e, p=p, b=b, tt=tt: e.copy(out=vs_tok[:, 4 * tt + b, :], in_=ps_p[p][:, 0:64]),
                      reads=[("psp", p)], writes=[("vs", 4 * tt + b)])
                p = proj_tok(576, 64, b)
                S.add("scalar", lambda e, p=p, b=b, s0=s0: e.copy(out=vwr[:, s0 + b, :], in_=ps_p[p][:, 0:64]),
                      reads=[("psp", p)], writes=[("vwr", s0 + b)])
            for gi in range(4):
                p = proj_fm(64 * gi, 64)
                emit_headnorm(S, ps_p[p][0:64, :], qaug[0:64, gi, :], gqs[:, 0:1], sqh, ps_n, rs, ones64, epsb, 512,
                              rd=[("psp", p)], wr=[("qaug", gi)])
            p = proj_fm(640, 12)
            S.add("scalar", lambda e, p=p: e.activation(out=gT[:], in_=ps_p[p][0:12, :], func=AF.Sigmoid),
                  reads=[("psp", p)], writes=["gT"])
            qreads = [("qaug", gi) for gi in range(4)] + ["qaugc"]
            for ql in range(4):
                qb = 4 * tt + ql
                rhs_q = qaug[:, :, ql * 128:(ql + 1) * 128]
                nkt = min(NCT, qb // 16 + 1)
                steps = []
                for kt in range(nkt):
                    dl = qb - 16 * kt
                    steps.append(dict(lhsT_k=kcmp[:, kt * 128:(kt + 1) * 128], rhs_q=rhs_q, qreads=qreads, kreads=["kcmp", "kcmpc"],
                                      mask=(cm[:, dl, :] if dl <= 16 else None), mreads=["cm"],
                                      v_lhsT=vcmp[:, kt, :], vreads=["vcmp"], pt=(PTc[:, kt, :], ("PTc", kt))))
                attn_steps(steps, 64, "cmp")
                S.add("vector", lambda e: e.tensor_scalar(out=rden[:], in0=den[:], scalar1=1e-30, scalar2=None, op0=ALU.max),
                      reads=["den"], writes=["rden"])
                S.add("vector", lambda e: e.reciprocal(out=rden[:], in_=rden[:]), reads=["rden"], writes=["rden"])
                S.add("vector", lambda e: e.tensor_tensor(out=obr[0][:], in0=num[0:64, :], in1=rden[0:64, :], op=ALU.mult),
                      reads=["num", "rden"], writes=[("obr", 0)])
                for kt in range(nkt):
                    S.add("gpsimd", lambda e, kt=kt: e.tensor_tensor(out=PTc[:, kt, :], in0=PTc[:, kt, :], in1=rden[:], op=ALU.mult),
                          reads=[("PTc", kt), "rden"], writes=[("PTc", kt)])
                    for gi in range(4):
                        S.add("tensor", lambda e, kt=kt, gi=gi, nkt=nkt: e.matmul(
                            imp[:, 0:NSEL], lhsT=PTc[:, kt, gi * 128:(gi + 1) * 128], rhs=ov[:, kt, :],
                            start=(kt == 0 and gi == 0), stop=(kt == nkt - 1 and gi == 3)),
                            reads=[("PTc", kt), "ov"], writes=["ps_s"])
                c0 = NSEL - 2 * qb
                S.add("vector", lambda e, c0=c0: e.tensor_tensor(out=imp2[:], in0=imp[:, 0:NSEL], in1=tm[:, 0, c0:c0 + NSEL], op=ALU.mult),
                      reads=["ps_s", "tm"], writes=["imp2"])
                S.add("vector", lambda e, c0=c0: e.tensor_tensor(out=imp2[:], in0=imp2[:], in1=tm[:, 1, c0:c0 + NSEL], op=ALU.add),
                      reads=["imp2", "tm"], writes=["imp2"])
                S.add("vector", lambda e: e.memset(imp2[:, 0:1], 1e9), reads=["imp2"], writes=["imp2"])
                S.add("vector", lambda e: e.max(out=m8[:], in_=imp2[:]), reads=["imp2"], writes=["m8"])
                S.add("vector", lambda e: e.match_replace(out=imp3[:], in_to_replace=m8[:], in_values=imp2[:], imm_value=-3e38),
                      reads=["imp2", "m8"], writes=["imp3"])
                S.add("vector", lambda e: e.max(out=m8[:], in_=imp3[:]), reads=["imp3"], writes=["m8"])
                S.add("vector", lambda e: e.match_replace(out=imp3[:], in_to_replace=m8[:], in_values=imp3[:], imm_value=-3e38),
                      reads=["imp3", "m8"], writes=["imp3"])
                S.add("vector", lambda e: e.tensor_tensor(out=imp3[:], in0=imp2[:], in1=imp3[:], op=ALU.not_equal),
                      reads=["imp2", "imp3"], writes=["imp3"])
                S.add("vector", lambda e: e.tensor_scalar(out=negsel[:], in0=imp3[:], scalar1=-1.0, scalar2=-NEGM, op0=ALU.add, op1=ALU.mult),
                      reads=["imp3"], writes=["negsel"])
                steps = []
                for jt in range(qb + 1):
                    xb = (jt // 8) % 2
                    pre = None
                    if jt % 8 == 0:
                        nb = min(16, 2 * (qb + 1) - 2 * jt)
                        pre = ("gpsimd", (lambda e, jt=jt, xb=xb, nb=nb: e.tensor_copy(
                            out=negx[xb][:, 0:nb, :], in_=negsel[:, 2 * jt:2 * jt + nb].unsqueeze(2).to_broadcast([128, nb, 64]))),
                            ["negsel"], [("negx", xb)])
                    steps.append(dict(lhsT_k=big[0:73, jt * 128:(jt + 1) * 128], rhs_q=rhs_q, qreads=qreads, kreads=[("big", jt), "bigc"],
                                      mask=(wm[:, 1, :] if jt == qb else None), mreads=["wm"], pre=pre,
                                      extra=(negx[xb][:, 2 * (jt % 8):2 * (jt % 8) + 2, :].rearrange("p a b -> p (a b)"), i4s[:],
                                             [("negx", xb), "i4s"]),
                                      v_lhsT=vs_tok[:, jt, :], vreads=[("vs", jt)]))
                attn_steps(steps, 64, "sel")
                S.add("vector", lambda e: e.reciprocal(out=rden[0:64, :], in_=den[0:64, :]), reads=["den"], writes=["rden"])
                S.add("vector", lambda e: e.tensor_tensor(out=obr[1][:], in0=num[0:64, :], in1=rden[0:64, :], op=ALU.mult),
                      reads=["num", "rden"], writes=[("obr", 1)])
                steps = []
                for jt in range(max(0, qb - 4), qb + 1):
                    slot = jt % 8
                    mk = wm[:, 1, :] if jt == qb else (wm[:, 0, :] if jt == qb - 4 else None)
                    steps.append(dict(lhsT_k=kwr[:, slot * 128:(slot + 1) * 128], rhs_q=rhs_q, qreads=qreads,
                                      kreads=[("kwr", slot), ("kwrc", slot)], mask=mk, mreads=["wm"],
                                      v_lhsT=vwr[:, slot, :], vreads=[("vwr", slot)]))
                attn_steps(steps, 64, "win")
                S.add("vector", lambda e: e.reciprocal(out=rden[0:64, :], in_=den[0:64, :]), reads=["den"], writes=["rden"])
                S.add("vector", lambda e: e.tensor_tensor(out=obr[2][:], in0=num[0:64, :], in1=rden[0:64, :], op=ALU.mult),
                      reads=["num", "rden"], writes=[("obr", 2)])
                osl = oTt[:, :, ql * 128:(ql + 1) * 128]
                if debug is not None:
                    S.add("vector", lambda e, osl=osl: e.tensor_copy(out=osl, in_=obr[debug][:].rearrange("p (a b) -> p a b", a=4)),
                          reads=[("obr", debug)], writes=["oTt"])
                    continue
                for br in range(3):
                    p = pcount[0] % 2
                    pcount[0] += 1
                    for gi in range(4):
                        S.add("tensor", lambda e, p=p, gi=gi, br=br, ql=ql: e.matmul(
                            ps_p[p][0:64, gi * 128:(gi + 1) * 128], lhsT=srow[:, gi * 3 + br, :], rhs=gT[:, ql * 128:(ql + 1) * 128],
                            start=True, stop=True), reads=["srow", "gT"], writes=[("psp", p)])
                    if br == 0:
                        S.add("vector", lambda e, p=p, osl=osl: e.tensor_tensor(
                            out=osl, in0=ps_p[p][0:64, :].rearrange("p (a b) -> p a b", a=4),
                            in1=obr[0][:].rearrange("p (a b) -> p a b", a=4), op=ALU.mult),
                            reads=[("psp", p), ("obr", 0)], writes=["oTt"])
                    else:
                        S.add("vector", lambda e, p=p, br=br: e.tensor_tensor(out=obr[br][:], in0=ps_p[p][0:64, :], in1=obr[br][:], op=ALU.mult),
                              reads=[("psp", p), ("obr", br)], writes=[("obr", br)])
                        S.add("vector", lambda e, br=br, osl=osl: e.tensor_tensor(
                            out=osl, in0=osl, in1=obr[br][:].rearrange("p (a b) -> p a b", a=4), op=ALU.add),
                            reads=["oTt", ("obr", br)], writes=["oTt"])
            S.add("sync", lambda e, tok0=tok0: e.dma_start(
                out=oT.rearrange("(a d) t -> d a t", d=64)[:, :, tok0:tok0 + 512], in_=oTt[:]), reads=["oTt"], dma="o_out")
        S.emit()
    return nc


def nsa_consts(S_seq, grp):
    NCMP = S_seq // 16 - 1
    NCT = (NCMP + 127) // 128
    NSEL = S_seq // 64
    slopes = alibi_slopes(16)[4 * grp:4 * grp + 4]
    pos = np.arange(S_seq)
    ktab, qtab = alibi_tabs(slopes, pos, pos)
    qtab = np.ascontiguousarray(qtab.transpose(1, 0, 2))
    kctab, _ = alibi_tabs(slopes[:1], pos[:1], 16 * np.arange(NCT * 128) + 31)
    nl = np.arange(128)[:, None]
    tl = np.arange(128)[None, :]
    cmask = np.stack([np.where(16 * nl + 31 <= 128 * dl + tl, 0.0, NEGM) for dl in range(17)], axis=1).astype(np.float32)
    wmask = np.stack([np.where(nl > tl, 0.0, NEGM), np.where(nl <= tl, 0.0, NEGM)], axis=1).astype(np.float32)
    rel = np.arange(2 * NSEL)[None, :] - NSEL
    cflag = (np.arange(128)[:, None] >= 64).astype(np.int64)
    forced = (rel == cflag) | (rel == cflag - 1)
    noncausal = rel > cflag
    keep = np.where(forced | noncausal, 0.0, 1.0)
    add = np.where(forced, 1e9, np.where(noncausal, -1e30, 0.0))
    tmpl = np.stack([keep, add], axis=1).astype(np.float32)
    n = np.arange(NCT * 128)[:, None]
    j = np.arange(NSEL)[None, :]
    ovm = ((16 * n < 64 * j + 64) & (16 * n + 32 > 64 * j) & (n < NCMP)).astype(np.float32)
    ovl = np.ascontiguousarray(ovm.reshape(NCT, 128, NSEL).transpose(1, 0, 2)).astype(NPBF)
    i4 = np.tile(np.eye(128, dtype=np.float32), (1, 4)).astype(NPBF)
    selrow = np.zeros((12, 12, 64), np.float32)
    for r in range(12):
        selrow[r, r, :] = 1.0
    return dict(ktab=ktab, qtab=qtab, kctab=kctab, cmask=np.ascontiguousarray(cmask), wmask=np.ascontiguousarray(wmask),
                tmpl=np.ascontiguousarray(tmpl), ovl=ovl, i4=i4, selrow=selrow)


def run_nsa(xT_full, B, S_seq, gv, w_in, q_norm, k_norm, cmp_pos, cmp_w1, cmp_w2, debug=None):
    assert B * 4 == N_CORES
    key = ("nsa", S_seq, debug)
    if key not in _NC_CACHE:
        _NC_CACHE[key] = build_nsa(S_seq, debug)
    nc = _NC_CACHE[key]
    w_in = np.asarray(w_in, np.float32)
    cmp_w1 = np.asarray(cmp_w1, np.float32)
    cmp_w2 = np.asarray(cmp_w2, np.float32)
    cmp_pos = np.asarray(cmp_pos, np.float32)
    common = {"g": gain_layout(gv),
              "gq": np.ascontiguousarray(np.asarray(q_norm, np.float32).reshape(64, 1)),
              "gk3": np.ascontiguousarray(np.asarray(k_norm, np.float32).reshape(3, 64).T),
              "cpos": np.ascontiguousarray(cmp_pos.transpose(0, 2, 1).reshape(128, 32)),
              "cw1": np.ascontiguousarray(cmp_w1.reshape(2, 32, 64, 256).transpose(0, 2, 1, 3)),
              "cw2": np.ascontiguousarray(cmp_w2.reshape(2, 2, 128, 64).transpose(2, 0, 1, 3))}
    in_maps = []
    for c in range(N_CORES):
        b, grp = c // 4, c % 4
        cols = [w_in[:, grp * 256:(grp + 1) * 256]]
        for i in range(6):
            cols.append(w_in[:, 1024 + 256 * i + 64 * grp:1024 + 256 * i + 64 * grp + 64])
        cols.append(w_in[:, 2560 + 12 * grp:2560 + 12 * grp + 12])
        m = dict(common, xT=np.ascontiguousarray(xT_full[:, b * S_seq:(b + 1) * S_seq]),
                 w_in=np.ascontiguousarray(np.concatenate(cols, axis=1)))
        m.update(nsa_consts(S_seq, grp))
        in_maps.append(m)
    res = run_bass_kernel_spmd(nc, in_maps, core_ids=list(range(N_CORES)))
    out = np.zeros((B, 1024, S_seq), np.float32)
    for c in range(N_CORES):
        b, grp = c // 4, c % 4
        out[b, grp * 256:(grp + 1) * 256, :] = res.results[c]["oT"]
    return out


def build_gdn(S_seq, NH):
    nc = bass.Bass("TRN2", target_bir_lowering=False)
    NT = S_seq // 512
    WC = 514
    with contextlib.ExitStack() as st:
        C = Ctx(nc, st)
        xT = C.din("xT", [D, S_seq]); g = C.din("g", [128, 8])
        w_in = C.din("w_in", [D, NH, WC])
        convw = C.din("convw", [128, NH, 3, 4])
        alog = C.din("alog", [128, NH]); dtb = C.din("dtb", [128, NH])
        onrm = C.din("onrm", [128, 128])
        cst = C.din("cst", [128, 6, 128])
        oT = C.dout("oT", [NH, S_seq, 128])
        xv = xT.rearrange("(c p) t -> p c t", p=128)
        wiv = w_in.rearrange("(c p) h f -> p c h f", p=128)

        w = C.sb("w", [128, 8, NH, WC], BF16)
        stage = [C.sb("stage%d" % i, [128, WC], F32) for i in range(2)]
        gs = C.sb("gs", [128, 8], F32); epsb = C.sb("epsb", [128, 1], F32)
        ones = C.sb("ones", [128, 128], BF16); onesf = C.sb("onesf", [128, 128], BF16)
        cw = C.sb("cw", [128, NH, 3, 4], F32)
        negA = C.sb("negA", [128, NH], F32); dtbs = C.sb("dtbs", [128, NH], F32)
        onr = C.sb("onr", [128, 128], F32)
        K_ = C.sb("cst_sb", [128, 6, 128], F32)
        ident, U2, Ublk, mTi, mS, sT01 = [K_[:, i, :] for i in range(6)]
        xt = C.sb("xt", [128, 8, 512], F32); h = C.sb("h", [128, 8, 512], BF16)
        rstd = C.sb("rstd", [128, 512], F32)
        pre = [[C.sb("pre%d_%d" % (a, b), [128, 515], F32) for b in range(3)] for a in range(NH)]
        post = [[C.sb("post%d_%d" % (a, b), [128, 512], F32) for b in range(3)] for a in range(NH)]
        cacc = C.sb("cacc", [128, 512], F32)
        sqf = C.sb("sqf", [128, 512], BF16); rsf = C.sb("rsf", [128, 512], F32)
        Sst = [C.sb("S%d" % a, [128, 128], F32) for a in range(NH)]
        def hs(name, shape, dt=F32):
            return [C.sb("%s%d" % (name, a), shape, dt) for a in range(NH)]
        cols = hs("cols", [128, 16]); grep = hs("grep", [128, 128]); brep = hs("brep", [128, 128])
        argA = hs("argA", [128, 128]); argB = hs("argB", [128, 128]); DT = hs("DT", [128, 128]); Dm = hs("Dm", [128, 128])
        Nm = hs("Nm", [128, 128]); NTm = hs("NTm", [128, 128]); attnT = hs("attnT", [128, 128])
        Xm = hs("Xm", [128, 128]); Pm = hs("Pm", [128, 2, 128]); PTm = hs("PTm", [128, 2, 128])
        vb = hs("vb", [128, 128]); kbg = hs("kbg", [128, 128]); kd = hs("kd", [128, 128]); egr = hs("egr", [128, 128])
        qgT = hs("qgT", [128, 128]); u = hs("u", [128, 128]); wT = hs("wT", [128, 128]); vnew = hs("vnew", [128, 128])
        gsil = hs("gsil", [128, 128]); osb = hs("osb", [128, 128]); osq = hs("osq", [128, 128]); glc = hs("glc", [128, 2])
        banks = [C.ps("bank%d" % i) for i in range(8)]
        ps_s = banks[0]
        ps_p = [banks[1], banks[2]]
        Q = lambda b, q: banks[b][:, q * 128:(q + 1) * 128]

        S = Sched(nc)
        S.add("vector", lambda e: e.memset(ones[:], 1.0 / D), writes=["ones"])
        S.add("vector", lambda e: e.memset(onesf[:], 1.0), writes=["onesf"])
        S.add("vector", lambda e: e.memset(epsb[:], EPS), writes=["epsb"])
        for a in range(NH):
            S.add("vector", lambda e, a=a: e.memset(Sst[a][:], 0.0), writes=[("S", a)])
            for b in range(3):
                S.add("gpsimd", lambda e, a=a, b=b: e.memset(pre[a][b][:, 0:3], 0.0), writes=[("pre", a, b)])
        ld = [(gs, g, "gs"), (cw, convw, "cw"), (negA, alog, "negA"), (dtbs, dtb, "dtbs"), (onr, onrm, "onr"), (K_, cst, "cst")]
        for n_, (dst, src, key) in enumerate(ld):
            S.add("sync", lambda e, dst=dst, src=src: e.dma_start(out=dst[:], in_=src), writes=[key], dma="c%d" % n_)
        S.add("scalar", lambda e: e.activation(out=negA[:], in_=negA[:], func=AF.Exp), reads=["negA"], writes=["negA"])
        S.add("vector", lambda e: e.tensor_scalar(out=negA[:], in0=negA[:], scalar1=-1.0, scalar2=None, op0=ALU.mult),
              reads=["negA"], writes=["negA"])
        kst = 0
        for c in range(8):
            for a in range(NH):
                sg = stage[kst % 2]
                S.add("sync", lambda e, sg=sg, c=c, a=a: e.dma_start(out=sg[:], in_=wiv[:, c, a, :]),
                      writes=[("stage", kst % 2)], dma=("stage", kst % 2))
                S.add(["vector", "gpsimd"][kst % 2], lambda e, sg=sg, c=c, a=a: e.tensor_scalar(
                    out=w[:, c, a, :], in0=sg[:], scalar1=gs[:, c:c + 1], scalar2=None, op0=ALU.mult),
                    reads=[("stage", kst % 2), "gs"], writes=[("w", c)])
                kst += 1
        XT = [("xt", c) for c in range(8)]
        pcount = [0]

        def mmf(out, lhsT, rhs, reads, writes, start=True, stop=True):
            S.add("tensor", lambda e: e.matmul(out, lhsT=lhsT, rhs=rhs, start=start, stop=stop), reads=reads, writes=writes)

        for tt in range(NT):
            tok0 = 512 * tt
            S.add("sync", lambda e, tok0=tok0: e.dma_start(out=xt[:], in_=xv[:, :, tok0:tok0 + 512]), writes=XT, dma="xt_in")
            emit_rmsnorm(S, xt, h, h, ps_s, rstd, ones, epsb, 512, sqtag="h")
            for a in range(NH):
                for b in range(3):
                    p = pcount[0] % 2
                    pcount[0] += 1
                    for c in range(8):
                        S.add("tensor", lambda e, c=c, p=p, a=a, b=b: e.matmul(
                            ps_p[p][:], lhsT=w[:, c, a, 128 * b:128 * b + 128], rhs=h[:, c, :], start=(c == 0), stop=(c == 7)),
                            reads=[("w", c), ("h", c)], writes=[("psp", p)])
                    pr_ = pre[a][b]
                    S.add("scalar", lambda e, p=p, pr_=pr_: e.copy(out=pr_[:, 3:515], in_=ps_p[p][:]),
                          reads=[("psp", p)], writes=[("pre", a, b)])
                    S.add("vector", lambda e, pr_=pr_, a=a, b=b: e.tensor_scalar(
                        out=cacc[:], in0=pr_[:, 0:512], scalar1=cw[:, a, b, 0:1], scalar2=None, op0=ALU.mult),
                        reads=[("pre", a, b), "cw"], writes=["cacc"])
                    for tap in range(1, 4):
                        S.add("vector", lambda e, pr_=pr_, a=a, b=b, tap=tap: e.scalar_tensor_tensor(
                            out=cacc[:], in0=pr_[:, tap:tap + 512], scalar=cw[:, a, b, tap:tap + 1], in1=cacc[:],
                            op0=ALU.mult, op1=ALU.add), reads=[("pre", a, b), "cw", "cacc"], writes=["cacc"])
                    S.add("vector", lambda e, pr_=pr_: e.tensor_copy(out=pr_[:, 0:3], in_=pr_[:, 512:515]),
                          reads=[("pre", a, b)], writes=[("pre", a, b)])
                    po = post[a][b]
                    S.add("scalar", lambda e, po=po: e.activation(out=po[:], in_=cacc[:], func=AF.Silu),
                          reads=["cacc"], writes=[("post", a, b)])
                    if b < 2:
                        S.add("scalar", lambda e, po=po: e.activation(out=sqf[:], in_=po[:], func=AF.Square),
                              reads=[("post", a, b)], writes=["sqf"])
                        S.add("tensor", lambda e: e.matmul(ps_s[:], lhsT=onesf[:], rhs=sqf[:], start=True, stop=True),
                              reads=["onesf", "sqf"], writes=["ps_s"])
                        S.add("scalar", lambda e: e.activation(out=rsf[:], in_=ps_s[:], func=AF.Sqrt, bias=epsb[:, 0:1]),
                              reads=["ps_s", "epsb"], writes=["rsf"])
                        S.add("vector", lambda e: e.reciprocal(out=rsf[:], in_=rsf[:]), reads=["rsf"], writes=["rsf"])
                        sc = (128.0 ** -0.5) if b == 0 else 1.0
                        S.add("vector", lambda e, po=po, sc=sc: e.scalar_tensor_tensor(
                            out=po[:], in0=po[:], scalar=sc, in1=rsf[:], op0=ALU.mult, op1=ALU.mult),
                            reads=[("post", a, b), "rsf"], writes=[("post", a, b)])
            for dc in range(4):
                cs = slice(dc * 128, (dc + 1) * 128)
                for a in range(NH):
                    A = lambda name: (name, a)
                    qT_, kT_, vT_ = post[a][0][:, cs], post[a][1][:, cs], post[a][2][:, cs]
                    RP = [("post", a, 0), ("post", a, 1), ("post", a, 2)]
                    for c in range(8):
                        S.add("tensor", lambda e, c=c, a=a, dc=dc: e.matmul(
                            Q(6, 2), lhsT=h[:, c, dc * 128:(dc + 1) * 128], rhs=w[:, c, a, 384:512], start=(c == 0), stop=(c == 7)),
                            reads=[("w", c), ("h", c)], writes=["b6"])
                    for c in range(8):
                        S.add("tensor", lambda e, c=c, a=a, dc=dc: e.matmul(
                            banks[6][:, 384:386], lhsT=h[:, c, dc * 128:(dc + 1) * 128], rhs=w[:, c, a, 512:514],
                            start=(c == 0), stop=(c == 7)), reads=[("w", c), ("h", c)], writes=["b6"])
                    S.add("scalar", lambda e, a=a: e.activation(out=gsil[a][:], in_=Q(6, 2), func=AF.Silu),
                          reads=["b6"], writes=[A("gsil")])
                    cl = cols[a]
                    S.add("scalar", lambda e, cl=cl: e.activation(out=cl[:, 0:1], in_=banks[6][:, 384:385], func=AF.Sigmoid),
                          reads=["b6"], writes=[A("cols")])
                    S.add("scalar", lambda e, cl=cl, a=a: e.activation(out=cl[:, 8:9], in_=banks[6][:, 385:386], func=AF.Exp,
                                                                       bias=dtbs[:, a:a + 1]),
                          reads=["b6", "dtbs"], writes=[A("cols")])
                    S.add("vector", lambda e, cl=cl: e.tensor_scalar(out=cl[:, 8:9], in0=cl[:, 8:9], scalar1=1.0, scalar2=None, op0=ALU.add),
                          reads=[A("cols")], writes=[A("cols")])
                    S.add("scalar", lambda e, cl=cl: e.activation(out=cl[:, 8:9], in_=cl[:, 8:9], func=AF.Ln),
                          reads=[A("cols")], writes=[A("cols")])
                    S.add("vector", lambda e, cl=cl, a=a: e.tensor_tensor(out=cl[:, 1:2], in0=cl[:, 8:9], in1=negA[:, a:a + 1], op=ALU.mult),
                          reads=[A("cols"), "negA"], writes=[A("cols")])
                    S.add("vector", lambda e, cl=cl, a=a: e.tensor_copy(out=grep[a][:], in_=cl[:, 1:2].to_broadcast([128, 128])),
                          reads=[A("cols")], writes=[A("grep")])
                    S.add("gpsimd", lambda e, cl=cl, a=a: e.tensor_copy(out=brep[a][:], in_=cl[:, 0:1].to_broadcast([128, 128])),
                          reads=[A("cols")], writes=[A("brep")])
                    mmf(Q(3, 0), grep[a][:], U2, [A("grep"), "cst"], ["b3"])
                    mmf(Q(3, 1), grep[a][:], Ublk, [A("grep"), "cst"], ["b3"])
                    mmf(Q(3, 2), brep[a][:], ident, [A("brep"), "cst"], ["b3"])
                    mmf(banks[3][:, 384:386], U2, grep[a][:, 0:2], [A("grep"), "cst"], ["b3"])
                    mmf(banks[3][:, 386:388], Ublk, grep[a][:, 0:2], [A("grep"), "cst"], ["b3"])
                    S.add("vector", lambda e, cl=cl: e.tensor_copy(out=cl[:, 2:3], in_=banks[3][:, 384:385]), reads=["b3"], writes=[A("cols")])
                    S.add("vector", lambda e, cl=cl: e.tensor_scalar(out=cl[:, 3:4], in0=banks[3][:, 384:385], scalar1=-1.0, scalar2=None,
                                                                     op0=ALU.mult), reads=["b3"], writes=[A("cols")])
                    S.add("scalar", lambda e, cl=cl: e.activation(out=cl[:, 4:5], in_=banks[3][:, 384:385], func=AF.Exp),
                          reads=["b3"], writes=[A("cols")])
                    S.add("scalar", lambda e, cl=cl: e.activation(out=cl[:, 5:6], in_=banks[3][:, 386:387], func=AF.Exp, bias=cl[:, 3:4]),
                          reads=["b3", A("cols")], writes=[A("cols")])
                    S.add("vector", lambda e, cl=cl: e.tensor_tensor(out=cl[:, 6:7], in0=cl[:, 4:5], in1=cl[:, 0:1], op=ALU.mult),
                          reads=[A("cols")], writes=[A("cols")])
                    S.add("vector", lambda e, cl=cl: e.tensor_scalar(out=cl[:, 7:8], in0=cl[:, 0:1], scalar1=-1.0, scalar2=None, op0=ALU.mult),
                          reads=[A("cols")], writes=[A("cols")])
                    S.add("scalar", lambda e, a=a: e.activation(out=glc[a][:], in_=banks[3][:, 128:256:64], func=AF.Exp),
                          reads=["b3"], writes=[A("glc")])
                    mmf(Q(4, 0), kT_, kT_, RP, ["b4"])
                    mmf(Q(4, 1), kT_, qT_, RP, ["b4"])
                    mmf(Q(4, 2), kT_, ident, RP + ["cst"], ["b4"])
                    mmf(Q(4, 3), vT_, ident, RP + ["cst"], ["b4"])
                    S.add("vector", lambda e, a=a: e.tensor_tensor(out=argA[a][:], in0=Q(3, 0), in1=mTi, op=ALU.add),
                          reads=["b3", "cst"], writes=[A("argA")])
                    S.add("scalar", lambda e, a=a, cl=cl: e.activation(out=DT[a][:], in_=argA[a][:], func=AF.Exp, bias=cl[:, 3:4]),
                          reads=[A("argA"), A("cols")], writes=[A("DT")])
                    S.add("vector", lambda e, a=a: e.scalar_tensor_tensor(out=argB[a][:], in0=Q(3, 0), scalar=-1.0, in1=mS,
                                                                          op0=ALU.mult, op1=ALU.add),
                          reads=["b3", "cst"], writes=[A("argB")])
                    S.add("scalar", lambda e, a=a, cl=cl: e.activation(out=Dm[a][:], in_=argB[a][:], func=AF.Exp, bias=cl[:, 2:3]),
                          reads=[A("argB"), A("cols")], writes=[A("Dm")])
                    S.add("scalar", lambda e, a=a: e.activation(out=egr[a][:], in_=Q(3, 0), func=AF.Exp), reads=["b3"], writes=[A("egr")])
                    S.add("vector", lambda e, a=a, cl=cl: e.scalar_tensor_tensor(out=NTm[a][:], in0=Q(4, 0), scalar=cl[:, 7:8], in1=Dm[a][:],
                                                                                 op0=ALU.mult, op1=ALU.mult),
                          reads=["b4", A("cols"), A("Dm")], writes=[A("NTm")])
                    S.add("vector", lambda e, a=a: e.tensor_tensor(out=Nm[a][:], in0=Q(4, 0), in1=DT[a][:], op=ALU.mult),
                          reads=["b4", A("DT")], writes=[A("Nm")])
                    S.add("vector", lambda e, a=a: e.tensor_tensor(out=Nm[a][:], in0=Nm[a][:], in1=sT01, op=ALU.mult),
                          reads=[A("Nm"), "cst"], writes=[A("Nm")])
                    S.add("vector", lambda e, a=a: e.scalar_tensor_tensor(out=Nm[a][:], in0=Q(3, 2), scalar=-1.0, in1=Nm[a][:],
                                                                          op0=ALU.mult, op1=ALU.mult),
                          reads=["b3", A("Nm")], writes=[A("Nm")])
                    S.add("vector", lambda e, a=a: e.tensor_tensor(out=attnT[a][:], in0=Q(4, 1), in1=DT[a][:], op=ALU.mult),
                          reads=["b4", A("DT")], writes=[A("attnT")])
                    S.add("vector", lambda e, a=a, cl=cl: e.tensor_scalar(out=vb[a][:], in0=Q(4, 3), scalar1=cl[:, 0:1], scalar2=None, op0=ALU.mult),
                          reads=["b4", A("cols")], writes=[A("vb")])
                    S.add("vector", lambda e, a=a, cl=cl: e.tensor_scalar(out=kbg[a][:], in0=Q(4, 2), scalar1=cl[:, 6:7], scalar2=None, op0=ALU.mult),
                          reads=["b4", A("cols")], writes=[A("kbg")])
                    S.add("vector", lambda e, a=a, cl=cl: e.tensor_scalar(out=kd[a][:], in0=Q(4, 2), scalar1=cl[:, 5:6], scalar2=None, op0=ALU.mult),
                          reads=["b4", A("cols")], writes=[A("kd")])
                    S.add("vector", lambda e, a=a, qT_=qT_: e.tensor_tensor(out=qgT[a][:], in0=qT_, in1=egr[a][:], op=ALU.mult),
                          reads=RP + [A("egr")], writes=[A("qgT")])
                    S.add("vector", lambda e, a=a: e.tensor_tensor(out=Xm[a][:], in0=Nm[a][:], in1=ident, op=ALU.add),
                          reads=[A("Nm"), "cst"], writes=[A("Xm")])
                    P_cur, PT_cur = Nm[a][:], NTm[a][:]
                    rdP, rdPT = [A("Nm")], [A("NTm")]
                    for lvl in range(5):
                        sl = lvl % 2
                        last = (lvl == 4)
                        mmf(Q(5, 1), P_cur, PT_cur, rdP + rdPT, ["b5"])
                        if not last:
                            mmf(Q(5, 0), PT_cur, P_cur, rdP + rdPT, ["b5"])
                        S.add("scalar", lambda e, a=a, sl=sl: e.copy(out=PTm[a][:, sl, :], in_=Q(5, 1)), reads=["b5"], writes=[("PTm", a, sl)])
                        if not last:
                            S.add("gpsimd" if False else "vector", lambda e, a=a, sl=sl: e.tensor_copy(out=Pm[a][:, sl, :], in_=Q(5, 0)),
                                  reads=["b5"], writes=[("Pm", a, sl)])
                        mmf(Q(5, 2), PTm[a][:, sl, :], Xm[a][:], [("PTm", a, sl), A("Xm")], ["b5"])
                        S.add("vector", lambda e, a=a: e.tensor_tensor(out=Xm[a][:], in0=Q(5, 2), in1=Xm[a][:], op=ALU.add),
                              reads=["b5", A("Xm")], writes=[A("Xm")])
                        P_cur, PT_cur = Pm[a][:, sl, :], PTm[a][:, sl, :]
                        rdP, rdPT = [("Pm", a, sl)], [("PTm", a, sl)]
                    mmf(Q(6, 0), Xm[a][:], vb[a][:], [A("Xm"), A("vb")], ["b6"])
                    mmf(Q(6, 1), kbg[a][:], Xm[a][:], [A("Xm"), A("kbg")], ["b6"])
                    S.add("scalar", lambda e, a=a: e.copy(out=u[a][:], in_=Q(6, 0)), reads=["b6"], writes=[A("u")])
                    S.add("vector", lambda e, a=a: e.tensor_copy(out=wT[a][:], in_=Q(6, 1)), reads=["b6"], writes=[A("wT")])
                for ch in range(2):
                    pr = slice(64 * ch, 64 * ch + 64)
                    for a in range(NH):
                        A = lambda name: (name, a)
                        SK = ("S", a)
                        mmf(banks[7][pr, 0:128], wT[a][:, pr], Sst[a][:], [A("wT"), SK], ["b7"])
                        S.add("vector", lambda e, a=a, pr=pr: e.tensor_tensor(out=vnew[a][pr, :], in0=u[a][pr, :], in1=banks[7][pr, 0:128],
                                                                              op=ALU.subtract),
                              reads=["b7", A("u")], writes=[A("vnew")])
                        mmf(banks[7][pr, 128:256], qgT[a][:, pr], Sst[a][:], [A("qgT"), SK], ["b7"], start=True, stop=False)
                        mmf(banks[7][pr, 128:256], attnT[a][pr, pr], vnew[a][pr, :], [A("attnT"), A("vnew")], ["b7"], start=False, stop=True)
                        mmf(banks[7][:, 256:384], kd[a][pr, :], vnew[a][pr, :], [A("kd"), A("vnew")], ["b7"])
                        S.add("vector", lambda e, a=a, ch=ch: e.scalar_tensor_tensor(
                            out=Sst[a][:], in0=Sst[a][:], scalar=glc[a][:, ch:ch + 1], in1=banks[7][:, 256:384], op0=ALU.mult, op1=ALU.add),
                            reads=["b7", SK, A("glc")], writes=[SK])
                        S.add("scalar", lambda e, a=a, pr=pr: e.activation(out=osq[a][pr, :], in_=banks[7][pr, 128:256], func=AF.Square,
                                                                           accum_out=cols[a][pr, 10:11]),
                              reads=["b7"], writes=[A("osq"), ("cols2", a, ch)])
                        S.add("scalar", lambda e, a=a, pr=pr: e.activation(out=cols[a][pr, 11:12], in_=cols[a][pr, 10:11], func=AF.Sqrt,
                                                                           bias=epsb[pr, 0:1], scale=1.0 / 128),
                              reads=[("cols2", a, ch), "epsb"], writes=[("cols2", a, ch)])
                        S.add("vector", lambda e, a=a, pr=pr: e.reciprocal(out=cols[a][pr, 11:12], in_=cols[a][pr, 11:12]),
                              reads=[("cols2", a, ch)], writes=[("cols2", a, ch)])
                        S.add("vector", lambda e, a=a, pr=pr: e.scalar_tensor_tensor(
                            out=osb[a][pr, :], in0=banks[7][pr, 128:256], scalar=cols[a][pr, 11:12], in1=onr[pr, :], op0=ALU.mult, op1=ALU.mult),
                            reads=["b7", ("cols2", a, ch), "onr"], writes=[("osb", a, ch)])
                        S.add("vector", lambda e, a=a, pr=pr: e.tensor_tensor(out=osb[a][pr, :], in0=osb[a][pr, :], in1=gsil[a][pr, :], op=ALU.mult),
                              reads=[("osb", a, ch), A("gsil")], writes=[("osb", a, ch)])
                for a in range(NH):
                    r0 = tok0 + dc * 128
                    S.add("sync", lambda e, a=a, r0=r0: e.dma_start(out=oT[a, r0:r0 + 128, :], in_=osb[a][:]),
                          reads=[("osb", a, 0), ("osb", a, 1)], dma=("o_out", a))
        S.emit()
    return nc


def gdn_consts():
    i = np.arange(128)
    same = (i[:, None] // 64) == (i[None, :] // 64)
    ident = np.eye(128)
    U2 = (same & (i[:, None] <= i[None, :])).astype(np.float64)
    Ublk = same.astype(np.float64)
    mTi = np.where(same & (i[None, :] >= i[:, None]), 0.0, -1e4)
    mS = np.where(same & (i[None, :] < i[:, None]), 0.0, -1e4)
    sT01 = (same & (i[None, :] > i[:, None])).astype(np.float64)
    return np.ascontiguousarray(np.stack([ident, U2, Ublk, mTi, mS, sT01], axis=1).astype(np.float32))


def run_gdn(xT_full, B, S_seq, gv, w_in, conv_w, a_log, dt_bias, o_norm):
    NH = 8 * B // N_CORES
    key = ("gdn", S_seq, NH)
    if key not in _NC_CACHE:
        _NC_CACHE[key] = build_gdn(S_seq, NH)
    nc = _NC_CACHE[key]
    w_in = np.asarray(w_in, np.float32)
    conv_w = np.asarray(conv_w, np.float32)
    common = {"g": gain_layout(gv), "cst": gdn_consts(),
              "onrm": np.ascontiguousarray(np.broadcast_to(np.asarray(o_norm, np.float32).reshape(1, 128), (128, 128)))}
    in_maps = []
    per_b = N_CORES // B
    for c in range(N_CORES):
        b = c // per_b
        heads = [(c % per_b) * NH + i for i in range(NH)]
        wsel = np.stack([np.concatenate([w_in[:, hh * 128:(hh + 1) * 128], w_in[:, 1024 + hh * 128:1024 + (hh + 1) * 128],
                                         w_in[:, 2048 + hh * 128:2048 + (hh + 1) * 128],
                                         w_in[:, 3088 + hh * 128:3088 + (hh + 1) * 128],
                                         w_in[:, 3072 + hh:3072 + hh + 1], w_in[:, 3080 + hh:3080 + hh + 1]], axis=1) for hh in heads], axis=1)
        cwl = np.stack([np.stack([conv_w[:, q * 1024 + hh * 128:q * 1024 + (hh + 1) * 128].T for q in range(3)], axis=1)
                        for hh in heads], axis=1)
        al = np.broadcast_to(np.asarray(a_log, np.float32)[heads].reshape(1, NH), (128, NH))
        db = np.broadcast_to(np.asarray(dt_bias, np.float32)[heads].reshape(1, NH), (128, NH))
        in_maps.append(dict(common, xT=np.ascontiguousarray(xT_full[:, b * S_seq:(b + 1) * S_seq]),
                            w_in=np.ascontiguousarray(wsel), convw=np.ascontiguousarray(cwl),
                            alog=np.ascontiguousarray(al), dtb=np.ascontiguousarray(db)))
    res = run_bass_kernel_spmd(nc, in_maps, core_ids=list(range(N_CORES)))
    out = np.zeros((B, 1024, S_seq), np.float32)
    for c in range(N_CORES):
        b = c // per_b
        for i in range(NH):
            hh = (c % per_b) * NH + i
            out[b, hh * 128:(hh + 1) * 128, :] = res.results[c]["oT"][i].T
    return out


def build_oproj(T):
    nc = bass.Bass("TRN2", target_bir_lowering=False)
    with contextlib.ExitStack() as st:
        C = Ctx(nc, st)
        xT = C.din("xT", [D, T]); oin = C.din("oin", [D, T]); w_o = C.din("w_o", [D, D])
        oT = C.dout("oT", [D, T])
        xv = xT.rearrange("(c p) t -> p c t", p=128)
        iv = oin.rearrange("(c p) t -> p c t", p=128)
        ov = oT.rearrange("(c p) t -> p c t", p=128)
        wv = w_o.rearrange("(c p) n -> p c n", p=128)
        wo = C.sb("wo", [128, 8, D], BF16)
        stage = [C.sb("stage%d" % i, [128, D], F32) for i in range(2)]
        xt = [C.sb("xt%d" % i, [128, 8, 512], F32) for i in range(2)]
        of = [C.sb("of%d" % i, [128, 8, 512], F32) for i in range(2)]
        ob = [C.sb("ob%d" % i, [128, 8, 512], BF16) for i in range(2)]
        ps = [C.ps("ps%d" % i) for i in range(4)]
        S = Sched(nc)
        for c in range(8):
            S.add("sync", lambda e, c=c: e.dma_start(out=stage[c % 2][:], in_=wv[:, c, :]), writes=[("stage", c % 2)], dma=("stage", c % 2))
            S.add(["vector", "gpsimd"][c % 2], lambda e, c=c: e.tensor_copy(out=wo[:, c, :], in_=stage[c % 2][:]),
                  reads=[("stage", c % 2)], writes=[("wo", c)])
        k = 0
        for t in range(T // 512):
            b = t % 2
            ts = slice(t * 512, (t + 1) * 512)
            S.add("sync", lambda e, ts=ts, b=b: e.dma_start(out=xt[b][:], in_=xv[:, :, ts]), writes=[("xt", b, m) for m in range(8)], dma=("xin", b))
            S.add("gpsimd", lambda e, ts=ts, b=b: e.dma_start(out=of[b][:], in_=iv[:, :, ts]), writes=[("of", b)], dma=("oin", b))
            S.add("scalar", lambda e, b=b: e.copy(out=ob[b][:, 0:4, :], in_=of[b][:, 0:4, :]), reads=[("of", b)], writes=[("ob", b, 0)])
            S.add("gpsimd", lambda e, b=b: e.tensor_copy(out=ob[b][:, 4:8, :], in_=of[b][:, 4:8, :]), reads=[("of", b)], writes=[("ob", b, 1)])
            for m in range(8):
                p = k % 4
                k += 1
                for c in range(8):
                    S.add("tensor", lambda e, c=c, m=m, p=p, b=b: e.matmul(ps[p][:], lhsT=wo[:, c, m * 128:(m + 1) * 128], rhs=ob[b][:, c, :],
                                                                           start=(c == 0), stop=(c == 7)),
                          reads=[("wo", c), ("ob", b, c // 4)], writes=[("psp", p)])
                S.add("vector", lambda e, m=m, p=p, b=b: e.tensor_tensor(out=xt[b][:, m, :], in0=ps[p][:], in1=xt[b][:, m, :], op=ALU.add),
                      reads=[("psp", p), ("xt", b, m)], writes=[("xt", b, m)])
            S.add("sync", lambda e, ts=ts, b=b: e.dma_start(out=ov[:, :, ts], in_=xt[b][:]), reads=[("xt", b, m) for m in range(8)], dma=("xout", b))
        S.emit()
    return nc


def run_oproj(xT_full, oT_full, w_o):
    ntok = xT_full.shape[1]
    T = ntok // N_CORES
    key = ("oproj", T)
    if key not in _NC_CACHE:
        _NC_CACHE[key] = build_oproj(T)
    nc = _NC_CACHE[key]
    w_o = np.ascontiguousarray(w_o, np.float32)
    in_maps = [{"xT": np.ascontiguousarray(xT_full[:, c * T:(c + 1) * T]), "oin": np.ascontiguousarray(oT_full[:, c * T:(c + 1) * T]),
                "w_o": w_o} for c in range(N_CORES)]
    res = run_bass_kernel_spmd(nc, in_maps, core_ids=list(range(N_CORES)))
    return np.concatenate([r["oT"] for r in res.results], axis=1)


def kernel(x, ffn1_norm, ffn1_w_in, ffn1_w_out, mix_norm, ffn2_norm, ffn2_w_in, ffn2_w_out,
           nsa_w_in, nsa_w_out, nsa_q_norm, nsa_k_norm, nsa_cmp_pos, nsa_cmp_w1, nsa_cmp_w2,
           diff_w_in, diff_w_out, diff_q_norm, diff_k_norm, diff_lambda, diff_subln,
           gdn_w_in, gdn_w_out, gdn_conv_w, gdn_a_log, gdn_dt_bias, gdn_o_norm,
           swa_w_in, swa_w_out, swa_q_norm, swa_k_norm, swa_sinks):
    import math
    x = np.asarray(x, np.float32)
    B, S_seq, _ = x.shape
    A = lambda v: np.asarray(v, np.float32)
    xT = np.ascontiguousarray(x.reshape(B * S_seq, D).T)
    bs = lambda o: np.ascontiguousarray(o.transpose(1, 0, 2).reshape(D, B * S_seq))
    depth = A(ffn1_norm).shape[0]
    for layer in range(depth):
        kind, j = layer % 4, layer // 4
        xT = run_ffn(xT, A(ffn1_norm)[layer], A(ffn1_w_in)[layer], A(ffn1_w_out)[layer])
        gm = A(mix_norm)[layer]
        if kind == 0:
            o = run_nsa(xT, B, S_seq, gm, A(nsa_w_in)[j], A(nsa_q_norm)[j], A(nsa_k_norm)[j], A(nsa_cmp_pos)[j],
                        A(nsa_cmp_w1)[j], A(nsa_cmp_w2)[j])
            xT = run_oproj(xT, bs(o), A(nsa_w_out)[j])
        elif kind == 1:
            lam_init = 0.8 - 0.6 * math.exp(-0.3 * layer)
            o = run_diff(xT, B, S_seq, gm, A(diff_w_in)[j], A(diff_q_norm)[j], A(diff_k_norm)[j], A(diff_lambda)[j],
                         A(diff_subln)[j], lam_init)
            xT = run_oproj(xT, bs(o), A(diff_w_out)[j])
        elif kind == 2:
            o = run_gdn(xT, B, S_seq, gm, A(gdn_w_in)[j], A(gdn_conv_w)[j], A(gdn_a_log)[j], A(gdn_dt_bias)[j], A(gdn_o_norm)[j])
            xT = run_oproj(xT, bs(o), A(gdn_w_out)[j])
        else:
            xT = run_swa(xT, S_seq, gm, A(swa_w_in)[j], A(swa_w_out)[j], A(swa_q_norm)[j], A(swa_k_norm)[j], A(swa_sinks)[j])
        xT = run_ffn(xT, A(ffn2_norm)[layer], A(ffn2_w_in)[layer], A(ffn2_w_out)[layer])
    return np.ascontiguousarray(xT.T).reshape(B, S_seq, D).astype(np.float32)
```

```python
import contextlib
import math
import numpy as np
import concourse.bass as bass
import concourse.mybir as mybir
from concourse.bass_utils import run_bass_kernel_spmd

F32 = mybir.dt.float32
BF16 = mybir.dt.bfloat16
ALU = mybir.AluOpType
AF = mybir.ActivationFunctionType

N_CORES = 8
D = 1024
DFF = 2816
EPS = 1e-6

COMPUTE = ("tensor", "vector", "scalar", "gpsimd")


PSUM_NAMES = {"ps_s", "ps_n", "psp", "psa", "psb", "psy", "st", "num", "den", "imp"}


def is_psum_key(k):
    n = k[0] if isinstance(k, tuple) else k
    return isinstance(n, str) and (n in PSUM_NAMES or (len(n) == 2 and n[0] == "b" and n[1].isdigit()))


class Sched:
    def __init__(self, nc):
        self.nc = nc
        self.ops = []

    LIMIT = None
    seen = 0

    def add(self, eng, fn, reads=(), writes=(), dma=None):
        Sched.seen += 1
        if Sched.LIMIT is not None and Sched.seen > Sched.LIMIT and not (dma is not None and not writes):
            return -1
        self.ops.append(dict(eng=eng, fn=fn, reads=tuple(reads), writes=tuple(writes), dma=dma))
        return len(self.ops) - 1

    def emit(self, final_wait_engine="sync"):
        nc = self.nc
        ops = self.ops
        last_writer = {}
        readers = {}
        deps = []
        for i, op in enumerate(ops):
            d = set()
            for r in op["reads"]:
                if r in last_writer:
                    d.add(last_writer[r])
                if is_psum_key(r):
                    for j in readers.get(r, ()):
                        if ops[j]["eng"] != op["eng"]:
                            d.add(j)
            for w in op["writes"]:
                if w in last_writer:
                    d.add(last_writer[w])
                for j in readers.get(w, ()):
                    d.add(j)
            d.discard(i)
            deps.append(d)
            for r in op["reads"]:
                readers.setdefault(r, []).append(i)
            for w in op["writes"]:
                last_writer[w] = i
                readers[w] = []

        def pe_pe(i, j):
            return ops[j]["eng"] == "tensor" and ops[i]["eng"] == "tensor" and ops[j]["dma"] is None \
                and ops[i]["dma"] is None

        needed = set()
        for i, d in enumerate(deps):
            for j in d:
                if not pe_pe(i, j):
                    needed.add(j)
        dma_keys = []
        for op in ops:
            if op["dma"] is not None and op["dma"] not in dma_keys:
                dma_keys.append(op["dma"])
        engines_used = []
        for op in ops:
            if op["eng"] not in engines_used:
                engines_used.append(op["eng"])
        if final_wait_engine not in engines_used:
            engines_used.append(final_wait_engine)

        with contextlib.ExitStack() as st:
            esem = {e: st.enter_context(nc.semaphore("s_" + e)) for e in COMPUTE}
            dsem = {k: st.enter_context(nc.semaphore("d_%d" % n)) for n, k in enumerate(dma_keys)}
            cnt = {e: 0 for e in COMPUTE}
            dcnt = {k: 0 for k in dma_keys}
            sig = {}
            for i, op in enumerate(ops):
                if op["dma"] is not None:
                    dcnt[op["dma"]] += 16
                    sig[i] = (dsem[op["dma"]], dcnt[op["dma"]], ("d", op["dma"]))
                elif i in needed:
                    cnt[op["eng"]] += 1
                    sig[i] = (esem[op["eng"]], cnt[op["eng"]], ("e", op["eng"]))
            per_eng = {e: [] for e in engines_used}
            for i, op in enumerate(ops):
                per_eng[op["eng"]].append(i)
            final = [(dsem[k], dcnt[k]) for k in dma_keys]
            block = st.enter_context(nc.Block())

            def make(ename):
                def body(e):
                    seen = {}
                    for i in per_eng[ename]:
                        op = ops[i]
                        waits = {}
                        for j in deps[i]:
                            if j not in sig or pe_pe(i, j):
                                continue
                            s, v, key = sig[j]
                            if seen.get(key, 0) >= v:
                                continue
                            if key not in waits or waits[key][1] < v:
                                waits[key] = (s, v)
                        for key, (s, v) in waits.items():
                            e.wait_ge(s, v)
                            seen[key] = v
                        ins = op["fn"](e)
                        if i in sig:
                            ins.then_inc(sig[i][0], 16 if op["dma"] is not None else 1)
                    if ename == final_wait_engine:
                        for s, v in final:
                            if v > 0:
                                e.wait_ge(s, v)
                return body

            for ename in engines_used:
                getattr(block, ename)(make(ename))
        return len(ops)


def build_ffn(T, TT=512):
    nc = bass.Bass("TRN2", target_bir_lowering=False)
    xT = nc.dram_tensor("xT", [D, T], F32, kind="ExternalInput").ap()
    g = nc.dram_tensor("g", [128, 8], F32, kind="ExternalInput").ap()
    w_in = nc.dram_tensor("w_in", [D, 2 * DFF], F32, kind="ExternalInput").ap()
    w_out = nc.dram_tensor("w_out", [DFF, D], F32, kind="ExternalInput").ap()
    oT = nc.dram_tensor("oT", [D, T], F32, kind="ExternalOutput").ap()
    NJ = DFF // 128
    GW = 704
    ntiles = T // TT
    xv = xT.rearrange("(c p) t -> p c t", p=128)
    ov = oT.rearrange("(c p) t -> p c t", p=128)
    wiv = w_in.rearrange("(c p) f -> p c f", p=128)
    wov = w_out.rearrange("(j p) d -> p j d", p=128)
    with contextlib.ExitStack() as st:
        sb = lambda name, shape, dt: st.enter_context(nc.sbuf_tensor(name, shape, dt))
        ps = lambda name: st.enter_context(nc.psum_tensor(name, [128, TT], F32))
        wi = sb("wi", [128, 8, 2 * DFF], BF16)
        wo = sb("wo", [128, NJ, D], BF16)
        stage = [sb("stage%d" % i, [128, GW], F32) for i in range(2)]
        gs = sb("gs", [128, 8], F32)
        ones = sb("ones", [128, 128], BF16)
        epsb = sb("epsb", [128, 1], F32)
        xts = [sb("xt%d" % i, [128, 8, TT], F32) for i in range(2)]
        h = sb("h", [128, 8, TT], BF16)
        rstd = sb("rstd", [128, TT], F32)
        sa = [sb("sa%d" % i, [128, TT], F32) for i in range(2)]
        act = sb("act", [128, NJ, TT], BF16)
        sq = act
        ps_s = ps("ps_s")
        ps_a = [ps("ps_a%d" % i) for i in range(2)]
        ps_b = [ps("ps_b%d" % i) for i in range(2)]
        ps_y = [ps("ps_y%d" % i) for i in range(2)]

        S = Sched(nc)
        S.add("vector", lambda e: e.memset(ones[:], 1.0 / D), writes=["ones"])
        S.add("vector", lambda e: e.memset(epsb[:], EPS), writes=["epsb"])
        S.add("sync", lambda e: e.dma_start(out=gs[:], in_=g), writes=["gs"], dma="gs")
        XTK = lambda b: [("xt", b, c) for c in range(8)]

        def load_x(t):
            b = t % 2
            S.add("sync", lambda e, t=t, b=b: e.dma_start(out=xts[b][:], in_=xv[:, :, t * TT:(t + 1) * TT]),
                  writes=XTK(b), dma=("xt_in", b))

        load_x(0)
        k = 0
        cast_engs = ["vector", "gpsimd"]
        ngrp = 2 * DFF // GW
        order = []
        for i in range(ngrp // 2):
            order += [i, ngrp // 2 + i]
        for grp in order:
            for c in range(8):
                sg = stage[k % 2]
                S.add("sync", lambda e, sg=sg, c=c, grp=grp: e.dma_start(out=sg[:], in_=wiv[:, c, grp * GW:(grp + 1) * GW]),
                      writes=[("stage", k % 2)], dma=("stage", k % 2))
                S.add(cast_engs[k % 2], lambda e, sg=sg, c=c, grp=grp: e.tensor_scalar(
                    out=wi[:, c, grp * GW:(grp + 1) * GW], in0=sg[:], scalar1=gs[:, c:c + 1], scalar2=None, op0=ALU.mult),
                    reads=[("stage", k % 2), "gs"], writes=[("wi", grp)])
                k += 1
        for j in range(NJ):
            for hf in range(2):
                sg = stage[k % 2]
                S.add("sync", lambda e, sg=sg, j=j, hf=hf: e.dma_start(out=sg[:, 0:512], in_=wov[:, j, hf * 512:(hf + 1) * 512]),
                      writes=[("stage", k % 2)], dma=("stage", k % 2))
                S.add(cast_engs[k % 2], lambda e, sg=sg, j=j, hf=hf: e.tensor_copy(out=wo[:, j, hf * 512:(hf + 1) * 512], in_=sg[:, 0:512]),
                      reads=[("stage", k % 2)], writes=[("wo", j)])
                k += 1
        wgrp = lambda c0: [("wi", gidx) for gidx in range(c0 // GW, (c0 + 127) // GW + 1)]
        for t in range(ntiles):
            b = t % 2
            xt = xts[b]
            ts = slice(t * TT, (t + 1) * TT)
            if t + 1 < ntiles:
                load_x(t + 1)
            S.add("scalar", lambda e, xt=xt: e.activation(out=sq[:, 0:8, :], in_=xt[:], func=AF.Square),
                  reads=XTK(b), writes=[("act", c) for c in range(8)])
            for c in range(8):
                S.add("tensor", lambda e, c=c: e.matmul(ps_s[:], lhsT=ones[:], rhs=sq[:, c, :],
                                                        start=(c == 0), stop=(c == 7)),
                      reads=["ones", ("act", c)], writes=["ps_s"])
            S.add("scalar", lambda e: e.activation(out=rstd[:], in_=ps_s[:], func=AF.Sqrt, bias=epsb[:, 0:1]),
                  reads=["ps_s", "epsb"], writes=["rstd"])
            S.add("vector", lambda e: e.reciprocal(out=rstd[:], in_=rstd[:]), reads=["rstd"], writes=["rstd"])
            for c in range(8):
                S.add("vector" if c % 2 == 0 else "gpsimd",
                      lambda e, c=c, xt=xt: e.tensor_tensor(out=h[:, c, :], in0=xt[:, c, :], in1=rstd[:], op=ALU.mult),
                      reads=[("xt", b, c), "rstd"], writes=[("h", c)])
            for j in range(NJ):
                pa, pb, sj = ps_a[j % 2], ps_b[j % 2], sa[j % 2]
                for c in range(8):
                    S.add("tensor", lambda e, c=c, j=j, pa=pa: e.matmul(
                        pa[:], lhsT=wi[:, c, j * 128:(j + 1) * 128], rhs=h[:, c, :],
                        start=(c == 0), stop=(c == 7)),
                        reads=wgrp(j * 128) + [("h", c)], writes=[("psa", j % 2)])
                for c in range(8):
                    S.add("tensor", lambda e, c=c, j=j, pb=pb: e.matmul(
                        pb[:], lhsT=wi[:, c, DFF + j * 128:DFF + (j + 1) * 128], rhs=h[:, c, :],
                        start=(c == 0), stop=(c == 7)),
                        reads=wgrp(DFF + j * 128) + [("h", c)], writes=[("psb", j % 2)])
                S.add("scalar", lambda e, pa=pa, sj=sj: e.activation(out=sj[:], in_=pa[:], func=AF.Silu),
                      reads=[("psa", j % 2)], writes=[("sa", j % 2)])
                S.add("vector", lambda e, pb=pb, sj=sj, j=j: e.tensor_tensor(
                    out=act[:, j, :], in0=pb[:], in1=sj[:], op=ALU.mult),
                    reads=[("psb", j % 2), ("sa", j % 2)], writes=[("act", j)])
            for m in range(8):
                py = ps_y[m % 2]
                for j in range(NJ):
                    S.add("tensor", lambda e, m=m, j=j, py=py: e.matmul(
                        py[:], lhsT=wo[:, j, m * 128:(m + 1) * 128], rhs=act[:, j, :],
                        start=(j == 0), stop=(j == NJ - 1)),
                        reads=[("wo", j), ("act", j)], writes=[("psy", m % 2)])
                S.add("vector", lambda e, m=m, py=py, xt=xt: e.scalar_tensor_tensor(
                    out=xt[:, m, :], in0=py[:], scalar=0.5, in1=xt[:, m, :], op0=ALU.mult, op1=ALU.add),
                    reads=[("psy", m % 2), ("xt", b, m)], writes=[("xt", b, m)])
            S.add("sync", lambda e, ts=ts, xt=xt: e.dma_start(out=ov[:, :, ts], in_=xt[:]),
                  reads=XTK(b), dma=("xt_out", b))
        S.emit()
    return nc


def gain_layout(gv):
    return np.ascontiguousarray(np.asarray(gv, np.float32).reshape(8, 128).T)


_NC_CACHE = {}


def run_ffn(xT_full, gv, w_in, w_out):
    ntok = xT_full.shape[1]
    T = ntok // N_CORES
    key = ("ffn", T)
    if key not in _NC_CACHE:
        _NC_CACHE[key] = build_ffn(T)
    nc = _NC_CACHE[key]
    gl = gain_layout(gv)
    w_in = np.ascontiguousarray(w_in, dtype=np.float32)
    w_out = np.ascontiguousarray(w_out, dtype=np.float32)
    in_maps = [{"xT": np.ascontiguousarray(xT_full[:, c * T:(c + 1) * T]), "g": gl,
                "w_in": w_in, "w_out": w_out} for c in range(N_CORES)]
    res = run_bass_kernel_spmd(nc, in_maps, core_ids=list(range(N_CORES)))
    return np.concatenate([r["oT"] for r in res.results], axis=1)


import ml_dtypes
NPBF = ml_dtypes.bfloat16
NEGM = -30000.0


def split3(v):
    v = np.asarray(v, np.float64)
    hi = v.astype(np.float32).astype(NPBF)
    r = v - hi.astype(np.float64)
    mid = r.astype(np.float32).astype(NPBF)
    r = r - mid.astype(np.float64)
    lo = r.astype(np.float32).astype(NPBF)
    return hi, mid, lo


def alibi_tabs(slopes, qpos, kpos):
    kpos = np.asarray(kpos, np.int64)
    jb, sl = kpos // 128, kpos % 128
    one = np.ones_like(jb)
    ktab = np.stack([jb, jb, jb, sl, sl, sl, one, one, one]).astype(np.float32).astype(NPBF)
    qt = []
    for s in slopes:
        a = split3(np.full(len(qpos), 128.0 * s))
        b = split3(np.full(len(qpos), float(s)))
        c = split3(-float(s) * np.asarray(qpos, np.float64))
        qt.append(np.stack(list(a) + list(b) + list(c)))
    return ktab, np.stack(qt).astype(NPBF)


class Ctx:
    def __init__(self, nc, st):
        self.nc, self.st = nc, st

    def sb(self, name, shape, dt):
        return self.st.enter_context(self.nc.sbuf_tensor(name, shape, dt))

    def ps(self, name, shape=(128, 512), dt=F32):
        return self.st.enter_context(self.nc.psum_tensor(name, list(shape), dt))

    def din(self, name, shape, dt=F32):
        return self.nc.dram_tensor(name, list(shape), dt, kind="ExternalInput").ap()

    def dout(self, name, shape, dt=F32):
        return self.nc.dram_tensor(name, list(shape), dt, kind="ExternalOutput").ap()


def emit_weight_load(S, wdram_v, wsb, gs, stage, ncols, kbase=0, chunk=1408, tag="w"):
    k = kbase
    engs = ["vector", "gpsimd"]
    for c in range(8):
        for c0 in range(0, ncols, chunk):
            w = min(chunk, ncols - c0)
            sg = stage[k % 2]
            S.add("sync", lambda e, sg=sg, c=c, c0=c0, w=w: e.dma_start(out=sg[:, 0:w], in_=wdram_v[:, c, c0:c0 + w]),
                  writes=[("stage", k % 2)], dma=("stage", k % 2))
            S.add(engs[k % 2], lambda e, sg=sg, c=c, c0=c0, w=w: e.tensor_scalar(
                out=wsb[:, c, c0:c0 + w], in0=sg[:, 0:w], scalar1=gs[:, c:c + 1], scalar2=None, op0=ALU.mult),
                reads=[("stage", k % 2), "gs"], writes=[(tag, c)])
            k += 1
    return k


def emit_rmsnorm(S, xt, h, sqbuf, ps_s, rstd, ones, epsb, n, tagx="xt", tagh="h", sqtag="sq", pskey="ps_s"):
    S.add("scalar", lambda e: e.activation(out=sqbuf[:, 0:8, 0:n], in_=xt[:, :, 0:n], func=AF.Square),
          reads=[(tagx, c) for c in range(8)], writes=[(sqtag, c) for c in range(8)])
    for c in range(8):
        S.add("tensor", lambda e, c=c: e.matmul(ps_s[:, 0:n], lhsT=ones[:], rhs=sqbuf[:, c, 0:n],
                                                start=(c == 0), stop=(c == 7)),
              reads=["ones", (sqtag, c)], writes=[pskey])
    S.add("scalar", lambda e: e.activation(out=rstd[:, 0:n], in_=ps_s[:, 0:n], func=AF.Sqrt, bias=epsb[:, 0:1]),
          reads=[pskey, "epsb"], writes=["rstd"])
    S.add("vector", lambda e: e.reciprocal(out=rstd[:, 0:n], in_=rstd[:, 0:n]), reads=["rstd"], writes=["rstd"])
    for c in range(8):
        S.add("vector" if c % 2 == 0 else "gpsimd",
              lambda e, c=c: e.tensor_tensor(out=h[:, c, 0:n], in0=xt[:, c, 0:n], in1=rstd[:, 0:n], op=ALU.mult),
              reads=[(tagx, c), "rstd"], writes=[(tagh, c)])


def emit_headnorm(S, src_ps, dst, gcol, sqh, ps_n, rs, ones64, epsb, n, rd, wr, P=64):
    S.add("scalar", lambda e: e.activation(out=sqh[0:P, 0:n], in_=src_ps, func=AF.Square),
          reads=rd, writes=["sqh"])
    S.add("tensor", lambda e: e.matmul(ps_n[0:P, 0:n], lhsT=ones64[0:P, 0:P], rhs=sqh[0:P, 0:n], start=True, stop=True),
          reads=["sqh", "ones64"], writes=["ps_n"])
    S.add("scalar", lambda e: e.activation(out=rs[0:P, 0:n], in_=ps_n[0:P, 0:n], func=AF.Sqrt, bias=epsb[0:P, 0:1]),
          reads=["ps_n", "epsb"], writes=["rs"])
    S.add("vector", lambda e: e.reciprocal(out=rs[0:P, 0:n], in_=rs[0:P, 0:n]), reads=["rs"], writes=["rs"])
    S.add("vector", lambda e: e.scalar_tensor_tensor(out=dst, in0=src_ps, scalar=gcol, in1=rs[0:P, 0:n],
                                                     op0=ALU.mult, op1=ALU.mult),
          reads=list(rd) + ["rs", "gcols"], writes=wr)


def build_swa(T):
    TH = T + 128
    nc = bass.Bass("TRN2", target_bir_lowering=False)
    with contextlib.ExitStack() as st:
        C = Ctx(nc, st)
        xT = C.din("xT", [D, TH]); g = C.din("g", [128, 8])
        w_in = C.din("w_in", [D, 1536]); w_out = C.din("w_out", [D, D])
        gq = C.din("gq", [64, 1]); gk = C.din("gk", [64, 1]); sinks = C.din("sinks", [64, 16])
        ktab = C.din("ktab", [9, TH], BF16); qtab = C.din("qtab", [9, 16, T], BF16)
        masks = C.din("masks", [128, 3, 512])
        oT = C.dout("oT", [D, T])
        xv = xT.rearrange("(c p) t -> p c t", p=128)
        ov = oT.rearrange("(c p) t -> p c t", p=128)
        wiv = w_in.rearrange("(c p) f -> p c f", p=128)
        wov = w_out.rearrange("(h d) n -> d h n", d=64)

        w = C.sb("w", [128, 8, 1536], BF16)
        wo = C.sb("wo", [64, 16, D], BF16)
        stage = [C.sb("stage%d" % i, [128, 1024], F32) for i in range(2)]
        gs = C.sb("gs", [128, 8], F32); epsb = C.sb("epsb", [128, 1], F32)
        ones = C.sb("ones", [128, 128], BF16); ones64 = C.sb("ones64", [64, 64], BF16)
        onesd = C.sb("onesd", [128, 64], BF16)
        gqs = C.sb("gqs", [64, 1], F32); gks = C.sb("gks", [64, 1], F32); es = C.sb("es", [64, 16], F32)
        msk = C.sb("msk", [128, 3, 512], F32)
        xt = C.sb("xt", [128, 8, 512], F32); h = C.sb("h", [128, 8, 512], BF16)
        rstd = C.sb("rstd", [128, 512], F32)
        sqh = C.sb("sqh", [64, 512], BF16); rs = C.sb("rs", [64, 512], F32)
        kaug = C.sb("kaug", [73, 4, TH], BF16)
        vtok = C.sb("vtok", [128, TH // 128, 256], BF16)
        qaug = C.sb("qaug", [73, 16, 512], BF16)
        PT = [C.sb("PT%d" % i, [128, 2, 512], BF16) for i in range(2)]
        tmp = [C.sb("tmp%d" % i, [128, 512], F32) for i in range(2)]
        oTt = C.sb("oTt", [64, 16, 512], BF16); rec = C.sb("rec", [64, 512], F32)
        ps_s = C.ps("ps_s"); ps_p = [C.ps("ps_p%d" % i) for i in range(2)]; ps_n = C.ps("ps_n")
        st_ = [C.ps("st%d" % i) for i in range(2)]; num = C.ps("num"); den = C.ps("den")

        S = Sched(nc)
        S.add("vector", lambda e: e.memset(ones[:], 1.0 / D), writes=["ones"])
        S.add("vector", lambda e: e.memset(ones64[:], 1.0 / 64), writes=["ones64"])
        S.add("vector", lambda e: e.memset(onesd[:], 1.0), writes=["onesd"])
        S.add("vector", lambda e: e.memset(epsb[:], EPS), writes=["epsb"])
        S.add("sync", lambda e: e.dma_start(out=gs[:], in_=g), writes=["gs"], dma="c0")
        S.add("sync", lambda e: e.dma_start(out=gqs[:], in_=gq), writes=["gcols"], dma="c1")
        S.add("sync", lambda e: e.dma_start(out=gks[:], in_=gk), writes=["gcols"], dma="c1")
        S.add("sync", lambda e: e.dma_start(out=es[:], in_=sinks), writes=["es"], dma="c2")
        S.add("sync", lambda e: e.dma_start(out=msk[:], in_=masks), writes=["msk"], dma="c3")
        for gg in range(4):
            S.add("sync", lambda e, gg=gg: e.dma_start(out=kaug[64:73, gg, :], in_=ktab), writes=[("kaugc", gg)], dma="c4")
        S.add("vector", lambda e: e.tensor_scalar(out=gqs[:], in0=gqs[:], scalar1=0.125, scalar2=None, op0=ALU.mult),
              reads=["gcols"], writes=["gcols"])
        S.add("scalar", lambda e: e.activation(out=es[:], in_=es[:], func=AF.Exp), reads=["es"], writes=["es"])
        k = emit_weight_load(S, wiv, w, gs, stage, 1536, chunk=768)
        for hh in range(16):
            sg = stage[k % 2]
            S.add("sync", lambda e, sg=sg, hh=hh: e.dma_start(out=sg[0:64, 0:D], in_=wov[:, hh, :]),
                  writes=[("stage", k % 2)], dma=("stage", k % 2))
            S.add(["vector", "gpsimd"][k % 2], lambda e, sg=sg, hh=hh: e.tensor_copy(out=wo[:, hh, :], in_=sg[0:64, 0:D]),
                  reads=[("stage", k % 2)], writes=[("wo", hh)])
            k += 1
        XT = [("xt", c) for c in range(8)]
        H = [("h", c) for c in range(8)]
        W = [("w", c) for c in range(8)]
        pcount = [0]

        def proj_fm(col0, M, n):
            p = pcount[0] % 2
            pcount[0] += 1
            for c in range(8):
                S.add("tensor", lambda e, c=c, p=p: e.matmul(ps_p[p][0:M, 0:n], lhsT=w[:, c, col0:col0 + M], rhs=h[:, c, 0:n],
                                                             start=(c == 0), stop=(c == 7)),
                      reads=[("w", c), ("h", c)], writes=[("psp", p)])
            return p

        def proj_tile(tok0, n, with_q):
            emit_rmsnorm(S, xt, h, h, ps_s, rstd, ones, epsb, n, sqtag="h")
            for gg in range(4):
                p = proj_fm(1024 + 64 * gg, 64, n)
                emit_headnorm(S, ps_p[p][0:64, 0:n], kaug[0:64, gg, tok0:tok0 + n], gks[:, 0:1], sqh, ps_n, rs, ones64, epsb,
                              n, rd=[("psp", p)], wr=[("kaug", gg, tok0 // 128 + b) for b in range(n // 128)])
            for b in range(n // 128):
                p = pcount[0] % 2
                pcount[0] += 1
                for c in range(8):
                    S.add("tensor", lambda e, c=c, p=p, b=b: e.matmul(ps_p[p][:, 0:256], lhsT=h[:, c, b * 128:(b + 1) * 128],
                                                                      rhs=w[:, c, 1280:1536], start=(c == 0), stop=(c == 7)),
                          reads=[("w", c), ("h", c)], writes=[("psp", p)])
                S.add("scalar", lambda e, p=p, b=b: e.copy(out=vtok[:, tok0 // 128 + b, :], in_=ps_p[p][:, 0:256]),
                      reads=[("psp", p)], writes=[("vtok", tok0 // 128 + b)])
            if with_q:
                for hq in range(16):
                    p = proj_fm(64 * hq, 64, n)
                    emit_headnorm(S, ps_p[p][0:64, 0:n], qaug[0:64, hq, 0:n], gqs[:, 0:1], sqh, ps_n, rs, ones64, epsb,
                                  n, rd=[("psp", p)], wr=[("qaug", hq)])

        S.add("sync", lambda e: e.dma_start(out=xt[:, :, 0:128], in_=xv[:, :, 0:128]), writes=XT, dma="xt_in")
        proj_tile(0, 128, False)
        acount = 0
        for tt in range(T // 512):
            tok0 = 128 + 512 * tt
            S.add("sync", lambda e, tok0=tok0: e.dma_start(out=xt[:], in_=xv[:, :, tok0:tok0 + 512]), writes=XT, dma="xt_in")
            S.add("gpsimd", lambda e, tt=tt: e.dma_start(out=qaug[64:73, :, :], in_=qtab[:, :, tt * 512:(tt + 1) * 512]),
                  writes=["qaugc"], dma="qc")
            proj_tile(tok0, 512, True)
            for qb in range(4):
                cur = tok0 // 128 + qb
                prev = cur - 1
                for gg in range(4):
                    pt = PT[acount % 2]
                    ptk = ("PT", acount % 2)
                    rhs_q = qaug[:, 4 * gg:4 * gg + 4, qb * 128:(qb + 1) * 128]
                    qreads = [("qaug", 4 * gg + i) for i in range(4)] + ["qaugc"]
                    for i, kb in enumerate((prev, cur)):
                        S.add("tensor", lambda e, i=i, kb=kb, gg=gg, rhs_q=rhs_q: e.matmul(
                            st_[i][:].rearrange("p (a b) -> p a b", a=4), lhsT=kaug[:, gg, kb * 128:(kb + 1) * 128], rhs=rhs_q,
                            start=True, stop=True),
                            reads=qreads + [("kaug", gg, kb), ("kaugc", gg)], writes=[("st", i)])
                        mi = (0 if (tt == 0 and qb == 0) else 1) if i == 0 else 2
                        S.add("vector", lambda e, i=i, mi=mi: e.tensor_tensor(out=tmp[i][:], in0=st_[i][:], in1=msk[:, mi, :],
                                                                              op=ALU.add),
                              reads=[("st", i), "msk"], writes=[("tmp", i)])
                        S.add("scalar", lambda e, i=i, pt=pt: e.activation(out=pt[:, i, :], in_=tmp[i][:], func=AF.Exp),
                              reads=[("tmp", i)], writes=[ptk + (i,)])
                    for i, kb in enumerate((prev, cur)):
                        S.add("tensor", lambda e, i=i, kb=kb, gg=gg, pt=pt: e.matmul(
                            num[0:64, :], lhsT=vtok[:, kb, 64 * gg:64 * gg + 64], rhs=pt[:, i, :], start=(i == 0), stop=(i == 1)),
                            reads=[("vtok", kb), ptk + (i,)], writes=["num"])
                    for i in range(2):
                        S.add("tensor", lambda e, i=i, pt=pt: e.matmul(
                            den[0:64, :], lhsT=onesd[:], rhs=pt[:, i, :], start=(i == 0), stop=(i == 1)),
                            reads=["onesd", ptk + (i,)], writes=["den"])
                    S.add("vector", lambda e, gg=gg: e.tensor_tensor(
                        out=rec[:].rearrange("p (a b) -> p a b", a=4), in0=den[0:64, :].rearrange("p (a b) -> p a b", a=4),
                        in1=es[:, 4 * gg:4 * gg + 4].unsqueeze(2).to_broadcast([64, 4, 128]), op=ALU.add),
                        reads=["den", "es"], writes=["rec"])
                    S.add("vector", lambda e: e.reciprocal(out=rec[:], in_=rec[:]), reads=["rec"], writes=["rec"])
                    S.add("vector", lambda e, gg=gg, qb=qb: e.tensor_tensor(
                        out=oTt[:, 4 * gg:4 * gg + 4, qb * 128:(qb + 1) * 128], in0=num[0:64, :].rearrange("p (a b) -> p a b", a=4),
                        in1=rec[:].rearrange("p (a b) -> p a b", a=4), op=ALU.mult),
                        reads=["num", "rec"], writes=[("oTt", 4 * gg + i) for i in range(4)])
                    acount += 1
            for m in range(8):
                p = pcount[0] % 2
                pcount[0] += 1
                for hh in range(16):
                    S.add("tensor", lambda e, m=m, hh=hh, p=p: e.matmul(ps_p[p][:], lhsT=wo[:, hh, m * 128:(m + 1) * 128],
                                                                        rhs=oTt[:, hh, :], start=(hh == 0), stop=(hh == 15)),
                          reads=[("wo", hh), ("oTt", hh)], writes=[("psp", p)])
                S.add("vector", lambda e, m=m, p=p: e.tensor_tensor(out=xt[:, m, :], in0=ps_p[p][:], in1=xt[:, m, :], op=ALU.add),
                      reads=[("psp", p), ("xt", m)], writes=[("xt", m)])
            S.add("sync", lambda e, tt=tt: e.dma_start(out=ov[:, :, tt * 512:(tt + 1) * 512], in_=xt[:]), reads=XT, dma="xt_out")
        S.emit()
    return nc


def alibi_slopes(n):
    return 2.0 ** (-8.0 * np.arange(1, n + 1, dtype=np.float64) / n)


def swa_consts(T):
    TH = T + 128
    ktab, qtab = alibi_tabs(alibi_slopes(16), np.arange(T) + 128, np.arange(TH))
    qtab = np.ascontiguousarray(qtab.transpose(1, 0, 2))
    sl = np.arange(128)[:, None]
    tl = np.arange(128)[None, :]
    mprev = np.where(sl > tl, 0.0, NEGM).astype(np.float32)
    mcur = np.where(sl <= tl, 0.0, NEGM).astype(np.float32)
    t4 = lambda m: np.tile(m, (1, 4))
    m_mid = np.stack([t4(mprev), t4(mprev), t4(mcur)], axis=1)
    m_first = np.stack([np.full((128, 512), NEGM, np.float32), t4(mprev), t4(mcur)], axis=1)
    return ktab, qtab, np.ascontiguousarray(m_first), np.ascontiguousarray(m_mid)


def run_swa(xT_full, S_seq, gv, w_in, w_out, q_norm, k_norm, sinks):
    ntok = xT_full.shape[1]
    T = ntok // N_CORES
    key = ("swa", T)
    if key not in _NC_CACHE:
        _NC_CACHE[key] = build_swa(T)
    nc = _NC_CACHE[key]
    ktab, qtab, m_first, m_mid = swa_consts(T)
    common = {"g": gain_layout(gv), "w_in": np.ascontiguousarray(w_in, np.float32),
              "w_out": np.ascontiguousarray(w_out, np.float32),
              "gq": np.ascontiguousarray(np.asarray(q_norm, np.float32).reshape(64, 1)),
              "gk": np.ascontiguousarray(np.asarray(k_norm, np.float32).reshape(64, 1)),
              "sinks": np.ascontiguousarray(np.broadcast_to(np.asarray(sinks, np.float32).reshape(1, 16), (64, 16))),
              "ktab": ktab, "qtab": qtab}
    in_maps = []
    for c in range(N_CORES):
        t0 = c * T
        first = (t0 % S_seq) == 0
        xh = np.zeros((D, T + 128), np.float32)
        xh[:, 128:] = xT_full[:, t0:t0 + T]
        if not first:
            xh[:, :128] = xT_full[:, t0 - 128:t0]
        in_maps.append(dict(common, xT=xh, masks=m_first if first else m_mid))
    res = run_bass_kernel_spmd(nc, in_maps, core_ids=list(range(N_CORES)))
    return np.concatenate([r["oT"] for r in res.results], axis=1)


def build_diff(S_seq, NH, lam_init, capdist=None):
    nc = bass.Bass("TRN2", target_bir_lowering=False)
    NT = S_seq // 512
    with contextlib.ExitStack() as st:
        C = Ctx(nc, st)
        xT = C.din("xT", [D, S_seq]); g = C.din("g", [128, 8])
        w_in = C.din("w_in", [D, NH, 384])
        gq = C.din("gq", [64, 1]); gk = C.din("gk", [64, 1]); gsub = C.din("gsub", [128, 1])
        lam = C.din("lam", [128, 4, 64])
        ktab = C.din("ktab", [9, S_seq], BF16); qtab = C.din("qtab", [NH, 9, S_seq], BF16)
        masks = C.din("masks", [128, 4, 512])
        oT = C.dout("oT", [NH, 128, S_seq])
        xv = xT.rearrange("(c p) t -> p c t", p=128)
        wiv = w_in.rearrange("(c p) h f -> p c h f", p=128)

        w = C.sb("w", [128, 8, 384], BF16)
        stage = [C.sb("stage%d" % i, [128, 384], F32) for i in range(2)]
        gs = C.sb("gs", [128, 8], F32); epsb = C.sb("epsb", [128, 1], F32)
        ones = C.sb("ones", [128, 128], BF16); ones64 = C.sb("ones64", [64, 64], BF16)
        ones128 = C.sb("ones128", [128, 128], BF16); onesd = C.sb("onesd", [128, 128], BF16)
        gqs = C.sb("gqs", [64, 1], F32); gks = C.sb("gks", [64, 1], F32); gsubs = C.sb("gsubs", [128, 1], F32)
        lams = C.sb("lams", [128, 4, 64], F32); lp = C.sb("lp", [128, 2, 64], F32); l2 = C.sb("l2", [128, 2], F32)
        nlam = C.sb("nlam", [128, 1], F32)
        msk = C.sb("msk", [128, 4, 512], F32)
        xt = C.sb("xt", [128, 8, 512], F32); h = C.sb("h", [128, 8, 512], BF16)
        rstd = C.sb("rstd", [128, 512], F32)
        sqh = C.sb("sqh", [128, 512], BF16); rs = C.sb("rs", [128, 512], F32)
        kaug = C.sb("kaug", [73, 2, S_seq], BF16)
        vtok = C.sb("vtok", [128, S_seq // 128, 128], BF16)
        qaug = C.sb("qaug", [73, 2, 512], BF16)
        NPT = 4
        PT = [C.sb("PT%d" % i, [128, 512], BF16) for i in range(NPT)]
        tmp = [C.sb("tmp%d" % i, [128, 512], F32) for i in range(2)]
        oc = [C.sb("oc%d" % i, [128, 512], F32) for i in range(2)]
        rec = C.sb("rec", [128, 512], F32); ot = C.sb("ot", [128, 512], F32)
        ps_s = C.ps("ps_s"); ps_p = [C.ps("ps_p%d" % i) for i in range(2)]
        st_ = [C.ps("st%d" % i) for i in range(3)]; num = C.ps("num"); den = C.ps("den")
        ps_n = ps_s

        S = Sched(nc)
        S.add("vector", lambda e: e.memset(ones[:], 1.0 / D), writes=["ones"])
        S.add("vector", lambda e: e.memset(ones64[:], 1.0 / 64), writes=["ones64"])
        S.add("vector", lambda e: e.memset(ones128[:], 1.0 / 128), writes=["ones128"])
        S.add("vector", lambda e: e.memset(onesd[:], 1.0), writes=["onesd"])
        S.add("vector", lambda e: e.memset(epsb[:], EPS), writes=["epsb"])
        S.add("sync", lambda e: e.dma_start(out=gs[:], in_=g), writes=["gs"], dma="c0")
        S.add("sync", lambda e: e.dma_start(out=gqs[:], in_=gq), writes=["gcols"], dma="c1")
        S.add("sync", lambda e: e.dma_start(out=gks[:], in_=gk), writes=["gcols"], dma="c1")
        S.add("sync", lambda e: e.dma_start(out=gsubs[:], in_=gsub), writes=["gcols"], dma="c1")
        S.add("sync", lambda e: e.dma_start(out=lams[:], in_=lam), writes=["lams"], dma="c2")
        S.add("sync", lambda e: e.dma_start(out=msk[:], in_=masks), writes=["msk"], dma="c3")
        for cc in range(2):
            S.add("sync", lambda e, cc=cc: e.dma_start(out=kaug[64:73, cc, :], in_=ktab), writes=[("kaugc", cc)], dma="c4")
        S.add("vector", lambda e: e.tensor_scalar(out=gqs[:], in0=gqs[:], scalar1=0.125, scalar2=None, op0=ALU.mult),
              reads=["gcols"], writes=["gcols"])
        S.add("vector", lambda e: e.tensor_scalar(out=gsubs[:], in0=gsubs[:], scalar1=1.0 - lam_init, scalar2=None, op0=ALU.mult),
              reads=["gcols"], writes=["gcols"])
        S.add("vector", lambda e: e.tensor_tensor(out=lp[:], in0=lams[:, 0:4:2, :], in1=lams[:, 1:4:2, :], op=ALU.mult),
              reads=["lams"], writes=["lp"])
        S.add("vector", lambda e: e.reduce_sum(out=l2[:], in_=lp[:], axis=mybir.AxisListType.X), reads=["lp"], writes=["l2"])
        S.add("scalar", lambda e: e.activation(out=l2[:], in_=l2[:], func=AF.Exp), reads=["l2"], writes=["l2"])
        S.add("vector", lambda e: e.tensor_tensor(out=nlam[:], in0=l2[:, 1:2], in1=l2[:, 0:1], op=ALU.subtract),
              reads=["l2"], writes=["nlam"])
        S.add("vector", lambda e: e.tensor_scalar(out=nlam[:], in0=nlam[:], scalar1=-lam_init, scalar2=None, op0=ALU.add),
              reads=["nlam"], writes=["nlam"])
        XT = [("xt", c) for c in range(8)]
        pcount = [0]
        kst = 0
        stc = 0
        ptc = 0
        for hl in range(NH):
            for c in range(8):
                sg = stage[kst % 2]
                S.add("sync", lambda e, sg=sg, c=c, hl=hl: e.dma_start(out=sg[:], in_=wiv[:, c, hl, :]),
                      writes=[("stage", kst % 2)], dma=("stage", kst % 2))
                S.add(["vector", "gpsimd"][kst % 2], lambda e, sg=sg, c=c: e.tensor_scalar(
                    out=w[:, c, :], in0=sg[:], scalar1=gs[:, c:c + 1], scalar2=None, op0=ALU.mult),
                    reads=[("stage", kst % 2), "gs"], writes=[("w", c)])
                kst += 1

            def proj_fm(col0, M, n=512):
                p = pcount[0] % 2
                pcount[0] += 1
                for c in range(8):
                    S.add("tensor", lambda e, c=c, p=p: e.matmul(ps_p[p][0:M, 0:n], lhsT=w[:, c, col0:col0 + M], rhs=h[:, c, 0:n],
                                                                 start=(c == 0), stop=(c == 7)),
                          reads=[("w", c), ("h", c)], writes=[("psp", p)])
                return p

            for tt in range(NT):
                tok0 = 512 * tt
                S.add("sync", lambda e, tok0=tok0: e.dma_start(out=xt[:], in_=xv[:, :, tok0:tok0 + 512]), writes=XT, dma="xt_in")
                S.add("gpsimd", lambda e, tok0=tok0, hl=hl: e.dma_start(
                    out=qaug[64:73, :, :], in_=qtab[hl, :, tok0:tok0 + 512].unsqueeze(1).to_broadcast([9, 2, 512])),
                    writes=["qaugc"], dma="qc")
                emit_rmsnorm(S, xt, h, h, ps_s, rstd, ones, epsb, 512, sqtag="h")
                for cc in range(2):
                    p = proj_fm(128 + 64 * cc, 64)
                    emit_headnorm(S, ps_p[p][0:64, :], kaug[0:64, cc, tok0:tok0 + 512], gks[:, 0:1], sqh, ps_n, rs, ones64, epsb,
                                  512, rd=[("psp", p)], wr=[("kaug", cc, 4 * tt + b) for b in range(4)])
                for b in range(4):
                    p = pcount[0] % 2
                    pcount[0] += 1
                    for c in range(8):
                        S.add("tensor", lambda e, c=c, p=p, b=b: e.matmul(ps_p[p][:, 0:128], lhsT=h[:, c, b * 128:(b + 1) * 128],
                                                                          rhs=w[:, c, 256:384], start=(c == 0), stop=(c == 7)),
                              reads=[("w", c), ("h", c)], writes=[("psp", p)])
                    S.add("scalar", lambda e, p=p, b=b, tt=tt: e.copy(out=vtok[:, 4 * tt + b, :], in_=ps_p[p][:, 0:128]),
                          reads=[("psp", p)], writes=[("vtok", 4 * tt + b)])
                for cc in range(2):
                    p = proj_fm(64 * cc, 64)
                    emit_headnorm(S, ps_p[p][0:64, :], qaug[0:64, cc, :], gqs[:, 0:1], sqh, ps_n, rs, ones64, epsb,
                                  512, rd=[("psp", p)], wr=[("qaug", cc)])
                for cc in range(2):
                    nj = 4 * tt + 4
                    LA = 2
                    slots = {}
                    jmin = 0
                    if capdist is not None and capdist[hl] is not None and 512 * tt - 127 - capdist[hl] >= 0:
                        jmin = (512 * tt - 127 - capdist[hl]) // 128 + 1

                    def emit_front(j, cc=cc, tt=tt, slots=slots):
                        nonlocal stc, ptc
                        sti = stc % 3
                        stc += 1
                        pti = ptc % NPT
                        ptc += 1
                        slots[j] = pti
                        S.add("tensor", lambda e, j=j, cc=cc, sti=sti: e.matmul(
                            st_[sti][:], lhsT=kaug[:, cc, j * 128:(j + 1) * 128], rhs=qaug[:, cc, :], start=True, stop=True),
                            reads=[("qaug", cc), "qaugc", ("kaug", cc, j), ("kaugc", cc)], writes=[("st", sti)])
                        if j >= 4 * tt:
                            bvar = j - 4 * tt
                            ti = j % 2
                            S.add("vector", lambda e, sti=sti, bvar=bvar, ti=ti: e.tensor_tensor(
                                out=tmp[ti][:], in0=st_[sti][:], in1=msk[:, bvar, :], op=ALU.add),
                                reads=[("st", sti), "msk"], writes=[("tmp", ti)])
                            S.add("scalar", lambda e, ti=ti, pti=pti: e.activation(out=PT[pti][:], in_=tmp[ti][:], func=AF.Exp),
                                  reads=[("tmp", ti)], writes=[("PT", pti)])
                        else:
                            S.add("scalar", lambda e, sti=sti, pti=pti: e.activation(out=PT[pti][:], in_=st_[sti][:], func=AF.Exp),
                                  reads=[("st", sti)], writes=[("PT", pti)])

                    def emit_back(j, nj=nj, slots=slots, jmin=jmin):
                        pti = slots[j]
                        S.add("tensor", lambda e, j=j, pti=pti, nj=nj, jmin=jmin: e.matmul(
                            num[:], lhsT=vtok[:, j, :], rhs=PT[pti][:], start=(j == jmin), stop=(j == nj - 1)),
                            reads=[("vtok", j), ("PT", pti)], writes=["num"])
                        S.add("tensor", lambda e, j=j, pti=pti, nj=nj, jmin=jmin: e.matmul(
                            den[:], lhsT=onesd[:], rhs=PT[pti][:], start=(j == jmin), stop=(j == nj - 1)),
                            reads=["onesd", ("PT", pti)], writes=["den"])

                    for j in range(jmin, nj):
                        emit_front(j)
                        if j - jmin >= LA:
                            emit_back(j - LA)
                    for j in range(max(jmin, nj - LA), nj):
                        emit_back(j)
                    S.add("vector", lambda e: e.reciprocal(out=rec[:], in_=den[:]), reads=["den"], writes=["rec"])
                    S.add("vector", lambda e, cc=cc: e.tensor_tensor(out=oc[cc][:], in0=num[:], in1=rec[:], op=ALU.mult),
                          reads=["num", "rec"], writes=[("oc", cc)])
                S.add("vector", lambda e: e.scalar_tensor_tensor(out=ot[:], in0=oc[1][:], scalar=nlam[:, 0:1], in1=oc[0][:],
                                                                 op0=ALU.mult, op1=ALU.add),
                      reads=[("oc", 0), ("oc", 1), "nlam"], writes=["ot"])
                emit_headnorm(S, ot[:], ot[:], gsubs[:, 0:1], sqh, ps_n, rs, ones128, epsb, 512, rd=["ot"], wr=["ot"], P=128)
                S.add("sync", lambda e, tok0=tok0, hl=hl: e.dma_start(out=oT[hl, :, tok0:tok0 + 512], in_=ot[:]),
                      reads=["ot"], dma="o_out")
        S.emit()
    return nc


def diff_masks():
    sl = np.arange(128)[:, None]
    tl = np.arange(128)[None, :]
    caus = np.where(sl <= tl, 0.0, NEGM).astype(np.float32)
    m = np.zeros((128, 4, 4, 128), np.float32)
    for b in range(4):
        for a in range(4):
            m[:, b, a, :] = 0.0 if a > b else (caus if a == b else NEGM)
    return np.ascontiguousarray(m.reshape(128, 4, 512))


def run_diff(xT_full, B, S_seq, gv, w_in, q_norm, k_norm, lam, subln, lam_init):
    NH = 8 * B // N_CORES
    slopes = alibi_slopes(8)
    per_b = N_CORES // B
    capdist = []
    for i in range(NH):
        smin = min(slopes[i * per_b + p] for p in range(per_b))
        cd = int(math.ceil((104.0 + 64.0) / smin))
        capdist.append(cd if cd < S_seq else None)
    key = ("diff", S_seq, NH, lam_init)
    if key not in _NC_CACHE:
        _NC_CACHE[key] = build_diff(S_seq, NH, lam_init, capdist)
    nc = _NC_CACHE[key]
    pos = np.arange(S_seq)
    ktab, qtab_all = alibi_tabs(slopes, pos, pos)
    w_in = np.asarray(w_in, np.float32)
    common = {"g": gain_layout(gv),
              "gq": np.ascontiguousarray(np.asarray(q_norm, np.float32).reshape(64, 1)),
              "gk": np.ascontiguousarray(np.asarray(k_norm, np.float32).reshape(64, 1)),
              "gsub": np.ascontiguousarray(np.asarray(subln, np.float32).reshape(128, 1)),
              "lam": np.ascontiguousarray(np.broadcast_to(np.asarray(lam, np.float32).reshape(1, 4, 64), (128, 4, 64))),
              "ktab": ktab, "masks": diff_masks()}
    in_maps = []
    for c in range(N_CORES):
        b = c // per_b
        heads = [i * per_b + (c % per_b) for i in range(NH)]
        wsel = np.stack([np.concatenate([w_in[:, hh * 128:(hh + 1) * 128], w_in[:, 1024 + hh * 128:1024 + (hh + 1) * 128],
                                         w_in[:, 2048 + hh * 128:2048 + (hh + 1) * 128]], axis=1) for hh in heads], axis=1)
        in_maps.append(dict(common, xT=np.ascontiguousarray(xT_full[:, b * S_seq:(b + 1) * S_seq]),
                            w_in=np.ascontiguousarray(wsel), qtab=np.ascontiguousarray(qtab_all[heads])))
    res = run_bass_kernel_spmd(nc, in_maps, core_ids=list(range(N_CORES)))
    out = np.zeros((B, 1024, S_seq), np.float32)
    for c in range(N_CORES):
        b = c // per_b
        for i in range(NH):
            hh = i * per_b + (c % per_b)
            out[b, hh * 128:(hh + 1) * 128, :] = res.results[c]["oT"][i]
    return out


def build_nsa(S_seq, debug=None):
    nc = bass.Bass("TRN2", target_bir_lowering=False)
    NT = S_seq // 512
    NBLK = S_seq // 128
    NCMP = S_seq // 16 - 1
    NCT = (NCMP + 127) // 128
    NSEL = S_seq // 64
    with contextlib.ExitStack() as st:
        C = Ctx(nc, st)
        xT = C.din("xT", [D, S_seq]); g = C.din("g", [128, 8])
        w_in = C.din("w_in", [D, 652])
        gq = C.din("gq", [64, 1]); gk3 = C.din("gk3", [64, 3])
        cpos = C.din("cpos", [128, 32])
        cw1 = C.din("cw1", [2, 64, 32, 256])
        cw2 = C.din("cw2", [128, 2, 2, 64])
        ktab = C.din("ktab", [9, S_seq], BF16); qtab = C.din("qtab", [9, 4, S_seq], BF16)
        kctab = C.din("kctab", [9, NCT * 128], BF16)
        cmask = C.din("cmask", [128, 17, 128]); wmask = C.din("wmask", [128, 2, 128])
        tmpl = C.din("tmpl", [128, 2, 2 * NSEL])
        ovl = C.din("ovl", [128, NCT, NSEL], BF16)
        i4 = C.din("i4", [128, 512], BF16)
        selrow = C.din("selrow", [12, 12, 64])
        oT = C.dout("oT", [256, S_seq])
        xv = xT.rearrange("(c p) t -> p c t", p=128)
        wiv = w_in.rearrange("(c p) f -> p c f", p=128)

        w = C.sb("w", [128, 8, 652], BF16)
        stage = [C.sb("stage%d" % i, [128, 1024], F32) for i in range(2)]
        gs = C.sb("gs", [128, 8], F32); epsb = C.sb("epsb", [128, 1], F32)
        ones = C.sb("ones", [128, 128], BF16); ones64 = C.sb("ones64", [64, 64], BF16)
        onesd = C.sb("onesd", [128, 128], BF16)
        gqs = C.sb("gqs", [64, 1], F32); gk3s = C.sb("gk3s", [64, 3], F32)
        w1sb = C.sb("w1sb", [128, 32, 256], BF16)
        w2sb = C.sb("w2sb", [128, 2, 2, 64], BF16)
        posT = C.sb("posT", [128, 32], BF16)
        cposb = C.sb("cposb", [128, 2, 2], F32)
        big = C.sb("big", [128, S_seq], BF16)
        kcmp = C.sb("kcmp", [73, NCT * 128], BF16)
        vcmp = C.sb("vcmp", [128, NCT, 64], BF16)
        ov = C.sb("ov", [128, NCT, NSEL], BF16)
        vs_tok = C.sb("vs_tok", [128, NBLK, 64], BF16)
        kwr = C.sb("kwr", [73, 8 * 128], BF16)
        vwr = C.sb("vwr", [128, 8, 64], BF16)
        qaug = C.sb("qaug", [73, 4, 512], BF16)
        xt = C.sb("xt", [128, 8, 512], F32); h = C.sb("h", [128, 8, 512], BF16)
        rstd = C.sb("rstd", [128, 512], F32)
        sqh = C.sb("sqh", [128, 512], BF16); rs = C.sb("rs", [128, 512], F32)
        gel = [C.sb("gel%d" % i, [128, 512], F32) for i in range(3)]
        gelu = C.sb("gelu", [128, 2, 512], BF16)
        PTc = C.sb("PTc", [128, NCT, 512], BF16)
        NPT = 4
        PT = [C.sb("PT%d" % i, [128, 512], BF16) for i in range(NPT)]
        tmp = [C.sb("tmp%d" % i, [128, 512], F32) for i in range(2)]
        cm = C.sb("cm", [128, 17, 128], F32); wm = C.sb("wm", [128, 2, 128], F32)
        tm = C.sb("tm", [128, 2, 2 * NSEL], F32)
        i4s = C.sb("i4s", [128, 512], BF16)
        srow = C.sb("srow", [12, 12, 64], F32)
        rden = C.sb("rden", [128, 512], F32)
        imp2 = C.sb("imp2", [128, NSEL], F32); imp3 = C.sb("imp3", [128, NSEL], F32)
        m8 = C.sb("m8", [128, 8], F32)
        negsel = C.sb("negsel", [128, NSEL], BF16)
        negx = [C.sb("negx%d" % i, [128, 16, 64], BF16) for i in range(2)]
        gT = C.sb("gT", [12, 512], F32)
        obr = [C.sb("obr%d" % i, [64, 512], F32) for i in range(3)]
        oTt = C.sb("oTt", [64, 4, 512], F32)
        ps_s = C.ps("ps_s"); ps_p = [C.ps("ps_p%d" % i) for i in range(2)]
        st_ = [C.ps("st%d" % i) for i in range(3)]; num = C.ps("num"); den = C.ps("den")
        imp = ps_s
        ps_n = ps_s

        S = Sched(nc)
        S.add("vector", lambda e: e.memset(ones[:], 1.0 / D), writes=["ones"])
        S.add("vector", lambda e: e.memset(ones64[:], 1.0 / 64), writes=["ones64"])
        S.add("vector", lambda e: e.memset(onesd[:], 1.0), writes=["onesd"])
        S.add("vector", lambda e: e.memset(epsb[:], EPS), writes=["epsb"])
        S.add("vector", lambda e: e.memset(kcmp[0:64, :], 0.0), writes=["kcmp"])
        S.add("vector", lambda e: e.memset(vcmp[:], 0.0), writes=["vcmp"])
        ld = [(gs, g, "gs"), (gqs, gq, "gcols"), (gk3s, gk3, "gcols"), (cm, cmask, "cm"), (wm, wmask, "wm"), (tm, tmpl, "tm"),
              (i4s, i4, "i4s"), (srow, selrow, "srow"), (ov, ovl, "ov")]
        for n_, (dst, src, key) in enumerate(ld):
            S.add("sync", lambda e, dst=dst, src=src: e.dma_start(out=dst[:], in_=src), writes=[key], dma="c%d" % n_)
        S.add("sync", lambda e: e.dma_start(out=kcmp[64:73, :], in_=kctab), writes=["kcmpc"], dma="ck")
        S.add("vector", lambda e: e.tensor_scalar(out=gqs[:], in0=gqs[:], scalar1=0.125, scalar2=None, op0=ALU.mult),
              reads=["gcols"], writes=["gcols"])
        kst = emit_weight_load(S, wiv, w, gs, stage, 652, chunk=652)
        for kv in range(2):
            for pg in range(8):
                sg = stage[kst % 2]
                S.add("sync", lambda e, sg=sg, kv=kv, pg=pg: e.dma_start(
                    out=sg[64 * kv:64 * kv + 64, :].rearrange("d (p f) -> d p f", p=4), in_=cw1[kv, :, 4 * pg:4 * pg + 4, :]),
                    writes=[("stage", kst % 2)], dma=("stage", kst % 2))
                S.add(["vector", "gpsimd"][kst % 2], lambda e, sg=sg, kv=kv, pg=pg: e.tensor_copy(
                    out=w1sb[64 * kv:64 * kv + 64, 4 * pg:4 * pg + 4, :],
                    in_=sg[64 * kv:64 * kv + 64, :].rearrange("d (p f) -> d p f", p=4)),
                    reads=[("stage", kst % 2)], writes=["w1sb"])
                kst += 1
        sg = stage[kst % 2]
        S.add("sync", lambda e, sg=sg: e.dma_start(out=sg[:, 0:256].rearrange("p (a b d) -> p a b d", a=2, b=2), in_=cw2),
              writes=[("stage", kst % 2)], dma=("stage", kst % 2))
        S.add("vector", lambda e, sg=sg: e.tensor_copy(out=w2sb[:], in_=sg[:, 0:256].rearrange("p (a b d) -> p a b d", a=2, b=2)),
              reads=[("stage", kst % 2)], writes=["w2sb"])
        kst += 1
        sg = stage[kst % 2]
        S.add("sync", lambda e, sg=sg: e.dma_start(out=sg[:, 0:32], in_=cpos), writes=[("stage", kst % 2)], dma=("stage", kst % 2))
        S.add("vector", lambda e, sg=sg: e.tensor_copy(out=posT[:], in_=sg[:, 0:32]), reads=[("stage", kst % 2)], writes=["posT"])
        kst += 1
        for kv in range(2):
            for hf in range(2):
                for p in range(32):
                    S.add("tensor", lambda e, kv=kv, hf=hf, p=p: e.matmul(
                        ps_p[0][:, 0:1], lhsT=w1sb[64 * kv:64 * kv + 64, p, 128 * hf:128 * hf + 128],
                        rhs=posT[64 * kv:64 * kv + 64, p:p + 1], start=(p == 0), stop=(p == 31)),
                        reads=["w1sb", "posT"], writes=[("psp", 0)])
                S.add("vector", lambda e, kv=kv, hf=hf: e.tensor_copy(out=cposb[:, kv, hf:hf + 1], in_=ps_p[0][:, 0:1]),
                      reads=[("psp", 0)], writes=["cposb"])
        XT = [("xt", c) for c in range(8)]
        pcount = [0]

        def proj_fm(col0, M, n=512):
            p = pcount[0] % 2
            pcount[0] += 1
            for c in range(8):
                S.add("tensor", lambda e, c=c, p=p: e.matmul(ps_p[p][0:M, 0:n], lhsT=w[:, c, col0:col0 + M], rhs=h[:, c, 0:n],
                                                             start=(c == 0), stop=(c == 7)),
                      reads=[("w", c), ("h", c)], writes=[("psp", p)])
            return p

        def proj_tok(col0, ncol, b):
            p = pcount[0] % 2
            pcount[0] += 1
            for c in range(8):
                S.add("tensor", lambda e, c=c, p=p: e.matmul(ps_p[p][:, 0:ncol], lhsT=h[:, c, b * 128:(b + 1) * 128],
                                                             rhs=w[:, c, col0:col0 + ncol], start=(c == 0), stop=(c == 7)),
                      reads=[("w", c), ("h", c)], writes=[("psp", p)])
            return p

        for tt in range(NT):
            tok0 = 512 * tt
            S.add("sync", lambda e, tok0=tok0: e.dma_start(out=xt[:], in_=xv[:, :, tok0:tok0 + 512]), writes=XT, dma="xt_in")
            emit_rmsnorm(S, xt, h, h, ps_s, rstd, ones, epsb, 512, sqtag="h")
            p = proj_fm(256, 128)
            S.add("scalar", lambda e, p=p, tok0=tok0: e.copy(out=big[:, tok0:tok0 + 512], in_=ps_p[p][:]),
                  reads=[("psp", p)], writes=[("big", 4 * tt + b) for b in range(4)])
        BIGALL = [("big", b) for b in range(NBLK)]
        for n0 in range(0, NCMP, 512):
            N = min(512, NCMP - n0)
            for kv in range(2):
                for hf in range(2):
                    pp = pcount[0] % 2
                    pcount[0] += 1
                    for p in range(32):
                        a0 = 16 * n0 + p
                        S.add("tensor", lambda e, kv=kv, hf=hf, p=p, pp=pp, a0=a0, N=N: e.matmul(
                            ps_p[pp][:, 0:N], lhsT=w1sb[64 * kv:64 * kv + 64, p, 128 * hf:128 * hf + 128],
                            rhs=big[64 * kv:64 * kv + 64, a0:a0 + 16 * (N - 1) + 1:16], start=(p == 0), stop=(p == 31)),
                            reads=["w1sb"] + BIGALL, writes=[("psp", pp)])
                    S.add("scalar", lambda e, kv=kv, hf=hf, pp=pp, N=N: e.activation(
                        out=gel[0][:, 0:N], in_=ps_p[pp][:, 0:N], func=AF.Identity, bias=cposb[:, kv, hf:hf + 1]),
                        reads=[("psp", pp), "cposb"], writes=["gel0"])
                    S.add("vector", lambda e, N=N: e.tensor_tensor(out=gel[1][:, 0:N], in0=gel[0][:, 0:N], in1=gel[0][:, 0:N], op=ALU.mult),
                          reads=["gel0"], writes=["gel1"])
                    S.add("vector", lambda e, N=N: e.tensor_scalar(out=gel[1][:, 0:N], in0=gel[1][:, 0:N], scalar1=0.044715, scalar2=1.0,
                                                                   op0=ALU.mult, op1=ALU.add), reads=["gel1"], writes=["gel1"])
                    S.add("vector", lambda e, N=N: e.tensor_tensor(out=gel[1][:, 0:N], in0=gel[1][:, 0:N], in1=gel[0][:, 0:N], op=ALU.mult),
                          reads=["gel1", "gel0"], writes=["gel1"])
                    S.add("scalar", lambda e, N=N: e.activation(out=gel[2][:, 0:N], in_=gel[1][:, 0:N], func=AF.Sigmoid,
                                                                scale=1.5957691216057308), reads=["gel1"], writes=["gel2"])
                    S.add("vector", lambda e, hf=hf, N=N: e.tensor_tensor(out=gelu[:, hf, 0:N], in0=gel[0][:, 0:N], in1=gel[2][:, 0:N],
                                                                          op=ALU.mult), reads=["gel0", "gel2"], writes=[("gelu", hf)])
                if kv == 0:
                    pp = pcount[0] % 2
                    pcount[0] += 1
                    for hf in range(2):
                        S.add("tensor", lambda e, hf=hf, pp=pp, N=N: e.matmul(ps_p[pp][0:64, 0:N], lhsT=w2sb[:, 0, hf, :], rhs=gelu[:, hf, 0:N],
                                                                              start=(hf == 0), stop=(hf == 1)),
                              reads=["w2sb", ("gelu", hf)], writes=[("psp", pp)])
                    emit_headnorm(S, ps_p[pp][0:64, 0:N], kcmp[0:64, n0:n0 + N], gk3s[:, 0:1], sqh, ps_n, rs, ones64, epsb, N,
                                  rd=[("psp", pp)], wr=["kcmp"])
                else:
                    for nt in range(0, N, 128):
                        M = min(128, N - nt)
                        pp = pcount[0] % 2
                        pcount[0] += 1
                        for hf in range(2):
                            S.add("tensor", lambda e, hf=hf, pp=pp, nt=nt, M=M: e.matmul(
                                ps_p[pp][0:M, 0:64], lhsT=gelu[:, hf, nt:nt + M], rhs=w2sb[:, 1, hf, :], start=(hf == 0), stop=(hf == 1)),
                                reads=["w2sb", ("gelu", hf)], writes=[("psp", pp)])
                        S.add("scalar", lambda e, pp=pp, nt=nt, M=M, n0=n0: e.copy(out=vcmp[0:M, (n0 + nt) // 128, :], in_=ps_p[pp][0:M, 0:64]),
                              reads=[("psp", pp)], writes=["vcmp"])
        S.add("sync", lambda e: e.dma_start(out=big[64:73, :], in_=ktab), reads=BIGALL, writes=BIGALL + ["bigc"], dma="ck2")
        stc = [0]
        ptc = [0]

        def attn_steps(steps, vM, final_name):
            n = len(steps)
            LA = 2
            slots = {}

            def emit_front(i):
                sp = steps[i]
                sti = stc[0] % 3
                stc[0] += 1
                if sp.get("pt") is None:
                    pti = ptc[0] % NPT
                    ptc[0] += 1
                    pt, ptkey = PT[pti][:], ("PT", pti)
                else:
                    pt, ptkey = sp["pt"]
                slots[i] = (pt, ptkey)
                if sp.get("pre") is not None:
                    S.add(sp["pre"][0], sp["pre"][1], reads=sp["pre"][2], writes=sp["pre"][3])
                S.add("tensor", lambda e, sp=sp, sti=sti: e.matmul(
                    st_[sti][:].rearrange("p (a b) -> p a b", a=4), lhsT=sp["lhsT_k"], rhs=sp["rhs_q"], start=True,
                    stop=(sp.get("extra") is None)),
                    reads=sp["qreads"] + sp["kreads"], writes=[("st", sti)])
                if sp.get("extra") is not None:
                    xl, xr, xreads = sp["extra"]
                    S.add("tensor", lambda e, xl=xl, xr=xr, sti=sti: e.matmul(st_[sti][:], lhsT=xl, rhs=xr, start=False, stop=True),
                          reads=xreads, writes=[("st", sti)])
                if sp.get("mask") is not None:
                    ti = i % 2
                    S.add("vector", lambda e, sp=sp, sti=sti, ti=ti: e.tensor_tensor(
                        out=tmp[ti][:].rearrange("p (a b) -> p a b", a=4), in0=st_[sti][:].rearrange("p (a b) -> p a b", a=4),
                        in1=sp["mask"].unsqueeze(1).to_broadcast([128, 4, 128]), op=ALU.add),
                        reads=[("st", sti)] + sp["mreads"], writes=[("tmp", ti)])
                    S.add("scalar", lambda e, ti=ti, pt=pt: e.activation(out=pt, in_=tmp[ti][:], func=AF.Exp),
                          reads=[("tmp", ti)], writes=[ptkey])
                else:
                    S.add("scalar", lambda e, sti=sti, pt=pt: e.activation(out=pt, in_=st_[sti][:], func=AF.Exp),
                          reads=[("st", sti)], writes=[ptkey])

            def emit_back(i):
                sp = steps[i]
                pt, ptkey = slots[i]
                S.add("tensor", lambda e, sp=sp, pt=pt, i=i: e.matmul(num[0:vM, :], lhsT=sp["v_lhsT"], rhs=pt, start=(i == 0), stop=(i == n - 1)),
                      reads=sp["vreads"] + [ptkey], writes=["num"])
                S.add("tensor", lambda e, pt=pt, i=i: e.matmul(den[:], lhsT=onesd[:], rhs=pt, start=(i == 0), stop=(i == n - 1)),
                      reads=["onesd", ptkey], writes=["den"])

            for i in range(n):
                emit_front(i)
                if i >= LA:
                    emit_back(i - LA)
            for i in range(max(0, n - LA), n):
                emit_back(i)

        for tt in range(NT):
            tok0 = 512 * tt
            S.add("sync", lambda e, tok0=tok0: e.dma_start(out=xt[:], in_=xv[:, :, tok0:tok0 + 512]), writes=XT, dma="xt_in")
            S.add("gpsimd", lambda e, tok0=tok0: e.dma_start(out=qaug[64:73, :, :], in_=qtab[:, :, tok0:tok0 + 512]),
                  writes=["qaugc"], dma="qc")
            for b in range(4):
                slot = (4 * tt + b) % 8
                S.add("gpsimd", lambda e, slot=slot, b=b, tok0=tok0: e.dma_start(
                    out=kwr[64:73, slot * 128:(slot + 1) * 128], in_=ktab[:, tok0 + b * 128:tok0 + (b + 1) * 128]),
                    writes=[("kwrc", slot)], dma=("kwc", slot))
            emit_rmsnorm(S, xt, h, h, ps_s, rstd, ones, epsb, 512, sqtag="h")
            p = proj_fm(384, 64)
            emit_headnorm(S, ps_p[p][0:64, :], big[0:64, tok0:tok0 + 512], gk3s[:, 1:2], sqh, ps_n, rs, ones64, epsb, 512,
                          rd=[("psp", p)], wr=[("big", 4 * tt + b) for b in range(4)])
            p = proj_fm(512, 64)
            s0 = (4 * tt) % 8
            emit_headnorm(S, ps_p[p][0:64, :], kwr[0:64, s0 * 128:(s0 + 4) * 128], gk3s[:, 2:3], sqh, ps_n, rs, ones64, epsb, 512,
                          rd=[("psp", p)], wr=[("kwr", s0 + b) for b in range(4)])
            for b in range(4):
                p = proj_tok(448, 64, b)
                S.add("scalar", lambda e, p=p, b=b, tt=tt: e.copy(out=vs_tok[:, 4 * tt + b, :], in_=ps_p[p][:, 0:64]),
                      reads=[("psp", p)], writes=[("vs", 4 * tt + b)])
                p = proj_tok(576, 64, b)
                S.add("scalar", lambda e, p=p, b=b, s0=s0: e.copy(out=vwr[:, s0 + b, :], in_=ps_p[p][:, 0:64]),
                      reads=[("psp", p)], writes=[("vwr", s0 + b)])
            for gi in range(4):
                p = proj_fm(64 * gi, 64)
                emit_headnorm(S, ps_p[p][0:64, :], qaug[0:64, gi, :], gqs[:, 0:1], sqh, ps_n, rs, ones64, epsb, 512,
                              rd=[("psp", p)], wr=[("qaug", gi)])
            p = proj_fm(640, 12)
            S.add("scalar", lambda e, p=p: e.activation(out=gT[:], in_=ps_p[p][0:12, :], func=AF.Sigmoid),
                  reads=[("psp", p)], writes=["gT"])
            qreads = [("qaug", gi) for gi in range(4)] + ["qaugc"]
            for ql in range(4):
                qb = 4 * tt + ql
                rhs_q = qaug[:, :, ql * 128:(ql + 1) * 128]
                nkt = min(NCT, qb // 16 + 1)
                steps = []
                for kt in range(nkt):
                    dl = qb - 16 * kt
                    steps.append(dict(lhsT_k=kcmp[:, kt * 128:(kt + 1) * 128], rhs_q=rhs_q, qreads=qreads, kreads=["kcmp", "kcmpc"],
                                      mask=(cm[:, dl, :] if dl <= 16 else None), mreads=["cm"],
                                      v_lhsT=vcmp[:, kt, :], vreads=["vcmp"], pt=(PTc[:, kt, :], ("PTc", kt))))
                attn_steps(steps, 64, "cmp")
                S.add("vector", lambda e: e.tensor_scalar(out=rden[:], in0=den[:], scalar1=1e-30, scalar2=None, op0=ALU.max),
                      reads=["den"], writes=["rden"])
                S.add("vector", lambda e: e.reciprocal(out=rden[:], in_=rden[:]), reads=["rden"], writes=["rden"])
                S.add("vector", lambda e: e.tensor_tensor(out=obr[0][:], in0=num[0:64, :], in1=rden[0:64, :], op=ALU.mult),
                      reads=["num", "rden"], writes=[("obr", 0)])
                for kt in range(nkt):
                    S.add("gpsimd", lambda e, kt=kt: e.tensor_tensor(out=PTc[:, kt, :], in0=PTc[:, kt, :], in1=rden[:], op=ALU.mult),
                          reads=[("PTc", kt), "rden"], writes=[("PTc", kt)])
                    for gi in range(4):
                        S.add("tensor", lambda e, kt=kt, gi=gi, nkt=nkt: e.matmul(
                            imp[:, 0:NSEL], lhsT=PTc[:, kt, gi * 128:(gi + 1) * 128], rhs=ov[:, kt, :],
                            start=(kt == 0 and gi == 0), stop=(kt == nkt - 1 and gi == 3)),
                            reads=[("PTc", kt), "ov"], writes=["ps_s"])
                c0 = NSEL - 2 * qb
                S.add("vector", lambda e, c0=c0: e.tensor_tensor(out=imp2[:], in0=imp[:, 0:NSEL], in1=tm[:, 0, c0:c0 + NSEL], op=ALU.mult),
                      reads=["ps_s", "tm"], writes=["imp2"])
                S.add("vector", lambda e, c0=c0: e.tensor_tensor(out=imp2[:], in0=imp2[:], in1=tm[:, 1, c0:c0 + NSEL], op=ALU.add),
                      reads=["imp2", "tm"], writes=["imp2"])
                S.add("vector", lambda e: e.memset(imp2[:, 0:1], 1e9), reads=["imp2"], writes=["imp2"])
                S.add("vector", lambda e: e.max(out=m8[:], in_=imp2[:]), reads=["imp2"], writes=["m8"])
                S.add("vector", lambda e: e.match_replace(out=imp3[:], in_to_replace=m8[:], in_values=imp2[:], imm_value=-3e38),
                      reads=["imp2", "m8"], writes=["imp3"])
                S.add("vector", lambda e: e.max(out=m8[:], in_=imp3[:]), reads=["imp3"], writes=["m8"])
                S.add("vector", lambda e: e.match_replace(out=imp3[:], in_to_replace=m8[:], in_values=imp3[:], imm_value=-3e38),
                      reads=["imp3", "m8"], writes=["imp3"])
                S.add("vector", lambda e: e.tensor_tensor(out=imp3[:], in0=imp2[:], in1=imp3[:], op=ALU.not_equal),
                      reads=["imp2", "imp3"], writes=["imp3"])
                S.add("vector", lambda e: e.tensor_scalar(out=negsel[:], in0=imp3[:], scalar1=-1.0, scalar2=-NEGM, op0=ALU.add, op1=ALU.mult),
                      reads=["imp3"], writes=["negsel"])
                steps = []
                for jt in range(qb + 1):
                    xb = (jt // 8) % 2
                    pre = None
                    if jt % 8 == 0:
                        nb = min(16, 2 * (qb + 1) - 2 * jt)
                        pre = ("gpsimd", (lambda e, jt=jt, xb=xb, nb=nb: e.tensor_copy(
                            out=negx[xb][:, 0:nb, :], in_=negsel[:, 2 * jt:2 * jt + nb].unsqueeze(2).to_broadcast([128, nb, 64]))),
                            ["negsel"], [("negx", xb)])
                    steps.append(dict(lhsT_k=big[0:73, jt * 128:(jt + 1) * 128], rhs_q=rhs_q, qreads=qreads, kreads=[("big", jt), "bigc"],
                                      mask=(wm[:, 1, :] if jt == qb else None), mreads=["wm"], pre=pre,
                                      extra=(negx[xb][:, 2 * (jt % 8):2 * (jt % 8) + 2, :].rearrange("p a b -> p (a b)"), i4s[:],
                                             [("negx", xb), "i4s"]),
                                      v_lhsT=vs_tok[:, jt, :], vreads=[("vs", jt)]))
                attn_steps(steps, 64, "sel")
                S.add("vector", lambda e: e.reciprocal(out=rden[0:64, :], in_=den[0:64, :]), reads=["den"], writes=["rden"])
                S.add("vector", lambda e: e.tensor_tensor(out=obr[1][:], in0=num[0:64, :], in1=rden[0:64, :], op=ALU.mult),
                      reads=["num", "rden"], writes=[("obr", 1)])
                steps = []
                for jt in range(max(0, qb - 4), qb + 1):
                    slot = jt % 8
                    mk = wm[:, 1, :] if jt == qb else (wm[:, 0, :] if jt == qb - 4 else None)
                    steps.append(dict(lhsT_k=kwr[:, slot * 128:(slot + 1) * 128], rhs_q=rhs_q, qreads=qreads,
                                      kreads=[("kwr", slot), ("kwrc", slot)], mask=mk, mreads=["wm"],
                                      v_lhsT=vwr[:, slot, :], vreads=[("vwr", slot)]))
                attn_steps(steps, 64, "win")
                S.add("vector", lambda e: e.reciprocal(out=rden[0:64, :], in_=den[0:64, :]), reads=["den"], writes=["rden"])
                S.add("vector", lambda e: e.tensor_tensor(out=obr[2][:], in0=num[0:64, :], in1=rden[0:64, :], op=ALU.mult),
                      reads=["num", "rden"], writes=[("obr", 2)])
                osl = oTt[:, :, ql * 128:(ql + 1) * 128]
                if debug is not None:
                    S.add("vector", lambda e, osl=osl: e.tensor_copy(out=osl, in_=obr[debug][:].rearrange("p (a b) -> p a b", a=4)),
                          reads=[("obr", debug)], writes=["oTt"])
                    continue
                for br in range(3):
                    p = pcount[0] % 2
                    pcount[0] += 1
                    for gi in range(4):
                        S.add("tensor", lambda e, p=p, gi=gi, br=br, ql=ql: e.matmul(
                            ps_p[p][0:64, gi * 128:(gi + 1) * 128], lhsT=srow[:, gi * 3 + br, :], rhs=gT[:, ql * 128:(ql + 1) * 128],
                            start=True, stop=True), reads=["srow", "gT"], writes=[("psp", p)])
                    if br == 0:
                        S.add("vector", lambda e, p=p, osl=osl: e.tensor_tensor(
                            out=osl, in0=ps_p[p][0:64, :].rearrange("p (a b) -> p a b", a=4),
                            in1=obr[0][:].rearrange("p (a b) -> p a b", a=4), op=ALU.mult),
                            reads=[("psp", p), ("obr", 0)], writes=["oTt"])
                    else:
                        S.add("vector", lambda e, p=p, br=br: e.tensor_tensor(out=obr[br][:], in0=ps_p[p][0:64, :], in1=obr[br][:], op=ALU.mult),
                              reads=[("psp", p), ("obr", br)], writes=[("obr", br)])
                        S.add("vector", lambda e, br=br, osl=osl: e.tensor_tensor(
                            out=osl, in0=osl, in1=obr[br][:].rearrange("p (a b) -> p a b", a=4), op=ALU.add),
                            reads=["oTt", ("obr", br)], writes=["oTt"])
            S.add("sync", lambda e, tok0=tok0: e.dma_start(
                out=oT.rearrange("(a d) t -> d a t", d=64)[:, :, tok0:tok0 + 512], in_=oTt[:]), reads=["oTt"], dma="o_out")
        S.emit()
    return nc


def nsa_consts(S_seq, grp):
    NCMP = S_seq // 16 - 1
    NCT = (NCMP + 127) // 128
    NSEL = S_seq // 64
    slopes = alibi_slopes(16)[4 * grp:4 * grp + 4]
    pos = np.arange(S_seq)
    ktab, qtab = alibi_tabs(slopes, pos, pos)
    qtab = np.ascontiguousarray(qtab.transpose(1, 0, 2))
    kctab, _ = alibi_tabs(slopes[:1], pos[:1], 16 * np.arange(NCT * 128) + 31)
    nl = np.arange(128)[:, None]
    tl = np.arange(128)[None, :]
    cmask = np.stack([np.where(16 * nl + 31 <= 128 * dl + tl, 0.0, NEGM) for dl in range(17)], axis=1).astype(np.float32)
    wmask = np.stack([np.where(nl > tl, 0.0, NEGM), np.where(nl <= tl, 0.0, NEGM)], axis=1).astype(np.float32)
    rel = np.arange(2 * NSEL)[None, :] - NSEL
    cflag = (np.arange(128)[:, None] >= 64).astype(np.int64)
    forced = (rel == cflag) | (rel == cflag - 1)
    noncausal = rel > cflag
    keep = np.where(forced | noncausal, 0.0, 1.0)
    add = np.where(forced, 1e9, np.where(noncausal, -1e30, 0.0))
    tmpl = np.stack([keep, add], axis=1).astype(np.float32)
    n = np.arange(NCT * 128)[:, None]
    j = np.arange(NSEL)[None, :]
    ovm = ((16 * n < 64 * j + 64) & (16 * n + 32 > 64 * j) & (n < NCMP)).astype(np.float32)
    ovl = np.ascontiguousarray(ovm.reshape(NCT, 128, NSEL).transpose(1, 0, 2)).astype(NPBF)
    i4 = np.tile(np.eye(128, dtype=np.float32), (1, 4)).astype(NPBF)
    selrow = np.zeros((12, 12, 64), np.float32)
    for r in range(12):
        selrow[r, r, :] = 1.0
    return dict(ktab=ktab, qtab=qtab, kctab=kctab, cmask=np.ascontiguousarray(cmask), wmask=np.ascontiguousarray(wmask),
                tmpl=np.ascontiguousarray(tmpl), ovl=ovl, i4=i4, selrow=selrow)


def run_nsa(xT_full, B, S_seq, gv, w_in, q_norm, k_norm, cmp_pos, cmp_w1, cmp_w2, debug=None):
    assert B * 4 == N_CORES
    key = ("nsa", S_seq, debug)
    if key not in _NC_CACHE:
        _NC_CACHE[key] = build_nsa(S_seq, debug)
    nc = _NC_CACHE[key]
    w_in = np.asarray(w_in, np.float32)
    cmp_w1 = np.asarray(cmp_w1, np.float32)
    cmp_w2 = np.asarray(cmp_w2, np.float32)
    cmp_pos = np.asarray(cmp_pos, np.float32)
    common = {"g": gain_layout(gv),
              "gq": np.ascontiguousarray(np.asarray(q_norm, np.float32).reshape(64, 1)),
              "gk3": np.ascontiguousarray(np.asarray(k_norm, np.float32).reshape(3, 64).T),
              "cpos": np.ascontiguousarray(cmp_pos.transpose(0, 2, 1).reshape(128, 32)),
              "cw1": np.ascontiguousarray(cmp_w1.reshape(2, 32, 64, 256).transpose(0, 2, 1, 3)),
              "cw2": np.ascontiguousarray(cmp_w2.reshape(2, 2, 128, 64).transpose(2, 0, 1, 3))}
    in_maps = []
    for c in range(N_CORES):
        b, grp = c // 4, c % 4
        cols = [w_in[:, grp * 256:(grp + 1) * 256]]
        for i in range(6):
            cols.append(w_in[:, 1024 + 256 * i + 64 * grp:1024 + 256 * i + 64 * grp + 64])
        cols.append(w_in[:, 2560 + 12 * grp:2560 + 12 * grp + 12])
        m = dict(common, xT=np.ascontiguousarray(xT_full[:, b * S_seq:(b + 1) * S_seq]),
                 w_in=np.ascontiguousarray(np.concatenate(cols, axis=1)))
        m.update(nsa_consts(S_seq, grp))
        in_maps.append(m)
    res = run_bass_kernel_spmd(nc, in_maps, core_ids=list(range(N_CORES)))
    out = np.zeros((B, 1024, S_seq), np.float32)
    for c in range(N_CORES):
        b, grp = c // 4, c % 4
        out[b, grp * 256:(grp + 1) * 256, :] = res.results[c]["oT"]
    return out


def build_gdn(S_seq, NH):
    nc = bass.Bass("TRN2", target_bir_lowering=False)
    NT = S_seq // 512
    WC = 514
    with contextlib.ExitStack() as st:
        C = Ctx(nc, st)
        xT = C.din("xT", [D, S_seq]); g = C.din("g", [128, 8])
        w_in = C.din("w_in", [D, NH, WC])
        convw = C.din("convw", [128, NH, 3, 4])
        alog = C.din("alog", [128, NH]); dtb = C.din("dtb", [128, NH])
        onrm = C.din("onrm", [128, 128])
        cst = C.din("cst", [128, 6, 128])
        oT = C.dout("oT", [NH, S_seq, 128])
        xv = xT.rearrange("(c p) t -> p c t", p=128)
        wiv = w_in.rearrange("(c p) h f -> p c h f", p=128)

        w = C.sb("w", [128, 8, NH, WC], BF16)
        stage = [C.sb("stage%d" % i, [128, WC], F32) for i in range(2)]
        gs = C.sb("gs", [128, 8], F32); epsb = C.sb("epsb", [128, 1], F32)
        ones = C.sb("ones", [128, 128], BF16); onesf = C.sb("onesf", [128, 128], BF16)
        cw = C.sb("cw", [128, NH, 3, 4], F32)
        negA = C.sb("negA", [128, NH], F32); dtbs = C.sb("dtbs", [128, NH], F32)
        onr = C.sb("onr", [128, 128], F32)
        K_ = C.sb("cst_sb", [128, 6, 128], F32)
        ident, U2, Ublk, mTi, mS, sT01 = [K_[:, i, :] for i in range(6)]
        xt = C.sb("xt", [128, 8, 512], F32); h = C.sb("h", [128, 8, 512], BF16)
        rstd = C.sb("rstd", [128, 512], F32)
        pre = [[C.sb("pre%d_%d" % (a, b), [128, 515], F32) for b in range(3)] for a in range(NH)]
        post = [[C.sb("post%d_%d" % (a, b), [128, 512], F32) for b in range(3)] for a in range(NH)]
        cacc = C.sb("cacc", [128, 512], F32)
        sqf = C.sb("sqf", [128, 512], BF16); rsf = C.sb("rsf", [128, 512], F32)
        Sst = [C.sb("S%d" % a, [128, 128], F32) for a in range(NH)]
        def hs(name, shape, n, dt=F32):
            return [C.sb("%s%d" % (name, a), shape, dt) for a in range(n)]
        NU = 4 * NH
        cols = hs("cols", [128, 16], NU); grep = hs("grep", [128, 128], 4); brep = hs("brep", [128, 128], 4)
        argA = hs("argA", [128, 128], 4); argB = hs("argB", [128, 128], 4); DT = hs("DT", [128, 128], 4); Dm = hs("Dm", [128, 128], 4)
        Nm = hs("Nm", [128, 128], 4); NTm = hs("NTm", [128, 128], 4); attnT = hs("attnT", [128, 128], NU)
        Xm = hs("Xm", [128, 128], 4); Pm = hs("Pm", [128, 2, 128], 4); PTm = hs("PTm", [128, 2, 128], 4)
        vb = hs("vb", [128, 128], 4); kbg = hs("kbg", [128, 128], 4); kd = hs("kd", [128, 128], NU); egr = hs("egr", [128, 128], 4)
        qgT = hs("qgT", [128, 128], NU); u = hs("u", [128, 128], NU); wT = hs("wT", [128, 128], NU); vnew = hs("vnew", [128, 128], NH)
        gsil = hs("gsil", [128, 128], NU); osb = hs("osb", [128, 128], NU); osq = hs("osq", [128, 128], NH); glc = hs("glc", [128, 2], NU)
        ocol = hs("ocol", [128, 2], NH)
        banks = [C.ps("bank%d" % i) for i in range(8)]
        ps_s = banks[0]
        ps_p = [banks[1], banks[2]]
        Q = lambda b, q: banks[b][:, q * 128:(q + 1) * 128]

        S = Sched(nc)
        S.add("vector", lambda e: e.memset(ones[:], 1.0 / D), writes=["ones"])
        S.add("vector", lambda e: e.memset(onesf[:], 1.0), writes=["onesf"])
        S.add("vector", lambda e: e.memset(epsb[:], EPS), writes=["epsb"])
        for a in range(NH):
            S.add("vector", lambda e, a=a: e.memset(Sst[a][:], 0.0), writes=[("S", a)])
            for b in range(3):
                S.add("gpsimd", lambda e, a=a, b=b: e.memset(pre[a][b][:, 0:3], 0.0), writes=[("pre", a, b)])
        ld = [(gs, g, "gs"), (cw, convw, "cw"), (negA, alog, "negA"), (dtbs, dtb, "dtbs"), (onr, onrm, "onr"), (K_, cst, "cst")]
        for n_, (dst, src, key) in enumerate(ld):
            S.add("sync", lambda e, dst=dst, src=src: e.dma_start(out=dst[:], in_=src), writes=[key], dma="c%d" % n_)
        S.add("scalar", lambda e: e.activation(out=negA[:], in_=negA[:], func=AF.Exp), reads=["negA"], writes=["negA"])
        S.add("vector", lambda e: e.tensor_scalar(out=negA[:], in0=negA[:], scalar1=-1.0, scalar2=None, op0=ALU.mult),
              reads=["negA"], writes=["negA"])
        kst = 0
        for c in range(8):
            for a in range(NH):
                sg = stage[kst % 2]
                S.add("sync", lambda e, sg=sg, c=c, a=a: e.dma_start(out=sg[:], in_=wiv[:, c, a, :]),
                      writes=[("stage", kst % 2)], dma=("stage", kst % 2))
                S.add(["vector", "gpsimd"][kst % 2], lambda e, sg=sg, c=c, a=a: e.tensor_scalar(
                    out=w[:, c, a, :], in0=sg[:], scalar1=gs[:, c:c + 1], scalar2=None, op0=ALU.mult),
                    reads=[("stage", kst % 2), "gs"], writes=[("w", c)])
                kst += 1
        XT = [("xt", c) for c in range(8)]
        pcount = [0]

        def mmf(out, lhsT, rhs, reads, writes, start=True, stop=True):
            S.add("tensor", lambda e: e.matmul(out, lhsT=lhsT, rhs=rhs, start=start, stop=stop), reads=reads, writes=writes)

        for tt in range(NT):
            tok0 = 512 * tt
            S.add("sync", lambda e, tok0=tok0: e.dma_start(out=xt[:], in_=xv[:, :, tok0:tok0 + 512]), writes=XT, dma="xt_in")
            emit_rmsnorm(S, xt, h, h, ps_s, rstd, ones, epsb, 512, sqtag="h", pskey="b0")
            for a in range(NH):
                for b in range(3):
                    p = pcount[0] % 2
                    pcount[0] += 1
                    for c in range(8):
                        S.add("tensor", lambda e, c=c, p=p, a=a, b=b: e.matmul(
                            ps_p[p][:], lhsT=w[:, c, a, 128 * b:128 * b + 128], rhs=h[:, c, :], start=(c == 0), stop=(c == 7)),
                            reads=[("w", c), ("h", c)], writes=["b%d" % (1 + p)])
                    pr_ = pre[a][b]
                    S.add("scalar", lambda e, p=p, pr_=pr_: e.copy(out=pr_[:, 3:515], in_=ps_p[p][:]),
                          reads=["b%d" % (1 + p)], writes=[("pre", a, b)])
                    S.add("vector", lambda e, pr_=pr_, a=a, b=b: e.tensor_scalar(
                        out=cacc[:], in0=pr_[:, 0:512], scalar1=cw[:, a, b, 0:1], scalar2=None, op0=ALU.mult),
                        reads=[("pre", a, b), "cw"], writes=["cacc"])
                    for tap in range(1, 4):
                        S.add("vector", lambda e, pr_=pr_, a=a, b=b, tap=tap: e.scalar_tensor_tensor(
                            out=cacc[:], in0=pr_[:, tap:tap + 512], scalar=cw[:, a, b, tap:tap + 1], in1=cacc[:],
                            op0=ALU.mult, op1=ALU.add), reads=[("pre", a, b), "cw", "cacc"], writes=["cacc"])
                    S.add("vector", lambda e, pr_=pr_: e.tensor_copy(out=pr_[:, 0:3], in_=pr_[:, 512:515]),
                          reads=[("pre", a, b)], writes=[("pre", a, b)])
                    po = post[a][b]
                    S.add("scalar", lambda e, po=po: e.activation(out=po[:], in_=cacc[:], func=AF.Silu),
                          reads=["cacc"], writes=[("post", a, b)])
                    if b < 2:
                        S.add("scalar", lambda e, po=po: e.activation(out=sqf[:], in_=po[:], func=AF.Square),
                              reads=[("post", a, b)], writes=["sqf"])
                        S.add("tensor", lambda e: e.matmul(ps_s[:], lhsT=onesf[:], rhs=sqf[:], start=True, stop=True),
                              reads=["onesf", "sqf"], writes=["b0"])
                        S.add("scalar", lambda e: e.activation(out=rsf[:], in_=ps_s[:], func=AF.Sqrt, bias=epsb[:, 0:1]),
                              reads=["b0", "epsb"], writes=["rsf"])
                        S.add("vector", lambda e: e.reciprocal(out=rsf[:], in_=rsf[:]), reads=["rsf"], writes=["rsf"])
                        sc = (128.0 ** -0.5) if b == 0 else 1.0
                        S.add("vector", lambda e, po=po, sc=sc: e.scalar_tensor_tensor(
                            out=po[:], in0=po[:], scalar=sc, in1=rsf[:], op0=ALU.mult, op1=ALU.mult),
                            reads=[("post", a, b), "rsf"], writes=[("post", a, b)])
            for dc in range(4):
                for a in range(NH):
                    ui = dc * NH + a
                    pb_ = 1 + (ui % 2)
                    pk = "b%d" % pb_
                    for c in range(8):
                        S.add("tensor", lambda e, c=c, a=a, dc=dc, pb_=pb_: e.matmul(
                            banks[pb_][:, 0:128], lhsT=h[:, c, dc * 128:(dc + 1) * 128], rhs=w[:, c, a, 384:512], start=(c == 0), stop=(c == 7)),
                            reads=[("w", c), ("h", c)], writes=[pk])
                    for c in range(8):
                        S.add("tensor", lambda e, c=c, a=a, dc=dc, pb_=pb_: e.matmul(
                            banks[pb_][:, 128:130], lhsT=h[:, c, dc * 128:(dc + 1) * 128], rhs=w[:, c, a, 512:514],
                            start=(c == 0), stop=(c == 7)), reads=[("w", c), ("h", c)], writes=[pk])
                    cl = cols[ui]
                    CK = ("cols", ui)
                    S.add("scalar", lambda e, ui=ui, pb_=pb_: e.activation(out=gsil[ui][:], in_=banks[pb_][:, 0:128], func=AF.Silu),
                          reads=[pk], writes=[("gsil", ui)])
                    S.add("scalar", lambda e, cl=cl, pb_=pb_: e.activation(out=cl[:, 0:1], in_=banks[pb_][:, 128:129], func=AF.Sigmoid),
                          reads=[pk], writes=[CK])
                    S.add("scalar", lambda e, cl=cl, a=a, pb_=pb_: e.activation(out=cl[:, 8:9], in_=banks[pb_][:, 129:130], func=AF.Exp,
                                                                                bias=dtbs[:, a:a + 1]),
                          reads=[pk, "dtbs"], writes=[CK])
                    S.add("vector", lambda e, cl=cl: e.tensor_scalar(out=cl[:, 8:9], in0=cl[:, 8:9], scalar1=1.0, scalar2=None, op0=ALU.add),
                          reads=[CK], writes=[CK])
                    S.add("scalar", lambda e, cl=cl: e.activation(out=cl[:, 8:9], in_=cl[:, 8:9], func=AF.Ln), reads=[CK], writes=[CK])
                    S.add("vector", lambda e, cl=cl, a=a: e.tensor_tensor(out=cl[:, 1:2], in0=cl[:, 8:9], in1=negA[:, a:a + 1], op=ALU.mult),
                          reads=[CK, "negA"], writes=[CK])

            def prep_unit(us, dc, a):
                ui = dc * NH + a
                XB, YB = banks[2 * us], banks[2 * us + 1]
                XK, YK = "b%d" % (2 * us), "b%d" % (2 * us + 1)
                XQ = lambda q: XB[:, q * 128:(q + 1) * 128]
                YQ = lambda q: YB[:, q * 128:(q + 1) * 128]
                cs = slice(dc * 128, (dc + 1) * 128)
                qT_, kT_, vT_ = post[a][0][:, cs], post[a][1][:, cs], post[a][2][:, cs]
                RP = [("post", a, 0), ("post", a, 1), ("post", a, 2)]
                cl = cols[ui]
                CK = ("cols", ui)
                U = lambda name: (name, us)
                P_ = lambda name: (name, ui)
                S.add("vector", lambda e: e.tensor_copy(out=grep[us][:], in_=cl[:, 1:2].to_broadcast([128, 128])), reads=[CK], writes=[U("grep")])
                S.add("gpsimd", lambda e: e.tensor_copy(out=brep[us][:], in_=cl[:, 0:1].to_broadcast([128, 128])), reads=[CK], writes=[U("brep")])
                yield
                mmf(XQ(0), grep[us][:], U2, [U("grep"), "cst"], [XK])
                mmf(XQ(1), grep[us][:], Ublk, [U("grep"), "cst"], [XK])
                mmf(XQ(2), brep[us][:], ident, [U("brep"), "cst"], [XK])
                mmf(XB[:, 384:386], U2, grep[us][:, 0:2], [U("grep"), "cst"], [XK])
                mmf(XB[:, 386:388], Ublk, grep[us][:, 0:2], [U("grep"), "cst"], [XK])
                mmf(YQ(0), kT_, kT_, RP, [YK])
                mmf(YQ(1), kT_, qT_, RP, [YK])
                mmf(YQ(2), kT_, ident, RP + ["cst"], [YK])
                mmf(YQ(3), vT_, ident, RP + ["cst"], [YK])
                yield
                S.add("vector", lambda e: e.tensor_copy(out=cl[:, 2:3], in_=XB[:, 384:385]), reads=[XK], writes=[CK])
                S.add("vector", lambda e: e.tensor_scalar(out=cl[:, 3:4], in0=XB[:, 384:385], scalar1=-1.0, scalar2=None, op0=ALU.mult),
                      reads=[XK], writes=[CK])
                S.add("vector", lambda e: e.tensor_tensor(out=argA[us][:], in0=XQ(0), in1=mTi, op=ALU.add), reads=[XK, "cst"], writes=[U("argA")])
                S.add("vector", lambda e: e.scalar_tensor_tensor(out=argB[us][:], in0=XQ(0), scalar=-1.0, in1=mS, op0=ALU.mult, op1=ALU.add),
                      reads=[XK, "cst"], writes=[U("argB")])
                yield
                S.add("scalar", lambda e: e.activation(out=cl[:, 4:5], in_=XB[:, 384:385], func=AF.Exp), reads=[XK], writes=[CK])
                S.add("scalar", lambda e: e.activation(out=cl[:, 5:6], in_=XB[:, 386:387], func=AF.Exp, bias=cl[:, 3:4]), reads=[XK, CK], writes=[CK])
                S.add("scalar", lambda e: e.activation(out=glc[ui][:], in_=XB[:, 128:256:64], func=AF.Exp), reads=[XK], writes=[P_("glc")])
                S.add("scalar", lambda e: e.activation(out=DT[us][:], in_=argA[us][:], func=AF.Exp, bias=cl[:, 3:4]), reads=[U("argA"), CK], writes=[U("DT")])
                S.add("scalar", lambda e: e.activation(out=Dm[us][:], in_=argB[us][:], func=AF.Exp, bias=cl[:, 2:3]), reads=[U("argB"), CK], writes=[U("Dm")])
                S.add("scalar", lambda e: e.activation(out=egr[us][:], in_=XQ(0), func=AF.Exp), reads=[XK], writes=[U("egr")])
                yield
                S.add("vector", lambda e: e.tensor_tensor(out=cl[:, 6:7], in0=cl[:, 4:5], in1=cl[:, 0:1], op=ALU.mult), reads=[CK], writes=[CK])
                S.add("vector", lambda e: e.tensor_scalar(out=cl[:, 7:8], in0=cl[:, 0:1], scalar1=-1.0, scalar2=None, op0=ALU.mult), reads=[CK], writes=[CK])
                S.add("vector", lambda e: e.scalar_tensor_tensor(out=NTm[us][:], in0=YQ(0), scalar=cl[:, 7:8], in1=Dm[us][:], op0=ALU.mult, op1=ALU.mult),
                      reads=[YK, CK, U("Dm")], writes=[U("NTm")])
                S.add("vector", lambda e: e.tensor_tensor(out=Nm[us][:], in0=YQ(0), in1=DT[us][:], op=ALU.mult), reads=[YK, U("DT")], writes=[U("Nm")])
                S.add("vector", lambda e: e.tensor_tensor(out=Nm[us][:], in0=Nm[us][:], in1=sT01, op=ALU.mult), reads=[U("Nm"), "cst"], writes=[U("Nm")])
                S.add("vector", lambda e: e.scalar_tensor_tensor(out=Nm[us][:], in0=XQ(2), scalar=-1.0, in1=Nm[us][:], op0=ALU.mult, op1=ALU.mult),
                      reads=[XK, U("Nm")], writes=[U("Nm")])
                S.add("vector", lambda e: e.tensor_tensor(out=attnT[ui][:], in0=YQ(1), in1=DT[us][:], op=ALU.mult), reads=[YK, U("DT")], writes=[P_("attnT")])
                S.add("vector", lambda e: e.tensor_scalar(out=vb[us][:], in0=YQ(3), scalar1=cl[:, 0:1], scalar2=None, op0=ALU.mult),
                      reads=[YK, CK], writes=[U("vb")])
                S.add("vector", lambda e: e.tensor_scalar(out=kbg[us][:], in0=YQ(2), scalar1=cl[:, 6:7], scalar2=None, op0=ALU.mult),
                      reads=[YK, CK], writes=[U("kbg")])
                S.add("vector", lambda e: e.tensor_scalar(out=kd[ui][:], in0=YQ(2), scalar1=cl[:, 5:6], scalar2=None, op0=ALU.mult),
                      reads=[YK, CK], writes=[P_("kd")])
                S.add("gpsimd", lambda e: e.tensor_tensor(out=qgT[ui][:], in0=qT_, in1=egr[us][:], op=ALU.mult), reads=RP + [U("egr")], writes=[P_("qgT")])
                S.add("vector", lambda e: e.tensor_tensor(out=Xm[us][:], in0=Nm[us][:], in1=ident, op=ALU.add), reads=[U("Nm"), "cst"], writes=[U("Xm")])
                yield
                P_cur, PT_cur = Nm[us][:], NTm[us][:]
                rdP, rdPT = [U("Nm")], [U("NTm")]
                for lvl in range(5):
                    sl = lvl % 2
                    last = (lvl == 4)
                    mmf(XQ(1), P_cur, PT_cur, rdP + rdPT, [XK])
                    if not last:
                        mmf(XQ(0), PT_cur, P_cur, rdP + rdPT, [XK])
                    yield
                    S.add("scalar", lambda e, sl=sl: e.copy(out=PTm[us][:, sl, :], in_=XQ(1)), reads=[XK], writes=[("PTm", us, sl)])
                    if not last:
                        S.add("scalar", lambda e, sl=sl: e.copy(out=Pm[us][:, sl, :], in_=XQ(0)), reads=[XK], writes=[("Pm", us, sl)])
                    yield
                    mmf(XQ(2), PTm[us][:, sl, :], Xm[us][:], [("PTm", us, sl), U("Xm")], [XK])
                    yield
                    S.add("vector", lambda e: e.tensor_tensor(out=Xm[us][:], in0=XQ(2), in1=Xm[us][:], op=ALU.add), reads=[XK, U("Xm")], writes=[U("Xm")])
                    P_cur, PT_cur = Pm[us][:, sl, :], PTm[us][:, sl, :]
                    rdP, rdPT = [("Pm", us, sl)], [("PTm", us, sl)]
                    yield
                mmf(YQ(0), Xm[us][:], vb[us][:], [U("Xm"), U("vb")], [YK])
                mmf(YQ(1), kbg[us][:], Xm[us][:], [U("Xm"), U("kbg")], [YK])
                yield
                S.add("scalar", lambda e: e.copy(out=u[ui][:], in_=YQ(0)), reads=[YK], writes=[P_("u")])
                S.add("scalar", lambda e: e.copy(out=wT[ui][:], in_=YQ(1)), reads=[YK], writes=[P_("wT")])

            units = [(dc, a) for dc in range(4) for a in range(NH)]
            for w0 in range(0, len(units), 4):
                gens = [prep_unit(us, dc, a) for us, (dc, a) in enumerate(units[w0:w0 + 4])]
                while gens:
                    for gen in list(gens):
                        try:
                            next(gen)
                        except StopIteration:
                            gens.remove(gen)
            for dc in range(4):
                for ch in range(2):
                    pr = slice(64 * ch, 64 * ch + 64)
                    for a in range(NH):
                        ui = dc * NH + a
                        P_ = lambda name: (name, ui)
                        SK = ("S", a)
                        RB = banks[6 + (a % 2)]
                        RK = "b%d" % (6 + (a % 2))
                        mmf(RB[pr, 0:128], wT[ui][:, pr], Sst[a][:], [P_("wT"), SK], [RK])
                        S.add("vector", lambda e, a=a, ui=ui, pr=pr, RB=RB: e.tensor_tensor(out=vnew[a][pr, :], in0=u[ui][pr, :], in1=RB[pr, 0:128],
                                                                                         op=ALU.subtract),
                              reads=[RK, P_("u")], writes=[("vnew", a)])
                        mmf(RB[pr, 128:256], qgT[ui][:, pr], Sst[a][:], [P_("qgT"), SK], [RK], start=True, stop=False)
                        mmf(RB[pr, 128:256], attnT[ui][pr, pr], vnew[a][pr, :], [P_("attnT"), ("vnew", a)], [RK], start=False, stop=True)
                        mmf(RB[:, 256:384], kd[ui][pr, :], vnew[a][pr, :], [P_("kd"), ("vnew", a)], [RK])
                        S.add("vector", lambda e, a=a, ui=ui, ch=ch, RB=RB: e.scalar_tensor_tensor(
                            out=Sst[a][:], in0=Sst[a][:], scalar=glc[ui][:, ch:ch + 1], in1=RB[:, 256:384], op0=ALU.mult, op1=ALU.add),
                            reads=[RK, SK, P_("glc")], writes=[SK])
                        OC = ("ocol", a, ch)
                        S.add("scalar", lambda e, a=a, pr=pr, RB=RB: e.activation(out=osq[a][pr, :], in_=RB[pr, 128:256], func=AF.Square,
                                                                                  accum_out=ocol[a][pr, 0:1]),
                              reads=[RK], writes=[("osq", a), OC])
                        S.add("scalar", lambda e, a=a, pr=pr: e.activation(out=ocol[a][pr, 1:2], in_=ocol[a][pr, 0:1], func=AF.Sqrt,
                                                                           bias=epsb[pr, 0:1], scale=1.0 / 128),
                              reads=[OC, "epsb"], writes=[OC])
                        S.add("vector", lambda e, a=a, pr=pr: e.reciprocal(out=ocol[a][pr, 1:2], in_=ocol[a][pr, 1:2]), reads=[OC], writes=[OC])
                        S.add("vector", lambda e, a=a, ui=ui, pr=pr, RB=RB: e.scalar_tensor_tensor(
                            out=osb[ui][pr, :], in0=RB[pr, 128:256], scalar=ocol[a][pr, 1:2], in1=onr[pr, :], op0=ALU.mult, op1=ALU.mult),
                            reads=[RK, OC, "onr"], writes=[("osb", ui, ch)])
                        S.add("gpsimd", lambda e, ui=ui, pr=pr: e.tensor_tensor(out=osb[ui][pr, :], in0=osb[ui][pr, :], in1=gsil[ui][pr, :], op=ALU.mult),
                              reads=[("osb", ui, ch), ("gsil", ui)], writes=[("osb", ui, ch)])
                for a in range(NH):
                    ui = dc * NH + a
                    r0 = tok0 + dc * 128
                    S.add("sync", lambda e, a=a, r0=r0, ui=ui: e.dma_start(out=oT[a, r0:r0 + 128, :], in_=osb[ui][:]),
                          reads=[("osb", ui, 0), ("osb", ui, 1)], dma=("o_out", ui))
        S.emit()
    return nc


def gdn_consts():
    i = np.arange(128)
    same = (i[:, None] // 64) == (i[None, :] // 64)
    ident = np.eye(128)
    U2 = (same & (i[:, None] <= i[None, :])).astype(np.float64)
    Ublk = same.astype(np.float64)
    mTi = np.where(same & (i[None, :] >= i[:, None]), 0.0, -1e4)
    mS = np.where(same & (i[None, :] < i[:, None]), 0.0, -1e4)
    sT01 = (same & (i[None, :] > i[:, None])).astype(np.float64)
    return np.ascontiguousarray(np.stack([ident, U2, Ublk, mTi, mS, sT01], axis=1).astype(np.float32))


def run_gdn(xT_full, B, S_seq, gv, w_in, conv_w, a_log, dt_bias, o_norm):
    NH = 8 * B // N_CORES
    key = ("gdn", S_seq, NH)
    if key not in _NC_CACHE:
        _NC_CACHE[key] = build_gdn(S_seq, NH)
    nc = _NC_CACHE[key]
    w_in = np.asarray(w_in, np.float32)
    conv_w = np.asarray(conv_w, np.float32)
    common = {"g": gain_layout(gv), "cst": gdn_consts(),
              "onrm": np.ascontiguousarray(np.broadcast_to(np.asarray(o_norm, np.float32).reshape(1, 128), (128, 128)))}
    in_maps = []
    per_b = N_CORES // B
    for c in range(N_CORES):
        b = c // per_b
        heads = [(c % per_b) * NH + i for i in range(NH)]
        wsel = np.stack([np.concatenate([w_in[:, hh * 128:(hh + 1) * 128], w_in[:, 1024 + hh * 128:1024 + (hh + 1) * 128],
                                         w_in[:, 2048 + hh * 128:2048 + (hh + 1) * 128],
                                         w_in[:, 3088 + hh * 128:3088 + (hh + 1) * 128],
                                         w_in[:, 3072 + hh:3072 + hh + 1], w_in[:, 3080 + hh:3080 + hh + 1]], axis=1) for hh in heads], axis=1)
        cwl = np.stack([np.stack([conv_w[:, q * 1024 + hh * 128:q * 1024 + (hh + 1) * 128].T for q in range(3)], axis=1)
                        for hh in heads], axis=1)
        al = np.broadcast_to(np.asarray(a_log, np.float32)[heads].reshape(1, NH), (128, NH))
        db = np.broadcast_to(np.asarray(dt_bias, np.float32)[heads].reshape(1, NH), (128, NH))
        in_maps.append(dict(common, xT=np.ascontiguousarray(xT_full[:, b * S_seq:(b + 1) * S_seq]),
                            w_in=np.ascontiguousarray(wsel), convw=np.ascontiguousarray(cwl),
                            alog=np.ascontiguousarray(al), dtb=np.ascontiguousarray(db)))
    res = run_bass_kernel_spmd(nc, in_maps, core_ids=list(range(N_CORES)))
    out = np.zeros((B, 1024, S_seq), np.float32)
    for c in range(N_CORES):
        b = c // per_b
        for i in range(NH):
            hh = (c % per_b) * NH + i
            out[b, hh * 128:(hh + 1) * 128, :] = res.results[c]["oT"][i].T
    return out


def build_oproj(T):
    nc = bass.Bass("TRN2", target_bir_lowering=False)
    with contextlib.ExitStack() as st:
        C = Ctx(nc, st)
        xT = C.din("xT", [D, T]); oin = C.din("oin", [D, T]); w_o = C.din("w_o", [D, D])
        oT = C.dout("oT", [D, T])
        xv = xT.rearrange("(c p) t -> p c t", p=128)
        iv = oin.rearrange("(c p) t -> p c t", p=128)
        ov = oT.rearrange("(c p) t -> p c t", p=128)
        wv = w_o.rearrange("(c p) n -> p c n", p=128)
        wo = C.sb("wo", [128, 8, D], BF16)
        stage = [C.sb("stage%d" % i, [128, D], F32) for i in range(2)]
        xt = [C.sb("xt%d" % i, [128, 8, 512], F32) for i in range(2)]
        of = [C.sb("of%d" % i, [128, 8, 512], F32) for i in range(2)]
        ob = [C.sb("ob%d" % i, [128, 8, 512], BF16) for i in range(2)]
        ps = [C.ps("ps%d" % i) for i in range(4)]
        S = Sched(nc)
        for c in range(8):
            S.add("sync", lambda e, c=c: e.dma_start(out=stage[c % 2][:], in_=wv[:, c, :]), writes=[("stage", c % 2)], dma=("stage", c % 2))
            S.add(["vector", "gpsimd"][c % 2], lambda e, c=c: e.tensor_copy(out=wo[:, c, :], in_=stage[c % 2][:]),
                  reads=[("stage", c % 2)], writes=[("wo", c)])
        k = 0
        for t in range(T // 512):
            b = t % 2
            ts = slice(t * 512, (t + 1) * 512)
            S.add("sync", lambda e, ts=ts, b=b: e.dma_start(out=xt[b][:], in_=xv[:, :, ts]), writes=[("xt", b, m) for m in range(8)], dma=("xin", b))
            S.add("gpsimd", lambda e, ts=ts, b=b: e.dma_start(out=of[b][:], in_=iv[:, :, ts]), writes=[("of", b)], dma=("oin", b))
            S.add("scalar", lambda e, b=b: e.copy(out=ob[b][:, 0:4, :], in_=of[b][:, 0:4, :]), reads=[("of", b)], writes=[("ob", b, 0)])
            S.add("gpsimd", lambda e, b=b: e.tensor_copy(out=ob[b][:, 4:8, :], in_=of[b][:, 4:8, :]), reads=[("of", b)], writes=[("ob", b, 1)])
            for m in range(8):
                p = k % 4
                k += 1
                for c in range(8):
                    S.add("tensor", lambda e, c=c, m=m, p=p, b=b: e.matmul(ps[p][:], lhsT=wo[:, c, m * 128:(m + 1) * 128], rhs=ob[b][:, c, :],
                                                                           start=(c == 0), stop=(c == 7)),
                          reads=[("wo", c), ("ob", b, c // 4)], writes=[("psp", p)])
                S.add("vector", lambda e, m=m, p=p, b=b: e.tensor_tensor(out=xt[b][:, m, :], in0=ps[p][:], in1=xt[b][:, m, :], op=ALU.add),
                      reads=[("psp", p), ("xt", b, m)], writes=[("xt", b, m)])
            S.add("sync", lambda e, ts=ts, b=b: e.dma_start(out=ov[:, :, ts], in_=xt[b][:]), reads=[("xt", b, m) for m in range(8)], dma=("xout", b))
        S.emit()
    return nc


def run_oproj(xT_full, oT_full, w_o):
    ntok = xT_full.shape[1]
    T = ntok // N_CORES
    key = ("oproj", T)
    if key not in _NC_CACHE:
        _NC_CACHE[key] = build_oproj(T)
    nc = _NC_CACHE[key]
    w_o = np.ascontiguousarray(w_o, np.float32)
    in_maps = [{"xT": np.ascontiguousarray(xT_full[:, c * T:(c + 1) * T]), "oin": np.ascontiguousarray(oT_full[:, c * T:(c + 1) * T]),
                "w_o": w_o} for c in range(N_CORES)]
    res = run_bass_kernel_spmd(nc, in_maps, core_ids=list(range(N_CORES)))
    return np.concatenate([r["oT"] for r in res.results], axis=1)


def kernel(x, ffn1_norm, ffn1_w_in, ffn1_w_out, mix_norm, ffn2_norm, ffn2_w_in, ffn2_w_out,
           nsa_w_in, nsa_w_out, nsa_q_norm, nsa_k_norm, nsa_cmp_pos, nsa_cmp_w1, nsa_cmp_w2,
           diff_w_in, diff_w_out, diff_q_norm, diff_k_norm, diff_lambda, diff_subln,
           gdn_w_in, gdn_w_out, gdn_conv_w, gdn_a_log, gdn_dt_bias, gdn_o_norm,
           swa_w_in, swa_w_out, swa_q_norm, swa_k_norm, swa_sinks):
    import math
    x = np.asarray(x, np.float32)
    B, S_seq, _ = x.shape
    A = lambda v: np.asarray(v, np.float32)
    xT = np.ascontiguousarray(x.reshape(B * S_seq, D).T)
    bs = lambda o: np.ascontiguousarray(o.transpose(1, 0, 2).reshape(D, B * S_seq))
    depth = A(ffn1_norm).shape[0]
    for layer in range(depth):
        kind, j = layer % 4, layer // 4
        xT = run_ffn(xT, A(ffn1_norm)[layer], A(ffn1_w_in)[layer], A(ffn1_w_out)[layer])
        gm = A(mix_norm)[layer]
        if kind == 0:
            o = run_nsa(xT, B, S_seq, gm, A(nsa_w_in)[j], A(nsa_q_norm)[j], A(nsa_k_norm)[j], A(nsa_cmp_pos)[j],
                        A(nsa_cmp_w1)[j], A(nsa_cmp_w2)[j])
            xT = run_oproj(xT, bs(o), A(nsa_w_out)[j])
        elif kind == 1:
            lam_init = 0.8 - 0.6 * math.exp(-0.3 * layer)
            o = run_diff(xT, B, S_seq, gm, A(diff_w_in)[j], A(diff_q_norm)[j], A(diff_k_norm)[j], A(diff_lambda)[j],
                         A(diff_subln)[j], lam_init)
            xT = run_oproj(xT, bs(o), A(diff_w_out)[j])
        elif kind == 2:
            o = run_gdn(xT, B, S_seq, gm, A(gdn_w_in)[j], A(gdn_conv_w)[j], A(gdn_a_log)[j], A(gdn_dt_bias)[j], A(gdn_o_norm)[j])
            xT = run_oproj(xT, bs(o), A(gdn_w_out)[j])
        else:
            xT = run_swa(xT, S_seq, gm, A(swa_w_in)[j], A(swa_w_out)[j], A(swa_q_norm)[j], A(swa_k_norm)[j], A(swa_sinks)[j])
        xT = run_ffn(xT, A(ffn2_norm)[layer], A(ffn2_w_in)[layer], A(ffn2_w_out)[layer])
    return np.ascontiguousarray(xT.T).reshape(B, S_seq, D).astype(np.float32)
```

```python
import contextlib
import math
import numpy as np
import concourse.bass as bass
import concourse.mybir as mybir
from concourse.bass_utils import run_bass_kernel_spmd

F32 = mybir.dt.float32
BF16 = mybir.dt.bfloat16
ALU = mybir.AluOpType
AF = mybir.ActivationFunctionType

N_CORES = 8
D = 1024
DFF = 2816
EPS = 1e-6

COMPUTE = ("tensor", "vector", "scalar", "gpsimd")


PSUM_NAMES = {"ps_s", "ps_n", "psp", "psa", "psb", "psy", "st", "num", "den", "imp"}


def is_psum_key(k):
    n = k[0] if isinstance(k, tuple) else k
    return isinstance(n, str) and (n in PSUM_NAMES or (len(n) == 2 and n[0] == "b" and n[1].isdigit()))


class Sched:
    def __init__(self, nc):
        self.nc = nc
        self.ops = []

    LIMIT = None
    seen = 0

    def add(self, eng, fn, reads=(), writes=(), dma=None):
        Sched.seen += 1
        if Sched.LIMIT is not None and Sched.seen > Sched.LIMIT and not (dma is not None and not writes):
            return -1
        self.ops.append(dict(eng=eng, fn=fn, reads=tuple(reads), writes=tuple(writes), dma=dma))
        return len(self.ops) - 1

    def emit(self, final_wait_engine="sync"):
        nc = self.nc
        ops = self.ops
        last_writer = {}
        readers = {}
        deps = []
        for i, op in enumerate(ops):
            d = set()
            for r in op["reads"]:
                if r in last_writer:
                    d.add(last_writer[r])
                if is_psum_key(r):
                    for j in readers.get(r, ()):
                        if ops[j]["eng"] != op["eng"]:
                            d.add(j)
            for w in op["writes"]:
                if w in last_writer:
                    d.add(last_writer[w])
                for j in readers.get(w, ()):
                    d.add(j)
            d.discard(i)
            deps.append(d)
            for r in op["reads"]:
                readers.setdefault(r, []).append(i)
            for w in op["writes"]:
                last_writer[w] = i
                readers[w] = []

        def pe_pe(i, j):
            return ops[j]["eng"] == "tensor" and ops[i]["eng"] == "tensor" and ops[j]["dma"] is None \
                and ops[i]["dma"] is None

        needed = set()
        for i, d in enumerate(deps):
            for j in d:
                if not pe_pe(i, j):
                    needed.add(j)
        dma_keys = []
        for op in ops:
            if op["dma"] is not None and op["dma"] not in dma_keys:
                dma_keys.append(op["dma"])
        engines_used = []
        for op in ops:
            if op["eng"] not in engines_used:
                engines_used.append(op["eng"])
        if final_wait_engine not in engines_used:
            engines_used.append(final_wait_engine)

        with contextlib.ExitStack() as st:
            esem = {e: st.enter_context(nc.semaphore("s_" + e)) for e in COMPUTE}
            dsem = {k: st.enter_context(nc.semaphore("d_%d" % n)) for n, k in enumerate(dma_keys)}
            cnt = {e: 0 for e in COMPUTE}
            dcnt = {k: 0 for k in dma_keys}
            sig = {}
            for i, op in enumerate(ops):
                if op["dma"] is not None:
                    dcnt[op["dma"]] += 16
                    sig[i] = (dsem[op["dma"]], dcnt[op["dma"]], ("d", op["dma"]))
                elif i in needed:
                    cnt[op["eng"]] += 1
                    sig[i] = (esem[op["eng"]], cnt[op["eng"]], ("e", op["eng"]))
            per_eng = {e: [] for e in engines_used}
            for i, op in enumerate(ops):
                per_eng[op["eng"]].append(i)
            final = [(dsem[k], dcnt[k]) for k in dma_keys]
            block = st.enter_context(nc.Block())

            def make(ename):
                def body(e):
                    seen = {}
                    for i in per_eng[ename]:
                        op = ops[i]
                        waits = {}
                        for j in deps[i]:
                            if j not in sig or pe_pe(i, j):
                                continue
                            s, v, key = sig[j]
                            if seen.get(key, 0) >= v:
                                continue
                            if key not in waits or waits[key][1] < v:
                                waits[key] = (s, v)
                        for key, (s, v) in waits.items():
                            e.wait_ge(s, v)
                            seen[key] = v
                        ins = op["fn"](e)
                        if i in sig:
                            ins.then_inc(sig[i][0], 16 if op["dma"] is not None else 1)
                    if ename == final_wait_engine:
                        for s, v in final:
                            if v > 0:
                                e.wait_ge(s, v)
                return body

            for ename in engines_used:
                getattr(block, ename)(make(ename))
        return len(ops)


def build_ffn(T, TT=512):
    nc = bass.Bass("TRN2", target_bir_lowering=False)
    xT = nc.dram_tensor("xT", [D, T], F32, kind="ExternalInput").ap()
    g = nc.dram_tensor("g", [128, 8], F32, kind="ExternalInput").ap()
    w_in = nc.dram_tensor("w_in", [D, 2 * DFF], F32, kind="ExternalInput").ap()
    w_out = nc.dram_tensor("w_out", [DFF, D], F32, kind="ExternalInput").ap()
    oT = nc.dram_tensor("oT", [D, T], F32, kind="ExternalOutput").ap()
    NJ = DFF // 128
    GW = 704
    ntiles = T // TT
    xv = xT.rearrange("(c p) t -> p c t", p=128)
    ov = oT.rearrange("(c p) t -> p c t", p=128)
    wiv = w_in.rearrange("(c p) f -> p c f", p=128)
    wov = w_out.rearrange("(j p) d -> p j d", p=128)
    with contextlib.ExitStack() as st:
        sb = lambda name, shape, dt: st.enter_context(nc.sbuf_tensor(name, shape, dt))
        ps = lambda name: st.enter_context(nc.psum_tensor(name, [128, TT], F32))
        wi = sb("wi", [128, 8, 2 * DFF], BF16)
        wo = sb("wo", [128, NJ, D], BF16)
        gs = sb("gs", [128, 8], F32)
        ones = sb("ones", [128, 128], BF16)
        epsb = sb("epsb", [128, 1], F32)
        xts = [sb("xt%d" % i, [128, 8, TT], F32) for i in range(2)]
        h = sb("h", [128, 8, TT], BF16)
        rstd = sb("rstd", [128, TT], F32)
        sa = [sb("sa%d" % i, [128, TT], F32) for i in range(2)]
        act = sb("act", [128, NJ, TT], BF16)
        sq = act
        ps_s = ps("ps_s")
        ps_a = [ps("ps_a%d" % i) for i in range(2)]
        ps_b = [ps("ps_b%d" % i) for i in range(2)]
        ps_y = [ps("ps_y%d" % i) for i in range(2)]

        S = Sched(nc)
        S.add("vector", lambda e: e.memset(ones[:], 1.0 / D), writes=["ones"])
        S.add("vector", lambda e: e.memset(epsb[:], EPS), writes=["epsb"])
        S.add("sync", lambda e: e.dma_start(out=gs[:], in_=g), writes=["gs"], dma="gs")
        XTK = lambda b: [("xt", b, c) for c in range(8)]

        def load_x(t):
            b = t % 2
            S.add("sync", lambda e, t=t, b=b: e.dma_start(out=xts[b][:], in_=xv[:, :, t * TT:(t + 1) * TT]),
                  writes=XTK(b), dma=("xt_in", b))

        load_x(0)
        ngrp = 2 * DFF // GW
        order = []
        for i in range(ngrp // 2):
            order += [i, ngrp // 2 + i]
        for grp in order:
            S.add("gpsimd", lambda e, grp=grp: e.dma_start(out=wi[:, :, grp * GW:(grp + 1) * GW], in_=wiv[:, :, grp * GW:(grp + 1) * GW]),
                  writes=[("wi", grp)], dma=("wi", grp))
        for j0 in range(0, NJ, 6):
            j1 = min(NJ, j0 + 6)
            S.add("gpsimd", lambda e, j0=j0, j1=j1: e.dma_start(out=wo[:, j0:j1, :], in_=wov[:, j0:j1, :]),
                  writes=[("wo", j) for j in range(j0, j1)], dma=("wo", j0))
        wgrp = lambda c0: [("wi", gidx) for gidx in range(c0 // GW, (c0 + 127) // GW + 1)]
        for t in range(ntiles):
            b = t % 2
            xt = xts[b]
            ts = slice(t * TT, (t + 1) * TT)
            if t + 1 < ntiles:
                load_x(t + 1)
            S.add("scalar", lambda e, xt=xt: e.activation(out=sq[:, 0:8, :], in_=xt[:], func=AF.Square),
                  reads=XTK(b), writes=[("act", c) for c in range(8)])
            for c in range(8):
                S.add("tensor", lambda e, c=c: e.matmul(ps_s[:], lhsT=ones[:], rhs=sq[:, c, :],
                                                        start=(c == 0), stop=(c == 7)),
                      reads=["ones", ("act", c)], writes=["ps_s"])
            S.add("scalar", lambda e: e.activation(out=rstd[:], in_=ps_s[:], func=AF.Sqrt, bias=epsb[:, 0:1]),
                  reads=["ps_s", "epsb"], writes=["rstd"])
            S.add("vector", lambda e: e.reciprocal(out=rstd[:], in_=rstd[:]), reads=["rstd"], writes=["rstd"])
            for c in range(8):
                S.add("vector", lambda e, c=c, xt=xt: e.scalar_tensor_tensor(
                    out=h[:, c, :], in0=xt[:, c, :], scalar=gs[:, c:c + 1], in1=rstd[:], op0=ALU.mult, op1=ALU.mult),
                    reads=[("xt", b, c), "rstd", "gs"], writes=[("h", c)])
            for j in range(NJ):
                pa, pb, sj = ps_a[j % 2], ps_b[j % 2], sa[j % 2]
                for c in range(8):
                    S.add("tensor", lambda e, c=c, j=j, pa=pa: e.matmul(
                        pa[:], lhsT=wi[:, c, j * 128:(j + 1) * 128], rhs=h[:, c, :],
                        start=(c == 0), stop=(c == 7)),
                        reads=wgrp(j * 128) + [("h", c)], writes=[("psa", j % 2)])
                for c in range(8):
                    S.add("tensor", lambda e, c=c, j=j, pb=pb: e.matmul(
                        pb[:], lhsT=wi[:, c, DFF + j * 128:DFF + (j + 1) * 128], rhs=h[:, c, :],
                        start=(c == 0), stop=(c == 7)),
                        reads=wgrp(DFF + j * 128) + [("h", c)], writes=[("psb", j % 2)])
                S.add("scalar", lambda e, pa=pa, sj=sj: e.activation(out=sj[:], in_=pa[:], func=AF.Silu),
                      reads=[("psa", j % 2)], writes=[("sa", j % 2)])
                S.add("vector", lambda e, pb=pb, sj=sj, j=j: e.tensor_tensor(
                    out=act[:, j, :], in0=pb[:], in1=sj[:], op=ALU.mult),
                    reads=[("psb", j % 2), ("sa", j % 2)], writes=[("act", j)])
            for m in range(8):
                py = ps_y[m % 2]
                for j in range(NJ):
                    S.add("tensor", lambda e, m=m, j=j, py=py: e.matmul(
                        py[:], lhsT=wo[:, j, m * 128:(m + 1) * 128], rhs=act[:, j, :],
                        start=(j == 0), stop=(j == NJ - 1)),
                        reads=[("wo", j), ("act", j)], writes=[("psy", m % 2)])
                S.add("vector", lambda e, m=m, py=py, xt=xt: e.scalar_tensor_tensor(
                    out=xt[:, m, :], in0=py[:], scalar=0.5, in1=xt[:, m, :], op0=ALU.mult, op1=ALU.add),
                    reads=[("psy", m % 2), ("xt", b, m)], writes=[("xt", b, m)])
            S.add("sync", lambda e, ts=ts, xt=xt: e.dma_start(out=ov[:, :, ts], in_=xt[:]),
                  reads=XTK(b), dma=("xt_out", b))
        S.emit()
    return nc


def gain_layout(gv):
    return np.ascontiguousarray(np.asarray(gv, np.float32).reshape(8, 128).T)


_NC_CACHE = {}


def run_ffn(xT_full, gv, w_in, w_out):
    ntok = xT_full.shape[1]
    T = ntok // N_CORES
    key = ("ffn", T)
    if key not in _NC_CACHE:
        _NC_CACHE[key] = build_ffn(T)
    nc = _NC_CACHE[key]
    gl = gain_layout(gv)
    w_in = np.ascontiguousarray(w_in, dtype=np.float32)
    w_out = np.ascontiguousarray(w_out, dtype=np.float32)
    in_maps = [{"xT": np.ascontiguousarray(xT_full[:, c * T:(c + 1) * T]), "g": gl,
                "w_in": w_in, "w_out": w_out} for c in range(N_CORES)]
    res = run_bass_kernel_spmd(nc, in_maps, core_ids=list(range(N_CORES)))
    return np.concatenate([r["oT"] for r in res.results], axis=1)


import ml_dtypes
NPBF = ml_dtypes.bfloat16
NEGM = -30000.0


def split3(v):
    v = np.asarray(v, np.float64)
    hi = v.astype(np.float32).astype(NPBF)
    r = v - hi.astype(np.float64)
    mid = r.astype(np.float32).astype(NPBF)
    r = r - mid.astype(np.float64)
    lo = r.astype(np.float32).astype(NPBF)
    return hi, mid, lo


def alibi_tabs(slopes, qpos, kpos):
    kpos = np.asarray(kpos, np.int64)
    jb, sl = kpos // 128, kpos % 128
    one = np.ones_like(jb)
    ktab = np.stack([jb, jb, jb, sl, sl, sl, one, one, one]).astype(np.float32).astype(NPBF)
    qt = []
    for s in slopes:
        a = split3(np.full(len(qpos), 128.0 * s))
        b = split3(np.full(len(qpos), float(s)))
        c = split3(-float(s) * np.asarray(qpos, np.float64))
        qt.append(np.stack(list(a) + list(b) + list(c)))
    return ktab, np.stack(qt).astype(NPBF)


class Ctx:
    def __init__(self, nc, st):
        self.nc, self.st = nc, st

    def sb(self, name, shape, dt):
        return self.st.enter_context(self.nc.sbuf_tensor(name, shape, dt))

    def ps(self, name, shape=(128, 512), dt=F32):
        return self.st.enter_context(self.nc.psum_tensor(name, list(shape), dt))

    def din(self, name, shape, dt=F32):
        return self.nc.dram_tensor(name, list(shape), dt, kind="ExternalInput").ap()

    def dout(self, name, shape, dt=F32):
        return self.nc.dram_tensor(name, list(shape), dt, kind="ExternalOutput").ap()


def emit_weight_load(S, wdram_v, wsb, gs, stage, ncols, kbase=0, chunk=1408, tag="w"):
    k = kbase
    engs = ["vector", "gpsimd"]
    for c in range(8):
        for c0 in range(0, ncols, chunk):
            w = min(chunk, ncols - c0)
            sg = stage[k % 2]
            S.add("sync", lambda e, sg=sg, c=c, c0=c0, w=w: e.dma_start(out=sg[:, 0:w], in_=wdram_v[:, c, c0:c0 + w]),
                  writes=[("stage", k % 2)], dma=("stage", k % 2))
            S.add(engs[k % 2], lambda e, sg=sg, c=c, c0=c0, w=w: e.tensor_scalar(
                out=wsb[:, c, c0:c0 + w], in0=sg[:, 0:w], scalar1=gs[:, c:c + 1], scalar2=None, op0=ALU.mult),
                reads=[("stage", k % 2), "gs"], writes=[(tag, c)])
            k += 1
    return k


def emit_rmsnorm(S, xt, h, sqbuf, ps_s, rstd, ones, epsb, n, tagx="xt", tagh="h", sqtag="sq", pskey="ps_s"):
    S.add("scalar", lambda e: e.activation(out=sqbuf[:, 0:8, 0:n], in_=xt[:, :, 0:n], func=AF.Square),
          reads=[(tagx, c) for c in range(8)], writes=[(sqtag, c) for c in range(8)])
    for c in range(8):
        S.add("tensor", lambda e, c=c: e.matmul(ps_s[:, 0:n], lhsT=ones[:], rhs=sqbuf[:, c, 0:n],
                                                start=(c == 0), stop=(c == 7)),
              reads=["ones", (sqtag, c)], writes=[pskey])
    S.add("scalar", lambda e: e.activation(out=rstd[:, 0:n], in_=ps_s[:, 0:n], func=AF.Sqrt, bias=epsb[:, 0:1]),
          reads=[pskey, "epsb"], writes=["rstd"])
    S.add("vector", lambda e: e.reciprocal(out=rstd[:, 0:n], in_=rstd[:, 0:n]), reads=["rstd"], writes=["rstd"])
    for c in range(8):
        S.add("vector" if c % 2 == 0 else "gpsimd",
              lambda e, c=c: e.tensor_tensor(out=h[:, c, 0:n], in0=xt[:, c, 0:n], in1=rstd[:, 0:n], op=ALU.mult),
              reads=[(tagx, c), "rstd"], writes=[(tagh, c)])


def emit_headnorm(S, src_ps, dst, gcol, sqh, ps_n, rs, ones64, epsb, n, rd, wr, P=64):
    S.add("scalar", lambda e: e.activation(out=sqh[0:P, 0:n], in_=src_ps, func=AF.Square),
          reads=rd, writes=["sqh"])
    S.add("tensor", lambda e: e.matmul(ps_n[0:P, 0:n], lhsT=ones64[0:P, 0:P], rhs=sqh[0:P, 0:n], start=True, stop=True),
          reads=["sqh", "ones64"], writes=["ps_n"])
    S.add("scalar", lambda e: e.activation(out=rs[0:P, 0:n], in_=ps_n[0:P, 0:n], func=AF.Sqrt, bias=epsb[0:P, 0:1]),
          reads=["ps_n", "epsb"], writes=["rs"])
    S.add("vector", lambda e: e.reciprocal(out=rs[0:P, 0:n], in_=rs[0:P, 0:n]), reads=["rs"], writes=["rs"])
    S.add("vector", lambda e: e.scalar_tensor_tensor(out=dst, in0=src_ps, scalar=gcol, in1=rs[0:P, 0:n],
                                                     op0=ALU.mult, op1=ALU.mult),
          reads=list(rd) + ["rs", "gcols"], writes=wr)


def build_swa(T):
    TH = T + 128
    nc = bass.Bass("TRN2", target_bir_lowering=False)
    with contextlib.ExitStack() as st:
        C = Ctx(nc, st)
        xT = C.din("xT", [D, TH]); g = C.din("g", [128, 8])
        w_in = C.din("w_in", [D, 1536]); w_out = C.din("w_out", [D, D])
        gq = C.din("gq", [64, 1]); gk = C.din("gk", [64, 1]); sinks = C.din("sinks", [64, 16])
        ktab = C.din("ktab", [9, TH], BF16); qtab = C.din("qtab", [9, 16, T], BF16)
        masks = C.din("masks", [128, 3, 512])
        oT = C.dout("oT", [D, T])
        xv = xT.rearrange("(c p) t -> p c t", p=128)
        ov = oT.rearrange("(c p) t -> p c t", p=128)
        wiv = w_in.rearrange("(c p) f -> p c f", p=128)
        wov = w_out.rearrange("(h d) n -> d h n", d=64)

        w = C.sb("w", [128, 8, 1536], BF16)
        wo = C.sb("wo", [64, 16, D], BF16)
        stage = [C.sb("stage%d" % i, [128, 1024], F32) for i in range(2)]
        gs = C.sb("gs", [128, 8], F32); epsb = C.sb("epsb", [128, 1], F32)
        ones = C.sb("ones", [128, 128], BF16); ones64 = C.sb("ones64", [64, 64], BF16)
        onesd = C.sb("onesd", [128, 64], BF16)
        gqs = C.sb("gqs", [64, 1], F32); gks = C.sb("gks", [64, 1], F32); es = C.sb("es", [64, 16], F32)
        msk = C.sb("msk", [128, 3, 512], F32)
        xt = C.sb("xt", [128, 8, 512], F32); h = C.sb("h", [128, 8, 512], BF16)
        rstd = C.sb("rstd", [128, 512], F32)
        sqh = C.sb("sqh", [64, 512], BF16); rs = C.sb("rs", [64, 512], F32)
        kaug = C.sb("kaug", [73, 4, TH], BF16)
        vtok = C.sb("vtok", [128, TH // 128, 256], BF16)
        qaug = C.sb("qaug", [73, 16, 512], BF16)
        PT = [C.sb("PT%d" % i, [128, 2, 512], BF16) for i in range(2)]
        tmp = [C.sb("tmp%d" % i, [128, 512], F32) for i in range(2)]
        oTt = C.sb("oTt", [64, 16, 512], BF16); rec = C.sb("rec", [64, 512], F32)
        ps_s = C.ps("ps_s"); ps_p = [C.ps("ps_p%d" % i) for i in range(2)]; ps_n = C.ps("ps_n")
        st_ = [C.ps("st%d" % i) for i in range(2)]; num = C.ps("num"); den = C.ps("den")

        S = Sched(nc)
        S.add("vector", lambda e: e.memset(ones[:], 1.0 / D), writes=["ones"])
        S.add("vector", lambda e: e.memset(ones64[:], 1.0 / 64), writes=["ones64"])
        S.add("vector", lambda e: e.memset(onesd[:], 1.0), writes=["onesd"])
        S.add("vector", lambda e: e.memset(epsb[:], EPS), writes=["epsb"])
        S.add("sync", lambda e: e.dma_start(out=gs[:], in_=g), writes=["gs"], dma="c0")
        S.add("sync", lambda e: e.dma_start(out=gqs[:], in_=gq), writes=["gcols"], dma="c1")
        S.add("sync", lambda e: e.dma_start(out=gks[:], in_=gk), writes=["gcols"], dma="c1")
        S.add("sync", lambda e: e.dma_start(out=es[:], in_=sinks), writes=["es"], dma="c2")
        S.add("sync", lambda e: e.dma_start(out=msk[:], in_=masks), writes=["msk"], dma="c3")
        for gg in range(4):
            S.add("sync", lambda e, gg=gg: e.dma_start(out=kaug[64:73, gg, :], in_=ktab), writes=[("kaugc", gg)], dma="c4")
        S.add("vector", lambda e: e.tensor_scalar(out=gqs[:], in0=gqs[:], scalar1=0.125, scalar2=None, op0=ALU.mult),
              reads=["gcols"], writes=["gcols"])
        S.add("scalar", lambda e: e.activation(out=es[:], in_=es[:], func=AF.Exp), reads=["es"], writes=["es"])
        k = emit_weight_load(S, wiv, w, gs, stage, 1536, chunk=768)
        for hh in range(16):
            sg = stage[k % 2]
            S.add("sync", lambda e, sg=sg, hh=hh: e.dma_start(out=sg[0:64, 0:D], in_=wov[:, hh, :]),
                  writes=[("stage", k % 2)], dma=("stage", k % 2))
            S.add(["vector", "gpsimd"][k % 2], lambda e, sg=sg, hh=hh: e.tensor_copy(out=wo[:, hh, :], in_=sg[0:64, 0:D]),
                  reads=[("stage", k % 2)], writes=[("wo", hh)])
            k += 1
        XT = [("xt", c) for c in range(8)]
        H = [("h", c) for c in range(8)]
        W = [("w", c) for c in range(8)]
        pcount = [0]

        def proj_fm(col0, M, n):
            p = pcount[0] % 2
            pcount[0] += 1
            for c in range(8):
                S.add("tensor", lambda e, c=c, p=p: e.matmul(ps_p[p][0:M, 0:n], lhsT=w[:, c, col0:col0 + M], rhs=h[:, c, 0:n],
                                                             start=(c == 0), stop=(c == 7)),
                      reads=[("w", c), ("h", c)], writes=[("psp", p)])
            return p

        def proj_tile(tok0, n, with_q):
            emit_rmsnorm(S, xt, h, h, ps_s, rstd, ones, epsb, n, sqtag="h")
            for gg in range(4):
                p = proj_fm(1024 + 64 * gg, 64, n)
                emit_headnorm(S, ps_p[p][0:64, 0:n], kaug[0:64, gg, tok0:tok0 + n], gks[:, 0:1], sqh, ps_n, rs, ones64, epsb,
                              n, rd=[("psp", p)], wr=[("kaug", gg, tok0 // 128 + b) for b in range(n // 128)])
            for b in range(n // 128):
                p = pcount[0] % 2
                pcount[0] += 1
                for c in range(8):
                    S.add("tensor", lambda e, c=c, p=p, b=b: e.matmul(ps_p[p][:, 0:256], lhsT=h[:, c, b * 128:(b + 1) * 128],
                                                                      rhs=w[:, c, 1280:1536], start=(c == 0), stop=(c == 7)),
                          reads=[("w", c), ("h", c)], writes=[("psp", p)])
                S.add("scalar", lambda e, p=p, b=b: e.copy(out=vtok[:, tok0 // 128 + b, :], in_=ps_p[p][:, 0:256]),
                      reads=[("psp", p)], writes=[("vtok", tok0 // 128 + b)])
            if with_q:
                for hq in range(16):
                    p = proj_fm(64 * hq, 64, n)
                    emit_headnorm(S, ps_p[p][0:64, 0:n], qaug[0:64, hq, 0:n], gqs[:, 0:1], sqh, ps_n, rs, ones64, epsb,
                                  n, rd=[("psp", p)], wr=[("qaug", hq)])

        S.add("sync", lambda e: e.dma_start(out=xt[:, :, 0:128], in_=xv[:, :, 0:128]), writes=XT, dma="xt_in")
        proj_tile(0, 128, False)
        acount = 0
        for tt in range(T // 512):
            tok0 = 128 + 512 * tt
            S.add("sync", lambda e, tok0=tok0: e.dma_start(out=xt[:], in_=xv[:, :, tok0:tok0 + 512]), writes=XT, dma="xt_in")
            S.add("gpsimd", lambda e, tt=tt: e.dma_start(out=qaug[64:73, :, :], in_=qtab[:, :, tt * 512:(tt + 1) * 512]),
                  writes=["qaugc"], dma="qc")
            proj_tile(tok0, 512, True)
            for qb in range(4):
                cur = tok0 // 128 + qb
                prev = cur - 1
                for gg in range(4):
                    pt = PT[acount % 2]
                    ptk = ("PT", acount % 2)
                    rhs_q = qaug[:, 4 * gg:4 * gg + 4, qb * 128:(qb + 1) * 128]
                    qreads = [("qaug", 4 * gg + i) for i in range(4)] + ["qaugc"]
                    for i, kb in enumerate((prev, cur)):
                        S.add("tensor", lambda e, i=i, kb=kb, gg=gg, rhs_q=rhs_q: e.matmul(
                            st_[i][:].rearrange("p (a b) -> p a b", a=4), lhsT=kaug[:, gg, kb * 128:(kb + 1) * 128], rhs=rhs_q,
                            start=True, stop=True),
                            reads=qreads + [("kaug", gg, kb), ("kaugc", gg)], writes=[("st", i)])
                        mi = (0 if (tt == 0 and qb == 0) else 1) if i == 0 else 2
                        S.add("vector", lambda e, i=i, mi=mi: e.tensor_tensor(out=tmp[i][:], in0=st_[i][:], in1=msk[:, mi, :],
                                                                              op=ALU.add),
                              reads=[("st", i), "msk"], writes=[("tmp", i)])
                        S.add("scalar", lambda e, i=i, pt=pt: e.activation(out=pt[:, i, :], in_=tmp[i][:], func=AF.Exp),
                              reads=[("tmp", i)], writes=[ptk + (i,)])
                    for i, kb in enumerate((prev, cur)):
                        S.add("tensor", lambda e, i=i, kb=kb, gg=gg, pt=pt: e.matmul(
                            num[0:64, :], lhsT=vtok[:, kb, 64 * gg:64 * gg + 64], rhs=pt[:, i, :], start=(i == 0), stop=(i == 1)),
                            reads=[("vtok", kb), ptk + (i,)], writes=["num"])
                    for i in range(2):
                        S.add("tensor", lambda e, i=i, pt=pt: e.matmul(
                            den[0:64, :], lhsT=onesd[:], rhs=pt[:, i, :], start=(i == 0), stop=(i == 1)),
                            reads=["onesd", ptk + (i,)], writes=["den"])
                    S.add("vector", lambda e, gg=gg: e.tensor_tensor(
                        out=rec[:].rearrange("p (a b) -> p a b", a=4), in0=den[0:64, :].rearrange("p (a b) -> p a b", a=4),
                        in1=es[:, 4 * gg:4 * gg + 4].unsqueeze(2).to_broadcast([64, 4, 128]), op=ALU.add),
                        reads=["den", "es"], writes=["rec"])
                    S.add("vector", lambda e: e.reciprocal(out=rec[:], in_=rec[:]), reads=["rec"], writes=["rec"])
                    S.add("vector", lambda e, gg=gg, qb=qb: e.tensor_tensor(
                        out=oTt[:, 4 * gg:4 * gg + 4, qb * 128:(qb + 1) * 128], in0=num[0:64, :].rearrange("p (a b) -> p a b", a=4),
                        in1=rec[:].rearrange("p (a b) -> p a b", a=4), op=ALU.mult),
                        reads=["num", "rec"], writes=[("oTt", 4 * gg + i) for i in range(4)])
                    acount += 1
            for m in range(8):
                p = pcount[0] % 2
                pcount[0] += 1
                for hh in range(16):
                    S.add("tensor", lambda e, m=m, hh=hh, p=p: e.matmul(ps_p[p][:], lhsT=wo[:, hh, m * 128:(m + 1) * 128],
                                                                        rhs=oTt[:, hh, :], start=(hh == 0), stop=(hh == 15)),
                          reads=[("wo", hh), ("oTt", hh)], writes=[("psp", p)])
                S.add("vector", lambda e, m=m, p=p: e.tensor_tensor(out=xt[:, m, :], in0=ps_p[p][:], in1=xt[:, m, :], op=ALU.add),
                      reads=[("psp", p), ("xt", m)], writes=[("xt", m)])
            S.add("sync", lambda e, tt=tt: e.dma_start(out=ov[:, :, tt * 512:(tt + 1) * 512], in_=xt[:]), reads=XT, dma="xt_out")
        S.emit()
    return nc


def alibi_slopes(n):
    return 2.0 ** (-8.0 * np.arange(1, n + 1, dtype=np.float64) / n)


def swa_consts(T):
    TH = T + 128
    ktab, qtab = alibi_tabs(alibi_slopes(16), np.arange(T) + 128, np.arange(TH))
    qtab = np.ascontiguousarray(qtab.transpose(1, 0, 2))
    sl = np.arange(128)[:, None]
    tl = np.arange(128)[None, :]
    mprev = np.where(sl > tl, 0.0, NEGM).astype(np.float32)
    mcur = np.where(sl <= tl, 0.0, NEGM).astype(np.float32)
    t4 = lambda m: np.tile(m, (1, 4))
    m_mid = np.stack([t4(mprev), t4(mprev), t4(mcur)], axis=1)
    m_first = np.stack([np.full((128, 512), NEGM, np.float32), t4(mprev), t4(mcur)], axis=1)
    return ktab, qtab, np.ascontiguousarray(m_first), np.ascontiguousarray(m_mid)


def run_swa(xT_full, S_seq, gv, w_in, w_out, q_norm, k_norm, sinks):
    ntok = xT_full.shape[1]
    T = ntok // N_CORES
    key = ("swa", T)
    if key not in _NC_CACHE:
        _NC_CACHE[key] = build_swa(T)
    nc = _NC_CACHE[key]
    ktab, qtab, m_first, m_mid = swa_consts(T)
    common = {"g": gain_layout(gv), "w_in": np.ascontiguousarray(w_in, np.float32),
              "w_out": np.ascontiguousarray(w_out, np.float32),
              "gq": np.ascontiguousarray(np.asarray(q_norm, np.float32).reshape(64, 1)),
              "gk": np.ascontiguousarray(np.asarray(k_norm, np.float32).reshape(64, 1)),
              "sinks": np.ascontiguousarray(np.broadcast_to(np.asarray(sinks, np.float32).reshape(1, 16), (64, 16))),
              "ktab": ktab, "qtab": qtab}
    in_maps = []
    for c in range(N_CORES):
        t0 = c * T
        first = (t0 % S_seq) == 0
        xh = np.zeros((D, T + 128), np.float32)
        xh[:, 128:] = xT_full[:, t0:t0 + T]
        if not first:
            xh[:, :128] = xT_full[:, t0 - 128:t0]
        in_maps.append(dict(common, xT=xh, masks=m_first if first else m_mid))
    res = run_bass_kernel_spmd(nc, in_maps, core_ids=list(range(N_CORES)))
    return np.concatenate([r["oT"] for r in res.results], axis=1)


def build_diff(S_seq, NH, lam_init, capdist=None):
    nc = bass.Bass("TRN2", target_bir_lowering=False)
    NT = S_seq // 512
    with contextlib.ExitStack() as st:
        C = Ctx(nc, st)
        xT = C.din("xT", [D, S_seq]); g = C.din("g", [128, 8])
        w_in = C.din("w_in", [D, NH, 384])
        gq = C.din("gq", [64, 1]); gk = C.din("gk", [64, 1]); gsub = C.din("gsub", [128, 1])
        lam = C.din("lam", [128, 4, 64])
        ktab = C.din("ktab", [9, S_seq], BF16); qtab = C.din("qtab", [NH, 9, S_seq], BF16)
        masks = C.din("masks", [128, 4, 512])
        oT = C.dout("oT", [NH, 128, S_seq])
        xv = xT.rearrange("(c p) t -> p c t", p=128)
        wiv = w_in.rearrange("(c p) h f -> p c h f", p=128)

        w = C.sb("w", [128, 8, 384], BF16)
        stage = [C.sb("stage%d" % i, [128, 384], F32) for i in range(2)]
        gs = C.sb("gs", [128, 8], F32); epsb = C.sb("epsb", [128, 1], F32)
        ones = C.sb("ones", [128, 128], BF16); ones64 = C.sb("ones64", [64, 64], BF16)
        ones128 = C.sb("ones128", [128, 128], BF16); onesd = C.sb("onesd", [128, 128], BF16)
        gqs = C.sb("gqs", [64, 1], F32); gks = C.sb("gks", [64, 1], F32); gsubs = C.sb("gsubs", [128, 1], F32)
        lams = C.sb("lams", [128, 4, 64], F32); lp = C.sb("lp", [128, 2, 64], F32); l2 = C.sb("l2", [128, 2], F32)
        nlam = C.sb("nlam", [128, 1], F32)
        msk = C.sb("msk", [128, 4, 512], F32)
        xt = C.sb("xt", [128, 8, 512], F32); h = C.sb("h", [128, 8, 512], BF16)
        rstd = C.sb("rstd", [128, 512], F32)
        sqh = C.sb("sqh", [128, 512], BF16); rs = C.sb("rs", [128, 512], F32)
        kaug = C.sb("kaug", [73, 2, S_seq], BF16)
        vtok = C.sb("vtok", [128, S_seq // 128, 128], BF16)
        qaug = C.sb("qaug", [73, 2, 512], BF16)
        NPT = 4
        PT = [C.sb("PT%d" % i, [128, 512], BF16) for i in range(NPT)]
        tmp = [C.sb("tmp%d" % i, [128, 512], F32) for i in range(2)]
        oc = [C.sb("oc%d" % i, [128, 512], F32) for i in range(2)]
        rec = C.sb("rec", [128, 512], F32); ot = C.sb("ot", [128, 512], F32)
        ps_s = C.ps("ps_s"); ps_p = [C.ps("ps_p%d" % i) for i in range(2)]
        st_ = [C.ps("st%d" % i) for i in range(3)]; num = C.ps("num"); den = C.ps("den")
        ps_n = ps_s

        S = Sched(nc)
        S.add("vector", lambda e: e.memset(ones[:], 1.0 / D), writes=["ones"])
        S.add("vector", lambda e: e.memset(ones64[:], 1.0 / 64), writes=["ones64"])
        S.add("vector", lambda e: e.memset(ones128[:], 1.0 / 128), writes=["ones128"])
        S.add("vector", lambda e: e.memset(onesd[:], 1.0), writes=["onesd"])
        S.add("vector", lambda e: e.memset(epsb[:], EPS), writes=["epsb"])
        S.add("sync", lambda e: e.dma_start(out=gs[:], in_=g), writes=["gs"], dma="c0")
        S.add("sync", lambda e: e.dma_start(out=gqs[:], in_=gq), writes=["gcols"], dma="c1")
        S.add("sync", lambda e: e.dma_start(out=gks[:], in_=gk), writes=["gcols"], dma="c1")
        S.add("sync", lambda e: e.dma_start(out=gsubs[:], in_=gsub), writes=["gcols"], dma="c1")
        S.add("sync", lambda e: e.dma_start(out=lams[:], in_=lam), writes=["lams"], dma="c2")
        S.add("sync", lambda e: e.dma_start(out=msk[:], in_=masks), writes=["msk"], dma="c3")
        for cc in range(2):
            S.add("sync", lambda e, cc=cc: e.dma_start(out=kaug[64:73, cc, :], in_=ktab), writes=[("kaugc", cc)], dma="c4")
        S.add("vector", lambda e: e.tensor_scalar(out=gqs[:], in0=gqs[:], scalar1=0.125, scalar2=None, op0=ALU.mult),
              reads=["gcols"], writes=["gcols"])
        S.add("vector", lambda e: e.tensor_scalar(out=gsubs[:], in0=gsubs[:], scalar1=1.0 - lam_init, scalar2=None, op0=ALU.mult),
              reads=["gcols"], writes=["gcols"])
        S.add("vector", lambda e: e.tensor_tensor(out=lp[:], in0=lams[:, 0:4:2, :], in1=lams[:, 1:4:2, :], op=ALU.mult),
              reads=["lams"], writes=["lp"])
        S.add("vector", lambda e: e.reduce_sum(out=l2[:], in_=lp[:], axis=mybir.AxisListType.X), reads=["lp"], writes=["l2"])
        S.add("scalar", lambda e: e.activation(out=l2[:], in_=l2[:], func=AF.Exp), reads=["l2"], writes=["l2"])
        S.add("vector", lambda e: e.tensor_tensor(out=nlam[:], in0=l2[:, 1:2], in1=l2[:, 0:1], op=ALU.subtract),
              reads=["l2"], writes=["nlam"])
        S.add("vector", lambda e: e.tensor_scalar(out=nlam[:], in0=nlam[:], scalar1=-lam_init, scalar2=None, op0=ALU.add),
              reads=["nlam"], writes=["nlam"])
        XT = [("xt", c) for c in range(8)]
        pcount = [0]
        kst = 0
        stc = 0
        ptc = 0
        for hl in range(NH):
            for c in range(8):
                sg = stage[kst % 2]
                S.add("sync", lambda e, sg=sg, c=c, hl=hl: e.dma_start(out=sg[:], in_=wiv[:, c, hl, :]),
                      writes=[("stage", kst % 2)], dma=("stage", kst % 2))
                S.add(["vector", "gpsimd"][kst % 2], lambda e, sg=sg, c=c: e.tensor_scalar(
                    out=w[:, c, :], in0=sg[:], scalar1=gs[:, c:c + 1], scalar2=None, op0=ALU.mult),
                    reads=[("stage", kst % 2), "gs"], writes=[("w", c)])
                kst += 1

            def proj_fm(col0, M, n=512):
                p = pcount[0] % 2
                pcount[0] += 1
                for c in range(8):
                    S.add("tensor", lambda e, c=c, p=p: e.matmul(ps_p[p][0:M, 0:n], lhsT=w[:, c, col0:col0 + M], rhs=h[:, c, 0:n],
                                                                 start=(c == 0), stop=(c == 7)),
                          reads=[("w", c), ("h", c)], writes=[("psp", p)])
                return p

            for tt in range(NT):
                tok0 = 512 * tt
                S.add("sync", lambda e, tok0=tok0: e.dma_start(out=xt[:], in_=xv[:, :, tok0:tok0 + 512]), writes=XT, dma="xt_in")
                S.add("gpsimd", lambda e, tok0=tok0, hl=hl: e.dma_start(
                    out=qaug[64:73, :, :], in_=qtab[hl, :, tok0:tok0 + 512].unsqueeze(1).to_broadcast([9, 2, 512])),
                    writes=["qaugc"], dma="qc")
                emit_rmsnorm(S, xt, h, h, ps_s, rstd, ones, epsb, 512, sqtag="h")
                for cc in range(2):
                    p = proj_fm(128 + 64 * cc, 64)
                    emit_headnorm(S, ps_p[p][0:64, :], kaug[0:64, cc, tok0:tok0 + 512], gks[:, 0:1], sqh, ps_n, rs, ones64, epsb,
                                  512, rd=[("psp", p)], wr=[("kaug", cc, 4 * tt + b) for b in range(4)])
                for b in range(4):
                    p = pcount[0] % 2
                    pcount[0] += 1
                    for c in range(8):
                        S.add("tensor", lambda e, c=c, p=p, b=b: e.matmul(ps_p[p][:, 0:128], lhsT=h[:, c, b * 128:(b + 1) * 128],
                                                                          rhs=w[:, c, 256:384], start=(c == 0), stop=(c == 7)),
                              reads=[("w", c), ("h", c)], writes=[("psp", p)])
                    S.add("scalar", lambda e, p=p, b=b, tt=tt: e.copy(out=vtok[:, 4 * tt + b, :], in_=ps_p[p][:, 0:128]),
                          reads=[("psp", p)], writes=[("vtok", 4 * tt + b)])
                for cc in range(2):
                    p = proj_fm(64 * cc, 64)
                    emit_headnorm(S, ps_p[p][0:64, :], qaug[0:64, cc, :], gqs[:, 0:1], sqh, ps_n, rs, ones64, epsb,
                                  512, rd=[("psp", p)], wr=[("qaug", cc)])
                for cc in range(2):
                    nj = 4 * tt + 4
                    LA = 2
                    slots = {}
                    jmin = 0
                    if capdist is not None and capdist[hl] is not None and 512 * tt - 127 - capdist[hl] >= 0:
                        jmin = (512 * tt - 127 - capdist[hl]) // 128 + 1

                    def emit_front(j, cc=cc, tt=tt, slots=slots):
                        nonlocal stc, ptc
                        sti = stc % 3
                        stc += 1
                        pti = ptc % NPT
                        ptc += 1
                        slots[j] = pti
                        S.add("tensor", lambda e, j=j, cc=cc, sti=sti: e.matmul(
                            st_[sti][:], lhsT=kaug[:, cc, j * 128:(j + 1) * 128], rhs=qaug[:, cc, :], start=True, stop=True),
                            reads=[("qaug", cc), "qaugc", ("kaug", cc, j), ("kaugc", cc)], writes=[("st", sti)])
                        if j >= 4 * tt:
                            bvar = j - 4 * tt
                            ti = j % 2
                            S.add("vector", lambda e, sti=sti, bvar=bvar, ti=ti: e.tensor_tensor(
                                out=tmp[ti][:], in0=st_[sti][:], in1=msk[:, bvar, :], op=ALU.add),
                                reads=[("st", sti), "msk"], writes=[("tmp", ti)])
                            S.add("scalar", lambda e, ti=ti, pti=pti: e.activation(out=PT[pti][:], in_=tmp[ti][:], func=AF.Exp),
                                  reads=[("tmp", ti)], writes=[("PT", pti)])
                        else:
                            S.add("scalar", lambda e, sti=sti, pti=pti: e.activation(out=PT[pti][:], in_=st_[sti][:], func=AF.Exp),
                                  reads=[("st", sti)], writes=[("PT", pti)])

                    def emit_back(j, nj=nj, slots=slots, jmin=jmin):
                        pti = slots[j]
                        S.add("tensor", lambda e, j=j, pti=pti, nj=nj, jmin=jmin: e.matmul(
                            num[:], lhsT=vtok[:, j, :], rhs=PT[pti][:], start=(j == jmin), stop=(j == nj - 1)),
                            reads=[("vtok", j), ("PT", pti)], writes=["num"])
                        S.add("tensor", lambda e, j=j, pti=pti, nj=nj, jmin=jmin: e.matmul(
                            den[:], lhsT=onesd[:], rhs=PT[pti][:], start=(j == jmin), stop=(j == nj - 1)),
                            reads=["onesd", ("PT", pti)], writes=["den"])

                    for j in range(jmin, nj):
                        emit_front(j)
                        if j - jmin >= LA:
                            emit_back(j - LA)
                    for j in range(max(jmin, nj - LA), nj):
                        emit_back(j)
                    S.add("vector", lambda e: e.reciprocal(out=rec[:], in_=den[:]), reads=["den"], writes=["rec"])
                    S.add("vector", lambda e, cc=cc: e.tensor_tensor(out=oc[cc][:], in0=num[:], in1=rec[:], op=ALU.mult),
                          reads=["num", "rec"], writes=[("oc", cc)])
                S.add("vector", lambda e: e.scalar_tensor_tensor(out=ot[:], in0=oc[1][:], scalar=nlam[:, 0:1], in1=oc[0][:],
                                                                 op0=ALU.mult, op1=ALU.add),
                      reads=[("oc", 0), ("oc", 1), "nlam"], writes=["ot"])
                emit_headnorm(S, ot[:], ot[:], gsubs[:, 0:1], sqh, ps_n, rs, ones128, epsb, 512, rd=["ot"], wr=["ot"], P=128)
                S.add("sync", lambda e, tok0=tok0, hl=hl: e.dma_start(out=oT[hl, :, tok0:tok0 + 512], in_=ot[:]),
                      reads=["ot"], dma="o_out")
        S.emit()
    return nc


def diff_masks():
    sl = np.arange(128)[:, None]
    tl = np.arange(128)[None, :]
    caus = np.where(sl <= tl, 0.0, NEGM).astype(np.float32)
    m = np.zeros((128, 4, 4, 128), np.float32)
    for b in range(4):
        for a in range(4):
            m[:, b, a, :] = 0.0 if a > b else (caus if a == b else NEGM)
    return np.ascontiguousarray(m.reshape(128, 4, 512))


def run_diff(xT_full, B, S_seq, gv, w_in, q_norm, k_norm, lam, subln, lam_init):
    NH = 8 * B // N_CORES
    slopes = alibi_slopes(8)
    per_b = N_CORES // B
    capdist = []
    for i in range(NH):
        smin = min(slopes[i * per_b + p] for p in range(per_b))
        cd = int(math.ceil((104.0 + 64.0) / smin))
        capdist.append(cd if cd < S_seq else None)
    key = ("diff", S_seq, NH, lam_init)
    if key not in _NC_CACHE:
        _NC_CACHE[key] = build_diff(S_seq, NH, lam_init, capdist)
    nc = _NC_CACHE[key]
    pos = np.arange(S_seq)
    ktab, qtab_all = alibi_tabs(slopes, pos, pos)
    w_in = np.asarray(w_in, np.float32)
    common = {"g": gain_layout(gv),
              "gq": np.ascontiguousarray(np.asarray(q_norm, np.float32).reshape(64, 1)),
              "gk": np.ascontiguousarray(np.asarray(k_norm, np.float32).reshape(64, 1)),
              "gsub": np.ascontiguousarray(np.asarray(subln, np.float32).reshape(128, 1)),
              "lam": np.ascontiguousarray(np.broadcast_to(np.asarray(lam, np.float32).reshape(1, 4, 64), (128, 4, 64))),
              "ktab": ktab, "masks": diff_masks()}
    in_maps = []
    for c in range(N_CORES):
        b = c // per_b
        heads = [i * per_b + (c % per_b) for i in range(NH)]
        wsel = np.stack([np.concatenate([w_in[:, hh * 128:(hh + 1) * 128], w_in[:, 1024 + hh * 128:1024 + (hh + 1) * 128],
                                         w_in[:, 2048 + hh * 128:2048 + (hh + 1) * 128]], axis=1) for hh in heads], axis=1)
        in_maps.append(dict(common, xT=np.ascontiguousarray(xT_full[:, b * S_seq:(b + 1) * S_seq]),
                            w_in=np.ascontiguousarray(wsel), qtab=np.ascontiguousarray(qtab_all[heads])))
    res = run_bass_kernel_spmd(nc, in_maps, core_ids=list(range(N_CORES)))
    out = np.zeros((B, 1024, S_seq), np.float32)
    for c in range(N_CORES):
        b = c // per_b
        for i in range(NH):
            hh = i * per_b + (c % per_b)
            out[b, hh * 128:(hh + 1) * 128, :] = res.results[c]["oT"][i]
    return out


def build_nsa(S_seq, debug=None):
    nc = bass.Bass("TRN2", target_bir_lowering=False)
    NT = S_seq // 512
    NBLK = S_seq // 128
    NCMP = S_seq // 16 - 1
    NCT = (NCMP + 127) // 128
    NSEL = S_seq // 64
    with contextlib.ExitStack() as st:
        C = Ctx(nc, st)
        xT = C.din("xT", [D, S_seq]); g = C.din("g", [128, 8])
        w_in = C.din("w_in", [D, 652])
        gq = C.din("gq", [64, 1]); gk3 = C.din("gk3", [64, 3])
        cpos = C.din("cpos", [128, 32])
        cw1 = C.din("cw1", [2, 64, 32, 256])
        cw2 = C.din("cw2", [128, 2, 2, 64])
        ktab = C.din("ktab", [9, S_seq], BF16); qtab = C.din("qtab", [9, 4, S_seq], BF16)
        kctab = C.din("kctab", [9, NCT * 128], BF16)
        cmask = C.din("cmask", [128, 17, 128]); wmask = C.din("wmask", [128, 2, 128])
        tmpl = C.din("tmpl", [128, 2, 2 * NSEL])
        ovl = C.din("ovl", [128, NCT, NSEL], BF16)
        i4 = C.din("i4", [128, 512], BF16)
        selrow = C.din("selrow", [12, 12, 64])
        oT = C.dout("oT", [256, S_seq])
        xv = xT.rearrange("(c p) t -> p c t", p=128)
        wiv = w_in.rearrange("(c p) f -> p c f", p=128)

        w = C.sb("w", [128, 8, 652], BF16)
        stage = [C.sb("stage%d" % i, [128, 1024], F32) for i in range(2)]
        gs = C.sb("gs", [128, 8], F32); epsb = C.sb("epsb", [128, 1], F32)
        ones = C.sb("ones", [128, 128], BF16); ones64 = C.sb("ones64", [64, 64], BF16)
        onesd = C.sb("onesd", [128, 128], BF16)
        gqs = C.sb("gqs", [64, 1], F32); gk3s = C.sb("gk3s", [64, 3], F32)
        w1sb = C.sb("w1sb", [128, 32, 256], BF16)
        w2sb = C.sb("w2sb", [128, 2, 2, 64], BF16)
        posT = C.sb("posT", [128, 32], BF16)
        cposb = C.sb("cposb", [128, 2, 2], F32)
        big = C.sb("big", [128, S_seq], BF16)
        kcmp = C.sb("kcmp", [73, NCT * 128], BF16)
        vcmp = C.sb("vcmp", [128, NCT, 64], BF16)
        ov = C.sb("ov", [128, NCT, NSEL], BF16)
        vs_tok = C.sb("vs_tok", [128, NBLK, 65], BF16)
        ones1 = C.sb("ones1", [128, 64], F32)
        kwr = C.sb("kwr", [73, 8 * 128], BF16)
        vwr = C.sb("vwr", [128, 8, 64], BF16)
        qaug = C.sb("qaug", [73, 4, 512], BF16)
        xt = C.sb("xt", [128, 8, 512], F32); h = C.sb("h", [128, 8, 512], BF16)
        rstd = C.sb("rstd", [128, 512], F32)
        sqh = C.sb("sqh", [128, 512], BF16); rs = C.sb("rs", [128, 512], F32)
        gel = [C.sb("gel%d" % i, [128, 512], F32) for i in range(3)]
        gelu = C.sb("gelu", [128, 2, 512], BF16)
        PTc = C.sb("PTc", [128, NCT, 512], BF16)
        NPT = 4
        PT = [C.sb("PT%d" % i, [128, 512], BF16) for i in range(NPT)]
        tmp = [C.sb("tmp%d" % i, [128, 512], F32) for i in range(2)]
        cm = C.sb("cm", [128, 17, 128], F32); wm = C.sb("wm", [128, 2, 128], F32)
        tm = C.sb("tm", [128, 2, 2 * NSEL], F32)
        i4s = C.sb("i4s", [128, 512], BF16)
        srow = C.sb("srow", [12, 12, 64], F32)
        rden = C.sb("rden", [128, 512], F32)
        imp2 = C.sb("imp2", [128, NSEL], F32); imp3 = C.sb("imp3", [128, NSEL], F32)
        m8 = C.sb("m8", [128, 8], F32)
        negsel = C.sb("negsel", [128, NSEL], BF16)
        negx = [C.sb("negx%d" % i, [128, 16, 64], BF16) for i in range(2)]
        gT = C.sb("gT", [12, 512], F32)
        obr = [C.sb("obr%d" % i, [64, 512], F32) for i in range(3)]
        oTt = C.sb("oTt", [64, 4, 512], F32)
        ps_s = C.ps("ps_s"); ps_p = [C.ps("ps_p%d" % i) for i in range(2)]
        st_ = [C.ps("st%d" % i) for i in range(3)]; num = C.ps("num"); den = C.ps("den")
        imp = ps_s
        ps_n = ps_s

        S = Sched(nc)
        S.add("vector", lambda e: e.memset(ones[:], 1.0 / D), writes=["ones"])
        S.add("vector", lambda e: e.memset(ones64[:], 1.0 / 64), writes=["ones64"])
        S.add("vector", lambda e: e.memset(onesd[:], 1.0), writes=["onesd"])
        S.add("vector", lambda e: e.memset(epsb[:], EPS), writes=["epsb"])
        S.add("vector", lambda e: e.memset(kcmp[0:64, :], 0.0), writes=["kcmp"])
        S.add("vector", lambda e: e.memset(vcmp[:], 0.0), writes=["vcmp"])
        S.add("gpsimd", lambda e: e.memset(vs_tok[:, :, 64:65], 1.0), writes=["vsones"])
        S.add("vector", lambda e: e.memset(ones1[:], 1.0), writes=["ones1"])
        ld = [(gs, g, "gs"), (gqs, gq, "gcols"), (gk3s, gk3, "gcols"), (cm, cmask, "cm"), (wm, wmask, "wm"), (tm, tmpl, "tm"),
              (i4s, i4, "i4s"), (srow, selrow, "srow"), (ov, ovl, "ov")]
        for n_, (dst, src, key) in enumerate(ld):
            S.add("sync", lambda e, dst=dst, src=src: e.dma_start(out=dst[:], in_=src), writes=[key], dma="c%d" % n_)
        S.add("sync", lambda e: e.dma_start(out=kcmp[64:73, :], in_=kctab), writes=["kcmpc"], dma="ck")
        S.add("vector", lambda e: e.tensor_scalar(out=gqs[:], in0=gqs[:], scalar1=0.125, scalar2=None, op0=ALU.mult),
              reads=["gcols"], writes=["gcols"])
        kst = emit_weight_load(S, wiv, w, gs, stage, 652, chunk=652)
        for kv in range(2):
            for pg in range(8):
                sg = stage[kst % 2]
                S.add("sync", lambda e, sg=sg, kv=kv, pg=pg: e.dma_start(
                    out=sg[64 * kv:64 * kv + 64, :].rearrange("d (p f) -> d p f", p=4), in_=cw1[kv, :, 4 * pg:4 * pg + 4, :]),
                    writes=[("stage", kst % 2)], dma=("stage", kst % 2))
                S.add(["vector", "gpsimd"][kst % 2], lambda e, sg=sg, kv=kv, pg=pg: e.tensor_copy(
                    out=w1sb[64 * kv:64 * kv + 64, 4 * pg:4 * pg + 4, :],
                    in_=sg[64 * kv:64 * kv + 64, :].rearrange("d (p f) -> d p f", p=4)),
                    reads=[("stage", kst % 2)], writes=["w1sb"])
                kst += 1
        sg = stage[kst % 2]
        S.add("sync", lambda e, sg=sg: e.dma_start(out=sg[:, 0:256].rearrange("p (a b d) -> p a b d", a=2, b=2), in_=cw2),
              writes=[("stage", kst % 2)], dma=("stage", kst % 2))
        S.add("vector", lambda e, sg=sg: e.tensor_copy(out=w2sb[:], in_=sg[:, 0:256].rearrange("p (a b d) -> p a b d", a=2, b=2)),
              reads=[("stage", kst % 2)], writes=["w2sb"])
        kst += 1
        sg = stage[kst % 2]
        S.add("sync", lambda e, sg=sg: e.dma_start(out=sg[:, 0:32], in_=cpos), writes=[("stage", kst % 2)], dma=("stage", kst % 2))
        S.add("vector", lambda e, sg=sg: e.tensor_copy(out=posT[:], in_=sg[:, 0:32]), reads=[("stage", kst % 2)], writes=["posT"])
        kst += 1
        for kv in range(2):
            for hf in range(2):
                for p in range(32):
                    S.add("tensor", lambda e, kv=kv, hf=hf, p=p: e.matmul(
                        ps_p[0][:, 0:1], lhsT=w1sb[64 * kv:64 * kv + 64, p, 128 * hf:128 * hf + 128],
                        rhs=posT[64 * kv:64 * kv + 64, p:p + 1], start=(p == 0), stop=(p == 31)),
                        reads=["w1sb", "posT"], writes=[("psp", 0)])
                S.add("vector", lambda e, kv=kv, hf=hf: e.tensor_copy(out=cposb[:, kv, hf:hf + 1], in_=ps_p[0][:, 0:1]),
                      reads=[("psp", 0)], writes=["cposb"])
        XT = [("xt", c) for c in range(8)]
        pcount = [0]

        def proj_fm(col0, M, n=512):
            p = pcount[0] % 2
            pcount[0] += 1
            for c in range(8):
                S.add("tensor", lambda e, c=c, p=p: e.matmul(ps_p[p][0:M, 0:n], lhsT=w[:, c, col0:col0 + M], rhs=h[:, c, 0:n],
                                                             start=(c == 0), stop=(c == 7)),
                      reads=[("w", c), ("h", c)], writes=[("psp", p)])
            return p

        def proj_tok(col0, ncol, b):
            p = pcount[0] % 2
            pcount[0] += 1
            for c in range(8):
                S.add("tensor", lambda e, c=c, p=p: e.matmul(ps_p[p][:, 0:ncol], lhsT=h[:, c, b * 128:(b + 1) * 128],
                                                             rhs=w[:, c, col0:col0 + ncol], start=(c == 0), stop=(c == 7)),
                      reads=[("w", c), ("h", c)], writes=[("psp", p)])
            return p

        for tt in range(NT):
            tok0 = 512 * tt
            S.add("sync", lambda e, tok0=tok0: e.dma_start(out=xt[:], in_=xv[:, :, tok0:tok0 + 512]), writes=XT, dma="xt_in")
            emit_rmsnorm(S, xt, h, h, ps_s, rstd, ones, epsb, 512, sqtag="h")
            p = proj_fm(256, 128)
            S.add("scalar", lambda e, p=p, tok0=tok0: e.copy(out=big[:, tok0:tok0 + 512], in_=ps_p[p][:]),
                  reads=[("psp", p)], writes=[("big", 4 * tt + b) for b in range(4)])
        BIGALL = [("big", b) for b in range(NBLK)]
        for n0 in range(0, NCMP, 512):
            N = min(512, NCMP - n0)
            for kv in range(2):
                for hf in range(2):
                    pp = pcount[0] % 2
                    pcount[0] += 1
                    for p in range(32):
                        a0 = 16 * n0 + p
                        S.add("tensor", lambda e, kv=kv, hf=hf, p=p, pp=pp, a0=a0, N=N: e.matmul(
                            ps_p[pp][:, 0:N], lhsT=w1sb[64 * kv:64 * kv + 64, p, 128 * hf:128 * hf + 128],
                            rhs=big[64 * kv:64 * kv + 64, a0:a0 + 16 * (N - 1) + 1:16], start=(p == 0), stop=(p == 31)),
                            reads=["w1sb"] + BIGALL, writes=[("psp", pp)])
                    S.add("scalar", lambda e, kv=kv, hf=hf, pp=pp, N=N: e.activation(
                        out=gel[0][:, 0:N], in_=ps_p[pp][:, 0:N], func=AF.Identity, bias=cposb[:, kv, hf:hf + 1]),
                        reads=[("psp", pp), "cposb"], writes=["gel0"])
                    S.add("vector", lambda e, N=N: e.tensor_tensor(out=gel[1][:, 0:N], in0=gel[0][:, 0:N], in1=gel[0][:, 0:N], op=ALU.mult),
                          reads=["gel0"], writes=["gel1"])
                    S.add("vector", lambda e, N=N: e.tensor_scalar(out=gel[1][:, 0:N], in0=gel[1][:, 0:N], scalar1=0.044715, scalar2=1.0,
                                                                   op0=ALU.mult, op1=ALU.add), reads=["gel1"], writes=["gel1"])
                    S.add("vector", lambda e, N=N: e.tensor_tensor(out=gel[1][:, 0:N], in0=gel[1][:, 0:N], in1=gel[0][:, 0:N], op=ALU.mult),
                          reads=["gel1", "gel0"], writes=["gel1"])
                    S.add("scalar", lambda e, N=N: e.activation(out=gel[2][:, 0:N], in_=gel[1][:, 0:N], func=AF.Sigmoid,
                                                                scale=1.5957691216057308), reads=["gel1"], writes=["gel2"])
                    S.add("vector", lambda e, hf=hf, N=N: e.tensor_tensor(out=gelu[:, hf, 0:N], in0=gel[0][:, 0:N], in1=gel[2][:, 0:N],
                                                                          op=ALU.mult), reads=["gel0", "gel2"], writes=[("gelu", hf)])
                if kv == 0:
                    pp = pcount[0] % 2
                    pcount[0] += 1
                    for hf in range(2):
                        S.add("tensor", lambda e, hf=hf, pp=pp, N=N: e.matmul(ps_p[pp][0:64, 0:N], lhsT=w2sb[:, 0, hf, :], rhs=gelu[:, hf, 0:N],
                                                                              start=(hf == 0), stop=(hf == 1)),
                              reads=["w2sb", ("gelu", hf)], writes=[("psp", pp)])
                    emit_headnorm(S, ps_p[pp][0:64, 0:N], kcmp[0:64, n0:n0 + N], gk3s[:, 0:1], sqh, ps_n, rs, ones64, epsb, N,
                                  rd=[("psp", pp)], wr=["kcmp"])
                else:
                    for nt in range(0, N, 128):
                        M = min(128, N - nt)
                        pp = pcount[0] % 2
                        pcount[0] += 1
                        for hf in range(2):
                            S.add("tensor", lambda e, hf=hf, pp=pp, nt=nt, M=M: e.matmul(
                                ps_p[pp][0:M, 0:64], lhsT=gelu[:, hf, nt:nt + M], rhs=w2sb[:, 1, hf, :], start=(hf == 0), stop=(hf == 1)),
                                reads=["w2sb", ("gelu", hf)], writes=[("psp", pp)])
                        S.add("scalar", lambda e, pp=pp, nt=nt, M=M, n0=n0: e.copy(out=vcmp[0:M, (n0 + nt) // 128, :], in_=ps_p[pp][0:M, 0:64]),
                              reads=[("psp", pp)], writes=["vcmp"])
        S.add("sync", lambda e: e.dma_start(out=big[64:73, :], in_=ktab), reads=BIGALL, writes=BIGALL + ["bigc"], dma="ck2")
        stc = [0]
        ptc = [0]

        def attn_steps(steps, vM, final_name, with_den=True):
            n = len(steps)
            LA = 2
            slots = {}

            def emit_front(i):
                sp = steps[i]
                sti = stc[0] % 3
                stc[0] += 1
                if sp.get("pt") is None:
                    pti = ptc[0] % NPT
                    ptc[0] += 1
                    pt, ptkey = PT[pti][:], ("PT", pti)
                else:
                    pt, ptkey = sp["pt"]
                slots[i] = (pt, ptkey)
                if sp.get("pre") is not None:
                    S.add(sp["pre"][0], sp["pre"][1], reads=sp["pre"][2], writes=sp["pre"][3])
                S.add("tensor", lambda e, sp=sp, sti=sti: e.matmul(
                    st_[sti][:].rearrange("p (a b) -> p a b", a=4), lhsT=sp["lhsT_k"], rhs=sp["rhs_q"], start=True,
                    stop=(sp.get("extra") is None)),
                    reads=sp["qreads"] + sp["kreads"], writes=[("st", sti)])
                if sp.get("extra") is not None:
                    xl, xr, xreads = sp["extra"]
                    S.add("tensor", lambda e, xl=xl, xr=xr, sti=sti: e.matmul(st_[sti][:], lhsT=xl, rhs=xr, start=False, stop=True),
                          reads=xreads, writes=[("st", sti)])
                if sp.get("mask") is not None:
                    ti = i % 2
                    S.add("vector", lambda e, sp=sp, sti=sti, ti=ti: e.tensor_tensor(
                        out=tmp[ti][:].rearrange("p (a b) -> p a b", a=4), in0=st_[sti][:].rearrange("p (a b) -> p a b", a=4),
                        in1=sp["mask"].unsqueeze(1).to_broadcast([128, 4, 128]), op=ALU.add),
                        reads=[("st", sti)] + sp["mreads"], writes=[("tmp", ti)])
                    S.add("scalar", lambda e, ti=ti, pt=pt: e.activation(out=pt, in_=tmp[ti][:], func=AF.Exp),
                          reads=[("tmp", ti)], writes=[ptkey])
                else:
                    S.add("scalar", lambda e, sti=sti, pt=pt: e.activation(out=pt, in_=st_[sti][:], func=AF.Exp),
                          reads=[("st", sti)], writes=[ptkey])

            def emit_back(i):
                sp = steps[i]
                pt, ptkey = slots[i]
                S.add("tensor", lambda e, sp=sp, pt=pt, i=i: e.matmul(num[0:vM, :], lhsT=sp["v_lhsT"], rhs=pt, start=(i == 0), stop=(i == n - 1)),
                      reads=sp["vreads"] + [ptkey], writes=["num"])
                if with_den:
                    S.add("tensor", lambda e, pt=pt, i=i: e.matmul(den[:], lhsT=onesd[:], rhs=pt, start=(i == 0), stop=(i == n - 1)),
                          reads=["onesd", ptkey], writes=["den"])

            for i in range(n):
                emit_front(i)
                if i >= LA:
                    emit_back(i - LA)
            for i in range(max(0, n - LA), n):
                emit_back(i)

        for tt in range(NT):
            tok0 = 512 * tt
            S.add("sync", lambda e, tok0=tok0: e.dma_start(out=xt[:], in_=xv[:, :, tok0:tok0 + 512]), writes=XT, dma="xt_in")
            S.add("gpsimd", lambda e, tok0=tok0: e.dma_start(out=qaug[64:73, :, :], in_=qtab[:, :, tok0:tok0 + 512]),
                  writes=["qaugc"], dma="qc")
            for b in range(4):
                slot = (4 * tt + b) % 8
                S.add("gpsimd", lambda e, slot=slot, b=b, tok0=tok0: e.dma_start(
                    out=kwr[64:73, slot * 128:(slot + 1) * 128], in_=ktab[:, tok0 + b * 128:tok0 + (b + 1) * 128]),
                    writes=[("kwrc", slot)], dma=("kwc", slot))
            emit_rmsnorm(S, xt, h, h, ps_s, rstd, ones, epsb, 512, sqtag="h")
            p = proj_fm(384, 64)
            emit_headnorm(S, ps_p[p][0:64, :], big[0:64, tok0:tok0 + 512], gk3s[:, 1:2], sqh, ps_n, rs, ones64, epsb, 512,
                          rd=[("psp", p)], wr=[("big", 4 * tt + b) for b in range(4)])
            p = proj_fm(512, 64)
            s0 = (4 * tt) % 8
            emit_headnorm(S, ps_p[p][0:64, :], kwr[0:64, s0 * 128:(s0 + 4) * 128], gk3s[:, 2:3], sqh, ps_n, rs, ones64, epsb, 512,
                          rd=[("psp", p)], wr=[("kwr", s0 + b) for b in range(4)])
            for b in range(4):
                p = proj_tok(448, 64, b)
                S.add("scalar", lambda e, p=p, b=b, tt=tt: e.copy(out=vs_tok[:, 4 * tt + b, 0:64], in_=ps_p[p][:, 0:64]),
                      reads=[("psp", p)], writes=[("vs", 4 * tt + b)])
                p = proj_tok(576, 64, b)
                S.add("scalar", lambda e, p=p, b=b, s0=s0: e.copy(out=vwr[:, s0 + b, :], in_=ps_p[p][:, 0:64]),
                      reads=[("psp", p)], writes=[("vwr", s0 + b)])
            for gi in range(4):
                p = proj_fm(64 * gi, 64)
                emit_headnorm(S, ps_p[p][0:64, :], qaug[0:64, gi, :], gqs[:, 0:1], sqh, ps_n, rs, ones64, epsb, 512,
                              rd=[("psp", p)], wr=[("qaug", gi)])
            p = proj_fm(640, 12)
            S.add("scalar", lambda e, p=p: e.activation(out=gT[:], in_=ps_p[p][0:12, :], func=AF.Sigmoid),
                  reads=[("psp", p)], writes=["gT"])
            qreads = [("qaug", gi) for gi in range(4)] + ["qaugc"]
            for ql in range(4):
                qb = 4 * tt + ql
                rhs_q = qaug[:, :, ql * 128:(ql + 1) * 128]
                nkt = min(NCT, qb // 16 + 1)
                steps = []
                for kt in range(nkt):
                    dl = qb - 16 * kt
                    steps.append(dict(lhsT_k=kcmp[:, kt * 128:(kt + 1) * 128], rhs_q=rhs_q, qreads=qreads, kreads=["kcmp", "kcmpc"],
                                      mask=(cm[:, dl, :] if dl <= 16 else None), mreads=["cm"],
                                      v_lhsT=vcmp[:, kt, :], vreads=["vcmp"], pt=(PTc[:, kt, :], ("PTc", kt))))
                attn_steps(steps, 64, "cmp")
                S.add("vector", lambda e: e.tensor_scalar(out=rden[:], in0=den[:], scalar1=1e-30, scalar2=None, op0=ALU.max),
                      reads=["den"], writes=["rden"])
                S.add("vector", lambda e: e.reciprocal(out=rden[:], in_=rden[:]), reads=["rden"], writes=["rden"])
                S.add("vector", lambda e: e.tensor_tensor(out=obr[0][:], in0=num[0:64, :], in1=rden[0:64, :], op=ALU.mult),
                      reads=["num", "rden"], writes=[("obr", 0)])
                for kt in range(nkt):
                    S.add("gpsimd", lambda e, kt=kt: e.tensor_tensor(out=PTc[:, kt, :], in0=PTc[:, kt, :], in1=rden[:], op=ALU.mult),
                          reads=[("PTc", kt), "rden"], writes=[("PTc", kt)])
                    for gi in range(4):
                        S.add("tensor", lambda e, kt=kt, gi=gi, nkt=nkt: e.matmul(
                            imp[:, 0:NSEL], lhsT=PTc[:, kt, gi * 128:(gi + 1) * 128], rhs=ov[:, kt, :],
                            start=(kt == 0 and gi == 0), stop=(kt == nkt - 1 and gi == 3)),
                            reads=[("PTc", kt), "ov"], writes=["ps_s"])
                c0 = NSEL - 2 * qb
                S.add("vector", lambda e, c0=c0: e.tensor_tensor(out=imp2[:], in0=imp[:, 0:NSEL], in1=tm[:, 0, c0:c0 + NSEL], op=ALU.mult),
                      reads=["ps_s", "tm"], writes=["imp2"])
                S.add("vector", lambda e, c0=c0: e.tensor_tensor(out=imp2[:], in0=imp2[:], in1=tm[:, 1, c0:c0 + NSEL], op=ALU.add),
                      reads=["imp2", "tm"], writes=["imp2"])
                S.add("vector", lambda e: e.memset(imp2[:, 0:1], 1e9), reads=["imp2"], writes=["imp2"])
                S.add("vector", lambda e: e.max(out=m8[:], in_=imp2[:]), reads=["imp2"], writes=["m8"])
                S.add("vector", lambda e: e.match_replace(out=imp3[:], in_to_replace=m8[:], in_values=imp2[:], imm_value=-3e38),
                      reads=["imp2", "m8"], writes=["imp3"])
                S.add("vector", lambda e: e.max(out=m8[:], in_=imp3[:]), reads=["imp3"], writes=["m8"])
                S.add("vector", lambda e: e.match_replace(out=imp3[:], in_to_replace=m8[:], in_values=imp3[:], imm_value=-3e38),
                      reads=["imp3", "m8"], writes=["imp3"])
                S.add("vector", lambda e: e.tensor_tensor(out=imp3[:], in0=imp2[:], in1=imp3[:], op=ALU.not_equal),
                      reads=["imp2", "imp3"], writes=["imp3"])
                S.add("vector", lambda e: e.tensor_scalar(out=negsel[:], in0=imp3[:], scalar1=-1.0, scalar2=-NEGM, op0=ALU.add, op1=ALU.mult),
                      reads=["imp3"], writes=["negsel"])
                steps = []
                for jt in range(qb + 1):
                    xb = (jt // 8) % 2
                    pre = None
                    if jt % 8 == 0:
                        nb = min(16, 2 * (qb + 1) - 2 * jt)
                        pre = ("gpsimd", (lambda e, jt=jt, xb=xb, nb=nb: e.tensor_copy(
                            out=negx[xb][:, 0:nb, :], in_=negsel[:, 2 * jt:2 * jt + nb].unsqueeze(2).to_broadcast([128, nb, 64]))),
                            ["negsel"], [("negx", xb)])
                    steps.append(dict(lhsT_k=big[0:73, jt * 128:(jt + 1) * 128], rhs_q=rhs_q, qreads=qreads, kreads=[("big", jt), "bigc"],
                                      mask=(wm[:, 1, :] if jt == qb else None), mreads=["wm"], pre=pre,
                                      extra=(negx[xb][:, 2 * (jt % 8):2 * (jt % 8) + 2, :].rearrange("p a b -> p (a b)"), i4s[:],
                                             [("negx", xb), "i4s"]),
                                      v_lhsT=vs_tok[:, jt, :], vreads=[("vs", jt), "vsones"]))
                attn_steps(steps, 65, "sel", with_den=False)
                S.add("vector", lambda e: e.reciprocal(out=rden[64:65, :], in_=num[64:65, :]), reads=["num"], writes=["rden"])
                S.add("tensor", lambda e: e.matmul(den[0:64, :], lhsT=ones1[64:65, :], rhs=rden[64:65, :], start=True, stop=True),
                      reads=["rden", "ones1"], writes=["den"])
                S.add("scalar", lambda e: e.copy(out=rden[0:64, :], in_=den[0:64, :]), reads=["den"], writes=["rden"])
                S.add("vector", lambda e: e.tensor_tensor(out=obr[1][:], in0=num[0:64, :], in1=rden[0:64, :], op=ALU.mult),
                      reads=["num", "rden"], writes=[("obr", 1)])
                steps = []
                for jt in range(max(0, qb - 4), qb + 1):
                    slot = jt % 8
                    mk = wm[:, 1, :] if jt == qb else (wm[:, 0, :] if jt == qb - 4 else None)
                    steps.append(dict(lhsT_k=kwr[:, slot * 128:(slot + 1) * 128], rhs_q=rhs_q, qreads=qreads,
                                      kreads=[("kwr", slot), ("kwrc", slot)], mask=mk, mreads=["wm"],
                                      v_lhsT=vwr[:, slot, :], vreads=[("vwr", slot)]))
                attn_steps(steps, 64, "win")
                S.add("vector", lambda e: e.reciprocal(out=rden[0:64, :], in_=den[0:64, :]), reads=["den"], writes=["rden"])
                S.add("vector", lambda e: e.tensor_tensor(out=obr[2][:], in0=num[0:64, :], in1=rden[0:64, :], op=ALU.mult),
                      reads=["num", "rden"], writes=[("obr", 2)])
                osl = oTt[:, :, ql * 128:(ql + 1) * 128]
                if debug is not None:
                    S.add("vector", lambda e, osl=osl: e.tensor_copy(out=osl, in_=obr[debug][:].rearrange("p (a b) -> p a b", a=4)),
                          reads=[("obr", debug)], writes=["oTt"])
                    continue
                for br in range(3):
                    p = pcount[0] % 2
                    pcount[0] += 1
                    for gi in range(4):
                        S.add("tensor", lambda e, p=p, gi=gi, br=br, ql=ql: e.matmul(
                            ps_p[p][0:64, gi * 128:(gi + 1) * 128], lhsT=srow[:, gi * 3 + br, :], rhs=gT[:, ql * 128:(ql + 1) * 128],
                            start=True, stop=True), reads=["srow", "gT"], writes=[("psp", p)])
                    if br == 0:
                        S.add("vector", lambda e, p=p, osl=osl: e.tensor_tensor(
                            out=osl, in0=ps_p[p][0:64, :].rearrange("p (a b) -> p a b", a=4),
                            in1=obr[0][:].rearrange("p (a b) -> p a b", a=4), op=ALU.mult),
                            reads=[("psp", p), ("obr", 0)], writes=["oTt"])
                    else:
                        S.add("vector", lambda e, p=p, br=br: e.tensor_tensor(out=obr[br][:], in0=ps_p[p][0:64, :], in1=obr[br][:], op=ALU.mult),
                              reads=[("psp", p), ("obr", br)], writes=[("obr", br)])
                        S.add("vector", lambda e, br=br, osl=osl: e.tensor_tensor(
                            out=osl, in0=osl, in1=obr[br][:].rearrange("p (a b) -> p a b", a=4), op=ALU.add),
                            reads=["oTt", ("obr", br)], writes=["oTt"])
            S.add("sync", lambda e, tok0=tok0: e.dma_start(
                out=oT.rearrange("(a d) t -> d a t", d=64)[:, :, tok0:tok0 + 512], in_=oTt[:]), reads=["oTt"], dma="o_out")
        S.emit()
    return nc


def nsa_consts(S_seq, grp):
    NCMP = S_seq // 16 - 1
    NCT = (NCMP + 127) // 128
    NSEL = S_seq // 64
    slopes = alibi_slopes(16)[4 * grp:4 * grp + 4]
    pos = np.arange(S_seq)
    ktab, qtab = alibi_tabs(slopes, pos, pos)
    qtab = np.ascontiguousarray(qtab.transpose(1, 0, 2))
    kctab, _ = alibi_tabs(slopes[:1], pos[:1], 16 * np.arange(NCT * 128) + 31)
    nl = np.arange(128)[:, None]
    tl = np.arange(128)[None, :]
    cmask = np.stack([np.where(16 * nl + 31 <= 128 * dl + tl, 0.0, NEGM) for dl in range(17)], axis=1).astype(np.float32)
    wmask = np.stack([np.where(nl > tl, 0.0, NEGM), np.where(nl <= tl, 0.0, NEGM)], axis=1).astype(np.float32)
    rel = np.arange(2 * NSEL)[None, :] - NSEL
    cflag = (np.arange(128)[:, None] >= 64).astype(np.int64)
    forced = (rel == cflag) | (rel == cflag - 1)
    noncausal = rel > cflag
    keep = np.where(forced | noncausal, 0.0, 1.0)
    add = np.where(forced, 1e9, np.where(noncausal, -1e30, 0.0))
    tmpl = np.stack([keep, add], axis=1).astype(np.float32)
    n = np.arange(NCT * 128)[:, None]
    j = np.arange(NSEL)[None, :]
    ovm = ((16 * n < 64 * j + 64) & (16 * n + 32 > 64 * j) & (n < NCMP)).astype(np.float32)
    ovl = np.ascontiguousarray(ovm.reshape(NCT, 128, NSEL).transpose(1, 0, 2)).astype(NPBF)
    i4 = np.tile(np.eye(128, dtype=np.float32), (1, 4)).astype(NPBF)
    selrow = np.zeros((12, 12, 64), np.float32)
    for r in range(12):
        selrow[r, r, :] = 1.0
    return dict(ktab=ktab, qtab=qtab, kctab=kctab, cmask=np.ascontiguousarray(cmask), wmask=np.ascontiguousarray(wmask),
                tmpl=np.ascontiguousarray(tmpl), ovl=ovl, i4=i4, selrow=selrow)


def run_nsa(xT_full, B, S_seq, gv, w_in, q_norm, k_norm, cmp_pos, cmp_w1, cmp_w2, debug=None):
    assert B * 4 == N_CORES
    key = ("nsa", S_seq, debug)
    if key not in _NC_CACHE:
        _NC_CACHE[key] = build_nsa(S_seq, debug)
    nc = _NC_CACHE[key]
    w_in = np.asarray(w_in, np.float32)
    cmp_w1 = np.asarray(cmp_w1, np.float32)
    cmp_w2 = np.asarray(cmp_w2, np.float32)
    cmp_pos = np.asarray(cmp_pos, np.float32)
    common = {"g": gain_layout(gv),
              "gq": np.ascontiguousarray(np.asarray(q_norm, np.float32).reshape(64, 1)),
              "gk3": np.ascontiguousarray(np.asarray(k_norm, np.float32).reshape(3, 64).T),
              "cpos": np.ascontiguousarray(cmp_pos.transpose(0, 2, 1).reshape(128, 32)),
              "cw1": np.ascontiguousarray(cmp_w1.reshape(2, 32, 64, 256).transpose(0, 2, 1, 3)),
              "cw2": np.ascontiguousarray(cmp_w2.reshape(2, 2, 128, 64).transpose(2, 0, 1, 3))}
    in_maps = []
    for c in range(N_CORES):
        b, grp = c // 4, c % 4
        cols = [w_in[:, grp * 256:(grp + 1) * 256]]
        for i in range(6):
            cols.append(w_in[:, 1024 + 256 * i + 64 * grp:1024 + 256 * i + 64 * grp + 64])
        cols.append(w_in[:, 2560 + 12 * grp:2560 + 12 * grp + 12])
        m = dict(common, xT=np.ascontiguousarray(xT_full[:, b * S_seq:(b + 1) * S_seq]),
                 w_in=np.ascontiguousarray(np.concatenate(cols, axis=1)))
        m.update(nsa_consts(S_seq, grp))
        in_maps.append(m)
    res = run_bass_kernel_spmd(nc, in_maps, core_ids=list(range(N_CORES)))
    out = np.zeros((B, 1024, S_seq), np.float32)
    for c in range(N_CORES):
        b, grp = c // 4, c % 4
        out[b, grp * 256:(grp + 1) * 256, :] = res.results[c]["oT"]
    return out


def build_gdn(S_seq, NH):
    nc = bass.Bass("TRN2", target_bir_lowering=False)
    NT = S_seq // 512
    WC = 514
    with contextlib.ExitStack() as st:
        C = Ctx(nc, st)
        xT = C.din("xT", [D, S_seq]); g = C.din("g", [128, 8])
        w_in = C.din("w_in", [D, NH, WC])
        convw = C.din("convw", [128, NH, 3, 4])
        alog = C.din("alog", [128, NH]); dtb = C.din("dtb", [128, NH])
        onrm = C.din("onrm", [128, 128])
        cst = C.din("cst", [128, 6, 128])
        oT = C.dout("oT", [NH, S_seq, 128])
        xv = xT.rearrange("(c p) t -> p c t", p=128)
        wiv = w_in.rearrange("(c p) h f -> p c h f", p=128)

        w = C.sb("w", [128, 8, NH, WC], BF16)
        stage = [C.sb("stage%d" % i, [128, WC], F32) for i in range(2)]
        gs = C.sb("gs", [128, 8], F32); epsb = C.sb("epsb", [128, 1], F32)
        ones = C.sb("ones", [128, 128], BF16); onesf = C.sb("onesf", [128, 128], BF16)
        cw = C.sb("cw", [128, NH, 3, 4], F32)
        negA = C.sb("negA", [128, NH], F32); dtbs = C.sb("dtbs", [128, NH], F32)
        onr = C.sb("onr", [128, 128], F32)
        K_ = C.sb("cst_sb", [128, 6, 128], F32)
        ident, U2, Ublk, mTi, mS, sT01 = [K_[:, i, :] for i in range(6)]
        xt = C.sb("xt", [128, 8, 512], F32); h = C.sb("h", [128, 8, 512], BF16)
        rstd = C.sb("rstd", [128, 512], F32)
        pre = [[C.sb("pre%d_%d" % (a, b), [128, 515], F32) for b in range(3)] for a in range(NH)]
        post = [[C.sb("post%d_%d" % (a, b), [128, 512], F32) for b in range(3)] for a in range(NH)]
        cacc = C.sb("cacc", [128, 512], F32)
        sqf = C.sb("sqf", [128, 512], BF16); rsf = C.sb("rsf", [128, 512], F32)
        Sst = [C.sb("S%d" % a, [128, 128], F32) for a in range(NH)]
        def hs(name, shape, n, dt=F32):
            return [C.sb("%s%d" % (name, a), shape, dt) for a in range(n)]
        NU = 4 * NH
        cols = hs("cols", [128, 16], NU); grep = hs("grep", [128, 128], 4); brep = hs("brep", [128, 128], 4)
        argA = hs("argA", [128, 128], 4); argB = hs("argB", [128, 128], 4); DT = hs("DT", [128, 128], 4); Dm = hs("Dm", [128, 128], 4)
        Nm = hs("Nm", [128, 128], 4); NTm = hs("NTm", [128, 128], 4); attnT = hs("attnT", [128, 128], NU)
        Xm = hs("Xm", [128, 128], 4); Pm = hs("Pm", [128, 2, 128], 4); PTm = hs("PTm", [128, 2, 128], 4)
        vb = hs("vb", [128, 128], 4); kbg = hs("kbg", [128, 128], 4); kd = hs("kd", [128, 128], NU); egr = hs("egr", [128, 128], 4)
        qgT = hs("qgT", [128, 128], NU); u = hs("u", [128, 128], NU); wT = hs("wT", [128, 128], NU); vnew = hs("vnew", [128, 128], NH)
        gsil = hs("gsil", [128, 128], NU); osb = hs("osb", [128, 128], NU); osq = hs("osq", [128, 128], NH); glc = hs("glc", [128, 2], NU)
        ocol = hs("ocol", [128, 2], NH)
        banks = [C.ps("bank%d" % i) for i in range(8)]
        ps_s = banks[0]
        ps_p = [banks[1], banks[2]]
        Q = lambda b, q: banks[b][:, q * 128:(q + 1) * 128]

        S = Sched(nc)
        S.add("vector", lambda e: e.memset(ones[:], 1.0 / D), writes=["ones"])
        S.add("vector", lambda e: e.memset(onesf[:], 1.0), writes=["onesf"])
        S.add("vector", lambda e: e.memset(epsb[:], EPS), writes=["epsb"])
        for a in range(NH):
            S.add("vector", lambda e, a=a: e.memset(Sst[a][:], 0.0), writes=[("S", a)])
            for b in range(3):
                S.add("gpsimd", lambda e, a=a, b=b: e.memset(pre[a][b][:, 0:3], 0.0), writes=[("pre", a, b)])
        ld = [(gs, g, "gs"), (cw, convw, "cw"), (negA, alog, "negA"), (dtbs, dtb, "dtbs"), (onr, onrm, "onr"), (K_, cst, "cst")]
        for n_, (dst, src, key) in enumerate(ld):
            S.add("sync", lambda e, dst=dst, src=src: e.dma_start(out=dst[:], in_=src), writes=[key], dma="c%d" % n_)
        S.add("scalar", lambda e: e.activation(out=negA[:], in_=negA[:], func=AF.Exp), reads=["negA"], writes=["negA"])
        S.add("vector", lambda e: e.tensor_scalar(out=negA[:], in0=negA[:], scalar1=-1.0, scalar2=None, op0=ALU.mult),
              reads=["negA"], writes=["negA"])
        kst = 0
        for c in range(8):
            for a in range(NH):
                sg = stage[kst % 2]
                S.add("sync", lambda e, sg=sg, c=c, a=a: e.dma_start(out=sg[:], in_=wiv[:, c, a, :]),
                      writes=[("stage", kst % 2)], dma=("stage", kst % 2))
                S.add(["vector", "gpsimd"][kst % 2], lambda e, sg=sg, c=c, a=a: e.tensor_scalar(
                    out=w[:, c, a, :], in0=sg[:], scalar1=gs[:, c:c + 1], scalar2=None, op0=ALU.mult),
                    reads=[("stage", kst % 2), "gs"], writes=[("w", c)])
                kst += 1
        XT = [("xt", c) for c in range(8)]
        pcount = [0]

        def mmf(out, lhsT, rhs, reads, writes, start=True, stop=True):
            S.add("tensor", lambda e: e.matmul(out, lhsT=lhsT, rhs=rhs, start=start, stop=stop), reads=reads, writes=writes)

        for tt in range(NT):
            tok0 = 512 * tt
            S.add("sync", lambda e, tok0=tok0: e.dma_start(out=xt[:], in_=xv[:, :, tok0:tok0 + 512]), writes=XT, dma="xt_in")
            emit_rmsnorm(S, xt, h, h, ps_s, rstd, ones, epsb, 512, sqtag="h", pskey="b0")
            for a in range(NH):
                for b in range(3):
                    p = pcount[0] % 2
                    pcount[0] += 1
                    for c in range(8):
                        S.add("tensor", lambda e, c=c, p=p, a=a, b=b: e.matmul(
                            ps_p[p][:], lhsT=w[:, c, a, 128 * b:128 * b + 128], rhs=h[:, c, :], start=(c == 0), stop=(c == 7)),
                            reads=[("w", c), ("h", c)], writes=["b%d" % (1 + p)])
                    pr_ = pre[a][b]
                    S.add("scalar", lambda e, p=p, pr_=pr_: e.copy(out=pr_[:, 3:515], in_=ps_p[p][:]),
                          reads=["b%d" % (1 + p)], writes=[("pre", a, b)])
                    S.add("vector", lambda e, pr_=pr_, a=a, b=b: e.tensor_scalar(
                        out=cacc[:], in0=pr_[:, 0:512], scalar1=cw[:, a, b, 0:1], scalar2=None, op0=ALU.mult),
                        reads=[("pre", a, b), "cw"], writes=["cacc"])
                    for tap in range(1, 4):
                        S.add("vector", lambda e, pr_=pr_, a=a, b=b, tap=tap: e.scalar_tensor_tensor(
                            out=cacc[:], in0=pr_[:, tap:tap + 512], scalar=cw[:, a, b, tap:tap + 1], in1=cacc[:],
                            op0=ALU.mult, op1=ALU.add), reads=[("pre", a, b), "cw", "cacc"], writes=["cacc"])
                    S.add("vector", lambda e, pr_=pr_: e.tensor_copy(out=pr_[:, 0:3], in_=pr_[:, 512:515]),
                          reads=[("pre", a, b)], writes=[("pre", a, b)])
                    po = post[a][b]
                    S.add("scalar", lambda e, po=po: e.activation(out=po[:], in_=cacc[:], func=AF.Silu),
                          reads=["cacc"], writes=[("post", a, b)])
                    if b < 2:
                        S.add("scalar", lambda e, po=po: e.activation(out=sqf[:], in_=po[:], func=AF.Square),
                              reads=[("post", a, b)], writes=["sqf"])
                        S.add("tensor", lambda e: e.matmul(ps_s[:], lhsT=onesf[:], rhs=sqf[:], start=True, stop=True),
                              reads=["onesf", "sqf"], writes=["b0"])
                        S.add("scalar", lambda e: e.activation(out=rsf[:], in_=ps_s[:], func=AF.Sqrt, bias=epsb[:, 0:1]),
                              reads=["b0", "epsb"], writes=["rsf"])
                        S.add("vector", lambda e: e.reciprocal(out=rsf[:], in_=rsf[:]), reads=["rsf"], writes=["rsf"])
                        sc = (128.0 ** -0.5) if b == 0 else 1.0
                        S.add("vector", lambda e, po=po, sc=sc: e.scalar_tensor_tensor(
                            out=po[:], in0=po[:], scalar=sc, in1=rsf[:], op0=ALU.mult, op1=ALU.mult),
                            reads=[("post", a, b), "rsf"], writes=[("post", a, b)])
            for dc in range(4):
                for a in range(NH):
                    ui = dc * NH + a
                    pb_ = 1 + (ui % 2)
                    pk = "b%d" % pb_
                    for c in range(8):
                        S.add("tensor", lambda e, c=c, a=a, dc=dc, pb_=pb_: e.matmul(
                            banks[pb_][:, 0:128], lhsT=h[:, c, dc * 128:(dc + 1) * 128], rhs=w[:, c, a, 384:512], start=(c == 0), stop=(c == 7)),
                            reads=[("w", c), ("h", c)], writes=[pk])
                    for c in range(8):
                        S.add("tensor", lambda e, c=c, a=a, dc=dc, pb_=pb_: e.matmul(
                            banks[pb_][:, 128:130], lhsT=h[:, c, dc * 128:(dc + 1) * 128], rhs=w[:, c, a, 512:514],
                            start=(c == 0), stop=(c == 7)), reads=[("w", c), ("h", c)], writes=[pk])
                    cl = cols[ui]
                    CK = ("cols", ui)
                    S.add("scalar", lambda e, ui=ui, pb_=pb_: e.activation(out=gsil[ui][:], in_=banks[pb_][:, 0:128], func=AF.Silu),
                          reads=[pk], writes=[("gsil", ui)])
                    S.add("scalar", lambda e, cl=cl, pb_=pb_: e.activation(out=cl[:, 0:1], in_=banks[pb_][:, 128:129], func=AF.Sigmoid),
                          reads=[pk], writes=[CK])
                    S.add("scalar", lambda e, cl=cl, a=a, pb_=pb_: e.activation(out=cl[:, 8:9], in_=banks[pb_][:, 129:130], func=AF.Exp,
                                                                                bias=dtbs[:, a:a + 1]),
                          reads=[pk, "dtbs"], writes=[CK])
                    S.add("vector", lambda e, cl=cl: e.tensor_scalar(out=cl[:, 8:9], in0=cl[:, 8:9], scalar1=1.0, scalar2=None, op0=ALU.add),
                          reads=[CK], writes=[CK])
                    S.add("scalar", lambda e, cl=cl: e.activation(out=cl[:, 8:9], in_=cl[:, 8:9], func=AF.Ln), reads=[CK], writes=[CK])
                    S.add("vector", lambda e, cl=cl, a=a: e.tensor_tensor(out=cl[:, 1:2], in0=cl[:, 8:9], in1=negA[:, a:a + 1], op=ALU.mult),
                          reads=[CK, "negA"], writes=[CK])

            def prep_unit(us, dc, a):
                ui = dc * NH + a
                XB, YB = banks[2 * us], banks[2 * us + 1]
                XK, YK = "b%d" % (2 * us), "b%d" % (2 * us + 1)
                XQ = lambda q: XB[:, q * 128:(q + 1) * 128]
                YQ = lambda q: YB[:, q * 128:(q + 1) * 128]
                cs = slice(dc * 128, (dc + 1) * 128)
                qT_, kT_, vT_ = post[a][0][:, cs], post[a][1][:, cs], post[a][2][:, cs]
                RP = [("post", a, 0), ("post", a, 1), ("post", a, 2)]
                cl = cols[ui]
                CK = ("cols", ui)
                U = lambda name: (name, us)
                P_ = lambda name: (name, ui)
                S.add("vector", lambda e: e.tensor_copy(out=grep[us][:], in_=cl[:, 1:2].to_broadcast([128, 128])), reads=[CK], writes=[U("grep")])
                S.add("gpsimd", lambda e: e.tensor_copy(out=brep[us][:], in_=cl[:, 0:1].to_broadcast([128, 128])), reads=[CK], writes=[U("brep")])
                yield
                mmf(XQ(0), grep[us][:], U2, [U("grep"), "cst"], [XK])
                mmf(XQ(1), grep[us][:], Ublk, [U("grep"), "cst"], [XK])
                mmf(XQ(2), brep[us][:], ident, [U("brep"), "cst"], [XK])
                mmf(XB[:, 384:386], U2, grep[us][:, 0:2], [U("grep"), "cst"], [XK])
                mmf(XB[:, 386:388], Ublk, grep[us][:, 0:2], [U("grep"), "cst"], [XK])
                mmf(YQ(0), kT_, kT_, RP, [YK])
                mmf(YQ(1), kT_, qT_, RP, [YK])
                mmf(YQ(2), kT_, ident, RP + ["cst"], [YK])
                mmf(YQ(3), vT_, ident, RP + ["cst"], [YK])
                yield
                S.add("vector", lambda e: e.tensor_copy(out=cl[:, 2:3], in_=XB[:, 384:385]), reads=[XK], writes=[CK])
                S.add("vector", lambda e: e.tensor_scalar(out=cl[:, 3:4], in0=XB[:, 384:385], scalar1=-1.0, scalar2=None, op0=ALU.mult),
                      reads=[XK], writes=[CK])
                S.add("vector", lambda e: e.tensor_tensor(out=argA[us][:], in0=XQ(0), in1=mTi, op=ALU.add), reads=[XK, "cst"], writes=[U("argA")])
                S.add("vector", lambda e: e.scalar_tensor_tensor(out=argB[us][:], in0=XQ(0), scalar=-1.0, in1=mS, op0=ALU.mult, op1=ALU.add),
                      reads=[XK, "cst"], writes=[U("argB")])
                yield
                S.add("scalar", lambda e: e.activation(out=cl[:, 4:5], in_=XB[:, 384:385], func=AF.Exp), reads=[XK], writes=[CK])
                S.add("scalar", lambda e: e.activation(out=cl[:, 5:6], in_=XB[:, 386:387], func=AF.Exp, bias=cl[:, 3:4]), reads=[XK, CK], writes=[CK])
                S.add("scalar", lambda e: e.activation(out=glc[ui][:], in_=XB[:, 128:256:64], func=AF.Exp), reads=[XK], writes=[P_("glc")])
                S.add("scalar", lambda e: e.activation(out=DT[us][:], in_=argA[us][:], func=AF.Exp, bias=cl[:, 3:4]), reads=[U("argA"), CK], writes=[U("DT")])
                S.add("scalar", lambda e: e.activation(out=Dm[us][:], in_=argB[us][:], func=AF.Exp, bias=cl[:, 2:3]), reads=[U("argB"), CK], writes=[U("Dm")])
                S.add("scalar", lambda e: e.activation(out=egr[us][:], in_=XQ(0), func=AF.Exp), reads=[XK], writes=[U("egr")])
                yield
                S.add("vector", lambda e: e.tensor_tensor(out=cl[:, 6:7], in0=cl[:, 4:5], in1=cl[:, 0:1], op=ALU.mult), reads=[CK], writes=[CK])
                S.add("vector", lambda e: e.tensor_scalar(out=cl[:, 7:8], in0=cl[:, 0:1], scalar1=-1.0, scalar2=None, op0=ALU.mult), reads=[CK], writes=[CK])
                S.add("vector", lambda e: e.scalar_tensor_tensor(out=NTm[us][:], in0=YQ(0), scalar=cl[:, 7:8], in1=Dm[us][:], op0=ALU.mult, op1=ALU.mult),
                      reads=[YK, CK, U("Dm")], writes=[U("NTm")])
                S.add("vector", lambda e: e.tensor_tensor(out=Nm[us][:], in0=YQ(0), in1=DT[us][:], op=ALU.mult), reads=[YK, U("DT")], writes=[U("Nm")])
                S.add("vector", lambda e: e.tensor_tensor(out=Nm[us][:], in0=Nm[us][:], in1=sT01, op=ALU.mult), reads=[U("Nm"), "cst"], writes=[U("Nm")])
                S.add("vector", lambda e: e.scalar_tensor_tensor(out=Nm[us][:], in0=XQ(2), scalar=-1.0, in1=Nm[us][:], op0=ALU.mult, op1=ALU.mult),
                      reads=[XK, U("Nm")], writes=[U("Nm")])
                S.add("vector", lambda e: e.tensor_tensor(out=attnT[ui][:], in0=YQ(1), in1=DT[us][:], op=ALU.mult), reads=[YK, U("DT")], writes=[P_("attnT")])
                S.add("vector", lambda e: e.tensor_scalar(out=vb[us][:], in0=YQ(3), scalar1=cl[:, 0:1], scalar2=None, op0=ALU.mult),
                      reads=[YK, CK], writes=[U("vb")])
                S.add("vector", lambda e: e.tensor_scalar(out=kbg[us][:], in0=YQ(2), scalar1=cl[:, 6:7], scalar2=None, op0=ALU.mult),
                      reads=[YK, CK], writes=[U("kbg")])
                S.add("vector", lambda e: e.tensor_scalar(out=kd[ui][:], in0=YQ(2), scalar1=cl[:, 5:6], scalar2=None, op0=ALU.mult),
                      reads=[YK, CK], writes=[P_("kd")])
                S.add("gpsimd", lambda e: e.tensor_tensor(out=qgT[ui][:], in0=qT_, in1=egr[us][:], op=ALU.mult), reads=RP + [U("egr")], writes=[P_("qgT")])
                S.add("vector", lambda e: e.tensor_tensor(out=Xm[us][:], in0=Nm[us][:], in1=ident, op=ALU.add), reads=[U("Nm"), "cst"], writes=[U("Xm")])
                yield
                P_cur, PT_cur = Nm[us][:], NTm[us][:]
                rdP, rdPT = [U("Nm")], [U("NTm")]
                for lvl in range(5):
                    sl = lvl % 2
                    last = (lvl == 4)
                    mmf(XQ(1), P_cur, PT_cur, rdP + rdPT, [XK])
                    if not last:
                        mmf(XQ(0), PT_cur, P_cur, rdP + rdPT, [XK])
                    yield
                    S.add("scalar", lambda e, sl=sl: e.copy(out=PTm[us][:, sl, :], in_=XQ(1)), reads=[XK], writes=[("PTm", us, sl)])
                    if not last:
                        S.add("scalar", lambda e, sl=sl: e.copy(out=Pm[us][:, sl, :], in_=XQ(0)), reads=[XK], writes=[("Pm", us, sl)])
                    yield
                    mmf(XQ(2), PTm[us][:, sl, :], Xm[us][:], [("PTm", us, sl), U("Xm")], [XK])
                    yield
                    S.add("vector", lambda e: e.tensor_tensor(out=Xm[us][:], in0=XQ(2), in1=Xm[us][:], op=ALU.add), reads=[XK, U("Xm")], writes=[U("Xm")])
                    P_cur, PT_cur = Pm[us][:, sl, :], PTm[us][:, sl, :]
                    rdP, rdPT = [("Pm", us, sl)], [("PTm", us, sl)]
                    yield
                mmf(YQ(0), Xm[us][:], vb[us][:], [U("Xm"), U("vb")], [YK])
                mmf(YQ(1), kbg[us][:], Xm[us][:], [U("Xm"), U("kbg")], [YK])
                yield
                S.add("scalar", lambda e: e.copy(out=u[ui][:], in_=YQ(0)), reads=[YK], writes=[P_("u")])
                S.add("scalar", lambda e: e.copy(out=wT[ui][:], in_=YQ(1)), reads=[YK], writes=[P_("wT")])

            units = [(dc, a) for dc in range(4) for a in range(NH)]
            for w0 in range(0, len(units), 4):
                gens = [prep_unit(us, dc, a) for us, (dc, a) in enumerate(units[w0:w0 + 4])]
                while gens:
                    for gen in list(gens):
                        try:
                            next(gen)
                        except StopIteration:
                            gens.remove(gen)
            for dc in range(4):
                for ch in range(2):
                    pr = slice(64 * ch, 64 * ch + 64)
                    for a in range(NH):
                        ui = dc * NH + a
                        P_ = lambda name: (name, ui)
                        SK = ("S", a)
                        RB = banks[6 + (a % 2)]
                        RK = "b%d" % (6 + (a % 2))
                        mmf(RB[pr, 0:128], wT[ui][:, pr], Sst[a][:], [P_("wT"), SK], [RK])
                        S.add("vector", lambda e, a=a, ui=ui, pr=pr, RB=RB: e.tensor_tensor(out=vnew[a][pr, :], in0=u[ui][pr, :], in1=RB[pr, 0:128],
                                                                                         op=ALU.subtract),
                              reads=[RK, P_("u")], writes=[("vnew", a)])
                        mmf(RB[pr, 128:256], qgT[ui][:, pr], Sst[a][:], [P_("qgT"), SK], [RK], start=True, stop=False)
                        mmf(RB[pr, 128:256], attnT[ui][pr, pr], vnew[a][pr, :], [P_("attnT"), ("vnew", a)], [RK], start=False, stop=True)
                        mmf(RB[:, 256:384], kd[ui][pr, :], vnew[a][pr, :], [P_("kd"), ("vnew", a)], [RK])
                        S.add("vector", lambda e, a=a, ui=ui, ch=ch, RB=RB: e.scalar_tensor_tensor(
                            out=Sst[a][:], in0=Sst[a][:], scalar=glc[ui][:, ch:ch + 1], in1=RB[:, 256:384], op0=ALU.mult, op1=ALU.add),
                            reads=[RK, SK, P_("glc")], writes=[SK])
                        OC = ("ocol", a, ch)
                        S.add("scalar", lambda e, a=a, pr=pr, RB=RB: e.activation(out=osq[a][pr, :], in_=RB[pr, 128:256], func=AF.Square,
                                                                                  accum_out=ocol[a][pr, 0:1]),
                              reads=[RK], writes=[("osq", a), OC])
                        S.add("scalar", lambda e, a=a, pr=pr: e.activation(out=ocol[a][pr, 1:2], in_=ocol[a][pr, 0:1], func=AF.Sqrt,
                                                                           bias=epsb[pr, 0:1], scale=1.0 / 128),
                              reads=[OC, "epsb"], writes=[OC])
                        S.add("vector", lambda e, a=a, pr=pr: e.reciprocal(out=ocol[a][pr, 1:2], in_=ocol[a][pr, 1:2]), reads=[OC], writes=[OC])
                        S.add("vector", lambda e, a=a, ui=ui, pr=pr, RB=RB: e.scalar_tensor_tensor(
                            out=osb[ui][pr, :], in0=RB[pr, 128:256], scalar=ocol[a][pr, 1:2], in1=onr[pr, :], op0=ALU.mult, op1=ALU.mult),
                            reads=[RK, OC, "onr"], writes=[("osb", ui, ch)])
                        S.add("gpsimd", lambda e, ui=ui, pr=pr: e.tensor_tensor(out=osb[ui][pr, :], in0=osb[ui][pr, :], in1=gsil[ui][pr, :], op=ALU.mult),
                              reads=[("osb", ui, ch), ("gsil", ui)], writes=[("osb", ui, ch)])
                for a in range(NH):
                    ui = dc * NH + a
                    r0 = tok0 + dc * 128
                    S.add("sync", lambda e, a=a, r0=r0, ui=ui: e.dma_start(out=oT[a, r0:r0 + 128, :], in_=osb[ui][:]),
                          reads=[("osb", ui, 0), ("osb", ui, 1)], dma=("o_out", ui))
        S.emit()
    return nc


def gdn_consts():
    i = np.arange(128)
    same = (i[:, None] // 64) == (i[None, :] // 64)
    ident = np.eye(128)
    U2 = (same & (i[:, None] <= i[None, :])).astype(np.float64)
    Ublk = same.astype(np.float64)
    mTi = np.where(same & (i[None, :] >= i[:, None]), 0.0, -1e4)
    mS = np.where(same & (i[None, :] < i[:, None]), 0.0, -1e4)
    sT01 = (same & (i[None, :] > i[:, None])).astype(np.float64)
    return np.ascontiguousarray(np.stack([ident, U2, Ublk, mTi, mS, sT01], axis=1).astype(np.float32))


def run_gdn(xT_full, B, S_seq, gv, w_in, conv_w, a_log, dt_bias, o_norm):
    NH = 8 * B // N_CORES
    key = ("gdn", S_seq, NH)
    if key not in _NC_CACHE:
        _NC_CACHE[key] = build_gdn(S_seq, NH)
    nc = _NC_CACHE[key]
    w_in = np.asarray(w_in, np.float32)
    conv_w = np.asarray(conv_w, np.float32)
    common = {"g": gain_layout(gv), "cst": gdn_consts(),
              "onrm": np.ascontiguousarray(np.broadcast_to(np.asarray(o_norm, np.float32).reshape(1, 128), (128, 128)))}
    in_maps = []
    per_b = N_CORES // B
    for c in range(N_CORES):
        b = c // per_b
        heads = [(c % per_b) * NH + i for i in range(NH)]
        wsel = np.stack([np.concatenate([w_in[:, hh * 128:(hh + 1) * 128], w_in[:, 1024 + hh * 128:1024 + (hh + 1) * 128],
                                         w_in[:, 2048 + hh * 128:2048 + (hh + 1) * 128],
                                         w_in[:, 3088 + hh * 128:3088 + (hh + 1) * 128],
                                         w_in[:, 3072 + hh:3072 + hh + 1], w_in[:, 3080 + hh:3080 + hh + 1]], axis=1) for hh in heads], axis=1)
        cwl = np.stack([np.stack([conv_w[:, q * 1024 + hh * 128:q * 1024 + (hh + 1) * 128].T for q in range(3)], axis=1)
                        for hh in heads], axis=1)
        al = np.broadcast_to(np.asarray(a_log, np.float32)[heads].reshape(1, NH), (128, NH))
        db = np.broadcast_to(np.asarray(dt_bias, np.float32)[heads].reshape(1, NH), (128, NH))
        in_maps.append(dict(common, xT=np.ascontiguousarray(xT_full[:, b * S_seq:(b + 1) * S_seq]),
                            w_in=np.ascontiguousarray(wsel), convw=np.ascontiguousarray(cwl),
                            alog=np.ascontiguousarray(al), dtb=np.ascontiguousarray(db)))
    res = run_bass_kernel_spmd(nc, in_maps, core_ids=list(range(N_CORES)))
    out = np.zeros((B, 1024, S_seq), np.float32)
    for c in range(N_CORES):
        b = c // per_b
        for i in range(NH):
            hh = (c % per_b) * NH + i
            out[b, hh * 128:(hh + 1) * 128, :] = res.results[c]["oT"][i].T
    return out


def build_oproj(T):
    nc = bass.Bass("TRN2", target_bir_lowering=False)
    with contextlib.ExitStack() as st:
        C = Ctx(nc, st)
        xT = C.din("xT", [D, T]); oin = C.din("oin", [D, T]); w_o = C.din("w_o", [D, D])
        oT = C.dout("oT", [D, T])
        xv = xT.rearrange("(c p) t -> p c t", p=128)
        iv = oin.rearrange("(c p) t -> p c t", p=128)
        ov = oT.rearrange("(c p) t -> p c t", p=128)
        wv = w_o.rearrange("(c p) n -> p c n", p=128)
        wo = C.sb("wo", [128, 8, D], BF16)
        stage = [C.sb("stage%d" % i, [128, D], F32) for i in range(2)]
        xt = [C.sb("xt%d" % i, [128, 8, 512], F32) for i in range(2)]
        of = [C.sb("of%d" % i, [128, 8, 512], F32) for i in range(2)]
        ob = [C.sb("ob%d" % i, [128, 8, 512], BF16) for i in range(2)]
        ps = [C.ps("ps%d" % i) for i in range(4)]
        S = Sched(nc)
        for c in range(8):
            S.add("sync", lambda e, c=c: e.dma_start(out=stage[c % 2][:], in_=wv[:, c, :]), writes=[("stage", c % 2)], dma=("stage", c % 2))
            S.add(["vector", "gpsimd"][c % 2], lambda e, c=c: e.tensor_copy(out=wo[:, c, :], in_=stage[c % 2][:]),
                  reads=[("stage", c % 2)], writes=[("wo", c)])
        k = 0
        for t in range(T // 512):
            b = t % 2
            ts = slice(t * 512, (t + 1) * 512)
            S.add("sync", lambda e, ts=ts, b=b: e.dma_start(out=xt[b][:], in_=xv[:, :, ts]), writes=[("xt", b, m) for m in range(8)], dma=("xin", b))
            S.add("gpsimd", lambda e, ts=ts, b=b: e.dma_start(out=of[b][:], in_=iv[:, :, ts]), writes=[("of", b)], dma=("oin", b))
            S.add("scalar", lambda e, b=b: e.copy(out=ob[b][:, 0:4, :], in_=of[b][:, 0:4, :]), reads=[("of", b)], writes=[("ob", b, 0)])
            S.add("gpsimd", lambda e, b=b: e.tensor_copy(out=ob[b][:, 4:8, :], in_=of[b][:, 4:8, :]), reads=[("of", b)], writes=[("ob", b, 1)])
            for m in range(8):
                p = k % 4
                k += 1
                for c in range(8):
                    S.add("tensor", lambda e, c=c, m=m, p=p, b=b: e.matmul(ps[p][:], lhsT=wo[:, c, m * 128:(m + 1) * 128], rhs=ob[b][:, c, :],
                                                                           start=(c == 0), stop=(c == 7)),
                          reads=[("wo", c), ("ob", b, c // 4)], writes=[("psp", p)])
                S.add("vector", lambda e, m=m, p=p, b=b: e.tensor_tensor(out=xt[b][:, m, :], in0=ps[p][:], in1=xt[b][:, m, :], op=ALU.add),
                      reads=[("psp", p), ("xt", b, m)], writes=[("xt", b, m)])
            S.add("sync", lambda e, ts=ts, b=b: e.dma_start(out=ov[:, :, ts], in_=xt[b][:]), reads=[("xt", b, m) for m in range(8)], dma=("xout", b))
        S.emit()
    return nc


def run_oproj(xT_full, oT_full, w_o):
    ntok = xT_full.shape[1]
    T = ntok // N_CORES
    key = ("oproj", T)
    if key not in _NC_CACHE:
        _NC_CACHE[key] = build_oproj(T)
    nc = _NC_CACHE[key]
    w_o = np.ascontiguousarray(w_o, np.float32)
    in_maps = [{"xT": np.ascontiguousarray(xT_full[:, c * T:(c + 1) * T]), "oin": np.ascontiguousarray(oT_full[:, c * T:(c + 1) * T]),
                "w_o": w_o} for c in range(N_CORES)]
    res = run_bass_kernel_spmd(nc, in_maps, core_ids=list(range(N_CORES)))
    return np.concatenate([r["oT"] for r in res.results], axis=1)


def kernel(x, ffn1_norm, ffn1_w_in, ffn1_w_out, mix_norm, ffn2_norm, ffn2_w_in, ffn2_w_out,
           nsa_w_in, nsa_w_out, nsa_q_norm, nsa_k_norm, nsa_cmp_pos, nsa_cmp_w1, nsa_cmp_w2,
           diff_w_in, diff_w_out, diff_q_norm, diff_k_norm, diff_lambda, diff_subln,
           gdn_w_in, gdn_w_out, gdn_conv_w, gdn_a_log, gdn_dt_bias, gdn_o_norm,
           swa_w_in, swa_w_out, swa_q_norm, swa_k_norm, swa_sinks):
    import math
    x = np.asarray(x, np.float32)
    B, S_seq, _ = x.shape
    A = lambda v: np.asarray(v, np.float32)
    xT = np.ascontiguousarray(x.reshape(B * S_seq, D).T)
    bs = lambda o: np.ascontiguousarray(o.transpose(1, 0, 2).reshape(D, B * S_seq))
    depth = A(ffn1_norm).shape[0]
    for layer in range(depth):
        kind, j = layer % 4, layer // 4
        xT = run_ffn(xT, A(ffn1_norm)[layer], A(ffn1_w_in)[layer], A(ffn1_w_out)[layer])
        gm = A(mix_norm)[layer]
        if kind == 0:
            o = run_nsa(xT, B, S_seq, gm, A(nsa_w_in)[j], A(nsa_q_norm)[j], A(nsa_k_norm)[j], A(nsa_cmp_pos)[j],
                        A(nsa_cmp_w1)[j], A(nsa_cmp_w2)[j])
            xT = run_oproj(xT, bs(o), A(nsa_w_out)[j])
        elif kind == 1:
            lam_init = 0.8 - 0.6 * math.exp(-0.3 * layer)
            o = run_diff(xT, B, S_seq, gm, A(diff_w_in)[j], A(diff_q_norm)[j], A(diff_k_norm)[j], A(diff_lambda)[j],
                         A(diff_subln)[j], lam_init)
            xT = run_oproj(xT, bs(o), A(diff_w_out)[j])
        elif kind == 2:
            o = run_gdn(xT, B, S_seq, gm, A(gdn_w_in)[j], A(gdn_conv_w)[j], A(gdn_a_log)[j], A(gdn_dt_bias)[j], A(gdn_o_norm)[j])
            xT = run_oproj(xT, bs(o), A(gdn_w_out)[j])
        else:
            xT = run_swa(xT, S_seq, gm, A(swa_w_in)[j], A(swa_w_out)[j], A(swa_q_norm)[j], A(swa_k_norm)[j], A(swa_sinks)[j])
        xT = run_ffn(xT, A(ffn2_norm)[layer], A(ffn2_w_in)[layer], A(ffn2_w_out)[layer])
    return np.ascontiguousarray(xT.T).reshape(B, S_seq, D).astype(np.float32)
```

```python
import contextlib
import math
import numpy as np
import concourse.bass as bass
import concourse.mybir as mybir
from concourse.bass_utils import run_bass_kernel_spmd

F32 = mybir.dt.float32
BF16 = mybir.dt.bfloat16
ALU = mybir.AluOpType
AF = mybir.ActivationFunctionType

N_CORES = 8
D = 1024
DFF = 2816
EPS = 1e-6

COMPUTE = ("tensor", "vector", "scalar", "gpsimd")


PSUM_NAMES = {"ps_s", "ps_n", "psp", "psa", "psb", "psy", "st", "num", "den", "imp"}


def is_psum_key(k):
    n = k[0] if isinstance(k, tuple) else k
    return isinstance(n, str) and (n in PSUM_NAMES or (len(n) == 2 and n[0] == "b" and n[1].isdigit()))


class Sched:
    def __init__(self, nc):
        self.nc = nc
        self.ops = []

    LIMIT = None
    seen = 0

    def add(self, eng, fn, reads=(), writes=(), dma=None):
        Sched.seen += 1
        if Sched.LIMIT is not None and Sched.seen > Sched.LIMIT and not (dma is not None and not writes):
            return -1
        self.ops.append(dict(eng=eng, fn=fn, reads=tuple(reads), writes=tuple(writes), dma=dma))
        return len(self.ops) - 1

    def emit(self, final_wait_engine="sync"):
        nc = self.nc
        ops = self.ops
        last_writer = {}
        readers = {}
        deps = []
        for i, op in enumerate(ops):
            d = set()
            for r in op["reads"]:
                if r in last_writer:
                    d.add(last_writer[r])
                if is_psum_key(r):
                    for j in readers.get(r, ()):
                        if ops[j]["eng"] != op["eng"]:
                            d.add(j)
            for w in op["writes"]:
                if w in last_writer:
                    d.add(last_writer[w])
                for j in readers.get(w, ()):
                    d.add(j)
            d.discard(i)
            deps.append(d)
            for r in op["reads"]:
                readers.setdefault(r, []).append(i)
            for w in op["writes"]:
                last_writer[w] = i
                readers[w] = []

        def pe_pe(i, j):
            return ops[j]["eng"] == "tensor" and ops[i]["eng"] == "tensor" and ops[j]["dma"] is None \
                and ops[i]["dma"] is None

        needed = set()
        for i, d in enumerate(deps):
            for j in d:
                if not pe_pe(i, j):
                    needed.add(j)
        dma_keys = []
        for op in ops:
            if op["dma"] is not None and op["dma"] not in dma_keys:
                dma_keys.append(op["dma"])
        engines_used = []
        for op in ops:
            if op["eng"] not in engines_used:
                engines_used.append(op["eng"])
        if final_wait_engine not in engines_used:
            engines_used.append(final_wait_engine)

        with contextlib.ExitStack() as st:
            esem = {e: st.enter_context(nc.semaphore("s_" + e)) for e in COMPUTE}
            dsem = {k: st.enter_context(nc.semaphore("d_%d" % n)) for n, k in enumerate(dma_keys)}
            cnt = {e: 0 for e in COMPUTE}
            dcnt = {k: 0 for k in dma_keys}
            sig = {}
            for i, op in enumerate(ops):
                if op["dma"] is not None:
                    dcnt[op["dma"]] += 16
                    sig[i] = (dsem[op["dma"]], dcnt[op["dma"]], ("d", op["dma"]))
                elif i in needed:
                    cnt[op["eng"]] += 1
                    sig[i] = (esem[op["eng"]], cnt[op["eng"]], ("e", op["eng"]))
            per_eng = {e: [] for e in engines_used}
            for i, op in enumerate(ops):
                per_eng[op["eng"]].append(i)
            final = [(dsem[k], dcnt[k]) for k in dma_keys]
            block = st.enter_context(nc.Block())

            def make(ename):
                def body(e):
                    seen = {}
                    for i in per_eng[ename]:
                        op = ops[i]
                        waits = {}
                        for j in deps[i]:
                            if j not in sig or pe_pe(i, j):
                                continue
                            s, v, key = sig[j]
                            if seen.get(key, 0) >= v:
                                continue
                            if key not in waits or waits[key][1] < v:
                                waits[key] = (s, v)
                        for key, (s, v) in waits.items():
                            e.wait_ge(s, v)
                            seen[key] = v
                        ins = op["fn"](e)
                        if i in sig:
                            ins.then_inc(sig[i][0], 16 if op["dma"] is not None else 1)
                    if ename == final_wait_engine:
                        for s, v in final:
                            if v > 0:
                                e.wait_ge(s, v)
                return body

            for ename in engines_used:
                getattr(block, ename)(make(ename))
        return len(ops)


def build_ffn(T, TT=512):
    nc = bass.Bass("TRN2", target_bir_lowering=False)
    xT = nc.dram_tensor("xT", [D, T], F32, kind="ExternalInput").ap()
    g = nc.dram_tensor("g", [128, 8], F32, kind="ExternalInput").ap()
    w_in = nc.dram_tensor("w_in", [D, 2 * DFF], F32, kind="ExternalInput").ap()
    w_out = nc.dram_tensor("w_out", [DFF, D], F32, kind="ExternalInput").ap()
    oT = nc.dram_tensor("oT", [D, T], F32, kind="ExternalOutput").ap()
    NJ = DFF // 128
    GW = 704
    ntiles = T // TT
    xv = xT.rearrange("(c p) t -> p c t", p=128)
    ov = oT.rearrange("(c p) t -> p c t", p=128)
    wiv = w_in.rearrange("(c p) f -> p c f", p=128)
    wov = w_out.rearrange("(j p) d -> p j d", p=128)
    with contextlib.ExitStack() as st:
        sb = lambda name, shape, dt: st.enter_context(nc.sbuf_tensor(name, shape, dt))
        ps = lambda name: st.enter_context(nc.psum_tensor(name, [128, TT], F32))
        wi = sb("wi", [128, 8, 2 * DFF], BF16)
        wo = sb("wo", [128, NJ, D], BF16)
        gs = sb("gs", [128, 8], F32)
        ones = sb("ones", [128, 128], BF16)
        epsb = sb("epsb", [128, 1], F32)
        xts = [sb("xt%d" % i, [128, 8, TT], F32) for i in range(2)]
        h = sb("h", [128, 8, TT], BF16)
        rstd = sb("rstd", [128, TT], F32)
        sa = [sb("sa%d" % i, [128, TT], F32) for i in range(2)]
        act = sb("act", [128, NJ, TT], BF16)
        sq = act
        ps_s = ps("ps_s")
        ps_a = [ps("ps_a%d" % i) for i in range(2)]
        ps_b = [ps("ps_b%d" % i) for i in range(2)]
        ps_y = [ps("ps_y%d" % i) for i in range(2)]

        S = Sched(nc)
        S.add("vector", lambda e: e.memset(ones[:], 1.0 / D), writes=["ones"])
        S.add("vector", lambda e: e.memset(epsb[:], EPS), writes=["epsb"])
        S.add("sync", lambda e: e.dma_start(out=gs[:], in_=g), writes=["gs"], dma="gs")
        XTK = lambda b: [("xt", b, c) for c in range(8)]

        def load_x(t):
            b = t % 2
            S.add("sync", lambda e, t=t, b=b: e.dma_start(out=xts[b][:], in_=xv[:, :, t * TT:(t + 1) * TT]),
                  writes=XTK(b), dma=("xt_in", b))

        load_x(0)
        ngrp = 2 * DFF // GW
        order = []
        for i in range(ngrp // 2):
            order += [i, ngrp // 2 + i]
        for grp in order:
            S.add("gpsimd", lambda e, grp=grp: e.dma_start(out=wi[:, :, grp * GW:(grp + 1) * GW], in_=wiv[:, :, grp * GW:(grp + 1) * GW]),
                  writes=[("wi", grp)], dma=("wi", grp))
        for j0 in range(0, NJ, 6):
            j1 = min(NJ, j0 + 6)
            S.add("gpsimd", lambda e, j0=j0, j1=j1: e.dma_start(out=wo[:, j0:j1, :], in_=wov[:, j0:j1, :]),
                  writes=[("wo", j) for j in range(j0, j1)], dma=("wo", j0))
        wgrp = lambda c0: [("wi", gidx) for gidx in range(c0 // GW, (c0 + 127) // GW + 1)]
        for t in range(ntiles):
            b = t % 2
            xt = xts[b]
            ts = slice(t * TT, (t + 1) * TT)
            if t + 1 < ntiles:
                load_x(t + 1)
            S.add("scalar", lambda e, xt=xt: e.activation(out=sq[:, 0:8, :], in_=xt[:], func=AF.Square),
                  reads=XTK(b), writes=[("act", c) for c in range(8)])
            for c in range(8):
                S.add("tensor", lambda e, c=c: e.matmul(ps_s[:], lhsT=ones[:], rhs=sq[:, c, :],
                                                        start=(c == 0), stop=(c == 7)),
                      reads=["ones", ("act", c)], writes=["ps_s"])
            S.add("scalar", lambda e: e.activation(out=rstd[:], in_=ps_s[:], func=AF.Sqrt, bias=epsb[:, 0:1]),
                  reads=["ps_s", "epsb"], writes=["rstd"])
            S.add("vector", lambda e: e.reciprocal(out=rstd[:], in_=rstd[:]), reads=["rstd"], writes=["rstd"])
            for c in range(8):
                S.add("vector", lambda e, c=c, xt=xt: e.scalar_tensor_tensor(
                    out=h[:, c, :], in0=xt[:, c, :], scalar=gs[:, c:c + 1], in1=rstd[:], op0=ALU.mult, op1=ALU.mult),
                    reads=[("xt", b, c), "rstd", "gs"], writes=[("h", c)])
            for j in range(NJ):
                pa, pb, sj = ps_a[j % 2], ps_b[j % 2], sa[j % 2]
                for c in range(8):
                    S.add("tensor", lambda e, c=c, j=j, pa=pa: e.matmul(
                        pa[:], lhsT=wi[:, c, j * 128:(j + 1) * 128], rhs=h[:, c, :],
                        start=(c == 0), stop=(c == 7)),
                        reads=wgrp(j * 128) + [("h", c)], writes=[("psa", j % 2)])
                for c in range(8):
                    S.add("tensor", lambda e, c=c, j=j, pb=pb: e.matmul(
                        pb[:], lhsT=wi[:, c, DFF + j * 128:DFF + (j + 1) * 128], rhs=h[:, c, :],
                        start=(c == 0), stop=(c == 7)),
                        reads=wgrp(DFF + j * 128) + [("h", c)], writes=[("psb", j % 2)])
                S.add("scalar", lambda e, pa=pa, sj=sj: e.activation(out=sj[:], in_=pa[:], func=AF.Silu),
                      reads=[("psa", j % 2)], writes=[("sa", j % 2)])
                S.add("vector", lambda e, pb=pb, sj=sj, j=j: e.tensor_tensor(
                    out=act[:, j, :], in0=pb[:], in1=sj[:], op=ALU.mult),
                    reads=[("psb", j % 2), ("sa", j % 2)], writes=[("act", j)])
            for m in range(8):
                py = ps_y[m % 2]
                for j in range(NJ):
                    S.add("tensor", lambda e, m=m, j=j, py=py: e.matmul(
                        py[:], lhsT=wo[:, j, m * 128:(m + 1) * 128], rhs=act[:, j, :],
                        start=(j == 0), stop=(j == NJ - 1)),
                        reads=[("wo", j), ("act", j)], writes=[("psy", m % 2)])
                S.add("vector", lambda e, m=m, py=py, xt=xt: e.scalar_tensor_tensor(
                    out=xt[:, m, :], in0=py[:], scalar=0.5, in1=xt[:, m, :], op0=ALU.mult, op1=ALU.add),
                    reads=[("psy", m % 2), ("xt", b, m)], writes=[("xt", b, m)])
            S.add("sync", lambda e, ts=ts, xt=xt: e.dma_start(out=ov[:, :, ts], in_=xt[:]),
                  reads=XTK(b), dma=("xt_out", b))
        S.emit()
    return nc


def gain_layout(gv):
    return np.ascontiguousarray(np.asarray(gv, np.float32).reshape(8, 128).T)


_NC_CACHE = {}


def run_ffn(xT_full, gv, w_in, w_out):
    ntok = xT_full.shape[1]
    T = ntok // N_CORES
    key = ("ffn", T)
    if key not in _NC_CACHE:
        _NC_CACHE[key] = build_ffn(T)
    nc = _NC_CACHE[key]
    gl = gain_layout(gv)
    w_in = np.ascontiguousarray(w_in, dtype=np.float32)
    w_out = np.ascontiguousarray(w_out, dtype=np.float32)
    in_maps = [{"xT": np.ascontiguousarray(xT_full[:, c * T:(c + 1) * T]), "g": gl,
                "w_in": w_in, "w_out": w_out} for c in range(N_CORES)]
    res = run_bass_kernel_spmd(nc, in_maps, core_ids=list(range(N_CORES)))
    return np.concatenate([r["oT"] for r in res.results], axis=1)


import ml_dtypes
NPBF = ml_dtypes.bfloat16
NEGM = -30000.0


def split3(v):
    v = np.asarray(v, np.float64)
    hi = v.astype(np.float32).astype(NPBF)
    r = v - hi.astype(np.float64)
    mid = r.astype(np.float32).astype(NPBF)
    r = r - mid.astype(np.float64)
    lo = r.astype(np.float32).astype(NPBF)
    return hi, mid, lo


def alibi_tabs(slopes, qpos, kpos):
    kpos = np.asarray(kpos, np.int64)
    jb, sl = kpos // 128, kpos % 128
    one = np.ones_like(jb)
    ktab = np.stack([jb, jb, jb, sl, sl, sl, one, one, one]).astype(np.float32).astype(NPBF)
    qt = []
    for s in slopes:
        a = split3(np.full(len(qpos), 128.0 * s))
        b = split3(np.full(len(qpos), float(s)))
        c = split3(-float(s) * np.asarray(qpos, np.float64))
        qt.append(np.stack(list(a) + list(b) + list(c)))
    return ktab, np.stack(qt).astype(NPBF)


class Ctx:
    def __init__(self, nc, st):
        self.nc, self.st = nc, st

    def sb(self, name, shape, dt):
        return self.st.enter_context(self.nc.sbuf_tensor(name, shape, dt))

    def ps(self, name, shape=(128, 512), dt=F32):
        return self.st.enter_context(self.nc.psum_tensor(name, list(shape), dt))

    def din(self, name, shape, dt=F32):
        return self.nc.dram_tensor(name, list(shape), dt, kind="ExternalInput").ap()

    def dout(self, name, shape, dt=F32):
        return self.nc.dram_tensor(name, list(shape), dt, kind="ExternalOutput").ap()


def emit_weight_load(S, wdram_v, wsb, gs, stage, ncols, kbase=0, chunk=1408, tag="w"):
    k = kbase
    engs = ["vector", "gpsimd"]
    for c in range(8):
        for c0 in range(0, ncols, chunk):
            w = min(chunk, ncols - c0)
            sg = stage[k % 2]
            S.add("sync", lambda e, sg=sg, c=c, c0=c0, w=w: e.dma_start(out=sg[:, 0:w], in_=wdram_v[:, c, c0:c0 + w]),
                  writes=[("stage", k % 2)], dma=("stage", k % 2))
            S.add(engs[k % 2], lambda e, sg=sg, c=c, c0=c0, w=w: e.tensor_scalar(
                out=wsb[:, c, c0:c0 + w], in0=sg[:, 0:w], scalar1=gs[:, c:c + 1], scalar2=None, op0=ALU.mult),
                reads=[("stage", k % 2), "gs"], writes=[(tag, c)])
            k += 1
    return k


def emit_rmsnorm(S, xt, h, sqbuf, ps_s, rstd, ones, epsb, n, tagx="xt", tagh="h", sqtag="sq", pskey="ps_s"):
    S.add("scalar", lambda e: e.activation(out=sqbuf[:, 0:8, 0:n], in_=xt[:, :, 0:n], func=AF.Square),
          reads=[(tagx, c) for c in range(8)], writes=[(sqtag, c) for c in range(8)])
    for c in range(8):
        S.add("tensor", lambda e, c=c: e.matmul(ps_s[:, 0:n], lhsT=ones[:], rhs=sqbuf[:, c, 0:n],
                                                start=(c == 0), stop=(c == 7)),
              reads=["ones", (sqtag, c)], writes=[pskey])
    S.add("scalar", lambda e: e.activation(out=rstd[:, 0:n], in_=ps_s[:, 0:n], func=AF.Sqrt, bias=epsb[:, 0:1]),
          reads=[pskey, "epsb"], writes=["rstd"])
    S.add("vector", lambda e: e.reciprocal(out=rstd[:, 0:n], in_=rstd[:, 0:n]), reads=["rstd"], writes=["rstd"])
    for c in range(8):
        S.add("vector" if c % 2 == 0 else "gpsimd",
              lambda e, c=c: e.tensor_tensor(out=h[:, c, 0:n], in0=xt[:, c, 0:n], in1=rstd[:, 0:n], op=ALU.mult),
              reads=[(tagx, c), "rstd"], writes=[(tagh, c)])


def emit_headnorm(S, src_ps, dst, gcol, sqh, ps_n, rs, ones64, epsb, n, rd, wr, P=64):
    S.add("scalar", lambda e: e.activation(out=sqh[0:P, 0:n], in_=src_ps, func=AF.Square),
          reads=rd, writes=["sqh"])
    S.add("tensor", lambda e: e.matmul(ps_n[0:P, 0:n], lhsT=ones64[0:P, 0:P], rhs=sqh[0:P, 0:n], start=True, stop=True),
          reads=["sqh", "ones64"], writes=["ps_n"])
    S.add("scalar", lambda e: e.activation(out=rs[0:P, 0:n], in_=ps_n[0:P, 0:n], func=AF.Sqrt, bias=epsb[0:P, 0:1]),
          reads=["ps_n", "epsb"], writes=["rs"])
    S.add("vector", lambda e: e.reciprocal(out=rs[0:P, 0:n], in_=rs[0:P, 0:n]), reads=["rs"], writes=["rs"])
    S.add("vector", lambda e: e.scalar_tensor_tensor(out=dst, in0=src_ps, scalar=gcol, in1=rs[0:P, 0:n],
                                                     op0=ALU.mult, op1=ALU.mult),
          reads=list(rd) + ["rs", "gcols"], writes=wr)


def build_swa(T):
    TH = T + 128
    nc = bass.Bass("TRN2", target_bir_lowering=False)
    with contextlib.ExitStack() as st:
        C = Ctx(nc, st)
        xT = C.din("xT", [D, TH]); g = C.din("g", [128, 8])
        w_in = C.din("w_in", [D, 1536]); w_out = C.din("w_out", [D, D])
        gq = C.din("gq", [64, 1]); gk = C.din("gk", [64, 1]); sinks = C.din("sinks", [64, 16])
        ktab = C.din("ktab", [9, TH], BF16); qtab = C.din("qtab", [9, 16, T], BF16)
        masks = C.din("masks", [128, 3, 512])
        oT = C.dout("oT", [D, T])
        xv = xT.rearrange("(c p) t -> p c t", p=128)
        ov = oT.rearrange("(c p) t -> p c t", p=128)
        wiv = w_in.rearrange("(c p) f -> p c f", p=128)
        wov = w_out.rearrange("(h d) n -> d h n", d=64)

        w = C.sb("w", [128, 8, 1536], BF16)
        wo = C.sb("wo", [64, 16, D], BF16)
        stage = [C.sb("stage%d" % i, [128, 1024], F32) for i in range(2)]
        gs = C.sb("gs", [128, 8], F32); epsb = C.sb("epsb", [128, 1], F32)
        ones = C.sb("ones", [128, 128], BF16); ones64 = C.sb("ones64", [64, 64], BF16)
        onesd = C.sb("onesd", [128, 64], BF16)
        gqs = C.sb("gqs", [64, 1], F32); gks = C.sb("gks", [64, 1], F32); es = C.sb("es", [64, 16], F32)
        msk = C.sb("msk", [128, 3, 512], F32)
        xt = C.sb("xt", [128, 8, 512], F32); h = C.sb("h", [128, 8, 512], BF16)
        rstd = C.sb("rstd", [128, 512], F32)
        sqh = C.sb("sqh", [64, 512], BF16); rs = C.sb("rs", [64, 512], F32)
        kaug = C.sb("kaug", [73, 4, TH], BF16)
        vtok = C.sb("vtok", [128, TH // 128, 256], BF16)
        qaug = C.sb("qaug", [73, 16, 512], BF16)
        PT = [C.sb("PT%d" % i, [128, 2, 512], BF16) for i in range(2)]
        tmp = [C.sb("tmp%d" % i, [128, 512], F32) for i in range(2)]
        oTt = C.sb("oTt", [64, 16, 512], BF16); rec = C.sb("rec", [64, 512], F32)
        ps_s = C.ps("ps_s"); ps_p = [C.ps("ps_p%d" % i) for i in range(2)]; ps_n = C.ps("ps_n")
        st_ = [C.ps("st%d" % i) for i in range(2)]; num = C.ps("num"); den = C.ps("den")

        S = Sched(nc)
        S.add("vector", lambda e: e.memset(ones[:], 1.0 / D), writes=["ones"])
        S.add("vector", lambda e: e.memset(ones64[:], 1.0 / 64), writes=["ones64"])
        S.add("vector", lambda e: e.memset(onesd[:], 1.0), writes=["onesd"])
        S.add("vector", lambda e: e.memset(epsb[:], EPS), writes=["epsb"])
        S.add("sync", lambda e: e.dma_start(out=gs[:], in_=g), writes=["gs"], dma="c0")
        S.add("sync", lambda e: e.dma_start(out=gqs[:], in_=gq), writes=["gcols"], dma="c1")
        S.add("sync", lambda e: e.dma_start(out=gks[:], in_=gk), writes=["gcols"], dma="c1")
        S.add("sync", lambda e: e.dma_start(out=es[:], in_=sinks), writes=["es"], dma="c2")
        S.add("sync", lambda e: e.dma_start(out=msk[:], in_=masks), writes=["msk"], dma="c3")
        for gg in range(4):
            S.add("sync", lambda e, gg=gg: e.dma_start(out=kaug[64:73, gg, :], in_=ktab), writes=[("kaugc", gg)], dma="c4")
        S.add("vector", lambda e: e.tensor_scalar(out=gqs[:], in0=gqs[:], scalar1=0.125, scalar2=None, op0=ALU.mult),
              reads=["gcols"], writes=["gcols"])
        S.add("scalar", lambda e: e.activation(out=es[:], in_=es[:], func=AF.Exp), reads=["es"], writes=["es"])
        k = emit_weight_load(S, wiv, w, gs, stage, 1536, chunk=768)
        for hh in range(16):
            sg = stage[k % 2]
            S.add("sync", lambda e, sg=sg, hh=hh: e.dma_start(out=sg[0:64, 0:D], in_=wov[:, hh, :]),
                  writes=[("stage", k % 2)], dma=("stage", k % 2))
            S.add(["vector", "gpsimd"][k % 2], lambda e, sg=sg, hh=hh: e.tensor_copy(out=wo[:, hh, :], in_=sg[0:64, 0:D]),
                  reads=[("stage", k % 2)], writes=[("wo", hh)])
            k += 1
        XT = [("xt", c) for c in range(8)]
        H = [("h", c) for c in range(8)]
        W = [("w", c) for c in range(8)]
        pcount = [0]

        def proj_fm(col0, M, n):
            p = pcount[0] % 2
            pcount[0] += 1
            for c in range(8):
                S.add("tensor", lambda e, c=c, p=p: e.matmul(ps_p[p][0:M, 0:n], lhsT=w[:, c, col0:col0 + M], rhs=h[:, c, 0:n],
                                                             start=(c == 0), stop=(c == 7)),
                      reads=[("w", c), ("h", c)], writes=[("psp", p)])
            return p

        def proj_tile(tok0, n, with_q):
            emit_rmsnorm(S, xt, h, h, ps_s, rstd, ones, epsb, n, sqtag="h")
            for gg in range(4):
                p = proj_fm(1024 + 64 * gg, 64, n)
                emit_headnorm(S, ps_p[p][0:64, 0:n], kaug[0:64, gg, tok0:tok0 + n], gks[:, 0:1], sqh, ps_n, rs, ones64, epsb,
                              n, rd=[("psp", p)], wr=[("kaug", gg, tok0 // 128 + b) for b in range(n // 128)])
            for b in range(n // 128):
                p = pcount[0] % 2
                pcount[0] += 1
                for c in range(8):
                    S.add("tensor", lambda e, c=c, p=p, b=b: e.matmul(ps_p[p][:, 0:256], lhsT=h[:, c, b * 128:(b + 1) * 128],
                                                                      rhs=w[:, c, 1280:1536], start=(c == 0), stop=(c == 7)),
                          reads=[("w", c), ("h", c)], writes=[("psp", p)])
                S.add("scalar", lambda e, p=p, b=b: e.copy(out=vtok[:, tok0 // 128 + b, :], in_=ps_p[p][:, 0:256]),
                      reads=[("psp", p)], writes=[("vtok", tok0 // 128 + b)])
            if with_q:
                for hq in range(16):
                    p = proj_fm(64 * hq, 64, n)
                    emit_headnorm(S, ps_p[p][0:64, 0:n], qaug[0:64, hq, 0:n], gqs[:, 0:1], sqh, ps_n, rs, ones64, epsb,
                                  n, rd=[("psp", p)], wr=[("qaug", hq)])

        S.add("sync", lambda e: e.dma_start(out=xt[:, :, 0:128], in_=xv[:, :, 0:128]), writes=XT, dma="xt_in")
        proj_tile(0, 128, False)
        acount = 0
        for tt in range(T // 512):
            tok0 = 128 + 512 * tt
            S.add("sync", lambda e, tok0=tok0: e.dma_start(out=xt[:], in_=xv[:, :, tok0:tok0 + 512]), writes=XT, dma="xt_in")
            S.add("gpsimd", lambda e, tt=tt: e.dma_start(out=qaug[64:73, :, :], in_=qtab[:, :, tt * 512:(tt + 1) * 512]),
                  writes=["qaugc"], dma="qc")
            proj_tile(tok0, 512, True)
            groups = [(qb, gg) for qb in range(4) for gg in range(4)]
            ginfo = {}

            def swa_front(idx):
                nonlocal acount
                qb, gg = groups[idx]
                cur = tok0 // 128 + qb
                prev = cur - 1
                pt = PT[acount % 2]
                ptk = ("PT", acount % 2)
                acount += 1
                ginfo[idx] = (pt, ptk, prev, cur)
                rhs_q = qaug[:, 4 * gg:4 * gg + 4, qb * 128:(qb + 1) * 128]
                qreads = [("qaug", 4 * gg + i) for i in range(4)] + ["qaugc"]
                for i, kb in enumerate((prev, cur)):
                    S.add("tensor", lambda e, i=i, kb=kb, gg=gg, rhs_q=rhs_q: e.matmul(
                        st_[i][:].rearrange("p (a b) -> p a b", a=4), lhsT=kaug[:, gg, kb * 128:(kb + 1) * 128], rhs=rhs_q,
                        start=True, stop=True),
                        reads=qreads + [("kaug", gg, kb), ("kaugc", gg)], writes=[("st", i)])
                    mi = (0 if (tt == 0 and qb == 0) else 1) if i == 0 else 2
                    S.add("vector", lambda e, i=i, mi=mi: e.tensor_tensor(out=tmp[i][:], in0=st_[i][:], in1=msk[:, mi, :],
                                                                          op=ALU.add),
                          reads=[("st", i), "msk"], writes=[("tmp", i)])
                    S.add("scalar", lambda e, i=i, pt=pt: e.activation(out=pt[:, i, :], in_=tmp[i][:], func=AF.Exp),
                          reads=[("tmp", i)], writes=[ptk + (i,)])

            def swa_back(idx):
                qb, gg = groups[idx]
                pt, ptk, prev, cur = ginfo[idx]
                for i, kb in enumerate((prev, cur)):
                    S.add("tensor", lambda e, i=i, kb=kb, gg=gg, pt=pt: e.matmul(
                        num[0:64, :], lhsT=vtok[:, kb, 64 * gg:64 * gg + 64], rhs=pt[:, i, :], start=(i == 0), stop=(i == 1)),
                        reads=[("vtok", kb), ptk + (i,)], writes=["num"])
                for i in range(2):
                    S.add("tensor", lambda e, i=i, pt=pt: e.matmul(
                        den[0:64, :], lhsT=onesd[:], rhs=pt[:, i, :], start=(i == 0), stop=(i == 1)),
                        reads=["onesd", ptk + (i,)], writes=["den"])
                S.add("vector", lambda e, gg=gg: e.tensor_tensor(
                    out=rec[:].rearrange("p (a b) -> p a b", a=4), in0=den[0:64, :].rearrange("p (a b) -> p a b", a=4),
                    in1=es[:, 4 * gg:4 * gg + 4].unsqueeze(2).to_broadcast([64, 4, 128]), op=ALU.add),
                    reads=["den", "es"], writes=["rec"])
                S.add("vector", lambda e: e.reciprocal(out=rec[:], in_=rec[:]), reads=["rec"], writes=["rec"])
                S.add("vector", lambda e, gg=gg, qb=qb: e.tensor_tensor(
                    out=oTt[:, 4 * gg:4 * gg + 4, qb * 128:(qb + 1) * 128], in0=num[0:64, :].rearrange("p (a b) -> p a b", a=4),
                    in1=rec[:].rearrange("p (a b) -> p a b", a=4), op=ALU.mult),
                    reads=["num", "rec"], writes=[("oTt", 4 * gg + i) for i in range(4)])

            for idx in range(len(groups)):
                swa_front(idx)
                if idx >= 1:
                    swa_back(idx - 1)
            swa_back(len(groups) - 1)
            for m in range(8):
                p = pcount[0] % 2
                pcount[0] += 1
                for hh in range(16):
                    S.add("tensor", lambda e, m=m, hh=hh, p=p: e.matmul(ps_p[p][:], lhsT=wo[:, hh, m * 128:(m + 1) * 128],
                                                                        rhs=oTt[:, hh, :], start=(hh == 0), stop=(hh == 15)),
                          reads=[("wo", hh), ("oTt", hh)], writes=[("psp", p)])
                S.add("vector", lambda e, m=m, p=p: e.tensor_tensor(out=xt[:, m, :], in0=ps_p[p][:], in1=xt[:, m, :], op=ALU.add),
                      reads=[("psp", p), ("xt", m)], writes=[("xt", m)])
            S.add("sync", lambda e, tt=tt: e.dma_start(out=ov[:, :, tt * 512:(tt + 1) * 512], in_=xt[:]), reads=XT, dma="xt_out")
        S.emit()
    return nc


def alibi_slopes(n):
    return 2.0 ** (-8.0 * np.arange(1, n + 1, dtype=np.float64) / n)


def swa_consts(T):
    TH = T + 128
    ktab, qtab = alibi_tabs(alibi_slopes(16), np.arange(T) + 128, np.arange(TH))
    qtab = np.ascontiguousarray(qtab.transpose(1, 0, 2))
    sl = np.arange(128)[:, None]
    tl = np.arange(128)[None, :]
    mprev = np.where(sl > tl, 0.0, NEGM).astype(np.float32)
    mcur = np.where(sl <= tl, 0.0, NEGM).astype(np.float32)
    t4 = lambda m: np.tile(m, (1, 4))
    m_mid = np.stack([t4(mprev), t4(mprev), t4(mcur)], axis=1)
    m_first = np.stack([np.full((128, 512), NEGM, np.float32), t4(mprev), t4(mcur)], axis=1)
    return ktab, qtab, np.ascontiguousarray(m_first), np.ascontiguousarray(m_mid)


def run_swa(xT_full, S_seq, gv, w_in, w_out, q_norm, k_norm, sinks):
    ntok = xT_full.shape[1]
    T = ntok // N_CORES
    key = ("swa", T)
    if key not in _NC_CACHE:
        _NC_CACHE[key] = build_swa(T)
    nc = _NC_CACHE[key]
    ktab, qtab, m_first, m_mid = swa_consts(T)
    common = {"g": gain_layout(gv), "w_in": np.ascontiguousarray(w_in, np.float32),
              "w_out": np.ascontiguousarray(w_out, np.float32),
              "gq": np.ascontiguousarray(np.asarray(q_norm, np.float32).reshape(64, 1)),
              "gk": np.ascontiguousarray(np.asarray(k_norm, np.float32).reshape(64, 1)),
              "sinks": np.ascontiguousarray(np.broadcast_to(np.asarray(sinks, np.float32).reshape(1, 16), (64, 16))),
              "ktab": ktab, "qtab": qtab}
    in_maps = []
    for c in range(N_CORES):
        t0 = c * T
        first = (t0 % S_seq) == 0
        xh = np.zeros((D, T + 128), np.float32)
        xh[:, 128:] = xT_full[:, t0:t0 + T]
        if not first:
            xh[:, :128] = xT_full[:, t0 - 128:t0]
        in_maps.append(dict(common, xT=xh, masks=m_first if first else m_mid))
    res = run_bass_kernel_spmd(nc, in_maps, core_ids=list(range(N_CORES)))
    return np.concatenate([r["oT"] for r in res.results], axis=1)


def build_diff(S_seq, NH, lam_init, capdist=None):
    nc = bass.Bass("TRN2", target_bir_lowering=False)
    NT = S_seq // 512
    with contextlib.ExitStack() as st:
        C = Ctx(nc, st)
        xT = C.din("xT", [D, S_seq]); g = C.din("g", [128, 8])
        w_in = C.din("w_in", [D, NH, 384])
        gq = C.din("gq", [64, 1]); gk = C.din("gk", [64, 1]); gsub = C.din("gsub", [128, 1])
        lam = C.din("lam", [128, 4, 64])
        ktab = C.din("ktab", [9, S_seq], BF16); qtab = C.din("qtab", [NH, 9, S_seq], BF16)
        masks = C.din("masks", [128, 4, 512])
        oT = C.dout("oT", [NH, 128, S_seq])
        xv = xT.rearrange("(c p) t -> p c t", p=128)
        wiv = w_in.rearrange("(c p) h f -> p c h f", p=128)

        w = C.sb("w", [128, 8, 384], BF16)
        stage = [C.sb("stage%d" % i, [128, 384], F32) for i in range(2)]
        gs = C.sb("gs", [128, 8], F32); epsb = C.sb("epsb", [128, 1], F32)
        ones = C.sb("ones", [128, 128], BF16); ones64 = C.sb("ones64", [64, 64], BF16)
        ones128 = C.sb("ones128", [128, 128], BF16); onesd = C.sb("onesd", [128, 128], BF16)
        gqs = C.sb("gqs", [64, 1], F32); gks = C.sb("gks", [64, 1], F32); gsubs = C.sb("gsubs", [128, 1], F32)
        lams = C.sb("lams", [128, 4, 64], F32); lp = C.sb("lp", [128, 2, 64], F32); l2 = C.sb("l2", [128, 2], F32)
        nlam = C.sb("nlam", [128, 1], F32)
        msk = C.sb("msk", [128, 4, 512], F32)
        xt = C.sb("xt", [128, 8, 512], F32); h = C.sb("h", [128, 8, 512], BF16)
        rstd = C.sb("rstd", [128, 512], F32)
        sqh = C.sb("sqh", [128, 512], BF16); rs = C.sb("rs", [128, 512], F32)
        kaug = C.sb("kaug", [73, 2, S_seq], BF16)
        vtok = C.sb("vtok", [128, S_seq // 128, 128], BF16)
        qaug = C.sb("qaug", [73, 2, 512], BF16)
        NPT = 4
        PT = [C.sb("PT%d" % i, [128, 512], BF16) for i in range(NPT)]
        tmp = [C.sb("tmp%d" % i, [128, 512], F32) for i in range(2)]
        oc = [C.sb("oc%d" % i, [128, 512], F32) for i in range(2)]
        rec = C.sb("rec", [128, 512], F32); ot = C.sb("ot", [128, 512], F32)
        ps_s = C.ps("ps_s"); ps_p = [C.ps("ps_p%d" % i) for i in range(2)]
        st_ = [C.ps("st%d" % i) for i in range(3)]; num = C.ps("num"); den = C.ps("den")
        ps_n = ps_s

        S = Sched(nc)
        S.add("vector", lambda e: e.memset(ones[:], 1.0 / D), writes=["ones"])
        S.add("vector", lambda e: e.memset(ones64[:], 1.0 / 64), writes=["ones64"])
        S.add("vector", lambda e: e.memset(ones128[:], 1.0 / 128), writes=["ones128"])
        S.add("vector", lambda e: e.memset(onesd[:], 1.0), writes=["onesd"])
        S.add("vector", lambda e: e.memset(epsb[:], EPS), writes=["epsb"])
        S.add("sync", lambda e: e.dma_start(out=gs[:], in_=g), writes=["gs"], dma="c0")
        S.add("sync", lambda e: e.dma_start(out=gqs[:], in_=gq), writes=["gcols"], dma="c1")
        S.add("sync", lambda e: e.dma_start(out=gks[:], in_=gk), writes=["gcols"], dma="c1")
        S.add("sync", lambda e: e.dma_start(out=gsubs[:], in_=gsub), writes=["gcols"], dma="c1")
        S.add("sync", lambda e: e.dma_start(out=lams[:], in_=lam), writes=["lams"], dma="c2")
        S.add("sync", lambda e: e.dma_start(out=msk[:], in_=masks), writes=["msk"], dma="c3")
        for cc in range(2):
            S.add("sync", lambda e, cc=cc: e.dma_start(out=kaug[64:73, cc, :], in_=ktab), writes=[("kaugc", cc)], dma="c4")
        S.add("vector", lambda e: e.tensor_scalar(out=gqs[:], in0=gqs[:], scalar1=0.125, scalar2=None, op0=ALU.mult),
              reads=["gcols"], writes=["gcols"])
        S.add("vector", lambda e: e.tensor_scalar(out=gsubs[:], in0=gsubs[:], scalar1=1.0 - lam_init, scalar2=None, op0=ALU.mult),
              reads=["gcols"], writes=["gcols"])
        S.add("vector", lambda e: e.tensor_tensor(out=lp[:], in0=lams[:, 0:4:2, :], in1=lams[:, 1:4:2, :], op=ALU.mult),
              reads=["lams"], writes=["lp"])
        S.add("vector", lambda e: e.reduce_sum(out=l2[:], in_=lp[:], axis=mybir.AxisListType.X), reads=["lp"], writes=["l2"])
        S.add("scalar", lambda e: e.activation(out=l2[:], in_=l2[:], func=AF.Exp), reads=["l2"], writes=["l2"])
        S.add("vector", lambda e: e.tensor_tensor(out=nlam[:], in0=l2[:, 1:2], in1=l2[:, 0:1], op=ALU.subtract),
              reads=["l2"], writes=["nlam"])
        S.add("vector", lambda e: e.tensor_scalar(out=nlam[:], in0=nlam[:], scalar1=-lam_init, scalar2=None, op0=ALU.add),
              reads=["nlam"], writes=["nlam"])
        XT = [("xt", c) for c in range(8)]
        pcount = [0]
        kst = 0
        stc = 0
        ptc = 0
        for hl in range(NH):
            for c in range(8):
                sg = stage[kst % 2]
                S.add("sync", lambda e, sg=sg, c=c, hl=hl: e.dma_start(out=sg[:], in_=wiv[:, c, hl, :]),
                      writes=[("stage", kst % 2)], dma=("stage", kst % 2))
                S.add(["vector", "gpsimd"][kst % 2], lambda e, sg=sg, c=c: e.tensor_scalar(
                    out=w[:, c, :], in0=sg[:], scalar1=gs[:, c:c + 1], scalar2=None, op0=ALU.mult),
                    reads=[("stage", kst % 2), "gs"], writes=[("w", c)])
                kst += 1

            def proj_fm(col0, M, n=512):
                p = pcount[0] % 2
                pcount[0] += 1
                for c in range(8):
                    S.add("tensor", lambda e, c=c, p=p: e.matmul(ps_p[p][0:M, 0:n], lhsT=w[:, c, col0:col0 + M], rhs=h[:, c, 0:n],
                                                                 start=(c == 0), stop=(c == 7)),
                          reads=[("w", c), ("h", c)], writes=[("psp", p)])
                return p

            for tt in range(NT):
                tok0 = 512 * tt
                S.add("sync", lambda e, tok0=tok0: e.dma_start(out=xt[:], in_=xv[:, :, tok0:tok0 + 512]), writes=XT, dma="xt_in")
                S.add("gpsimd", lambda e, tok0=tok0, hl=hl: e.dma_start(
                    out=qaug[64:73, :, :], in_=qtab[hl, :, tok0:tok0 + 512].unsqueeze(1).to_broadcast([9, 2, 512])),
                    writes=["qaugc"], dma="qc")
                emit_rmsnorm(S, xt, h, h, ps_s, rstd, ones, epsb, 512, sqtag="h")
                for cc in range(2):
                    p = proj_fm(128 + 64 * cc, 64)
                    emit_headnorm(S, ps_p[p][0:64, :], kaug[0:64, cc, tok0:tok0 + 512], gks[:, 0:1], sqh, ps_n, rs, ones64, epsb,
                                  512, rd=[("psp", p)], wr=[("kaug", cc, 4 * tt + b) for b in range(4)])
                for b in range(4):
                    p = pcount[0] % 2
                    pcount[0] += 1
                    for c in range(8):
                        S.add("tensor", lambda e, c=c, p=p, b=b: e.matmul(ps_p[p][:, 0:128], lhsT=h[:, c, b * 128:(b + 1) * 128],
                                                                          rhs=w[:, c, 256:384], start=(c == 0), stop=(c == 7)),
                              reads=[("w", c), ("h", c)], writes=[("psp", p)])
                    S.add("scalar", lambda e, p=p, b=b, tt=tt: e.copy(out=vtok[:, 4 * tt + b, :], in_=ps_p[p][:, 0:128]),
                          reads=[("psp", p)], writes=[("vtok", 4 * tt + b)])
                for cc in range(2):
                    p = proj_fm(64 * cc, 64)
                    emit_headnorm(S, ps_p[p][0:64, :], qaug[0:64, cc, :], gqs[:, 0:1], sqh, ps_n, rs, ones64, epsb,
                                  512, rd=[("psp", p)], wr=[("qaug", cc)])
                for cc in range(2):
                    nj = 4 * tt + 4
                    LA = 2
                    slots = {}
                    jmin = 0
                    if capdist is not None and capdist[hl] is not None and 512 * tt - 127 - capdist[hl] >= 0:
                        jmin = (512 * tt - 127 - capdist[hl]) // 128 + 1

                    def emit_front(j, cc=cc, tt=tt, slots=slots):
                        nonlocal stc, ptc
                        sti = stc % 3
                        stc += 1
                        pti = ptc % NPT
                        ptc += 1
                        slots[j] = pti
                        S.add("tensor", lambda e, j=j, cc=cc, sti=sti: e.matmul(
                            st_[sti][:], lhsT=kaug[:, cc, j * 128:(j + 1) * 128], rhs=qaug[:, cc, :], start=True, stop=True),
                            reads=[("qaug", cc), "qaugc", ("kaug", cc, j), ("kaugc", cc)], writes=[("st", sti)])
                        if j >= 4 * tt:
                            bvar = j - 4 * tt
                            ti = j % 2
                            S.add("vector", lambda e, sti=sti, bvar=bvar, ti=ti: e.tensor_tensor(
                                out=tmp[ti][:], in0=st_[sti][:], in1=msk[:, bvar, :], op=ALU.add),
                                reads=[("st", sti), "msk"], writes=[("tmp", ti)])
                            S.add("scalar", lambda e, ti=ti, pti=pti: e.activation(out=PT[pti][:], in_=tmp[ti][:], func=AF.Exp),
                                  reads=[("tmp", ti)], writes=[("PT", pti)])
                        else:
                            S.add("scalar", lambda e, sti=sti, pti=pti: e.activation(out=PT[pti][:], in_=st_[sti][:], func=AF.Exp),
                                  reads=[("st", sti)], writes=[("PT", pti)])

                    def emit_back(j, nj=nj, slots=slots, jmin=jmin):
                        pti = slots[j]
                        S.add("tensor", lambda e, j=j, pti=pti, nj=nj, jmin=jmin: e.matmul(
                            num[:], lhsT=vtok[:, j, :], rhs=PT[pti][:], start=(j == jmin), stop=(j == nj - 1)),
                            reads=[("vtok", j), ("PT", pti)], writes=["num"])
                        S.add("tensor", lambda e, j=j, pti=pti, nj=nj, jmin=jmin: e.matmul(
                            den[:], lhsT=onesd[:], rhs=PT[pti][:], start=(j == jmin), stop=(j == nj - 1)),
                            reads=["onesd", ("PT", pti)], writes=["den"])

                    for j in range(jmin, nj):
                        emit_front(j)
                        if j - jmin >= LA:
                            emit_back(j - LA)
                    for j in range(max(jmin, nj - LA), nj):
                        emit_back(j)
                    S.add("vector", lambda e: e.reciprocal(out=rec[:], in_=den[:]), reads=["den"], writes=["rec"])
                    S.add("vector", lambda e, cc=cc: e.tensor_tensor(out=oc[cc][:], in0=num[:], in1=rec[:], op=ALU.mult),
                          reads=["num", "rec"], writes=[("oc", cc)])
                S.add("vector", lambda e: e.scalar_tensor_tensor(out=ot[:], in0=oc[1][:], scalar=nlam[:, 0:1], in1=oc[0][:],
                                                                 op0=ALU.mult, op1=ALU.add),
                      reads=[("oc", 0), ("oc", 1), "nlam"], writes=["ot"])
                emit_headnorm(S, ot[:], ot[:], gsubs[:, 0:1], sqh, ps_n, rs, ones128, epsb, 512, rd=["ot"], wr=["ot"], P=128)
                S.add("sync", lambda e, tok0=tok0, hl=hl: e.dma_start(out=oT[hl, :, tok0:tok0 + 512], in_=ot[:]),
                      reads=["ot"], dma="o_out")
        S.emit()
    return nc


def diff_masks():
    sl = np.arange(128)[:, None]
    tl = np.arange(128)[None, :]
    caus = np.where(sl <= tl, 0.0, NEGM).astype(np.float32)
    m = np.zeros((128, 4, 4, 128), np.float32)
    for b in range(4):
        for a in range(4):
            m[:, b, a, :] = 0.0 if a > b else (caus if a == b else NEGM)
    return np.ascontiguousarray(m.reshape(128, 4, 512))


def run_diff(xT_full, B, S_seq, gv, w_in, q_norm, k_norm, lam, subln, lam_init):
    NH = 8 * B // N_CORES
    slopes = alibi_slopes(8)
    per_b = N_CORES // B
    capdist = []
    for i in range(NH):
        smin = min(slopes[i * per_b + p] for p in range(per_b))
        cd = int(math.ceil((104.0 + 64.0) / smin))
        capdist.append(cd if cd < S_seq else None)
    key = ("diff", S_seq, NH, lam_init)
    if key not in _NC_CACHE:
        _NC_CACHE[key] = build_diff(S_seq, NH, lam_init, capdist)
    nc = _NC_CACHE[key]
    pos = np.arange(S_seq)
    ktab, qtab_all = alibi_tabs(slopes, pos, pos)
    w_in = np.asarray(w_in, np.float32)
    common = {"g": gain_layout(gv),
              "gq": np.ascontiguousarray(np.asarray(q_norm, np.float32).reshape(64, 1)),
              "gk": np.ascontiguousarray(np.asarray(k_norm, np.float32).reshape(64, 1)),
              "gsub": np.ascontiguousarray(np.asarray(subln, np.float32).reshape(128, 1)),
              "lam": np.ascontiguousarray(np.broadcast_to(np.asarray(lam, np.float32).reshape(1, 4, 64), (128, 4, 64))),
              "ktab": ktab, "masks": diff_masks()}
    in_maps = []
    for c in range(N_CORES):
        b = c // per_b
        heads = [i * per_b + (c % per_b) for i in range(NH)]
        wsel = np.stack([np.concatenate([w_in[:, hh * 128:(hh + 1) * 128], w_in[:, 1024 + hh * 128:1024 + (hh + 1) * 128],
                                         w_in[:, 2048 + hh * 128:2048 + (hh + 1) * 128]], axis=1) for hh in heads], axis=1)
        in_maps.append(dict(common, xT=np.ascontiguousarray(xT_full[:, b * S_seq:(b + 1) * S_seq]),
                            w_in=np.ascontiguousarray(wsel), qtab=np.ascontiguousarray(qtab_all[heads])))
    res = run_bass_kernel_spmd(nc, in_maps, core_ids=list(range(N_CORES)))
    out = np.zeros((B, 1024, S_seq), np.float32)
    for c in range(N_CORES):
        b = c // per_b
        for i in range(NH):
            hh = i * per_b + (c % per_b)
            out[b, hh * 128:(hh + 1) * 128, :] = res.results[c]["oT"][i]
    return out


def build_nsa(S_seq, debug=None):
    nc = bass.Bass("TRN2", target_bir_lowering=False)
    NT = S_seq // 512
    NBLK = S_seq // 128
    NCMP = S_seq // 16 - 1
    NCT = (NCMP + 127) // 128
    NSEL = S_seq // 64
    with contextlib.ExitStack() as st:
        C = Ctx(nc, st)
        xT = C.din("xT", [D, S_seq]); g = C.din("g", [128, 8])
        w_in = C.din("w_in", [D, 652])
        gq = C.din("gq", [64, 1]); gk3 = C.din("gk3", [64, 3])
        cpos = C.din("cpos", [128, 32])
        cw1 = C.din("cw1", [2, 64, 32, 256])
        cw2 = C.din("cw2", [128, 2, 2, 64])
        ktab = C.din("ktab", [9, S_seq], BF16); qtab = C.din("qtab", [9, 4, S_seq], BF16)
        kctab = C.din("kctab", [9, NCT * 128], BF16)
        cmask = C.din("cmask", [128, 17, 128]); wmask = C.din("wmask", [128, 2, 128])
        tmpl = C.din("tmpl", [128, 2, 2 * NSEL])
        ovl = C.din("ovl", [128, NCT, NSEL], BF16)
        i4 = C.din("i4", [128, 512], BF16)
        selrow = C.din("selrow", [12, 12, 64])
        oT = C.dout("oT", [256, S_seq])
        xv = xT.rearrange("(c p) t -> p c t", p=128)
        wiv = w_in.rearrange("(c p) f -> p c f", p=128)

        w = C.sb("w", [128, 8, 652], BF16)
        stage = [C.sb("stage%d" % i, [128, 1024], F32) for i in range(2)]
        gs = C.sb("gs", [128, 8], F32); epsb = C.sb("epsb", [128, 1], F32)
        ones = C.sb("ones", [128, 128], BF16); ones64 = C.sb("ones64", [64, 64], BF16)
        onesd = C.sb("onesd", [128, 128], BF16)
        gqs = C.sb("gqs", [64, 1], F32); gk3s = C.sb("gk3s", [64, 3], F32)
        w1sb = C.sb("w1sb", [128, 32, 256], BF16)
        w2sb = C.sb("w2sb", [128, 2, 2, 64], BF16)
        posT = C.sb("posT", [128, 32], BF16)
        cposb = C.sb("cposb", [128, 2, 2], F32)
        big = C.sb("big", [128, S_seq], BF16)
        kcmp = C.sb("kcmp", [73, NCT * 128], BF16)
        vcmp = C.sb("vcmp", [128, NCT, 64], BF16)
        ov = C.sb("ov", [128, NCT, NSEL], BF16)
        vs_tok = C.sb("vs_tok", [128, NBLK, 65], BF16)
        ones1 = C.sb("ones1", [128, 64], F32)
        kwr = C.sb("kwr", [73, 8 * 128], BF16)
        vwr = C.sb("vwr", [128, 8, 64], BF16)
        qaug = C.sb("qaug", [73, 4, 512], BF16)
        xt = C.sb("xt", [128, 8, 512], F32); h = C.sb("h", [128, 8, 512], BF16)
        rstd = C.sb("rstd", [128, 512], F32)
        sqh = C.sb("sqh", [128, 512], BF16); rs = C.sb("rs", [128, 512], F32)
        gel = [C.sb("gel%d" % i, [128, 512], F32) for i in range(3)]
        gelu = C.sb("gelu", [128, 2, 512], BF16)
        PTc = C.sb("PTc", [128, NCT, 512], BF16)
        NPT = 4
        PT = [C.sb("PT%d" % i, [128, 512], BF16) for i in range(NPT)]
        tmp = [C.sb("tmp%d" % i, [128, 512], F32) for i in range(2)]
        cm = C.sb("cm", [128, 17, 128], F32); wm = C.sb("wm", [128, 2, 128], F32)
        tm = C.sb("tm", [128, 2, 2 * NSEL], F32)
        i4s = C.sb("i4s", [128, 512], BF16)
        srow = C.sb("srow", [12, 12, 64], F32)
        rden = C.sb("rden", [128, 512], F32)
        imp2 = C.sb("imp2", [128, NSEL], F32); imp3 = C.sb("imp3", [128, NSEL], F32)
        m8 = C.sb("m8", [128, 8], F32)
        negsel = C.sb("negsel", [128, NSEL], BF16)
        negx = [C.sb("negx%d" % i, [128, 16, 64], BF16) for i in range(2)]
        gT = C.sb("gT", [12, 512], F32)
        obr = [C.sb("obr%d" % i, [64, 512], F32) for i in range(3)]
        oTt = C.sb("oTt", [64, 4, 512], F32)
        ps_s = C.ps("ps_s"); ps_p = [C.ps("ps_p%d" % i) for i in range(2)]
        st_ = [C.ps("st%d" % i) for i in range(3)]; num = C.ps("num"); den = C.ps("den")
        imp = ps_s
        ps_n = ps_s

        S = Sched(nc)
        S.add("vector", lambda e: e.memset(ones[:], 1.0 / D), writes=["ones"])
        S.add("vector", lambda e: e.memset(ones64[:], 1.0 / 64), writes=["ones64"])
        S.add("vector", lambda e: e.memset(onesd[:], 1.0), writes=["onesd"])
        S.add("vector", lambda e: e.memset(epsb[:], EPS), writes=["epsb"])
        S.add("vector", lambda e: e.memset(kcmp[0:64, :], 0.0), writes=["kcmp"])
        S.add("vector", lambda e: e.memset(vcmp[:], 0.0), writes=["vcmp"])
        S.add("gpsimd", lambda e: e.memset(vs_tok[:, :, 64:65], 1.0), writes=["vsones"])
        S.add("vector", lambda e: e.memset(ones1[:], 1.0), writes=["ones1"])
        ld = [(gs, g, "gs"), (gqs, gq, "gcols"), (gk3s, gk3, "gcols"), (cm, cmask, "cm"), (wm, wmask, "wm"), (tm, tmpl, "tm"),
              (i4s, i4, "i4s"), (srow, selrow, "srow"), (ov, ovl, "ov")]
        for n_, (dst, src, key) in enumerate(ld):
            S.add("sync", lambda e, dst=dst, src=src: e.dma_start(out=dst[:], in_=src), writes=[key], dma="c%d" % n_)
        S.add("sync", lambda e: e.dma_start(out=kcmp[64:73, :], in_=kctab), writes=["kcmpc"], dma="ck")
        S.add("vector", lambda e: e.tensor_scalar(out=gqs[:], in0=gqs[:], scalar1=0.125, scalar2=None, op0=ALU.mult),
              reads=["gcols"], writes=["gcols"])
        kst = emit_weight_load(S, wiv, w, gs, stage, 652, chunk=652)
        for kv in range(2):
            for pg in range(8):
                sg = stage[kst % 2]
                S.add("sync", lambda e, sg=sg, kv=kv, pg=pg: e.dma_start(
                    out=sg[64 * kv:64 * kv + 64, :].rearrange("d (p f) -> d p f", p=4), in_=cw1[kv, :, 4 * pg:4 * pg + 4, :]),
                    writes=[("stage", kst % 2)], dma=("stage", kst % 2))
                S.add(["vector", "gpsimd"][kst % 2], lambda e, sg=sg, kv=kv, pg=pg: e.tensor_copy(
                    out=w1sb[64 * kv:64 * kv + 64, 4 * pg:4 * pg + 4, :],
                    in_=sg[64 * kv:64 * kv + 64, :].rearrange("d (p f) -> d p f", p=4)),
                    reads=[("stage", kst % 2)], writes=["w1sb"])
                kst += 1
        sg = stage[kst % 2]
        S.add("sync", lambda e, sg=sg: e.dma_start(out=sg[:, 0:256].rearrange("p (a b d) -> p a b d", a=2, b=2), in_=cw2),
              writes=[("stage", kst % 2)], dma=("stage", kst % 2))
        S.add("vector", lambda e, sg=sg: e.tensor_copy(out=w2sb[:], in_=sg[:, 0:256].rearrange("p (a b d) -> p a b d", a=2, b=2)),
              reads=[("stage", kst % 2)], writes=["w2sb"])
        kst += 1
        sg = stage[kst % 2]
        S.add("sync", lambda e, sg=sg: e.dma_start(out=sg[:, 0:32], in_=cpos), writes=[("stage", kst % 2)], dma=("stage", kst % 2))
        S.add("vector", lambda e, sg=sg: e.tensor_copy(out=posT[:], in_=sg[:, 0:32]), reads=[("stage", kst % 2)], writes=["posT"])
        kst += 1
        for kv in range(2):
            for hf in range(2):
                for p in range(32):
                    S.add("tensor", lambda e, kv=kv, hf=hf, p=p: e.matmul(
                        ps_p[0][:, 0:1], lhsT=w1sb[64 * kv:64 * kv + 64, p, 128 * hf:128 * hf + 128],
                        rhs=posT[64 * kv:64 * kv + 64, p:p + 1], start=(p == 0), stop=(p == 31)),
                        reads=["w1sb", "posT"], writes=[("psp", 0)])
                S.add("vector", lambda e, kv=kv, hf=hf: e.tensor_copy(out=cposb[:, kv, hf:hf + 1], in_=ps_p[0][:, 0:1]),
                      reads=[("psp", 0)], writes=["cposb"])
        XT = [("xt", c) for c in range(8)]
        pcount = [0]

        def proj_fm(col0, M, n=512):
            p = pcount[0] % 2
            pcount[0] += 1
            for c in range(8):
                S.add("tensor", lambda e, c=c, p=p: e.matmul(ps_p[p][0:M, 0:n], lhsT=w[:, c, col0:col0 + M], rhs=h[:, c, 0:n],
                                                             start=(c == 0), stop=(c == 7)),
                      reads=[("w", c), ("h", c)], writes=[("psp", p)])
            return p

        def proj_tok(col0, ncol, b):
            p = pcount[0] % 2
            pcount[0] += 1
            for c in range(8):
                S.add("tensor", lambda e, c=c, p=p: e.matmul(ps_p[p][:, 0:ncol], lhsT=h[:, c, b * 128:(b + 1) * 128],
                                                             rhs=w[:, c, col0:col0 + ncol], start=(c == 0), stop=(c == 7)),
                      reads=[("w", c), ("h", c)], writes=[("psp", p)])
            return p

        for tt in range(NT):
            tok0 = 512 * tt
            S.add("sync", lambda e, tok0=tok0: e.dma_start(out=xt[:], in_=xv[:, :, tok0:tok0 + 512]), writes=XT, dma="xt_in")
            emit_rmsnorm(S, xt, h, h, ps_s, rstd, ones, epsb, 512, sqtag="h")
            p = proj_fm(256, 128)
            S.add("scalar", lambda e, p=p, tok0=tok0: e.copy(out=big[:, tok0:tok0 + 512], in_=ps_p[p][:]),
                  reads=[("psp", p)], writes=[("big", 4 * tt + b) for b in range(4)])
        BIGALL = [("big", b) for b in range(NBLK)]
        for n0 in range(0, NCMP, 512):
            N = min(512, NCMP - n0)
            for kv in range(2):
                for hf in range(2):
                    pp = pcount[0] % 2
                    pcount[0] += 1
                    for p in range(32):
                        a0 = 16 * n0 + p
                        S.add("tensor", lambda e, kv=kv, hf=hf, p=p, pp=pp, a0=a0, N=N: e.matmul(
                            ps_p[pp][:, 0:N], lhsT=w1sb[64 * kv:64 * kv + 64, p, 128 * hf:128 * hf + 128],
                            rhs=big[64 * kv:64 * kv + 64, a0:a0 + 16 * (N - 1) + 1:16], start=(p == 0), stop=(p == 31)),
                            reads=["w1sb"] + BIGALL, writes=[("psp", pp)])
                    S.add("scalar", lambda e, kv=kv, hf=hf, pp=pp, N=N: e.activation(
                        out=gel[0][:, 0:N], in_=ps_p[pp][:, 0:N], func=AF.Identity, bias=cposb[:, kv, hf:hf + 1]),
                        reads=[("psp", pp), "cposb"], writes=["gel0"])
                    S.add("vector", lambda e, N=N: e.tensor_tensor(out=gel[1][:, 0:N], in0=gel[0][:, 0:N], in1=gel[0][:, 0:N], op=ALU.mult),
                          reads=["gel0"], writes=["gel1"])
                    S.add("vector", lambda e, N=N: e.tensor_scalar(out=gel[1][:, 0:N], in0=gel[1][:, 0:N], scalar1=0.044715, scalar2=1.0,
                                                                   op0=ALU.mult, op1=ALU.add), reads=["gel1"], writes=["gel1"])
                    S.add("vector", lambda e, N=N: e.tensor_tensor(out=gel[1][:, 0:N], in0=gel[1][:, 0:N], in1=gel[0][:, 0:N], op=ALU.mult),
                          reads=["gel1", "gel0"], writes=["gel1"])
                    S.add("scalar", lambda e, N=N: e.activation(out=gel[2][:, 0:N], in_=gel[1][:, 0:N], func=AF.Sigmoid,
                                                                scale=1.5957691216057308), reads=["gel1"], writes=["gel2"])
                    S.add("vector", lambda e, hf=hf, N=N: e.tensor_tensor(out=gelu[:, hf, 0:N], in0=gel[0][:, 0:N], in1=gel[2][:, 0:N],
                                                                          op=ALU.mult), reads=["gel0", "gel2"], writes=[("gelu", hf)])
                if kv == 0:
                    pp = pcount[0] % 2
                    pcount[0] += 1
                    for hf in range(2):
                        S.add("tensor", lambda e, hf=hf, pp=pp, N=N: e.matmul(ps_p[pp][0:64, 0:N], lhsT=w2sb[:, 0, hf, :], rhs=gelu[:, hf, 0:N],
                                                                              start=(hf == 0), stop=(hf == 1)),
                              reads=["w2sb", ("gelu", hf)], writes=[("psp", pp)])
                    emit_headnorm(S, ps_p[pp][0:64, 0:N], kcmp[0:64, n0:n0 + N], gk3s[:, 0:1], sqh, ps_n, rs, ones64, epsb, N,
                                  rd=[("psp", pp)], wr=["kcmp"])
                else:
                    for nt in range(0, N, 128):
                        M = min(128, N - nt)
                        pp = pcount[0] % 2
                        pcount[0] += 1
                        for hf in range(2):
                            S.add("tensor", lambda e, hf=hf, pp=pp, nt=nt, M=M: e.matmul(
                                ps_p[pp][0:M, 0:64], lhsT=gelu[:, hf, nt:nt + M], rhs=w2sb[:, 1, hf, :], start=(hf == 0), stop=(hf == 1)),
                                reads=["w2sb", ("gelu", hf)], writes=[("psp", pp)])
                        S.add("scalar", lambda e, pp=pp, nt=nt, M=M, n0=n0: e.copy(out=vcmp[0:M, (n0 + nt) // 128, :], in_=ps_p[pp][0:M, 0:64]),
                              reads=[("psp", pp)], writes=["vcmp"])
        S.add("sync", lambda e: e.dma_start(out=big[64:73, :], in_=ktab), reads=BIGALL, writes=BIGALL + ["bigc"], dma="ck2")
        stc = [0]
        ptc = [0]

        def attn_steps(steps, vM, final_name, with_den=True):
            n = len(steps)
            LA = 2
            slots = {}

            def emit_front(i):
                sp = steps[i]
                sti = stc[0] % 3
                stc[0] += 1
                if sp.get("pt") is None:
                    pti = ptc[0] % NPT
                    ptc[0] += 1
                    pt, ptkey = PT[pti][:], ("PT", pti)
                else:
                    pt, ptkey = sp["pt"]
                slots[i] = (pt, ptkey)
                if sp.get("pre") is not None:
                    S.add(sp["pre"][0], sp["pre"][1], reads=sp["pre"][2], writes=sp["pre"][3])
                S.add("tensor", lambda e, sp=sp, sti=sti: e.matmul(
                    st_[sti][:].rearrange("p (a b) -> p a b", a=4), lhsT=sp["lhsT_k"], rhs=sp["rhs_q"], start=True,
                    stop=(sp.get("extra") is None)),
                    reads=sp["qreads"] + sp["kreads"], writes=[("st", sti)])
                if sp.get("extra") is not None:
                    xl, xr, xreads = sp["extra"]
                    S.add("tensor", lambda e, xl=xl, xr=xr, sti=sti: e.matmul(st_[sti][:], lhsT=xl, rhs=xr, start=False, stop=True),
                          reads=xreads, writes=[("st", sti)])
                if sp.get("mask") is not None:
                    ti = i % 2
                    S.add("vector", lambda e, sp=sp, sti=sti, ti=ti: e.tensor_tensor(
                        out=tmp[ti][:].rearrange("p (a b) -> p a b", a=4), in0=st_[sti][:].rearrange("p (a b) -> p a b", a=4),
                        in1=sp["mask"].unsqueeze(1).to_broadcast([128, 4, 128]), op=ALU.add),
                        reads=[("st", sti)] + sp["mreads"], writes=[("tmp", ti)])
                    S.add("scalar", lambda e, ti=ti, pt=pt: e.activation(out=pt, in_=tmp[ti][:], func=AF.Exp),
                          reads=[("tmp", ti)], writes=[ptkey])
                else:
                    S.add("scalar", lambda e, sti=sti, pt=pt: e.activation(out=pt, in_=st_[sti][:], func=AF.Exp),
                          reads=[("st", sti)], writes=[ptkey])

            def emit_back(i):
                sp = steps[i]
                pt, ptkey = slots[i]
                S.add("tensor", lambda e, sp=sp, pt=pt, i=i: e.matmul(num[0:vM, :], lhsT=sp["v_lhsT"], rhs=pt, start=(i == 0), stop=(i == n - 1)),
                      reads=sp["vreads"] + [ptkey], writes=["num"])
                if with_den:
                    S.add("tensor", lambda e, pt=pt, i=i: e.matmul(den[:], lhsT=onesd[:], rhs=pt, start=(i == 0), stop=(i == n - 1)),
                          reads=["onesd", ptkey], writes=["den"])

            for i in range(n):
                emit_front(i)
                if i >= LA:
                    emit_back(i - LA)
            for i in range(max(0, n - LA), n):
                emit_back(i)

        for tt in range(NT):
            tok0 = 512 * tt
            S.add("sync", lambda e, tok0=tok0: e.dma_start(out=xt[:], in_=xv[:, :, tok0:tok0 + 512]), writes=XT, dma="xt_in")
            S.add("gpsimd", lambda e, tok0=tok0: e.dma_start(out=qaug[64:73, :, :], in_=qtab[:, :, tok0:tok0 + 512]),
                  writes=["qaugc"], dma="qc")
            for b in range(4):
                slot = (4 * tt + b) % 8
                S.add("gpsimd", lambda e, slot=slot, b=b, tok0=tok0: e.dma_start(
                    out=kwr[64:73, slot * 128:(slot + 1) * 128], in_=ktab[:, tok0 + b * 128:tok0 + (b + 1) * 128]),
                    writes=[("kwrc", slot)], dma=("kwc", slot))
            emit_rmsnorm(S, xt, h, h, ps_s, rstd, ones, epsb, 512, sqtag="h")
            p = proj_fm(384, 64)
            emit_headnorm(S, ps_p[p][0:64, :], big[0:64, tok0:tok0 + 512], gk3s[:, 1:2], sqh, ps_n, rs, ones64, epsb, 512,
                          rd=[("psp", p)], wr=[("big", 4 * tt + b) for b in range(4)])
            p = proj_fm(512, 64)
            s0 = (4 * tt) % 8
            emit_headnorm(S, ps_p[p][0:64, :], kwr[0:64, s0 * 128:(s0 + 4) * 128], gk3s[:, 2:3], sqh, ps_n, rs, ones64, epsb, 512,
                          rd=[("psp", p)], wr=[("kwr", s0 + b) for b in range(4)])
            for b in range(4):
                p = proj_tok(448, 64, b)
                S.add("scalar", lambda e, p=p, b=b, tt=tt: e.copy(out=vs_tok[:, 4 * tt + b, 0:64], in_=ps_p[p][:, 0:64]),
                      reads=[("psp", p)], writes=[("vs", 4 * tt + b)])
                p = proj_tok(576, 64, b)
                S.add("scalar", lambda e, p=p, b=b, s0=s0: e.copy(out=vwr[:, s0 + b, :], in_=ps_p[p][:, 0:64]),
                      reads=[("psp", p)], writes=[("vwr", s0 + b)])
            for gi in range(4):
                p = proj_fm(64 * gi, 64)
                emit_headnorm(S, ps_p[p][0:64, :], qaug[0:64, gi, :], gqs[:, 0:1], sqh, ps_n, rs, ones64, epsb, 512,
                              rd=[("psp", p)], wr=[("qaug", gi)])
            p = proj_fm(640, 12)
            S.add("scalar", lambda e, p=p: e.activation(out=gT[:], in_=ps_p[p][0:12, :], func=AF.Sigmoid),
                  reads=[("psp", p)], writes=["gT"])
            qreads = [("qaug", gi) for gi in range(4)] + ["qaugc"]
            for ql in range(4):
                qb = 4 * tt + ql
                rhs_q = qaug[:, :, ql * 128:(ql + 1) * 128]
                nkt = min(NCT, qb // 16 + 1)
                steps = []
                for kt in range(nkt):
                    dl = qb - 16 * kt
                    steps.append(dict(lhsT_k=kcmp[:, kt * 128:(kt + 1) * 128], rhs_q=rhs_q, qreads=qreads, kreads=["kcmp", "kcmpc"],
                                      mask=(cm[:, dl, :] if dl <= 16 else None), mreads=["cm"],
                                      v_lhsT=vcmp[:, kt, :], vreads=["vcmp"], pt=(PTc[:, kt, :], ("PTc", kt))))
                attn_steps(steps, 64, "cmp")
                S.add("vector", lambda e: e.tensor_scalar(out=rden[:], in0=den[:], scalar1=1e-30, scalar2=None, op0=ALU.max),
                      reads=["den"], writes=["rden"])
                S.add("vector", lambda e: e.reciprocal(out=rden[:], in_=rden[:]), reads=["rden"], writes=["rden"])
                S.add("vector", lambda e: e.tensor_tensor(out=obr[0][:], in0=num[0:64, :], in1=rden[0:64, :], op=ALU.mult),
                      reads=["num", "rden"], writes=[("obr", 0)])
                for kt in range(nkt):
                    S.add("gpsimd", lambda e, kt=kt: e.tensor_tensor(out=PTc[:, kt, :], in0=PTc[:, kt, :], in1=rden[:], op=ALU.mult),
                          reads=[("PTc", kt), "rden"], writes=[("PTc", kt)])
                    for gi in range(4):
                        S.add("tensor", lambda e, kt=kt, gi=gi, nkt=nkt: e.matmul(
                            imp[:, 0:NSEL], lhsT=PTc[:, kt, gi * 128:(gi + 1) * 128], rhs=ov[:, kt, :],
                            start=(kt == 0 and gi == 0), stop=(kt == nkt - 1 and gi == 3)),
                            reads=[("PTc", kt), "ov"], writes=["ps_s"])
                c0 = NSEL - 2 * qb
                S.add("vector", lambda e, c0=c0: e.tensor_tensor(out=imp2[:], in0=imp[:, 0:NSEL], in1=tm[:, 0, c0:c0 + NSEL], op=ALU.mult),
                      reads=["ps_s", "tm"], writes=["imp2"])
                S.add("vector", lambda e, c0=c0: e.tensor_tensor(out=imp2[:], in0=imp2[:], in1=tm[:, 1, c0:c0 + NSEL], op=ALU.add),
                      reads=["imp2", "tm"], writes=["imp2"])
                S.add("vector", lambda e: e.memset(imp2[:, 0:1], 1e9), reads=["imp2"], writes=["imp2"])
                S.add("vector", lambda e: e.max(out=m8[:], in_=imp2[:]), reads=["imp2"], writes=["m8"])
                S.add("vector", lambda e: e.match_replace(out=imp3[:], in_to_replace=m8[:], in_values=imp2[:], imm_value=-3e38),
                      reads=["imp2", "m8"], writes=["imp3"])
                S.add("vector", lambda e: e.max(out=m8[:], in_=imp3[:]), reads=["imp3"], writes=["m8"])
                S.add("vector", lambda e: e.match_replace(out=imp3[:], in_to_replace=m8[:], in_values=imp3[:], imm_value=-3e38),
                      reads=["imp3", "m8"], writes=["imp3"])
                S.add("vector", lambda e: e.tensor_tensor(out=imp3[:], in0=imp2[:], in1=imp3[:], op=ALU.not_equal),
                      reads=["imp2", "imp3"], writes=["imp3"])
                S.add("vector", lambda e: e.tensor_scalar(out=negsel[:], in0=imp3[:], scalar1=-1.0, scalar2=-NEGM, op0=ALU.add, op1=ALU.mult),
                      reads=["imp3"], writes=["negsel"])
                steps = []
                for jt in range(qb + 1):
                    xb = (jt // 8) % 2
                    pre = None
                    if jt % 8 == 0:
                        nb = min(16, 2 * (qb + 1) - 2 * jt)
                        pre = ("gpsimd", (lambda e, jt=jt, xb=xb, nb=nb: e.tensor_copy(
                            out=negx[xb][:, 0:nb, :], in_=negsel[:, 2 * jt:2 * jt + nb].unsqueeze(2).to_broadcast([128, nb, 64]))),
                            ["negsel"], [("negx", xb)])
                    steps.append(dict(lhsT_k=big[0:73, jt * 128:(jt + 1) * 128], rhs_q=rhs_q, qreads=qreads, kreads=[("big", jt), "bigc"],
                                      mask=(wm[:, 1, :] if jt == qb else None), mreads=["wm"], pre=pre,
                                      extra=(negx[xb][:, 2 * (jt % 8):2 * (jt % 8) + 2, :].rearrange("p a b -> p (a b)"), i4s[:],
                                             [("negx", xb), "i4s"]),
                                      v_lhsT=vs_tok[:, jt, :], vreads=[("vs", jt), "vsones"]))
                attn_steps(steps, 65, "sel", with_den=False)
                S.add("vector", lambda e: e.reciprocal(out=rden[64:65, :], in_=num[64:65, :]), reads=["num"], writes=["rden"])
                S.add("tensor", lambda e: e.matmul(den[0:64, :], lhsT=ones1[64:65, :], rhs=rden[64:65, :], start=True, stop=True),
                      reads=["rden", "ones1"], writes=["den"])
                S.add("scalar", lambda e: e.copy(out=rden[0:64, :], in_=den[0:64, :]), reads=["den"], writes=["rden"])
                S.add("vector", lambda e: e.tensor_tensor(out=obr[1][:], in0=num[0:64, :], in1=rden[0:64, :], op=ALU.mult),
                      reads=["num", "rden"], writes=[("obr", 1)])
                steps = []
                for jt in range(max(0, qb - 4), qb + 1):
                    slot = jt % 8
                    mk = wm[:, 1, :] if jt == qb else (wm[:, 0, :] if jt == qb - 4 else None)
                    steps.append(dict(lhsT_k=kwr[:, slot * 128:(slot + 1) * 128], rhs_q=rhs_q, qreads=qreads,
                                      kreads=[("kwr", slot), ("kwrc", slot)], mask=mk, mreads=["wm"],
                                      v_lhsT=vwr[:, slot, :], vreads=[("vwr", slot)]))
                attn_steps(steps, 64, "win")
                S.add("vector", lambda e: e.reciprocal(out=rden[0:64, :], in_=den[0:64, :]), reads=["den"], writes=["rden"])
                S.add("vector", lambda e: e.tensor_tensor(out=obr[2][:], in0=num[0:64, :], in1=rden[0:64, :], op=ALU.mult),
                      reads=["num", "rden"], writes=[("obr", 2)])
                osl = oTt[:, :, ql * 128:(ql + 1) * 128]
                if debug is not None:
                    S.add("vector", lambda e, osl=osl: e.tensor_copy(out=osl, in_=obr[debug][:].rearrange("p (a b) -> p a b", a=4)),
                          reads=[("obr", debug)], writes=["oTt"])
                    continue
                for br in range(3):
                    p = pcount[0] % 2
                    pcount[0] += 1
                    for gi in range(4):
                        S.add("tensor", lambda e, p=p, gi=gi, br=br, ql=ql: e.matmul(
                            ps_p[p][0:64, gi * 128:(gi + 1) * 128], lhsT=srow[:, gi * 3 + br, :], rhs=gT[:, ql * 128:(ql + 1) * 128],
                            start=True, stop=True), reads=["srow", "gT"], writes=[("psp", p)])
                    if br == 0:
                        S.add("vector", lambda e, p=p, osl=osl: e.tensor_tensor(
                            out=osl, in0=ps_p[p][0:64, :].rearrange("p (a b) -> p a b", a=4),
                            in1=obr[0][:].rearrange("p (a b) -> p a b", a=4), op=ALU.mult),
                            reads=[("psp", p), ("obr", 0)], writes=["oTt"])
                    else:
                        S.add("vector", lambda e, p=p, br=br: e.tensor_tensor(out=obr[br][:], in0=ps_p[p][0:64, :], in1=obr[br][:], op=ALU.mult),
                              reads=[("psp", p), ("obr", br)], writes=[("obr", br)])
                        S.add("vector", lambda e, br=br, osl=osl: e.tensor_tensor(
                            out=osl, in0=osl, in1=obr[br][:].rearrange("p (a b) -> p a b", a=4), op=ALU.add),
                            reads=["oTt", ("obr", br)], writes=["oTt"])
            S.add("sync", lambda e, tok0=tok0: e.dma_start(
                out=oT.rearrange("(a d) t -> d a t", d=64)[:, :, tok0:tok0 + 512], in_=oTt[:]), reads=["oTt"], dma="o_out")
        S.emit()
    return nc


def nsa_consts(S_seq, grp):
    NCMP = S_seq // 16 - 1
    NCT = (NCMP + 127) // 128
    NSEL = S_seq // 64
    slopes = alibi_slopes(16)[4 * grp:4 * grp + 4]
    pos = np.arange(S_seq)
    ktab, qtab = alibi_tabs(slopes, pos, pos)
    qtab = np.ascontiguousarray(qtab.transpose(1, 0, 2))
    kctab, _ = alibi_tabs(slopes[:1], pos[:1], 16 * np.arange(NCT * 128) + 31)
    nl = np.arange(128)[:, None]
    tl = np.arange(128)[None, :]
    cmask = np.stack([np.where(16 * nl + 31 <= 128 * dl + tl, 0.0, NEGM) for dl in range(17)], axis=1).astype(np.float32)
    wmask = np.stack([np.where(nl > tl, 0.0, NEGM), np.where(nl <= tl, 0.0, NEGM)], axis=1).astype(np.float32)
    rel = np.arange(2 * NSEL)[None, :] - NSEL
    cflag = (np.arange(128)[:, None] >= 64).astype(np.int64)
    forced = (rel == cflag) | (rel == cflag - 1)
    noncausal = rel > cflag
    keep = np.where(forced | noncausal, 0.0, 1.0)
    add = np.where(forced, 1e9, np.where(noncausal, -1e30, 0.0))
    tmpl = np.stack([keep, add], axis=1).astype(np.float32)
    n = np.arange(NCT * 128)[:, None]
    j = np.arange(NSEL)[None, :]
    ovm = ((16 * n < 64 * j + 64) & (16 * n + 32 > 64 * j) & (n < NCMP)).astype(np.float32)
    ovl = np.ascontiguousarray(ovm.reshape(NCT, 128, NSEL).transpose(1, 0, 2)).astype(NPBF)
    i4 = np.tile(np.eye(128, dtype=np.float32), (1, 4)).astype(NPBF)
    selrow = np.zeros((12, 12, 64), np.float32)
    for r in range(12):
        selrow[r, r, :] = 1.0
    return dict(ktab=ktab, qtab=qtab, kctab=kctab, cmask=np.ascontiguousarray(cmask), wmask=np.ascontiguousarray(wmask),
                tmpl=np.ascontiguousarray(tmpl), ovl=ovl, i4=i4, selrow=selrow)


def run_nsa(xT_full, B, S_seq, gv, w_in, q_norm, k_norm, cmp_pos, cmp_w1, cmp_w2, debug=None):
    assert B * 4 == N_CORES
    key = ("nsa", S_seq, debug)
    if key not in _NC_CACHE:
        _NC_CACHE[key] = build_nsa(S_seq, debug)
    nc = _NC_CACHE[key]
    w_in = np.asarray(w_in, np.float32)
    cmp_w1 = np.asarray(cmp_w1, np.float32)
    cmp_w2 = np.asarray(cmp_w2, np.float32)
    cmp_pos = np.asarray(cmp_pos, np.float32)
    common = {"g": gain_layout(gv),
              "gq": np.ascontiguousarray(np.asarray(q_norm, np.float32).reshape(64, 1)),
              "gk3": np.ascontiguousarray(np.asarray(k_norm, np.float32).reshape(3, 64).T),
              "cpos": np.ascontiguousarray(cmp_pos.transpose(0, 2, 1).reshape(128, 32)),
              "cw1": np.ascontiguousarray(cmp_w1.reshape(2, 32, 64, 256).transpose(0, 2, 1, 3)),
              "cw2": np.ascontiguousarray(cmp_w2.reshape(2, 2, 128, 64).transpose(2, 0, 1, 3))}
    in_maps = []
    for c in range(N_CORES):
        b, grp = c // 4, c % 4
        cols = [w_in[:, grp * 256:(grp + 1) * 256]]
        for i in range(6):
            cols.append(w_in[:, 1024 + 256 * i + 64 * grp:1024 + 256 * i + 64 * grp + 64])
        cols.append(w_in[:, 2560 + 12 * grp:2560 + 12 * grp + 12])
        m = dict(common, xT=np.ascontiguousarray(xT_full[:, b * S_seq:(b + 1) * S_seq]),
                 w_in=np.ascontiguousarray(np.concatenate(cols, axis=1)))
        m.update(nsa_consts(S_seq, grp))
        in_maps.append(m)
    res = run_bass_kernel_spmd(nc, in_maps, core_ids=list(range(N_CORES)))
    out = np.zeros((B, 1024, S_seq), np.float32)
    for c in range(N_CORES):
        b, grp = c // 4, c % 4
        out[b, grp * 256:(grp + 1) * 256, :] = res.results[c]["oT"]
    return out


def build_gdn(S_seq, NH):
    nc = bass.Bass("TRN2", target_bir_lowering=False)
    NT = S_seq // 512
    WC = 514
    with contextlib.ExitStack() as st:
        C = Ctx(nc, st)
        xT = C.din("xT", [D, S_seq]); g = C.din("g", [128, 8])
        w_in = C.din("w_in", [D, NH, WC])
        convw = C.din("convw", [128, NH, 3, 4])
        alog = C.din("alog", [128, NH]); dtb = C.din("dtb", [128, NH])
        onrm = C.din("onrm", [128, 128])
        cst = C.din("cst", [128, 6, 128])
        oT = C.dout("oT", [NH, S_seq, 128])
        xv = xT.rearrange("(c p) t -> p c t", p=128)
        wiv = w_in.rearrange("(c p) h f -> p c h f", p=128)

        w = C.sb("w", [128, 8, NH, WC], BF16)
        stage = [C.sb("stage%d" % i, [128, WC], F32) for i in range(2)]
        gs = C.sb("gs", [128, 8], F32); epsb = C.sb("epsb", [128, 1], F32)
        ones = C.sb("ones", [128, 128], BF16); onesf = C.sb("onesf", [128, 128], BF16)
        cw = C.sb("cw", [128, NH, 3, 4], F32)
        negA = C.sb("negA", [128, NH], F32); dtbs = C.sb("dtbs", [128, NH], F32)
        onr = C.sb("onr", [128, 128], F32)
        K_ = C.sb("cst_sb", [128, 6, 128], F32)
        ident, U2, Ublk, mTi, mS, sT01 = [K_[:, i, :] for i in range(6)]
        xt = C.sb("xt", [128, 8, 512], F32); h = C.sb("h", [128, 8, 512], BF16)
        rstd = C.sb("rstd", [128, 512], F32)
        pre = [[C.sb("pre%d_%d" % (a, b), [128, 515], F32) for b in range(3)] for a in range(NH)]
        post = [[C.sb("post%d_%d" % (a, b), [128, 512], F32) for b in range(3)] for a in range(NH)]
        cacc = C.sb("cacc", [128, 512], F32)
        sqf = C.sb("sqf", [128, 512], BF16); rsf = C.sb("rsf", [128, 512], F32)
        Sst = [C.sb("S%d" % a, [128, 128], F32) for a in range(NH)]
        def hs(name, shape, n, dt=F32):
            return [C.sb("%s%d" % (name, a), shape, dt) for a in range(n)]
        NU = 4 * NH
        cols = hs("cols", [128, 16], NU); grep = hs("grep", [128, 128], 4); brep = hs("brep", [128, 128], 4)
        argA = hs("argA", [128, 128], 4); argB = hs("argB", [128, 128], 4); DT = hs("DT", [128, 128], 4); Dm = hs("Dm", [128, 128], 4)
        Nm = hs("Nm", [128, 128], 4); NTm = hs("NTm", [128, 128], 4); attnT = hs("attnT", [128, 128], NU)
        Xm = hs("Xm", [128, 128], 4); Pm = hs("Pm", [128, 2, 128], 4); PTm = hs("PTm", [128, 2, 128], 4)
        vb = hs("vb", [128, 128], 4); kbg = hs("kbg", [128, 128], 4); kd = hs("kd", [128, 128], NU); egr = hs("egr", [128, 128], 4)
        qgT = hs("qgT", [128, 128], NU); u = hs("u", [128, 128], NU); wT = hs("wT", [128, 128], NU); vnew = hs("vnew", [128, 128], NH)
        gsil = hs("gsil", [128, 128], NU); osb = hs("osb", [128, 128], NU); osq = hs("osq", [128, 128], NH); glc = hs("glc", [128, 2], NU)
        ocol = hs("ocol", [128, 2], NH)
        banks = [C.ps("bank%d" % i) for i in range(8)]
        ps_s = banks[0]
        ps_p = [banks[1], banks[2]]
        Q = lambda b, q: banks[b][:, q * 128:(q + 1) * 128]

        S = Sched(nc)
        S.add("vector", lambda e: e.memset(ones[:], 1.0 / D), writes=["ones"])
        S.add("vector", lambda e: e.memset(onesf[:], 1.0), writes=["onesf"])
        S.add("vector", lambda e: e.memset(epsb[:], EPS), writes=["epsb"])
        for a in range(NH):
            S.add("vector", lambda e, a=a: e.memset(Sst[a][:], 0.0), writes=[("S", a)])
            for b in range(3):
                S.add("gpsimd", lambda e, a=a, b=b: e.memset(pre[a][b][:, 0:3], 0.0), writes=[("pre", a, b)])
        ld = [(gs, g, "gs"), (cw, convw, "cw"), (negA, alog, "negA"), (dtbs, dtb, "dtbs"), (onr, onrm, "onr"), (K_, cst, "cst")]
        for n_, (dst, src, key) in enumerate(ld):
            S.add("sync", lambda e, dst=dst, src=src: e.dma_start(out=dst[:], in_=src), writes=[key], dma="c%d" % n_)
        S.add("scalar", lambda e: e.activation(out=negA[:], in_=negA[:], func=AF.Exp), reads=["negA"], writes=["negA"])
        S.add("vector", lambda e: e.tensor_scalar(out=negA[:], in0=negA[:], scalar1=-1.0, scalar2=None, op0=ALU.mult),
              reads=["negA"], writes=["negA"])
        kst = 0
        for c in range(8):
            for a in range(NH):
                sg = stage[kst % 2]
                S.add("sync", lambda e, sg=sg, c=c, a=a: e.dma_start(out=sg[:], in_=wiv[:, c, a, :]),
                      writes=[("stage", kst % 2)], dma=("stage", kst % 2))
                S.add(["vector", "gpsimd"][kst % 2], lambda e, sg=sg, c=c, a=a: e.tensor_scalar(
                    out=w[:, c, a, :], in0=sg[:], scalar1=gs[:, c:c + 1], scalar2=None, op0=ALU.mult),
                    reads=[("stage", kst % 2), "gs"], writes=[("w", c)])
                kst += 1
        XT = [("xt", c) for c in range(8)]
        pcount = [0]

        def mmf(out, lhsT, rhs, reads, writes, start=True, stop=True):
            S.add("tensor", lambda e: e.matmul(out, lhsT=lhsT, rhs=rhs, start=start, stop=stop), reads=reads, writes=writes)

        for tt in range(NT):
            tok0 = 512 * tt
            S.add("sync", lambda e, tok0=tok0: e.dma_start(out=xt[:], in_=xv[:, :, tok0:tok0 + 512]), writes=XT, dma="xt_in")
            emit_rmsnorm(S, xt, h, h, ps_s, rstd, ones, epsb, 512, sqtag="h", pskey="b0")
            for a in range(NH):
                for b in range(3):
                    p = pcount[0] % 2
                    pcount[0] += 1
                    for c in range(8):
                        S.add("tensor", lambda e, c=c, p=p, a=a, b=b: e.matmul(
                            ps_p[p][:], lhsT=w[:, c, a, 128 * b:128 * b + 128], rhs=h[:, c, :], start=(c == 0), stop=(c == 7)),
                            reads=[("w", c), ("h", c)], writes=["b%d" % (1 + p)])
                    pr_ = pre[a][b]
                    S.add("scalar", lambda e, p=p, pr_=pr_: e.copy(out=pr_[:, 3:515], in_=ps_p[p][:]),
                          reads=["b%d" % (1 + p)], writes=[("pre", a, b)])
                    S.add("vector", lambda e, pr_=pr_, a=a, b=b: e.tensor_scalar(
                        out=cacc[:], in0=pr_[:, 0:512], scalar1=cw[:, a, b, 0:1], scalar2=None, op0=ALU.mult),
                        reads=[("pre", a, b), "cw"], writes=["cacc"])
                    for tap in range(1, 4):
                        S.add("vector", lambda e, pr_=pr_, a=a, b=b, tap=tap: e.scalar_tensor_tensor(
                            out=cacc[:], in0=pr_[:, tap:tap + 512], scalar=cw[:, a, b, tap:tap + 1], in1=cacc[:],
                            op0=ALU.mult, op1=ALU.add), reads=[("pre", a, b), "cw", "cacc"], writes=["cacc"])
                    S.add("vector", lambda e, pr_=pr_: e.tensor_copy(out=pr_[:, 0:3], in_=pr_[:, 512:515]),
                          reads=[("pre", a, b)], writes=[("pre", a, b)])
                    po = post[a][b]
                    S.add("scalar", lambda e, po=po: e.activation(out=po[:], in_=cacc[:], func=AF.Silu),
                          reads=["cacc"], writes=[("post", a, b)])
                    if b < 2:
                        S.add("scalar", lambda e, po=po: e.activation(out=sqf[:], in_=po[:], func=AF.Square),
                              reads=[("post", a, b)], writes=["sqf"])
                        S.add("tensor", lambda e: e.matmul(ps_s[:], lhsT=onesf[:], rhs=sqf[:], start=True, stop=True),
                              reads=["onesf", "sqf"], writes=["b0"])
                        S.add("scalar", lambda e: e.activation(out=rsf[:], in_=ps_s[:], func=AF.Sqrt, bias=epsb[:, 0:1]),
                              reads=["b0", "epsb"], writes=["rsf"])
                        S.add("vector", lambda e: e.reciprocal(out=rsf[:], in_=rsf[:]), reads=["rsf"], writes=["rsf"])
                        sc = (128.0 ** -0.5) if b == 0 else 1.0
                        S.add("vector", lambda e, po=po, sc=sc: e.scalar_tensor_tensor(
                            out=po[:], in0=po[:], scalar=sc, in1=rsf[:], op0=ALU.mult, op1=ALU.mult),
                            reads=[("post", a, b), "rsf"], writes=[("post", a, b)])
            for dc in range(4):
                for a in range(NH):
                    ui = dc * NH + a
                    pb_ = 1 + (ui % 2)
                    pk = "b%d" % pb_
                    for c in range(8):
                        S.add("tensor", lambda e, c=c, a=a, dc=dc, pb_=pb_: e.matmul(
                            banks[pb_][:, 0:128], lhsT=h[:, c, dc * 128:(dc + 1) * 128], rhs=w[:, c, a, 384:512], start=(c == 0), stop=(c == 7)),
                            reads=[("w", c), ("h", c)], writes=[pk])
                    for c in range(8):
                        S.add("tensor", lambda e, c=c, a=a, dc=dc, pb_=pb_: e.matmul(
                            banks[pb_][:, 128:130], lhsT=h[:, c, dc * 128:(dc + 1) * 128], rhs=w[:, c, a, 512:514],
                            start=(c == 0), stop=(c == 7)), reads=[("w", c), ("h", c)], writes=[pk])
                    cl = cols[ui]
                    CK = ("cols", ui)
                    S.add("scalar", lambda e, ui=ui, pb_=pb_: e.activation(out=gsil[ui][:], in_=banks[pb_][:, 0:128], func=AF.Silu),
                          reads=[pk], writes=[("gsil", ui)])
                    S.add("scalar", lambda e, cl=cl, pb_=pb_: e.activation(out=cl[:, 0:1], in_=banks[pb_][:, 128:129], func=AF.Sigmoid),
                          reads=[pk], writes=[CK])
                    S.add("scalar", lambda e, cl=cl, a=a, pb_=pb_: e.activation(out=cl[:, 8:9], in_=banks[pb_][:, 129:130], func=AF.Exp,
                                                                                bias=dtbs[:, a:a + 1]),
                          reads=[pk, "dtbs"], writes=[CK])
                    S.add("vector", lambda e, cl=cl: e.tensor_scalar(out=cl[:, 8:9], in0=cl[:, 8:9], scalar1=1.0, scalar2=None, op0=ALU.add),
                          reads=[CK], writes=[CK])
                    S.add("scalar", lambda e, cl=cl: e.activation(out=cl[:, 8:9], in_=cl[:, 8:9], func=AF.Ln), reads=[CK], writes=[CK])
                    S.add("vector", lambda e, cl=cl, a=a: e.tensor_tensor(out=cl[:, 1:2], in0=cl[:, 8:9], in1=negA[:, a:a + 1], op=ALU.mult),
                          reads=[CK, "negA"], writes=[CK])

            def prep_unit(us, dc, a):
                ui = dc * NH + a
                XB, YB = banks[2 * us], banks[2 * us + 1]
                XK, YK = "b%d" % (2 * us), "b%d" % (2 * us + 1)
                XQ = lambda q: XB[:, q * 128:(q + 1) * 128]
                YQ = lambda q: YB[:, q * 128:(q + 1) * 128]
                cs = slice(dc * 128, (dc + 1) * 128)
                qT_, kT_, vT_ = post[a][0][:, cs], post[a][1][:, cs], post[a][2][:, cs]
                RP = [("post", a, 0), ("post", a, 1), ("post", a, 2)]
                cl = cols[ui]
                CK = ("cols", ui)
                U = lambda name: (name, us)
                P_ = lambda name: (name, ui)
                S.add("vector", lambda e: e.tensor_copy(out=grep[us][:], in_=cl[:, 1:2].to_broadcast([128, 128])), reads=[CK], writes=[U("grep")])
                S.add("gpsimd", lambda e: e.tensor_copy(out=brep[us][:], in_=cl[:, 0:1].to_broadcast([128, 128])), reads=[CK], writes=[U("brep")])
                yield
                mmf(XQ(0), grep[us][:], U2, [U("grep"), "cst"], [XK])
                mmf(XQ(1), grep[us][:], Ublk, [U("grep"), "cst"], [XK])
                mmf(XQ(2), brep[us][:], ident, [U("brep"), "cst"], [XK])
                mmf(XB[:, 384:386], U2, grep[us][:, 0:2], [U("grep"), "cst"], [XK])
                mmf(XB[:, 386:388], Ublk, grep[us][:, 0:2], [U("grep"), "cst"], [XK])
                mmf(YQ(0), kT_, kT_, RP, [YK])
                mmf(YQ(1), kT_, qT_, RP, [YK])
                mmf(YQ(2), kT_, ident, RP + ["cst"], [YK])
                mmf(YQ(3), vT_, ident, RP + ["cst"], [YK])
                yield
                S.add("vector", lambda e: e.tensor_copy(out=cl[:, 2:3], in_=XB[:, 384:385]), reads=[XK], writes=[CK])
                S.add("vector", lambda e: e.tensor_scalar(out=cl[:, 3:4], in0=XB[:, 384:385], scalar1=-1.0, scalar2=None, op0=ALU.mult),
                      reads=[XK], writes=[CK])
                S.add("vector", lambda e: e.tensor_tensor(out=argA[us][:], in0=XQ(0), in1=mTi, op=ALU.add), reads=[XK, "cst"], writes=[U("argA")])
                S.add("vector", lambda e: e.scalar_tensor_tensor(out=argB[us][:], in0=XQ(0), scalar=-1.0, in1=mS, op0=ALU.mult, op1=ALU.add),
                      reads=[XK, "cst"], writes=[U("argB")])
                yield
                S.add("scalar", lambda e: e.activation(out=cl[:, 4:5], in_=XB[:, 384:385], func=AF.Exp), reads=[XK], writes=[CK])
                S.add("scalar", lambda e: e.activation(out=cl[:, 5:6], in_=XB[:, 386:387], func=AF.Exp, bias=cl[:, 3:4]), reads=[XK, CK], writes=[CK])
                S.add("scalar", lambda e: e.activation(out=glc[ui][:], in_=XB[:, 128:256:64], func=AF.Exp), reads=[XK], writes=[P_("glc")])
                S.add("scalar", lambda e: e.activation(out=DT[us][:], in_=argA[us][:], func=AF.Exp, bias=cl[:, 3:4]), reads=[U("argA"), CK], writes=[U("DT")])
                S.add("scalar", lambda e: e.activation(out=Dm[us][:], in_=argB[us][:], func=AF.Exp, bias=cl[:, 2:3]), reads=[U("argB"), CK], writes=[U("Dm")])
                S.add("scalar", lambda e: e.activation(out=egr[us][:], in_=XQ(0), func=AF.Exp), reads=[XK], writes=[U("egr")])
                yield
                S.add("vector", lambda e: e.tensor_tensor(out=cl[:, 6:7], in0=cl[:, 4:5], in1=cl[:, 0:1], op=ALU.mult), reads=[CK], writes=[CK])
                S.add("vector", lambda e: e.tensor_scalar(out=cl[:, 7:8], in0=cl[:, 0:1], scalar1=-1.0, scalar2=None, op0=ALU.mult), reads=[CK], writes=[CK])
                S.add("vector", lambda e: e.scalar_tensor_tensor(out=NTm[us][:], in0=YQ(0), scalar=cl[:, 7:8], in1=Dm[us][:], op0=ALU.mult, op1=ALU.mult),
                      reads=[YK, CK, U("Dm")], writes=[U("NTm")])
                S.add("vector", lambda e: e.tensor_tensor(out=Nm[us][:], in0=YQ(0), in1=DT[us][:], op=ALU.mult), reads=[YK, U("DT")], writes=[U("Nm")])
                S.add("vector", lambda e: e.tensor_tensor(out=Nm[us][:], in0=Nm[us][:], in1=sT01, op=ALU.mult), reads=[U("Nm"), "cst"], writes=[U("Nm")])
                S.add("vector", lambda e: e.scalar_tensor_tensor(out=Nm[us][:], in0=XQ(2), scalar=-1.0, in1=Nm[us][:], op0=ALU.mult, op1=ALU.mult),
                      reads=[XK, U("Nm")], writes=[U("Nm")])
                S.add("vector", lambda e: e.tensor_tensor(out=attnT[ui][:], in0=YQ(1), in1=DT[us][:], op=ALU.mult), reads=[YK, U("DT")], writes=[P_("attnT")])
                S.add("vector", lambda e: e.tensor_scalar(out=vb[us][:], in0=YQ(3), scalar1=cl[:, 0:1], scalar2=None, op0=ALU.mult),
                      reads=[YK, CK], writes=[U("vb")])
                S.add("vector", lambda e: e.tensor_scalar(out=kbg[us][:], in0=YQ(2), scalar1=cl[:, 6:7], scalar2=None, op0=ALU.mult),
                      reads=[YK, CK], writes=[U("kbg")])
                S.add("vector", lambda e: e.tensor_scalar(out=kd[ui][:], in0=YQ(2), scalar1=cl[:, 5:6], scalar2=None, op0=ALU.mult),
                      reads=[YK, CK], writes=[P_("kd")])
                S.add("gpsimd", lambda e: e.tensor_tensor(out=qgT[ui][:], in0=qT_, in1=egr[us][:], op=ALU.mult), reads=RP + [U("egr")], writes=[P_("qgT")])
                S.add("vector", lambda e: e.tensor_tensor(out=Xm[us][:], in0=Nm[us][:], in1=ident, op=ALU.add), reads=[U("Nm"), "cst"], writes=[U("Xm")])
                yield
                P_cur, PT_cur = Nm[us][:], NTm[us][:]
                rdP, rdPT = [U("Nm")], [U("NTm")]
                for lvl in range(5):
                    sl = lvl % 2
                    last = (lvl == 4)
                    mmf(XQ(1), P_cur, PT_cur, rdP + rdPT, [XK])
                    if not last:
                        mmf(XQ(0), PT_cur, P_cur, rdP + rdPT, [XK])
                    yield
                    S.add("scalar", lambda e, sl=sl: e.copy(out=PTm[us][:, sl, :], in_=XQ(1)), reads=[XK], writes=[("PTm", us, sl)])
                    if not last:
                        S.add("scalar", lambda e, sl=sl: e.copy(out=Pm[us][:, sl, :], in_=XQ(0)), reads=[XK], writes=[("Pm", us, sl)])
                    yield
                    mmf(XQ(2), PTm[us][:, sl, :], Xm[us][:], [("PTm", us, sl), U("Xm")], [XK])
                    yield
                    S.add("vector", lambda e: e.tensor_tensor(out=Xm[us][:], in0=XQ(2), in1=Xm[us][:], op=ALU.add), reads=[XK, U("Xm")], writes=[U("Xm")])
                    P_cur, PT_cur = Pm[us][:, sl, :], PTm[us][:, sl, :]
                    rdP, rdPT = [("Pm", us, sl)], [("PTm", us, sl)]
                    yield
                mmf(YQ(0), Xm[us][:], vb[us][:], [U("Xm"), U("vb")], [YK])
                mmf(YQ(1), kbg[us][:], Xm[us][:], [U("Xm"), U("kbg")], [YK])
                yield
                S.add("scalar", lambda e: e.copy(out=u[ui][:], in_=YQ(0)), reads=[YK], writes=[P_("u")])
                S.add("scalar", lambda e: e.copy(out=wT[ui][:], in_=YQ(1)), reads=[YK], writes=[P_("wT")])

            units = [(dc, a) for dc in range(4) for a in range(NH)]
            for w0 in range(0, len(units), 4):
                gens = [prep_unit(us, dc, a) for us, (dc, a) in enumerate(units[w0:w0 + 4])]
                while gens:
                    for gen in list(gens):
                        try:
                            next(gen)
                        except StopIteration:
                            gens.remove(gen)
            for dc in range(4):
                for ch in range(2):
                    pr = slice(64 * ch, 64 * ch + 64)
                    for a in range(NH):
                        ui = dc * NH + a
                        P_ = lambda name: (name, ui)
                        SK = ("S", a)
                        RB = banks[6 + (a % 2)]
                        RK = "b%d" % (6 + (a % 2))
                        mmf(RB[pr, 0:128], wT[ui][:, pr], Sst[a][:], [P_("wT"), SK], [RK])
                        S.add("vector", lambda e, a=a, ui=ui, pr=pr, RB=RB: e.tensor_tensor(out=vnew[a][pr, :], in0=u[ui][pr, :], in1=RB[pr, 0:128],
                                                                                         op=ALU.subtract),
                              reads=[RK, P_("u")], writes=[("vnew", a)])
                        mmf(RB[pr, 128:256], qgT[ui][:, pr], Sst[a][:], [P_("qgT"), SK], [RK], start=True, stop=False)
                        mmf(RB[pr, 128:256], attnT[ui][pr, pr], vnew[a][pr, :], [P_("attnT"), ("vnew", a)], [RK], start=False, stop=True)
                        mmf(RB[:, 256:384], kd[ui][pr, :], vnew[a][pr, :], [P_("kd"), ("vnew", a)], [RK])
                        S.add("vector", lambda e, a=a, ui=ui, ch=ch, RB=RB: e.scalar_tensor_tensor(
                            out=Sst[a][:], in0=Sst[a][:], scalar=glc[ui][:, ch:ch + 1], in1=RB[:, 256:384], op0=ALU.mult, op1=ALU.add),
                            reads=[RK, SK, P_("glc")], writes=[SK])
                        OC = ("ocol", a, ch)
                        S.add("scalar", lambda e, a=a, pr=pr, RB=RB: e.activation(out=osq[a][pr, :], in_=RB[pr, 128:256], func=AF.Square,
                                                                                  accum_out=ocol[a][pr, 0:1]),
                              reads=[RK], writes=[("osq", a), OC])
                        S.add("scalar", lambda e, a=a, pr=pr: e.activation(out=ocol[a][pr, 1:2], in_=ocol[a][pr, 0:1], func=AF.Sqrt,
                                                                           bias=epsb[pr, 0:1], scale=1.0 / 128),
                              reads=[OC, "epsb"], writes=[OC])
                        S.add("vector", lambda e, a=a, pr=pr: e.reciprocal(out=ocol[a][pr, 1:2], in_=ocol[a][pr, 1:2]), reads=[OC], writes=[OC])
                        S.add("vector", lambda e, a=a, ui=ui, pr=pr, RB=RB: e.scalar_tensor_tensor(
                            out=osb[ui][pr, :], in0=RB[pr, 128:256], scalar=ocol[a][pr, 1:2], in1=onr[pr, :], op0=ALU.mult, op1=ALU.mult),
                            reads=[RK, OC, "onr"], writes=[("osb", ui, ch)])
                        S.add("gpsimd", lambda e, ui=ui, pr=pr: e.tensor_tensor(out=osb[ui][pr, :], in0=osb[ui][pr, :], in1=gsil[ui][pr, :], op=ALU.mult),
                              reads=[("osb", ui, ch), ("gsil", ui)], writes=[("osb", ui, ch)])
                for a in range(NH):
                    ui = dc * NH + a
                    r0 = tok0 + dc * 128
                    S.add("sync", lambda e, a=a, r0=r0, ui=ui: e.dma_start(out=oT[a, r0:r0 + 128, :], in_=osb[ui][:]),
                          reads=[("osb", ui, 0), ("osb", ui, 1)], dma=("o_out", ui))
        S.emit()
    return nc


def gdn_consts():
    i = np.arange(128)
    same = (i[:, None] // 64) == (i[None, :] // 64)
    ident = np.eye(128)
    U2 = (same & (i[:, None] <= i[None, :])).astype(np.float64)
    Ublk = same.astype(np.float64)
    mTi = np.where(same & (i[None, :] >= i[:, None]), 0.0, -1e4)
    mS = np.where(same & (i[None, :] < i[:, None]), 0.0, -1e4)
    sT01 = (same & (i[None, :] > i[:, None])).astype(np.float64)
    return np.ascontiguousarray(np.stack([ident, U2, Ublk, mTi, mS, sT01], axis=1).astype(np.float32))


def run_gdn(xT_full, B, S_seq, gv, w_in, conv_w, a_log, dt_bias, o_norm):
    NH = 8 * B // N_CORES
    key = ("gdn", S_seq, NH)
    if key not in _NC_CACHE:
        _NC_CACHE[key] = build_gdn(S_seq, NH)
    nc = _NC_CACHE[key]
    w_in = np.asarray(w_in, np.float32)
    conv_w = np.asarray(conv_w, np.float32)
    common = {"g": gain_layout(gv), "cst": gdn_consts(),
              "onrm": np.ascontiguousarray(np.broadcast_to(np.asarray(o_norm, np.float32).reshape(1, 128), (128, 128)))}
    in_maps = []
    per_b = N_CORES // B
    for c in range(N_CORES):
        b = c // per_b
        heads = [(c % per_b) * NH + i for i in range(NH)]
        wsel = np.stack([np.concatenate([w_in[:, hh * 128:(hh + 1) * 128], w_in[:, 1024 + hh * 128:1024 + (hh + 1) * 128],
                                         w_in[:, 2048 + hh * 128:2048 + (hh + 1) * 128],
                                         w_in[:, 3088 + hh * 128:3088 + (hh + 1) * 128],
                                         w_in[:, 3072 + hh:3072 + hh + 1], w_in[:, 3080 + hh:3080 + hh + 1]], axis=1) for hh in heads], axis=1)
        cwl = np.stack([np.stack([conv_w[:, q * 1024 + hh * 128:q * 1024 + (hh + 1) * 128].T for q in range(3)], axis=1)
                        for hh in heads], axis=1)
        al = np.broadcast_to(np.asarray(a_log, np.float32)[heads].reshape(1, NH), (128, NH))
        db = np.broadcast_to(np.asarray(dt_bias, np.float32)[heads].reshape(1, NH), (128, NH))
        in_maps.append(dict(common, xT=np.ascontiguousarray(xT_full[:, b * S_seq:(b + 1) * S_seq]),
                            w_in=np.ascontiguousarray(wsel), convw=np.ascontiguousarray(cwl),
                            alog=np.ascontiguousarray(al), dtb=np.ascontiguousarray(db)))
    res = run_bass_kernel_spmd(nc, in_maps, core_ids=list(range(N_CORES)))
    out = np.zeros((B, 1024, S_seq), np.float32)
    for c in range(N_CORES):
        b = c // per_b
        for i in range(NH):
            hh = (c % per_b) * NH + i
            out[b, hh * 128:(hh + 1) * 128, :] = res.results[c]["oT"][i].T
    return out


def build_oproj(T):
    nc = bass.Bass("TRN2", target_bir_lowering=False)
    with contextlib.ExitStack() as st:
        C = Ctx(nc, st)
        xT = C.din("xT", [D, T]); oin = C.din("oin", [D, T]); w_o = C.din("w_o", [D, D])
        oT = C.dout("oT", [D, T])
        xv = xT.rearrange("(c p) t -> p c t", p=128)
        iv = oin.rearrange("(c p) t -> p c t", p=128)
        ov = oT.rearrange("(c p) t -> p c t", p=128)
        wv = w_o.rearrange("(c p) n -> p c n", p=128)
        wo = C.sb("wo", [128, 8, D], BF16)
        xt = [C.sb("xt%d" % i, [128, 8, 512], F32) for i in range(2)]
        ob = [C.sb("ob%d" % i, [128, 8, 512], BF16) for i in range(2)]
        ps = [C.ps("ps%d" % i) for i in range(4)]
        S = Sched(nc)
        S.add("gpsimd", lambda e: e.dma_start(out=wo[:], in_=wv), writes=[("wo", c) for c in range(8)], dma="wo")
        k = 0
        for t in range(T // 512):
            b = t % 2
            ts = slice(t * 512, (t + 1) * 512)
            S.add("sync", lambda e, ts=ts, b=b: e.dma_start(out=xt[b][:], in_=xv[:, :, ts]), writes=[("xt", b, m) for m in range(8)], dma=("xin", b))
            S.add("gpsimd", lambda e, ts=ts, b=b: e.dma_start(out=ob[b][:], in_=iv[:, :, ts]), writes=[("ob", b, 0), ("ob", b, 1)],
                  dma=("oin", b))
            for m in range(8):
                p = k % 4
                k += 1
                for c in range(8):
                    S.add("tensor", lambda e, c=c, m=m, p=p, b=b: e.matmul(ps[p][:], lhsT=wo[:, c, m * 128:(m + 1) * 128], rhs=ob[b][:, c, :],
                                                                           start=(c == 0), stop=(c == 7)),
                          reads=[("wo", c), ("ob", b, c // 4)], writes=[("psp", p)])
                S.add("vector", lambda e, m=m, p=p, b=b: e.tensor_tensor(out=xt[b][:, m, :], in0=ps[p][:], in1=xt[b][:, m, :], op=ALU.add),
                      reads=[("psp", p), ("xt", b, m)], writes=[("xt", b, m)])
            S.add("sync", lambda e, ts=ts, b=b: e.dma_start(out=ov[:, :, ts], in_=xt[b][:]), reads=[("xt", b, m) for m in range(8)], dma=("xout", b))
        S.emit()
    return nc


def run_oproj(xT_full, oT_full, w_o):
    ntok = xT_full.shape[1]
    T = ntok // N_CORES
    key = ("oproj", T)
    if key not in _NC_CACHE:
        _NC_CACHE[key] = build_oproj(T)
    nc = _NC_CACHE[key]
    w_o = np.ascontiguousarray(w_o, np.float32)
    in_maps = [{"xT": np.ascontiguousarray(xT_full[:, c * T:(c + 1) * T]), "oin": np.ascontiguousarray(oT_full[:, c * T:(c + 1) * T]),
                "w_o": w_o} for c in range(N_CORES)]
    res = run_bass_kernel_spmd(nc, in_maps, core_ids=list(range(N_CORES)))
    return np.concatenate([r["oT"] for r in res.results], axis=1)


def kernel(x, ffn1_norm, ffn1_w_in, ffn1_w_out, mix_norm, ffn2_norm, ffn2_w_in, ffn2_w_out,
           nsa_w_in, nsa_w_out, nsa_q_norm, nsa_k_norm, nsa_cmp_pos, nsa_cmp_w1, nsa_cmp_w2,
           diff_w_in, diff_w_out, diff_q_norm, diff_k_norm, diff_lambda, diff_subln,
           gdn_w_in, gdn_w_out, gdn_conv_w, gdn_a_log, gdn_dt_bias, gdn_o_norm,
           swa_w_in, swa_w_out, swa_q_norm, swa_k_norm, swa_sinks):
    import math
    x = np.asarray(x, np.float32)
    B, S_seq, _ = x.shape
    A = lambda v: np.asarray(v, np.float32)
    xT = np.ascontiguousarray(x.reshape(B * S_seq, D).T)
    bs = lambda o: np.ascontiguousarray(o.transpose(1, 0, 2).reshape(D, B * S_seq))
    depth = A(ffn1_norm).shape[0]
    for layer in range(depth):
        kind, j = layer % 4, layer // 4
        xT = run_ffn(xT, A(ffn1_norm)[layer], A(ffn1_w_in)[layer], A(ffn1_w_out)[layer])
        gm = A(mix_norm)[layer]
        if kind == 0:
            o = run_nsa(xT, B, S_seq, gm, A(nsa_w_in)[j], A(nsa_q_norm)[j], A(nsa_k_norm)[j], A(nsa_cmp_pos)[j],
                        A(nsa_cmp_w1)[j], A(nsa_cmp_w2)[j])
            xT = run_oproj(xT, bs(o), A(nsa_w_out)[j])
        elif kind == 1:
            lam_init = 0.8 - 0.6 * math.exp(-0.3 * layer)
            o = run_diff(xT, B, S_seq, gm, A(diff_w_in)[j], A(diff_q_norm)[j], A(diff_k_norm)[j], A(diff_lambda)[j],
                         A(diff_subln)[j], lam_init)
            xT = run_oproj(xT, bs(o), A(diff_w_out)[j])
        elif kind == 2:
            o = run_gdn(xT, B, S_seq, gm, A(gdn_w_in)[j], A(gdn_conv_w)[j], A(gdn_a_log)[j], A(gdn_dt_bias)[j], A(gdn_o_norm)[j])
            xT = run_oproj(xT, bs(o), A(gdn_w_out)[j])
        else:
            xT = run_swa(xT, S_seq, gm, A(swa_w_in)[j], A(swa_w_out)[j], A(swa_q_norm)[j], A(swa_k_norm)[j], A(swa_sinks)[j])
        xT = run_ffn(xT, A(ffn2_norm)[layer], A(ffn2_w_in)[layer], A(ffn2_w_out)[layer])
    return np.ascontiguousarray(xT.T).reshape(B, S_seq, D).astype(np.float32)
```
